# Optimizing a Trainium2 kernel written in Bass

```python
import jax
import jax.numpy as jnp
from jax import lax
import numpy as np

D_MODEL = 1024
BATCH = 2
SEQ = 8192
DEPTH = 4
DEC_BATCH = 32
DEC_SEQ = 4
PAST_LEN = 8192
PAGE_SIZE = 128

N_MIXERS = 3
NORM_EPS = 1e-6
D_FF = -(-(8 * D_MODEL) // (3 * 256)) * 256

SSM_D_INNER = 2 * D_MODEL
SSM_HEAD_DIM = 64
SSM_N_HEADS = SSM_D_INNER // SSM_HEAD_DIM
SSM_N_GROUPS = 4
SSM_D_STATE = 128
SSM_CONV = 4
SSM_CHUNK = 128
SSM_CONV_DIM = SSM_D_INNER + 2 * SSM_N_GROUPS * SSM_D_STATE
SSM_IN_DIM = SSM_D_INNER + SSM_CONV_DIM + SSM_N_HEADS

GLA_N_HEADS = 4
GLA_DK = D_MODEL // 2
GLA_DV = D_MODEL
GLA_HEAD_K = GLA_DK // GLA_N_HEADS
GLA_HEAD_V = GLA_DV // GLA_N_HEADS
GLA_GATE_RANK = 16
GLA_GATE_NORM = 16.0
GLA_CHUNK = 32
GLA_IN_DIM = 2 * GLA_DK + 2 * GLA_DV + GLA_GATE_RANK

ATT_GROUPS = ((128, 1), (512, 4), (2048, 16))
ATT_HEADS_PER_GROUP = 4
ATT_HEAD_DIM = 64
ATT_N_HEADS = ATT_HEADS_PER_GROUP * len(ATT_GROUPS)
ATT_ROT_DIM = ATT_HEAD_DIM // 4
ROPE_THETA = 500000.0
ATT_BLOCK = 128

N_SSM_LAYERS = len(range(0, DEPTH, N_MIXERS))
N_GLA_LAYERS = len(range(1, DEPTH, N_MIXERS))
N_ATT_LAYERS = len(range(2, DEPTH, N_MIXERS))

kernel_name = 'hybrid_ssd_gla_dilated_swa_step'


def rmsnorm(x, g):
    xf = x.astype(jnp.float32)
    inv = lax.rsqrt(jnp.mean(xf * xf, axis=-1, keepdims=True) + NORM_EPS)
    return (xf * inv).astype(x.dtype) * g


def swiglu(x, w_gate, w_up, w_down):
    return (jax.nn.silu(x @ w_gate) * (x @ w_up)) @ w_down


def causal_dwconv(x, buf, w, b):
    xp = jnp.concatenate([buf, x], axis=1)
    L = x.shape[1]
    y = b
    for t in range(SSM_CONV):
        y = y + xp[:, t:t + L] * w[t]
    return y, xp[:, -(SSM_CONV - 1):]


def segsum(a):
    T = a.shape[-1]
    cs = jnp.cumsum(a, axis=-1)
    diff = cs[..., :, None] - cs[..., None, :]
    lower = jnp.arange(T)[:, None] >= jnp.arange(T)[None, :]
    return jnp.where(lower, diff, -jnp.inf)


def ssd_scan(x, dt, A, Bm, Cm, h0, chunk):
    b, l, H, P = x.shape
    G, N = Bm.shape[-2:]
    hg = H // G
    c = l // chunk
    xdt = (x * dt[..., None].astype(x.dtype)).reshape(b, c, chunk, G, hg, P)
    a = (A * dt).reshape(b, c, chunk, G, hg).transpose(0, 3, 4, 1, 2)
    Bc = Bm.reshape(b, c, chunk, G, N)
    Cc = Cm.reshape(b, c, chunk, G, N)
    a_cs = jnp.cumsum(a, axis=-1)
    Ldec = jnp.exp(segsum(a)).astype(x.dtype)
    CB = jnp.einsum('bcqgn,bcsgn->bgcqs', Cc, Bc)
    y_diag = jnp.einsum('bgcqs,bghcqs,bcsghp->bcqghp', CB, Ldec, xdt)
    decay_st = jnp.exp(a_cs[..., -1:] - a_cs).astype(x.dtype)
    st = jnp.einsum('bcqgn,bghcq,bcqghp->bcghpn', Bc, decay_st, xdt)
    st = jnp.concatenate([h0.reshape(b, 1, G, hg, P, N).astype(st.dtype), st], axis=1)
    a_last = jnp.pad(a_cs[..., -1], ((0, 0), (0, 0), (0, 0), (1, 0)))
    dchunk = jnp.exp(segsum(a_last)).astype(x.dtype)
    new_st = jnp.einsum('bghzc,bcghpn->bzghpn', dchunk, st)
    st_in, h_final = new_st[:, :-1], new_st[:, -1]
    y_off = jnp.einsum('bcqgn,bcghpn,bghcq->bcqghp', Cc, st_in, jnp.exp(a_cs).astype(x.dtype))
    y = (y_diag + y_off).reshape(b, l, H, P)
    return y, h_final.reshape(b, H, P, N)


def mamba2_mixer(h, conv_buf, ssm_state, w_in, conv_w, conv_b, dt_bias, a_log, d_skip, norm_w, w_out, chunk):
    b, l, _ = h.shape
    z, xbc, dt = jnp.split(h @ w_in, [SSM_D_INNER, SSM_D_INNER + SSM_CONV_DIM], axis=-1)
    xbc, conv_new = causal_dwconv(xbc, conv_buf, conv_w, conv_b)
    xbc = jax.nn.silu(xbc)
    xs, Bm, Cm = jnp.split(xbc, [SSM_D_INNER, SSM_D_INNER + SSM_N_GROUPS * SSM_D_STATE], axis=-1)
    xs = xs.reshape(b, l, SSM_N_HEADS, SSM_HEAD_DIM)
    Bm = Bm.reshape(b, l, SSM_N_GROUPS, SSM_D_STATE)
    Cm = Cm.reshape(b, l, SSM_N_GROUPS, SSM_D_STATE)
    dt = jax.nn.softplus((dt + dt_bias).astype(jnp.float32))
    A = -jnp.exp(a_log.astype(jnp.float32))
    y, ssm_new = ssd_scan(xs, dt, A, Bm, Cm, ssm_state, chunk)
    y = (y + xs * d_skip[:, None]).reshape(b, l, SSM_D_INNER) * jax.nn.silu(z)
    y = rmsnorm(y.reshape(b, l, SSM_N_GROUPS, -1), norm_w.reshape(SSM_N_GROUPS, -1)).reshape(b, l, SSM_D_INNER)
    return y @ w_out, conv_new, ssm_new.astype(ssm_state.dtype)


def gla_chunk_scan(q, k, v, log_a, s0, chunk):
    b, l, H, _ = q.shape
    c = l // chunk
    causal = jnp.arange(chunk)[:, None] >= jnp.arange(chunk)[None, :]

    def to_chunks(t):
        return t.reshape(b, c, chunk, H, t.shape[-1]).swapaxes(0, 1)

    def step(S, inp):
        qc, kc, vc, gc = inp
        bcum = jnp.cumsum(gc, axis=1)
        q_t = qc * jnp.exp(bcum).astype(qc.dtype)
        k_t = kc * jnp.exp(-bcum).astype(kc.dtype)
        att = jnp.where(causal, jnp.einsum('bqhk,bshk->bhqs', q_t, k_t), 0.0)
        o = jnp.einsum('bhqs,bshv->bqhv', att, vc) + jnp.einsum('bqhk,bhkv->bqhv', q_t, S)
        b_last = bcum[:, -1]
        k_dec = kc * jnp.exp(b_last[:, None] - bcum).astype(kc.dtype)
        S_new = S * jnp.exp(b_last)[..., None].astype(S.dtype) + jnp.einsum('bqhk,bqhv->bhkv', k_dec, vc)
        return S_new.astype(S.dtype), o

    S, o = lax.scan(step, s0, (to_chunks(q), to_chunks(k), to_chunks(v), to_chunks(log_a)))
    return o.swapaxes(0, 1).reshape(b, l, H, v.shape[-1]), S


def gla_mixer(h, s0, w_in, w_gate, gate_bias, norm_w, w_out, chunk):
    b, l, _ = h.shape
    q, k, v, r, g_low = jnp.split(h @ w_in, [GLA_DK, 2 * GLA_DK, 2 * GLA_DK + GLA_DV, 2 * GLA_DK + 2 * GLA_DV], axis=-1)
    log_a = jax.nn.log_sigmoid((g_low @ w_gate + gate_bias).astype(jnp.float32)) / GLA_GATE_NORM
    hk = lambda t: t.reshape(b, l, GLA_N_HEADS, GLA_HEAD_K)
    o, s_new = gla_chunk_scan(hk(q) * GLA_HEAD_K ** -0.5, hk(k), v.reshape(b, l, GLA_N_HEADS, GLA_HEAD_V),
                              hk(log_a), s0, chunk)
    o = rmsnorm(o, norm_w).reshape(b, l, GLA_DV) * jax.nn.silu(r)
    return o @ w_out, s_new


def rope_partial(x, pos):
    inv_freq = ROPE_THETA ** (-jnp.arange(0, ATT_ROT_DIM, 2, dtype=jnp.float32) / ATT_ROT_DIM)
    ang = pos.astype(jnp.float32)[:, None] * inv_freq
    cos, sin = jnp.cos(ang)[:, None, :], jnp.sin(ang)[:, None, :]
    half = ATT_ROT_DIM // 2
    x1, x2, rest = x[..., :half], x[..., half:ATT_ROT_DIM], x[..., ATT_ROT_DIM:]
    rot = jnp.concatenate([x1 * cos - x2 * sin, x2 * cos + x1 * sin], axis=-1).astype(x.dtype)
    return jnp.concatenate([rot, rest], axis=-1)


def masked_softmax_stats(s, valid):
    s = jnp.where(valid, s, -jnp.inf)
    mx = jnp.max(s, axis=-1, keepdims=True)
    p = jnp.exp(s - mx)
    den = jnp.sum(p, axis=-1, keepdims=True)
    return p / den, (mx + jnp.log(den))[..., 0]


def dilated_window_prompt(q, kv, window, dilation):
    b, l, hg, dh = q.shape
    m = l // dilation
    span = window // dilation
    N = b * dilation

    def by_residue(t):
        t = t.reshape((b, m, dilation) + t.shape[2:])
        return jnp.moveaxis(t, 2, 1).reshape((N, m) + t.shape[3:])

    def pad_seq(t, front, back):
        return jnp.pad(t, [(0, 0), (front, back)] + [(0, 0)] * (t.ndim - 2))

    qs, kvs = by_residue(q), by_residue(kv)
    nb = -(-m // ATT_BLOCK)
    mp = nb * ATT_BLOCK
    qb = pad_seq(qs, 0, mp - m).reshape(N, nb, ATT_BLOCK, hg, dh)
    kvp = pad_seq(kvs, ATT_BLOCK, mp - m).reshape(N, nb + 1, ATT_BLOCK, 2, hg, dh)
    kvb = jnp.concatenate([kvp[:, :-1], kvp[:, 1:]], axis=2)
    s = jnp.einsum('nbqhd,nbkhd->nbhqk', qb, kvb[:, :, :, 0]).astype(jnp.float32) * ATT_HEAD_DIM ** -0.5
    qi = jnp.arange(ATT_BLOCK)[:, None]
    kj = jnp.arange(2 * ATT_BLOCK)[None, :]
    dist = ATT_BLOCK + qi - kj
    key_pos = jnp.arange(nb)[:, None, None] * ATT_BLOCK - ATT_BLOCK + kj[None]
    valid = (dist >= 0)[None] & (dist <= span)[None] & (key_pos >= 0)
    p, lse = masked_softmax_stats(s, valid[None, :, None])
    o = jnp.einsum('nbhqk,nbkhd->nbqhd', p.astype(q.dtype), kvb[:, :, :, 1])
    o = o.reshape(N, mp, hg, dh)[:, :m].reshape(b, dilation, m, hg, dh)
    o = jnp.moveaxis(o, 1, 2).reshape(b, l, hg, dh)
    lse = lse.transpose(0, 1, 3, 2).reshape(N, mp, hg)[:, :m].reshape(b, dilation, m, hg)
    lse = jnp.moveaxis(lse, 1, 2).reshape(b, l, hg)
    return o, lse


def dilated_window_sample(q, kv_new, kv_buf, window, dilation):
    L = kv_buf.shape[1]
    s_len = q.shape[1]
    kvx = jnp.concatenate([kv_buf, kv_new], axis=1)
    span = window // dilation
    idx = L + jnp.arange(s_len)[:, None] - dilation * jnp.arange(span + 1)[None, :]
    valid = idx >= 0
    g = kvx[:, jnp.maximum(idx, 0)]
    sc = jnp.einsum('bshd,bsjhd->bshj', q, g[:, :, :, 0]).astype(jnp.float32) * ATT_HEAD_DIM ** -0.5
    p, lse = masked_softmax_stats(sc, valid[None, :, None, :])
    o = jnp.einsum('bshj,bsjhd->bshd', p.astype(q.dtype), g[:, :, :, 1])
    keep = min(window, L + s_len)
    return o, lse, kvx[:, -keep:]


def dilated_attn_mixer(h, pos, kv_bufs, w_qkv, w_out):
    b, l, _ = h.shape
    qkv = (h @ w_qkv).reshape(b, l, 3, ATT_N_HEADS, ATT_HEAD_DIM)
    q = rope_partial(qkv[:, :, 0], pos)
    k = rope_partial(qkv[:, :, 1], pos)
    v = qkv[:, :, 2]
    outs, lses, new_kv = [], [], []
    for gi, (window, dilation) in enumerate(ATT_GROUPS):
        hs = slice(gi * ATT_HEADS_PER_GROUP, (gi + 1) * ATT_HEADS_PER_GROUP)
        kv = jnp.stack([k[:, :, hs], v[:, :, hs]], axis=2)
        if kv_bufs is None:
            o, lse = dilated_window_prompt(q[:, :, hs], kv, window, dilation)
            new_kv.append(kv[:, -min(window, l):])
        else:
            o, lse, kv_upd = dilated_window_sample(q[:, :, hs], kv, kv_bufs[gi], window, dilation)
            new_kv.append(kv_upd)
        outs.append(o)
        lses.append(lse)
    alpha = jax.nn.softmax(jnp.stack(lses, axis=0), axis=0)
    mixed = jnp.concatenate([o * alpha[gi][..., None].astype(o.dtype) for gi, o in enumerate(outs)], axis=2)
    return mixed.reshape(b, l, ATT_N_HEADS * ATT_HEAD_DIM) @ w_out, new_kv


def setup_inputs(seed: int = 0) -> dict:
    key = jax.random.key(seed)
    k = jax.random.split(key, 32)
    nrm = lambda kk, shape, scale: jax.random.normal(kk, shape, jnp.float32) * scale
    u = jax.random.uniform(k[16], (N_SSM_LAYERS, SSM_N_HEADS), jnp.float32)
    dt0 = jnp.exp(u * (jnp.log(0.1) - jnp.log(0.001)) + jnp.log(0.001))
    kv_shape = lambda w: (N_ATT_LAYERS, DEC_BATCH, min(w, PAST_LEN), 2, ATT_HEADS_PER_GROUP, ATT_HEAD_DIM)
    return {
        'x_prompt': nrm(k[0], (BATCH, SEQ, D_MODEL), 1.0),
        'x_sample': nrm(k[1], (DEC_BATCH, DEC_SEQ, D_MODEL), 1.0),
        'state_ssm': nrm(k[2], (N_SSM_LAYERS, DEC_BATCH, SSM_N_HEADS, SSM_HEAD_DIM, SSM_D_STATE), 0.1),
        'state_ssm_conv': nrm(k[3], (N_SSM_LAYERS, DEC_BATCH, SSM_CONV - 1, SSM_CONV_DIM), 1.0),
        'state_gla': nrm(k[4], (N_GLA_LAYERS, DEC_BATCH, GLA_N_HEADS, GLA_HEAD_K, GLA_HEAD_V), 0.1),
        'cache_kv_g0': nrm(k[5], kv_shape(ATT_GROUPS[0][0]), 1.0),
        'cache_kv_g1': nrm(k[6], kv_shape(ATT_GROUPS[1][0]), 1.0),
        'cache_kv_g2': nrm(k[7], kv_shape(ATT_GROUPS[2][0]), 1.0),
        'norm_mix': 1.0 + nrm(k[8], (DEPTH, D_MODEL), 0.02),
        'norm_ffn': 1.0 + nrm(k[9], (DEPTH, D_MODEL), 0.02),
        'ffn_gate': nrm(k[10], (DEPTH, D_MODEL, D_FF), D_MODEL ** -0.5),
        'ffn_up': nrm(k[11], (DEPTH, D_MODEL, D_FF), D_MODEL ** -0.5),
        'ffn_down': nrm(k[12], (DEPTH, D_FF, D_MODEL), D_FF ** -0.5),
        'ssm_w_in': nrm(k[13], (N_SSM_LAYERS, D_MODEL, SSM_IN_DIM), D_MODEL ** -0.5),
        'ssm_conv_w': nrm(k[14], (N_SSM_LAYERS, SSM_CONV, SSM_CONV_DIM), SSM_CONV ** -0.5),
        'ssm_conv_b': nrm(k[15], (N_SSM_LAYERS, SSM_CONV_DIM), 0.01),
        'ssm_dt_bias': dt0 + jnp.log(-jnp.expm1(-dt0)),
        'ssm_a_log': jnp.log(jax.random.uniform(k[17], (N_SSM_LAYERS, SSM_N_HEADS), jnp.float32, 1.0, 16.0)),
        'ssm_d': 1.0 + nrm(k[18], (N_SSM_LAYERS, SSM_N_HEADS), 0.1),
        'ssm_norm': 1.0 + nrm(k[19], (N_SSM_LAYERS, SSM_D_INNER), 0.02),
        'ssm_w_out': nrm(k[20], (N_SSM_LAYERS, SSM_D_INNER, D_MODEL), SSM_D_INNER ** -0.5),
        'gla_w_in': nrm(k[21], (N_GLA_LAYERS, D_MODEL, GLA_IN_DIM), D_MODEL ** -0.5),
        'gla_w_gate': nrm(k[22], (N_GLA_LAYERS, GLA_GATE_RANK, GLA_DK), GLA_GATE_RANK ** -0.5),
        'gla_gate_bias': nrm(k[23], (N_GLA_LAYERS, GLA_DK), 0.1),
        'gla_norm': 1.0 + nrm(k[24], (N_GLA_LAYERS, GLA_HEAD_V), 0.02),
        'gla_w_out': nrm(k[25], (N_GLA_LAYERS, GLA_DV, D_MODEL), GLA_DV ** -0.5),
        'att_w_qkv': nrm(k[26], (N_ATT_LAYERS, D_MODEL, 3 * ATT_N_HEADS * ATT_HEAD_DIM), D_MODEL ** -0.5),
        'att_w_out': nrm(k[27], (N_ATT_LAYERS, ATT_N_HEADS * ATT_HEAD_DIM, D_MODEL), (ATT_N_HEADS * ATT_HEAD_DIM) ** -0.5),
        'norm_final': 1.0 + nrm(k[28], (D_MODEL,), 0.02),
    }


def reference(x_prompt, x_sample, state_ssm, state_ssm_conv, state_gla, cache_kv_g0, cache_kv_g1, cache_kv_g2,
              norm_mix, norm_ffn, ffn_gate, ffn_up, ffn_down,
              ssm_w_in, ssm_conv_w, ssm_conv_b, ssm_dt_bias, ssm_a_log, ssm_d, ssm_norm, ssm_w_out,
              gla_w_in, gla_w_gate, gla_gate_bias, gla_norm, gla_w_out,
              att_w_qkv, att_w_out, norm_final):
    bp, lp = x_prompt.shape[:2]
    ls = x_sample.shape[1]
    pos_p = jnp.arange(lp)
    pos_s = PAST_LEN + jnp.arange(ls)
    xp, xs = x_prompt, x_sample
    att_caches = (cache_kv_g0, cache_kv_g1, cache_kv_g2)
    ssm_p, ssm_s, conv_p, conv_s, gla_p, gla_s = [], [], [], [], [], []
    kv_p, kv_s = [[], [], []], [[], [], []]
    for i in range(DEPTH):
        m, j = i % N_MIXERS, i // N_MIXERS
        hp = rmsnorm(xp, norm_mix[i])
        hs = rmsnorm(xs, norm_mix[i])
        if m == 0:
            w = (ssm_w_in[j], ssm_conv_w[j], ssm_conv_b[j], ssm_dt_bias[j], ssm_a_log[j], ssm_d[j], ssm_norm[j], ssm_w_out[j])
            conv0 = jnp.zeros((bp, SSM_CONV - 1, SSM_CONV_DIM), hp.dtype)
            h0 = jnp.zeros((bp, SSM_N_HEADS, SSM_HEAD_DIM, SSM_D_STATE), hp.dtype)
            op, cp, sp = mamba2_mixer(hp, conv0, h0, *w, chunk=min(SSM_CHUNK, lp))
            os_, cs, ss = mamba2_mixer(hs, state_ssm_conv[j], state_ssm[j], *w, chunk=ls)
            conv_p.append(cp); conv_s.append(cs); ssm_p.append(sp); ssm_s.append(ss)
        elif m == 1:
            w = (gla_w_in[j], gla_w_gate[j], gla_gate_bias[j], gla_norm[j], gla_w_out[j])
            s0 = jnp.zeros((bp, GLA_N_HEADS, GLA_HEAD_K, GLA_HEAD_V), hp.dtype)
            op, sp = gla_mixer(hp, s0, *w, chunk=min(GLA_CHUNK, lp))
            os_, ss = gla_mixer(hs, state_gla[j], *w, chunk=ls)
            gla_p.append(sp); gla_s.append(ss)
        else:
            op, nkp = dilated_attn_mixer(hp, pos_p, None, att_w_qkv[j], att_w_out[j])
            os_, nks = dilated_attn_mixer(hs, pos_s, [c[j] for c in att_caches], att_w_qkv[j], att_w_out[j])
            for gi in range(len(ATT_GROUPS)):
                kv_p[gi].append(nkp[gi]); kv_s[gi].append(nks[gi])
        xp = xp + op
        xs = xs + os_
        xp = xp + swiglu(rmsnorm(xp, norm_ffn[i]), ffn_gate[i], ffn_up[i], ffn_down[i])
        xs = xs + swiglu(rmsnorm(xs, norm_ffn[i]), ffn_gate[i], ffn_up[i], ffn_down[i])
    y_prompt = rmsnorm(xp, norm_final)
    y_sample = rmsnorm(xs, norm_final)
    return (y_prompt, y_sample,
            jnp.stack(ssm_p), jnp.stack(ssm_s), jnp.stack(conv_p), jnp.stack(conv_s),
            jnp.stack(gla_p), jnp.stack(gla_s),
            jnp.stack(kv_p[0]), jnp.stack(kv_s[0]), jnp.stack(kv_p[1]), jnp.stack(kv_s[1]),
            jnp.stack(kv_p[2]), jnp.stack(kv_s[2]))
```

```python
import contextlib
import numpy as np
import concourse.bass as bass
import concourse.mybir as mybir
from concourse.bass_utils import run_bass_kernel_spmd

F32 = mybir.dt.float32
F32R = mybir.dt.float32r
BF16 = mybir.dt.bfloat16
AF = mybir.ActivationFunctionType
ALU = mybir.AluOpType
AX = mybir.AxisListType

ENGS = ("pe", "act", "dve", "pool", "sp")
SAME_ENGINE_SYNC = True

D = 1024
DFF = 2816
EPS = 1e-6
DI, HD, NH, NG, NS, CONVK, CD, SIN = 2048, 64, 32, 4, 128, 4, 3072, 5152
GH, GDK, GDV, GHK, GHV, GRANK, GIN = 4, 512, 1024, 128, 256, 16, 3088
AGROUPS = ((128, 1), (512, 4), (2048, 16))
AHPG, ADH, ANH, AROT = 4, 64, 12, 16
ROPE_THETA = 500000.0


class Cfg:
    def __init__(self, seq=8192, tp=256, ns_core=4, dec=4, past=8192, depth=4, ncores=8, mixers=True):
        self.SEQ, self.TP, self.NSC, self.DEC, self.PAST, self.DEPTH, self.NCORES = seq, tp, ns_core, dec, past, depth, ncores
        self.mixers = mixers
        self.NT = seq // tp


class T:
    def __init__(self, t, name):
        self.t, self.name, self.w, self.r = t, name, {}, {}

    def __getitem__(self, k):
        return self.t[k]


def TV(parent, ap, name="view"):
    v = T(ap, name)
    v.w, v.r = parent.w, parent.r
    return v


class DSem:
    def __init__(self, sem, name):
        self.sem, self.name, self.total = sem, name, 0


class KB:
    def __init__(self, nc, stack):
        self.nc, self.stack = nc, stack
        self.q = {e: [] for e in ENGS}
        self.cnt = {e: 0 for e in ENGS}
        self.sem = {e: stack.enter_context(nc.semaphore("s_" + e)) for e in ENGS if e != "sp"}
        self.waited = {}
        self.n = 0

    def sbuf(self, shape, dtype, name=None):
        self.n += 1
        name = name or f"sb{self.n}"
        return T(self.stack.enter_context(self.nc.sbuf_tensor(name, list(shape), dtype)), name)

    def psum(self, shape, dtype=F32, name=None):
        self.n += 1
        name = name or f"ps{self.n}"
        return T(self.stack.enter_context(self.nc.psum_tensor(name, list(shape), dtype)), name)

    def dram(self, name, shape, dtype, kind="Internal"):
        return T(self.nc.dram_tensor(name, list(shape), dtype, kind=kind).ap(), name)

    def dsem(self, name=None):
        self.n += 1
        name = name or f"d{self.n}"
        return DSem(self.stack.enter_context(self.nc.semaphore("ds_" + name)), name)

    def _deps(self, eng, reads, writes):
        deps = {}
        for t in reads:
            for k, c in t.w.items():
                if deps.get(k, 0) < c:
                    deps[k] = c
        for t in writes:
            for dd in (t.w, t.r):
                for k, c in dd.items():
                    if deps.get(k, 0) < c:
                        deps[k] = c
        out = []
        for k, c in deps.items():
            if k == eng and (eng == "pe" or not SAME_ENGINE_SYNC):
                continue
            if self.waited.get((eng, k), 0) >= c:
                continue
            self.waited[(eng, k)] = c
            out.append(((k.sem if isinstance(k, DSem) else self.sem[k]), c))
        return out

    def op(self, eng, fn, r=(), w=()):
        sems = self._deps(eng, r, w)
        self.cnt[eng] += 1
        c = self.cnt[eng]
        mysem = self.sem[eng]

        def emit(e, sems=sems, fn=fn, mysem=mysem):
            for s, v in sems:
                e.wait_ge(s, v)
            fn(e).then_inc(mysem, 1)

        self.q[eng].append(emit)
        for t in r:
            t.r[eng] = c
        for t in w:
            t.w[eng] = c

    def dma(self, qeng, dsem, out, in_, r=(), w=(), **kw):
        sems = self._deps(qeng, r, w)
        if dsem.total > 0 and self.waited.get((qeng, dsem), 0) < dsem.total:
            self.waited[(qeng, dsem)] = dsem.total
            sems = sems + [(dsem.sem, dsem.total)]
        dsem.total += 16
        c = dsem.total

        def emit(e, sems=sems, out=out, in_=in_, kw=kw, ds=dsem.sem):
            for s, v in sems:
                e.wait_ge(s, v)
            e.dma_start(out=out, in_=in_, **kw).then_inc(ds, 16)

        self.q[qeng].append(emit)
        for t in r:
            t.r[dsem] = c
        for t in w:
            t.w[dsem] = c

    def wait_all(self, eng, dsems):
        sems = [(d.sem, d.total) for d in dsems if d.total > 0]

        def emit(e, sems=sems):
            for s, v in sems:
                e.wait_ge(s, v)

        self.q[eng].append(emit)

    def finish(self):
        with self.nc.Block() as block:
            @block.tensor
            def _(e):
                for f in self.q["pe"]:
                    f(e)

            @block.scalar
            def _(e):
                for f in self.q["act"]:
                    f(e)

            @block.vector
            def _(e):
                for f in self.q["dve"]:
                    f(e)

            @block.gpsimd
            def _(e):
                for f in self.q["pool"]:
                    f(e)

            @block.sync
            def _(e):
                for f in self.q["sp"]:
                    f(e)


class Ring:
    def __init__(self, bufs):
        self.bufs, self.i = bufs, 0

    def next(self):
        b = self.bufs[self.i % len(self.bufs)]
        self.i += 1
        return b


def _bw(K):
    kc = K // 128
    return 512 if kc <= 8 else (256 if kc <= 16 else 128)


class WPack:
    def __init__(self):
        self.blocks = {}
        self.tot = 0
        self.parts = []

    def add(self, key, W):
        K, N = W.shape
        kc, bw = K // 128, _bw(K)
        nblk = -(-N // bw)
        Wp = np.zeros((K, nblk * bw), np.float32)
        Wp[:, :N] = W
        til = Wp.reshape(kc, 128, nblk, bw).transpose(2, 1, 0, 3).reshape(nblk, 128, kc * bw)
        lst = []
        for b in range(nblk):
            lst.append((self.tot, kc, bw, min(bw, N - b * bw)))
            self.parts.append(til[b])
            self.tot += kc * bw
        self.blocks[key] = lst

    def add_meta(self, key, K, N):
        kc, bw = K // 128, _bw(K)
        nblk = -(-N // bw)
        lst = []
        for b in range(nblk):
            lst.append((self.tot, kc, bw, min(bw, N - b * bw)))
            self.tot += kc * bw
        self.blocks[key] = lst

    def cat(self):
        pad = (-self.tot) % 2048
        arr = np.concatenate(self.parts + [np.zeros((128, pad), np.float32)], axis=1)
        return np.ascontiguousarray(arr)


W_SHAPES = {"ssm_in": (D, SIN), "ssm_out": (DI, D), "gla_in": (D, GIN), "gla_out": (GDV, D),
            "att_qkv": (D, 3 * ANH * ADH), "att_out": (ANH * ADH, D),
            "ffn_gate": (D, DFF), "ffn_up": (D, DFF), "ffn_down": (DFF, D)}


def weight_plan(depth, inputs=None):
    wp = WPack()

    def add(key, name, j):
        if inputs is None:
            wp.add_meta(key, *W_SHAPES[key[0]])
        else:
            wp.add(key, np.asarray(inputs[name][j], np.float32))

    for i in range(depth):
        m, j = i % 3, i // 3
        if m == 0:
            add(("ssm_in", i), "ssm_w_in", j)
            add(("ssm_out", i), "ssm_w_out", j)
        elif m == 1:
            add(("gla_in", i), "gla_w_in", j)
            add(("gla_out", i), "gla_w_out", j)
        else:
            add(("att_qkv", i), "att_w_qkv", j)
            add(("att_out", i), "att_w_out", j)
        add(("ffn_gate", i), "ffn_gate", i)
        add(("ffn_up", i), "ffn_up", i)
        add(("ffn_down", i), "ffn_down", i)
    return wp


def fm(v, nchunk):
    return np.ascontiguousarray(np.asarray(v, np.float32).reshape(nchunk, 128).T)


class PPack:
    def __init__(self):
        self.off, self.tot, self.parts = {}, 0, []

    def add(self, key, arr=None, width=None):
        if arr is not None:
            arr = np.asarray(arr, np.float32)
            assert arr.shape[0] == 128
            arr = arr.reshape(128, -1)
            width = arr.shape[1]
            self.parts.append(arr)
        self.off[key] = (self.tot, width)
        self.tot += width

    def cat(self):
        return np.ascontiguousarray(np.concatenate(self.parts, axis=1))


def param_plan(depth, inputs=None):
    pp = PPack()
    g = (lambda name: np.asarray(inputs[name], np.float32)) if inputs is not None else None

    def add(key, fn, width):
        pp.add(key, fn() if inputs is not None else None, width)

    for i in range(depth):
        add(("nmix", i), lambda: fm(g("norm_mix")[i], 8), 8)
        add(("nffn", i), lambda: fm(g("norm_ffn")[i], 8), 8)
    add(("nfin",), lambda: fm(g("norm_final"), 8), 8)
    for i in range(depth):
        m, j = i % 3, i // 3
        if m == 0:
            add(("ssm_convw", i), lambda: np.ascontiguousarray(g("ssm_conv_w")[j].reshape(4, 24, 128).transpose(2, 1, 0)).reshape(128, 96), 96)
            add(("ssm_convb", i), lambda: fm(g("ssm_conv_b")[j], 24), 24)
            add(("ssm_dtb", i), lambda: np.pad(g("ssm_dt_bias")[j].reshape(32, 1), ((0, 96), (0, 0))), 1)
            add(("ssm_alog", i), lambda: np.pad(g("ssm_a_log")[j].reshape(32, 1), ((0, 96), (0, 0))), 1)
            add(("ssm_dbc", i), lambda: np.broadcast_to(g("ssm_d")[j].reshape(1, 32), (128, 32)).copy(), 32)
            add(("ssm_normw", i), lambda: fm(g("ssm_norm")[j], 16), 16)
        elif m == 1:
            add(("gla_wg", i), lambda: np.pad(g("gla_w_gate")[j], ((0, 112), (0, 0))), 512)
            add(("gla_gb", i), lambda: fm(g("gla_gate_bias")[j], 4), 4)
            add(("gla_nw", i), lambda: np.broadcast_to(g("gla_norm")[j].reshape(1, 256), (128, 256)).copy(), 256)
    return pp


class Prog:
    def __init__(self, cfg):
        self.cfg = cfg
        self.wp = weight_plan(cfg.DEPTH)
        self.pp = param_plan(cfg.DEPTH)

    def build(self):
        cfg = self.cfg
        nc = bass.Bass("TRN2", target_bir_lowering=False)
        self.nc = nc
        with contextlib.ExitStack() as st:
            k = KB(nc, st)
            self.k = k
            self.declare_io()
            self.alloc()
            self.prologue()
            for ti in range(cfg.NT):
                self.tile_prompt(ti)
            self.tile_sample()
            self.epilogue()
            k.finish()
        return nc

    def declare_io(self):
        nc, cfg = self.nc, self.cfg
        I = lambda n, s: nc.dram_tensor(n, list(s), F32, kind="ExternalInput").ap()
        O = lambda n, s: nc.dram_tensor(n, list(s), F32, kind="ExternalOutput").ap()
        TS = cfg.NSC * cfg.DEC
        self.TS = TS
        self.xp = I("xp", (cfg.SEQ, D))
        self.xs = I("xs", (TS, D))
        self.wcat = I("wcat", (128, self.wp.tot + ((-self.wp.tot) % 2048)))
        self.pcat = I("pcat", (128, self.pp.tot))
        self.cmat_in = I("cmat", (128, 640))
        self.y_p = O("y_p", (cfg.SEQ, D))
        self.y_s = O("y_s", (TS, D))

    def alloc(self):
        k, cfg = self.k, self.cfg
        TP = cfg.TP
        self.TMAX = TP
        self.out_sems = []
        self.wtot = self.wp.tot + ((-self.wp.tot) % 2048)
        self.wb = k.dram("wcat_b", (128, self.wtot), BF16)
        self.params = k.sbuf((128, self.pp.tot), F32, "params")
        self.cmat = k.sbuf((128, 640), F32, "cmat_s")
        self.cmat_r = k.sbuf((128, 640), F32R, "cmat_r")
        self.ident_b = k.sbuf((128, 128), BF16, "ident_b")
        self.ones_b = k.sbuf((128, 128), BF16, "ones_b")
        self.eps_t = k.sbuf((128, 1), F32, "eps_t")
        self.one_t = k.sbuf((128, 1), F32, "one_t")
        self.x = k.sbuf((128, 8, TP), F32, "x")
        self.h = k.sbuf((128, 8, TP), BF16, "h")
        self.sq = Ring([k.sbuf((128, TP), BF16, f"sq{i}") for i in range(2)])
        self.rstd = k.sbuf((128, TP), F32, "rstd")
        self.act = k.sbuf((128, 22, TP), BF16, "act")
        self.sil = Ring([k.sbuf((128, TP), F32, f"sil{i}") for i in range(2)])
        self.wring = Ring([k.sbuf((128, 4096), BF16, f"wr{i}") for i in range(3)])
        self.wsem = [k.dsem(f"w{i}") for i in range(3)]
        self.PS = Ring([k.psum((128, 512), F32, f"psb{i}") for i in range(8)])
        self.pp_ps = self.PS
        self.pt_ps = self.PS
        self.alloc_mixers()
        self.xin = Ring([k.sbuf((128, D), F32, f"xin{i}") for i in range(2)])
        self.xin_sem = [k.dsem(f"xin{i}") for i in range(2)]
        self.yout = Ring(self.xin.bufs)
        self.yout_sem = [k.dsem(f"yo{i}") for i in range(2)]
        self.misc_sem = k.dsem("misc")
        self.out_sems += list(self.yout_sem)

    def P(self, key):
        off, w = self.pp.off[key]
        return self.params, off, w

    def prologue(self):
        k = self.k
        k.dma("pool", self.misc_sem, self.params[:], self.pcat, w=[self.params])
        k.dma("pool", self.misc_sem, self.cmat[:], self.cmat_in, w=[self.cmat])
        k.op("dve", lambda e: e.tensor_copy(out=self.ident_b[:], in_=self.cmat[:, 0:128]), r=[self.cmat], w=[self.ident_b])
        k.op("dve", lambda e: e.tensor_copy(out=self.cmat_r[:], in_=self.cmat[:]), r=[self.cmat], w=[self.cmat_r])
        k.op("dve", lambda e: e.memset(self.one_t[:], 1.0), w=[self.one_t])
        self.prologue_mixers()
        k.op("dve", lambda e: e.memset(self.ones_b[:], 1.0), w=[self.ones_b])
        k.op("dve", lambda e: e.memset(self.eps_t[:], EPS), w=[self.eps_t])
        CW = 1024
        stf = self.xin.bufs
        stb = [self.xs_tok, self.xdt]
        sin = [k.dsem(f"ci{i}") for i in range(2)]
        sout = [k.dsem(f"co{i}") for i in range(2)]
        engs = ["act", "dve", "pool"]
        n = self.wtot // CW
        for i in range(n):
            s = i % 2
            k.dma("sp", sin[s], stf[s][:], self.wcat[:, i * CW:(i + 1) * CW], w=[stf[s]])
            eng = engs[i % 3]
            if eng == "act":
                k.op("act", lambda e, s=s: e.copy(out=stb[s][:, 0:CW], in_=stf[s][:]), r=[stf[s]], w=[stb[s]])
            else:
                k.op(eng, lambda e, s=s: e.tensor_copy(out=stb[s][:, 0:CW], in_=stf[s][:]), r=[stf[s]], w=[stb[s]])
            k.dma("sp", sout[s], self.wb[:, i * CW:(i + 1) * CW], stb[s][:, 0:CW], r=[stb[s]], w=[self.wb])

    def wload(self, blk):
        off, kc, bw, nv = blk
        i = self.wring.i % 3
        buf = self.wring.next()
        k = self.k
        k.dma("sp", self.wsem[i], buf[:, 0:kc * bw], self.wb[:, off:off + kc * bw], r=[self.wb], w=[buf])
        return buf

    def proj_fm(self, wkey, src, kc_n, ntok, consumer, col_lo=0, col_hi=None):
        k = self.k
        blocks = self.wp.blocks[wkey]
        for b, blk in enumerate(blocks):
            off, kc, bw, nv = blk
            assert kc == kc_n
            c0 = b * bw
            if col_hi is not None and c0 >= col_hi:
                break
            if c0 + nv <= col_lo:
                continue
            buf = self.wload(blk)
            for j in range(0, nv, 128):
                col = c0 + j
                if col < col_lo or (col_hi is not None and col >= col_hi):
                    continue
                ncols = min(128, nv - j)
                ps = self.pp_ps.next()
                for kc_i in range(kc):
                    k.op("pe", lambda e, ps=ps, buf=buf, kc_i=kc_i, j=j, bw=bw, ncols=ncols:
                         e.matmul(ps[0:ncols, 0:ntok], lhsT=buf[:, kc_i * bw + j:kc_i * bw + j + ncols],
                                  rhs=src[:, kc_i, 0:ntok], start=(kc_i == 0), stop=(kc_i == kc - 1)),
                         r=[buf, src], w=[ps])
                consumer(col, ncols, ps)

    def rmsnorm(self, gkey, ntok, out, out_is_f32=False):
        k = self.k
        x = self.x
        ps = self.pp_ps.next()
        for c in range(8):
            sq = self.sq.next()
            k.op("act", lambda e, sq=sq, c=c: e.activation(out=sq[:, 0:ntok], in_=x[:, c, 0:ntok], func=AF.Square), r=[x], w=[sq])
            k.op("pe", lambda e, sq=sq, c=c, ps=ps: e.matmul(ps[:, 0:ntok], lhsT=self.ones_b[:], rhs=sq[:, 0:ntok], start=(c == 0), stop=(c == 7)),
                 r=[sq, self.ones_b], w=[ps])
        rstd = self.rstd
        k.op("act", lambda e: e.activation(out=rstd[:, 0:ntok], in_=ps[:, 0:ntok], func=AF.Sqrt, bias=self.eps_t[:, 0:1], scale=1.0 / D), r=[ps, self.eps_t], w=[rstd])
        k.op("dve", lambda e: e.reciprocal(out=rstd[:, 0:ntok], in_=rstd[:, 0:ntok]), r=[rstd], w=[rstd])
        prm, off, _ = self.P(gkey)
        for c in range(8):
            k.op("dve", lambda e, c=c: e.scalar_tensor_tensor(out=out[:, c, 0:ntok], in0=x[:, c, 0:ntok], scalar=prm[:, off + c:off + c + 1],
                                                              in1=rstd[:, 0:ntok], op0=ALU.mult, op1=ALU.mult), r=[x, rstd, prm], w=[out])

    def ffn(self, i, ntok):
        k = self.k
        self.rmsnorm(("nffn", i), ntok, self.h)
        act = self.act
        gb, ub = self.wp.blocks[("ffn_gate", i)], self.wp.blocks[("ffn_up", i)]
        for b in range(len(gb)):
            off, kc, bw, nv = gb[b]
            bufg = self.wload(gb[b])
            bufu = self.wload(ub[b])
            for j in range(0, nv, 128):
                ch = (b * bw + j) // 128
                psg, psu = self.pp_ps.next(), self.pp_ps.next()
                for buf, ps in ((bufg, psg), (bufu, psu)):
                    for kc_i in range(kc):
                        k.op("pe", lambda e, ps=ps, buf=buf, kc_i=kc_i, j=j, bw=bw:
                             e.matmul(ps[:, 0:ntok], lhsT=buf[:, kc_i * bw + j:kc_i * bw + j + 128],
                                      rhs=self.h[:, kc_i, 0:ntok], start=(kc_i == 0), stop=(kc_i == kc - 1)),
                             r=[buf, self.h], w=[ps])
                sil = self.sil.next()
                k.op("act", lambda e, sil=sil, psg=psg: e.activation(out=sil[:, 0:ntok], in_=psg[:, 0:ntok], func=AF.Silu), r=[psg], w=[sil])
                k.op("dve", lambda e, sil=sil, psu=psu, ch=ch: e.tensor_tensor(out=act[:, ch, 0:ntok], in0=sil[:, 0:ntok], in1=psu[:, 0:ntok], op=ALU.mult),
                     r=[sil, psu], w=[act])

        x = self.x

        def cons_down(col, ncols, ps):
            c = col // 128
            k.op("dve", lambda e: e.tensor_tensor(out=x[:, c, 0:ntok], in0=x[:, c, 0:ntok], in1=ps[:, 0:ntok], op=ALU.add), r=[x, ps], w=[x])

        self.proj_fm(("ffn_down", i), act, 22, ntok, cons_down)

    def load_x(self, src_ap, ntok):
        k = self.k
        for t0 in range(0, ntok, 128):
            n = min(128, ntok - t0)
            i = self.xin.i % 2
            xin = self.xin.next()
            k.dma("pool", self.xin_sem[i], xin[0:n, :], src_ap[t0:t0 + n, :], w=[xin])
            for c4 in range(2):
                ps = self.pt_ps.next()
                for cc in range(4):
                    c = c4 * 4 + cc
                    k.op("pe", lambda e, ps=ps, cc=cc, c=c, xin=xin, n=n: e.transpose(ps[:, cc * 128:cc * 128 + n], xin[0:n, c * 128:(c + 1) * 128], self.cmat[0:n, 0:n]),
                         r=[xin, self.cmat], w=[ps])
                k.op("act", lambda e, ps=ps, c4=c4, n=n, t0=t0: e.copy(out=self.x[:, c4 * 4:c4 * 4 + 4, t0:t0 + n],
                                                                         in_=ps[:, :].rearrange("p (c t) -> p c t", c=4)[:, :, 0:n]), r=[ps], w=[self.x])

    def store_y(self, dst_ap, ntok):
        k = self.k
        self.hf = T(self.ysb.t[:, :].rearrange("p (c t) -> p c t", c=8), "hfview")
        self.hf.w, self.hf.r = self.ysb.w, self.ysb.r
        self.rmsnorm(("nfin",), ntok, self.hf)
        for t0 in range(0, ntok, 128):
            n = min(128, ntok - t0)
            i = self.yout.i % 2
            yo = self.yout.next()
            for c4 in range(2):
                ps = self.pt_ps.next()
                for cc in range(4):
                    c = c4 * 4 + cc
                    k.op("pe", lambda e, ps=ps, cc=cc, c=c, n=n, t0=t0: e.transpose(ps[0:n, cc * 128:(cc + 1) * 128], self.hf[:, c, t0:t0 + n], self.cmat[:, 0:128]),
                         r=[self.hf, self.cmat], w=[ps])
                k.op("act", lambda e, ps=ps, c4=c4, n=n, yo=yo: e.copy(out=yo[0:n, c4 * 512:(c4 + 1) * 512], in_=ps[0:n, :]), r=[ps], w=[yo])
            k.dma("pool", self.yout_sem[i], dst_ap[t0:t0 + n, :], yo[0:n, :], r=[yo])

    def layers(self, ntok, tile):
        cfg = self.cfg
        for i in range(cfg.DEPTH):
            if cfg.mixers:
                self.mixer(i, ntok, tile)
            self.ffn(i, ntok)

    def alloc_mixers(self):
        k, cfg = self.k, self.cfg
        TP = cfg.TP
        self.A = k.sbuf((128, 24, TP), BF16, "A")
        self.zt = k.sbuf((128, 4, 2048), BF16, "zt")
        self.pre = Ring([k.sbuf((128, TP + 16), F32, f"pre{i}") for i in range(2)])
        self.ctmp = Ring([k.sbuf((128, TP), F32, f"ctmp{i}") for i in range(2)])
        self.hist = {}
        self.HT = {}
        for i in range(cfg.DEPTH):
            if i % 3 == 0:
                self.hist[i] = k.sbuf((128, 24, 3), F32, f"hist{i}")
                self.HT[i] = k.sbuf((128, 2048), F32, f"HT{i}")
        self.HTb = k.sbuf((128, 2048), BF16, "HTb")
        self.acol = k.sbuf((128, 4), F32, "acol")
        self.dtT = k.sbuf((32, TP), F32, "dtT")
        self.aT = k.sbuf((32, TP), F32, "aT")
        self.xs_tok = k.sbuf((128, 2048), BF16, "xs_tok")
        self.xdt = k.sbuf((128, 2048), BF16, "xdt")
        self.xdd = k.sbuf((128, 2048), BF16, "xdd")
        self.xsD = k.sbuf((128, 2048), BF16, "xsD")
        self.B_tok = k.sbuf((128, 512), BF16, "B_tok")
        self.a_tok = k.sbuf((128, 32), F32R, "a_tok")
        self.dt_tok = k.sbuf((128, 32), F32, "dt_tok")
        self.acs = k.sbuf((128, 32), F32, "acs")
        self.eacs = k.sbuf((128, 32), F32, "eacs")
        self.dst = k.sbuf((128, 32), F32, "dst")
        self.edec = k.sbuf((128, 32), F32, "edec")
        self.CBT = Ring([k.sbuf((128, 128), F32, f"CBT{i}") for i in range(2)])
        self.XD = Ring([k.sbuf((128, 512), F32R, f"XD{i}") for i in range(2)])
        self.TMP = Ring([k.sbuf((128, 512), F32, f"TMP{i}") for i in range(2)])
        self.LX = Ring([k.sbuf((128, 512), F32, f"LX{i}") for i in range(2)])
        self.MT = Ring([k.sbuf((128, 512), BF16, f"MT{i}") for i in range(2)])
        self.ysb = k.sbuf((128, 2048), F32, "ysb")
        self.ygn = k.sbuf((128, 2048), BF16, "ygn")
        self.junk = k.sbuf((128, 512), F32, "junk")
        self.ss = k.sbuf((128, 8), F32, "ss")
        self.rs = k.sbuf((128, 8), F32, "rs")
        self.st_sem = k.dsem("st")
        self.out_sems.append(self.st_sem)
        nc = self.nc
        NSC = cfg.NSC
        nssm = len(self.HT)
        I = lambda n, s: nc.dram_tensor(n, list(s), F32, kind="ExternalInput").ap()
        O = lambda n, s: nc.dram_tensor(n, list(s), F32, kind="ExternalOutput").ap()
        self.ssm_in = I("ssm_in_t", (nssm, NSC, 128, 2048))
        self.conv_in = I("conv_in_fm", (nssm, 128, 24 * NSC * 3))
        self.ssm_p_o = O("ssm_p_t", (nssm, 128, 2048))
        self.ssm_s_o = O("ssm_s_t", (nssm, NSC, 128, 2048))
        self.conv_p_o = O("conv_p_fm", (nssm, 128, 72))
        self.conv_s_o = O("conv_s_fm", (nssm, 128, 24 * NSC * 3))
        self.convst = k.sbuf((128, nssm, 24 * NSC * 3), F32, "convst")
        self.convout = k.sbuf((128, 24 * NSC * 3), F32, "convout")
        if cfg.DEPTH > 1:
            self.alloc_gla()
        if cfg.DEPTH > 2:
            self.alloc_att()

    def prologue_mixers(self):
        k, cfg = self.k, self.cfg
        for n, i in enumerate(sorted(self.HT)):
            k.op("dve", lambda e, i=i: e.memset(self.hist[i][:], 0.0), w=[self.hist[i]])
            k.op("dve", lambda e, i=i: e.memset(self.HT[i][:], 0.0), w=[self.HT[i]])
            prm, off, _ = self.P(("ssm_alog", i))
            k.op("act", lambda e, n=n, off=off: e.activation(out=self.acol[:, n:n + 1], in_=prm[:, off:off + 1], func=AF.Exp), r=[prm], w=[self.acol])
            k.op("dve", lambda e, n=n: e.tensor_scalar(out=self.acol[:, n:n + 1], in0=self.acol[:, n:n + 1], scalar1=-1.0, scalar2=None, op0=ALU.mult), r=[self.acol], w=[self.acol])
            k.dma("pool", self.misc_sem, self.convst[:, n, :], self.conv_in[n], w=[self.convst])
        if cfg.DEPTH > 2:
            self.prologue_att()

    def mixer(self, i, ntok, tile):
        m = i % 3
        if m == 0:
            self.mixer_ssd(i, ntok, tile)
        elif m == 1:
            self.mixer_gla(i, ntok, tile)
        else:
            self.mixer_att(i, ntok, tile)

    def add_to_x(self, ntok):
        k, x = self.k, self.x

        def cons(col, ncols, ps):
            c = col // 128
            k.op("dve", lambda e: e.tensor_tensor(out=x[:, c, 0:ntok], in0=x[:, c, 0:ntok], in1=ps[:, 0:ntok], op=ALU.add), r=[x, ps], w=[x])
        return cons

    def alloc_gla(self):
        k, cfg, nc = self.k, self.cfg, self.nc
        TP, NSC = cfg.TP, cfg.NSC
        self.glow = k.sbuf((16, TP), F32, "glow")
        self.G3 = k.sbuf((128, 3, 4, TP), F32, "G3")
        self.la = [TV(self.G3, self.G3[:, i], f"la{i}") for i in range(2)]
        self.ebx = TV(self.G3, self.G3[:, 2], "ebx")
        self.eblast = k.sbuf((128, 16), F32, "eblast")
        self.S = k.sbuf((128, 1024), F32, "S")
        self.Sb = k.sbuf((128, 1024), BF16, "Sb")
        I = lambda n, s: nc.dram_tensor(n, list(s), F32, kind="ExternalInput").ap()
        O = lambda n, s: nc.dram_tensor(n, list(s), F32, kind="ExternalOutput").ap()
        self.gla_in = I("gla_in", (NSC, 4, 128, 256))
        self.gla_p_o = O("gla_p", (4, 128, 256))
        self.gla_s_o = O("gla_s", (NSC, 4, 128, 256))

    def mixer_gla(self, i, ntok, tile):
        k, cfg = self.k, self.cfg
        kind, ti = tile
        NSC = cfg.NSC
        if kind == "p":
            units = [(u * 128, 128, 0) for u in range(ntok // 128)]
        else:
            units = [(s * cfg.DEC, cfg.DEC, s) for s in range(NSC)]
        nun, Q = len(units), units[0][1]
        self.rmsnorm(("nmix", i), ntok, self.h)
        A, h, zt, act = self.A, self.h, self.zt, self.act
        cm, ident_b = self.cmat, self.ident_b
        S, Sb = self.S, self.Sb
        glow, la, ebx, eblast = self.glow, self.la, self.ebx, self.eblast
        blocks = self.wp.blocks[("gla_in", i)]
        prm_wg, off_wg, _ = self.P(("gla_wg", i))
        prm_gb, off_gb, _ = self.P(("gla_gb", i))
        prm_nw, off_nw, _ = self.P(("gla_nw", i))
        if kind == "p" and ti == 0:
            k.op("dve", lambda e: e.memset(S[:, :], 0.0), w=[S])
            k.op("dve", lambda e: e.memset(Sb[:, :], 0.0), w=[Sb])

        def cons(col, ncols, ps):
            if col < 1024:
                k.op("act", lambda e: e.copy(out=A[:, col // 128, 0:ntok], in_=ps[:, 0:ntok]), r=[ps], w=[A])
            else:
                k.op("act", lambda e: e.copy(out=glow[:, 0:ntok], in_=ps[0:16, 0:ntok]), r=[ps], w=[glow])
        self.proj_fm(("gla_in", i), h, 8, ntok, cons, col_lo=0, col_hi=1024)
        self.proj_fm(("gla_in", i), h, 8, ntok, cons, col_lo=3072)
        for b in (2, 3, 4, 5):
            buf = self.wload(blocks[b])
            for ui, (o, Qu, s) in enumerate(units):
                ps = self.PS.next()
                for kc in range(8):
                    k.op("pe", lambda e, ps=ps, kc=kc, o=o, buf=buf: e.matmul(ps[0:Q, 0:512], lhsT=h[:, kc, o:o + Q], rhs=buf[:, kc * 512:(kc + 1) * 512],
                                                                              start=(kc == 0), stop=(kc == 7)), r=[h, buf], w=[ps])
                if b >= 4:
                    k.op("act", lambda e, ps=ps, ui=ui, b=b: e.activation(out=zt[0:Q, ui, (b - 4) * 512:(b - 3) * 512], in_=ps[0:Q, 0:512], func=AF.Silu), r=[ps], w=[zt])
                else:
                    k.op("act", lambda e, ps=ps, ui=ui, b=b: e.copy(out=zt[0:Q, ui, 1024 + (b - 2) * 512:1024 + (b - 1) * 512], in_=ps[0:Q, 0:512]), r=[ps], w=[zt])
        for hh in range(4):
            ps = self.PS.next()
            k.op("pe", lambda e, ps=ps, hh=hh: e.matmul(ps[:, 0:ntok], lhsT=prm_wg[0:16, off_wg + hh * 128:off_wg + (hh + 1) * 128], rhs=glow[:, 0:ntok], start=True, stop=True),
                 r=[prm_wg, glow], w=[ps])
            xb, ax = self.ctmp.next(), self.pre.next()
            k.op("dve", lambda e, ps=ps, xb=xb, hh=hh: e.tensor_scalar(out=xb[:, 0:ntok], in0=ps[:, 0:ntok], scalar1=prm_gb[:, off_gb + hh:off_gb + hh + 1], scalar2=None, op0=ALU.add),
                 r=[ps, prm_gb], w=[xb])
            k.op("dve", lambda e, xb=xb, ax=ax: e.scalar_tensor_tensor(out=ax[:, 0:ntok], in0=xb[:, 0:ntok], scalar=-1.0, in1=xb[:, 0:ntok], op0=ALU.mult, op1=ALU.max), r=[xb], w=[ax])
            k.op("act", lambda e, ax=ax: e.activation(out=ax[:, 0:ntok], in_=ax[:, 0:ntok], func=AF.Exp, scale=-1.0), r=[ax], w=[ax])
            k.op("act", lambda e, ax=ax: e.activation(out=ax[:, 0:ntok], in_=ax[:, 0:ntok], func=AF.Ln, bias=self.one_t[:, 0:1], scale=1.0), r=[ax, self.one_t], w=[ax])
            k.op("dve", lambda e, xb=xb, ax=ax, hh=hh: e.scalar_tensor_tensor(out=la[0][:, hh, 0:ntok], in0=xb[:, 0:ntok], scalar=0.0, in1=ax[:, 0:ntok], op0=ALU.min, op1=ALU.subtract),
                 r=[xb, ax], w=[la[0]])
        cur = 0
        sh = 1
        vw = lambda t: t[:, :, 0:ntok].rearrange("p h (u q) -> p h u q", q=Q)
        while sh < Q:
            src, dst_ = la[cur], la[1 - cur]
            k.op("dve", lambda e, src=src, dst_=dst_, sh=sh: e.tensor_copy(out=vw(dst_)[:, :, :, 0:sh], in_=vw(src)[:, :, :, 0:sh]), r=[src], w=[dst_])
            k.op("dve", lambda e, src=src, dst_=dst_, sh=sh: e.tensor_tensor(out=vw(dst_)[:, :, :, sh:Q], in0=vw(src)[:, :, :, sh:Q], in1=vw(src)[:, :, :, 0:Q - sh], op=ALU.add), r=[src], w=[dst_])
            cur = 1 - cur
            sh *= 2
        bc_ = la[cur]
        k.op("act", lambda e: e.activation(out=ebx[:, :, 0:ntok], in_=bc_[:, :, 0:ntok], func=AF.Exp, scale=1.0 / 16), r=[bc_], w=[ebx])
        k.op("dve", lambda e: e.scalar_tensor_tensor(out=A[:, 8:12, 0:ntok], in0=A[:, 0:4, 0:ntok], scalar=float(GHK) ** -0.5, in1=ebx[:, :, 0:ntok], op0=ALU.mult, op1=ALU.mult),
             r=[A, ebx], w=[A])
        k.op("act", lambda e: e.activation(out=eblast[:, 0:4 * nun].rearrange("p (h u) -> p h u", h=4), in_=vw(bc_)[:, :, :, Q - 1], func=AF.Exp, scale=1.0 / 16), r=[bc_], w=[eblast])
        k.op("act", lambda e: e.activation(out=ebx[:, :, 0:ntok], in_=bc_[:, :, 0:ntok], func=AF.Exp, scale=-1.0 / 16), r=[bc_], w=[ebx])
        k.op("dve", lambda e: e.tensor_tensor(out=A[:, 12:16, 0:ntok], in0=A[:, 4:8, 0:ntok], in1=ebx[:, :, 0:ntok], op=ALU.mult), r=[A, ebx], w=[A])
        k.op("dve", lambda e: e.tensor_tensor(out=vw(A[:, 16:20, :]), in0=vw(A[:, 12:16, :]), in1=eblast[:, 0:4 * nun].rearrange("p (h u) -> p h u", h=4).unsqueeze(3).to_broadcast([128, 4, nun, Q]), op=ALU.mult),
             r=[A, eblast], w=[A])
        ysb, ygn, kd_tok = self.ysb, self.ygn, self.B_tok
        for ui, (o, Qu, s) in enumerate(units):
            if kind == "s":
                k.dma("pool", self.st_sem, S[:, :].rearrange("p (h v) -> p h v", h=4), self.gla_in[s].rearrange("h k v -> k h v"), w=[S])
                k.op("act", lambda e: e.copy(out=Sb[:, :], in_=S[:, :]), r=[S], w=[Sb])
            ps = self.PS.next()
            pb = ps[:, :].bitcast(BF16)
            for hh in range(4):
                k.op("pe", lambda e, pb=pb, hh=hh, o=o: e.transpose(pb[0:Q, hh * 128:(hh + 1) * 128], A[:, 16 + hh, o:o + Q], ident_b[:, :]), r=[A, ident_b], w=[ps])
            k.op("act", lambda e, pb=pb: e.copy(out=kd_tok[0:Q, :], in_=pb[0:Q, 0:512]), r=[ps], w=[kd_tok])
            MT = self.MT.next()
            k.op("dve", lambda e: e.memset(self.ss[:, :], 0.0), w=[self.ss])
            ops = []
            for hh in range(4):
                psa = self.PS.next()
                k.op("pe", lambda e, psa=psa, hh=hh, o=o: e.matmul(psa[0:Q, 0:Q], lhsT=A[:, 12 + hh, o:o + Q], rhs=A[:, 8 + hh, o:o + Q], start=True, stop=True), r=[A], w=[psa])
                k.op("dve", lambda e, psa=psa, hh=hh, MT=MT: e.tensor_tensor(out=MT[0:Q, hh * Q:(hh + 1) * Q], in0=psa[0:Q, 0:Q], in1=cm[0:Q, 128:128 + Q], op=ALU.mult), r=[psa, cm], w=[MT])
                pso = self.PS.next()
                vt = zt[0:Q, ui, 1024 + hh * 256:1024 + (hh + 1) * 256]
                k.op("pe", lambda e, pso=pso, hh=hh, MT=MT, vt=vt: e.matmul(pso[0:Q, 0:256], lhsT=MT[0:Q, hh * Q:(hh + 1) * Q], rhs=vt, start=True, stop=False), r=[MT, zt], w=[pso])
                k.op("pe", lambda e, pso=pso, hh=hh, o=o: e.matmul(pso[0:Q, 0:256], lhsT=A[:, 8 + hh, o:o + Q], rhs=Sb[:, hh * 256:(hh + 1) * 256], start=False, stop=True), r=[A, Sb], w=[pso])
                k.op("act", lambda e, pso=pso, hh=hh: e.activation(out=self.junk[0:Q, 0:256], in_=pso[0:Q, 0:256], func=AF.Square, accum_out=self.ss[0:Q, hh:hh + 1]),
                     r=[pso], w=[self.junk, self.ss])
                k.op("act", lambda e, pso=pso, hh=hh: e.copy(out=ysb[0:Q, hh * 256:(hh + 1) * 256], in_=pso[0:Q, 0:256]), r=[pso], w=[ysb])
                psu = self.PS.next()
                k.op("pe", lambda e, psu=psu, hh=hh, vt=vt: e.matmul(psu[:, 0:256], lhsT=kd_tok[0:Q, hh * 128:(hh + 1) * 128], rhs=vt, start=True, stop=True), r=[kd_tok, zt], w=[psu])
                k.op("dve", lambda e, psu=psu, hh=hh, ui=ui: e.scalar_tensor_tensor(out=S[:, hh * 256:(hh + 1) * 256], in0=S[:, hh * 256:(hh + 1) * 256], scalar=eblast[:, hh * nun + ui:hh * nun + ui + 1],
                                                                                   in1=psu[:, 0:256], op0=ALU.mult, op1=ALU.add), r=[S, eblast, psu], w=[S])
            k.op("act", lambda e: e.copy(out=Sb[:, :], in_=S[:, :]), r=[S], w=[Sb])
            k.op("act", lambda e: e.activation(out=self.rs[0:Q, 0:4], in_=self.ss[0:Q, 0:4], func=AF.Sqrt, bias=self.eps_t[0:Q, 0:1], scale=1.0 / GHV), r=[self.ss, self.eps_t], w=[self.rs])
            k.op("dve", lambda e: e.reciprocal(out=self.rs[0:Q, 0:4], in_=self.rs[0:Q, 0:4]), r=[self.rs], w=[self.rs])
            for hh in range(4):
                k.op("dve", lambda e, hh=hh: e.scalar_tensor_tensor(out=ysb[0:Q, hh * 256:(hh + 1) * 256], in0=ysb[0:Q, hh * 256:(hh + 1) * 256], scalar=self.rs[0:Q, hh:hh + 1],
                                                                   in1=prm_nw[0:Q, off_nw:off_nw + 256], op0=ALU.mult, op1=ALU.mult), r=[ysb, self.rs, prm_nw], w=[ysb])
            k.op("dve", lambda e, ui=ui: e.tensor_tensor(out=ygn[0:Q, 0:1024], in0=ysb[0:Q, 0:1024], in1=zt[0:Q, ui, 0:1024], op=ALU.mult), r=[ysb, zt], w=[ygn])
            ps = self.PS.next()
            pb = ps[:, :].bitcast(BF16)
            for c in range(8):
                k.op("pe", lambda e, pb=pb, c=c: e.transpose(pb[:, c * Q:(c + 1) * Q], ygn[0:Q, c * 128:(c + 1) * 128], ident_b[0:Q, 0:Q]), r=[ygn, ident_b], w=[ps])
            k.op("act", lambda e, pb=pb, o=o: e.copy(out=act[:, 0:8, o:o + Q], in_=pb[:, 0:8 * Q].rearrange("p (c q) -> p c q", c=8)), r=[ps], w=[act])
            if kind == "s":
                k.dma("pool", self.st_sem, self.gla_s_o[s].rearrange("h k v -> k h v"), S[:, :].rearrange("p (h v) -> p h v", h=4), r=[S])
        if kind == "p" and ti == cfg.NT - 1:
            k.dma("pool", self.st_sem, self.gla_p_o.rearrange("h k v -> k h v"), S[:, :].rearrange("p (h v) -> p h v", h=4), r=[S])
        self.proj_fm(("gla_out", i), act, 8, ntok, self.add_to_x(ntok))

    def alloc_att(self):
        k, cfg, nc = self.k, self.cfg, self.nc
        NSC, SEQ = cfg.NSC, cfg.SEQ
        I = lambda n, s: nc.dram_tensor(n, list(s), F32, kind="ExternalInput").ap()
        O = lambda n, s: nc.dram_tensor(n, list(s), F32, kind="ExternalOutput").ap()
        self.NU = SEQ // 128
        self.vtok = k.sbuf((128, 768), F32, "vtok")
        self.maskb = k.sbuf((128, 3072), BF16, "maskb")
        self.ropeT = k.sbuf((128, (self.NU + 1) * 16), F32, "ropeT")
        self.ast = k.sbuf((128, 64), F32, "ast")
        self.rope_in = I("rope", (128, (self.NU + 1) * 16))
        self.mask_in = I("amask", (128, 3072))
        self.kt_hist = k.dram("kt_hist", (6, 128, SEQ), BF16)
        self.v_hist = k.dram("v_hist", (SEQ, 768), BF16)
        self.cache = [I(f"kvc{g}", (NSC, W, 512)) for g, (W, d) in enumerate(AGROUPS)]
        self.kvp_o = [O(f"kvp{g}", (min(W, SEQ), 512)) for g, (W, d) in enumerate(AGROUPS)]
        self.kvs_o = [O(f"kvs{g}", (NSC, W, 512)) for g, (W, d) in enumerate(AGROUPS)]
        self.att_sems = [k.dsem(f"att{i}") for i in range(4)]
        self.att_i = 0
        self.out_sems += self.att_sems
        ztf = self.zt.t[:, :, :].rearrange("p a b -> p (a b)")
        self.KT_sb = TV(self.zt, ztf[:, 0:4608].rearrange("p (c t) -> p c t", c=2), "KT_sb")
        self.Pb = TV(self.zt, ztf[:, 4608:4608 + 2304], "Pb")
        g3f = self.G3.t[:, :, :, :].rearrange("p a b c -> p (a b c)").bitcast(BF16)
        self.V_sb = TV(self.G3, g3f[:, 0:4608].rearrange("p (b f) -> p b f", f=256), "V_sb")

    def prologue_att(self):
        k = self.k
        k.dma("pool", self.misc_sem, self.ropeT[:, :], self.rope_in, w=[self.ropeT])
        for i in range(3):
            st = self.xin.bufs[i % 2]
            k.dma("pool", self.misc_sem, st[:, :], self.mask_in[:, i * 1024:(i + 1) * 1024], w=[st])
            k.op("dve", lambda e, st=st, i=i: e.tensor_copy(out=self.maskb[:, i * 1024:(i + 1) * 1024], in_=st[:, :]), r=[st], w=[self.maskb])

    def adma(self, out, in_, r=(), w=()):
        s = self.att_sems[self.att_i % 4]
        self.att_i += 1
        self.k.dma("pool", s, out, in_, r=r, w=w)

    def mixer_att(self, i, ntok, tile):
        k, cfg = self.k, self.cfg
        kind, ti = tile
        SEQ, NSC, TP = cfg.SEQ, cfg.NSC, cfg.TP
        if kind == "p":
            units = [(u * 128, 128, 0, ti * TP + u * 128, (ti * TP) // 128 + u) for u in range(ntok // 128)]
        else:
            units = [(s * cfg.DEC, cfg.DEC, s, cfg.PAST, self.NU) for s in range(NSC)]
        self.rmsnorm(("nmix", i), ntok, self.h)
        A, h, act, cm, ident_b = self.A, self.h, self.act, self.cmat, self.ident_b
        ysb, ygn, vtok, vb, ast = self.ysb, self.ygn, self.vtok, self.xs_tok, self.ast
        KT_sb, V_sb, Pb, U = self.KT_sb, self.V_sb, self.Pb, self.xin.bufs[0]
        maskb, ropeT = self.maskb, self.ropeT
        blocks = self.wp.blocks[("att_qkv", i)]
        def unit_body(o, Q, s, q0, ru):
            for b, blk in enumerate(blocks):
                buf = self.wload(blk)
                nv = blk[3]
                ps = self.PS.next()
                for kc in range(8):
                    k.op("pe", lambda e, ps=ps, kc=kc, buf=buf, nv=nv: e.matmul(ps[0:Q, 0:nv], lhsT=h[:, kc, o:o + Q], rhs=buf[:, kc * 512:kc * 512 + nv],
                                                                               start=(kc == 0), stop=(kc == 7)), r=[h, buf], w=[ps])
                c0 = b * 512
                if c0 < 1536:
                    k.op("act", lambda e, ps=ps, c0=c0, nv=nv: e.copy(out=ysb[0:Q, c0:c0 + nv], in_=ps[0:Q, 0:nv]), r=[ps], w=[ysb])
                else:
                    k.op("act", lambda e, ps=ps, c0=c0, nv=nv: e.copy(out=vtok[0:Q, c0 - 1536:c0 - 1536 + nv], in_=ps[0:Q, 0:nv]), r=[ps], w=[vtok])
            qk = ysb[0:Q, 0:1536].rearrange("p (h d) -> p h d", d=64)
            cosb = ropeT[0:Q, ru * 16:ru * 16 + 8].unsqueeze(1).to_broadcast([Q, 24, 8])
            sinb = ropeT[0:Q, ru * 16 + 8:ru * 16 + 16].unsqueeze(1).to_broadcast([Q, 24, 8])
            t0_, t1_ = self.TMP.next(), self.LX.next()
            ta = t0_[0:Q, 0:192].rearrange("p (h d) -> p h d", d=8)
            tb = t0_[0:Q, 192:384].rearrange("p (h d) -> p h d", d=8)
            tc = t1_[0:Q, 0:192].rearrange("p (h d) -> p h d", d=8)
            td = t1_[0:Q, 192:384].rearrange("p (h d) -> p h d", d=8)
            x1, x2 = qk[:, :, 0:8], qk[:, :, 8:16]
            k.op("dve", lambda e: e.tensor_tensor(out=ta, in0=x1, in1=cosb, op=ALU.mult), r=[ysb, ropeT], w=[t0_])
            k.op("dve", lambda e: e.tensor_tensor(out=tb, in0=x2, in1=sinb, op=ALU.mult), r=[ysb, ropeT], w=[t0_])
            k.op("dve", lambda e: e.tensor_tensor(out=tc, in0=x2, in1=cosb, op=ALU.mult), r=[ysb, ropeT], w=[t1_])
            k.op("dve", lambda e: e.tensor_tensor(out=td, in0=x1, in1=sinb, op=ALU.mult), r=[ysb, ropeT], w=[t1_])
            k.op("dve", lambda e: e.tensor_tensor(out=x1, in0=ta, in1=tb, op=ALU.subtract), r=[t0_], w=[ysb])
            k.op("dve", lambda e: e.tensor_tensor(out=x2, in0=tc, in1=td, op=ALU.add), r=[t1_], w=[ysb])
            k.op("dve", lambda e: e.tensor_copy(out=ygn[0:Q, 0:1536], in_=ysb[0:Q, 0:1536]), r=[ysb], w=[ygn])
            k.op("act", lambda e: e.copy(out=vb[0:Q, 0:768], in_=vtok[0:Q, :]), r=[vtok], w=[vb])
            for part in range(2):
                ps = self.PS.next()
                pb = ps[:, :].bitcast(BF16)
                for c in range(6):
                    k.op("pe", lambda e, pb=pb, c=c, part=part: e.transpose(pb[:, c * Q:(c + 1) * Q], ygn[0:Q, part * 768 + c * 128:part * 768 + (c + 1) * 128], ident_b[0:Q, 0:Q]),
                         r=[ygn, ident_b], w=[ps])
                k.op("act", lambda e, pb=pb, part=part: e.copy(out=A[:, part * 6:(part + 1) * 6, 0:Q], in_=pb[:, 0:6 * Q].rearrange("p (c q) -> p c q", c=6)), r=[ps], w=[A])
            if kind == "p":
                self.adma(self.kt_hist[:, :, q0:q0 + Q].rearrange("c p t -> p c t"), A[:, 6:12, 0:Q], r=[A], w=[self.kt_hist])
                self.adma(self.v_hist[q0:q0 + Q, :], vb[0:Q, 0:768], r=[vb], w=[self.v_hist])
                for g, (W, d) in enumerate(AGROUPS):
                    Wc = min(W, SEQ)
                    if q0 + Q > SEQ - Wc:
                        r0 = q0 - (SEQ - Wc)
                        self.adma(self.kvp_o[g][r0:r0 + Q, 0:256], ysb[0:Q, 768 + g * 256:768 + (g + 1) * 256], r=[ysb])
                        self.adma(self.kvp_o[g][r0:r0 + Q, 256:512], vtok[0:Q, g * 256:(g + 1) * 256], r=[vtok])
            else:
                for g, (W, d) in enumerate(AGROUPS):
                    self.adma(self.kvs_o[g][s, W - Q:W, 0:256], ysb[0:Q, 768 + g * 256:768 + (g + 1) * 256], r=[ysb])
                    self.adma(self.kvs_o[g][s, W - Q:W, 256:512], vtok[0:Q, g * 256:(g + 1) * 256], r=[vtok])
                    self.adma(self.kvs_o[g][s, 0:W - Q, :], self.cache[g][s, Q:W, :])
            for g, (W, d) in enumerate(AGROUPS):
                if kind == "p":
                    lo = max(q0 - W, 0)
                    nk = q0 + Q - lo
                    moff = max(0, W - q0)
                    self.adma(KT_sb[:, :, 0:nk], self.kt_hist[2 * g:2 * g + 2, :, lo:lo + nk].rearrange("c p t -> p c t"), r=[self.kt_hist], w=[KT_sb])
                    self.adma(V_sb[:, 0:nk // 128, :], self.v_hist[lo:lo + nk, g * 256:(g + 1) * 256].rearrange("(b p) f -> p b f", p=128), r=[self.v_hist], w=[V_sb])
                else:
                    nk, moff = W + Q, 0
                    stg = self.xin.bufs[1]
                    for blk in range(W // 128):
                        self.adma(stg[:, 0:512], self.cache[g][s, blk * 128:(blk + 1) * 128, :], w=[stg])
                        k.op("act", lambda e, blk=blk: e.copy(out=V_sb[:, blk, :], in_=stg[:, 256:512]), r=[stg], w=[V_sb])
                        ps = self.PS.next()
                        for c in range(2):
                            k.op("pe", lambda e, ps=ps, c=c: e.transpose(ps[:, c * 128:(c + 1) * 128], stg[:, c * 128:(c + 1) * 128], cm[:, 0:128]), r=[stg, cm], w=[ps])
                        k.op("act", lambda e, ps=ps, blk=blk: e.copy(out=KT_sb[:, :, blk * 128:(blk + 1) * 128], in_=ps[:, 0:256].rearrange("p (c t) -> p c t", c=2)), r=[ps], w=[KT_sb])
                    k.op("act", lambda e, g=g, W=W: e.copy(out=KT_sb[:, :, W:W + Q], in_=A[:, 6 + 2 * g:8 + 2 * g, 0:Q]), r=[A], w=[KT_sb])
                    k.op("act", lambda e, g=g, W=W: e.copy(out=V_sb[0:Q, W // 128, :], in_=vb[0:Q, g * 256:(g + 1) * 256]), r=[vb], w=[V_sb])
                nkb = -(-nk // 128)
                mcol0 = (0, 256, 896)[g] + moff
                for j in range(4):
                    hd = 4 * g + j
                    c, half = j // 2, j % 2
                    pl, ph = half * 64, half * 64 + 64
                    chunks = [(k0, min(512, nk - k0)) for k0 in range(0, nk, 512)]
                    pss = []
                    for ci, (k0, n) in enumerate(chunks):
                        ps = self.PS.next()
                        pss.append(ps)
                        k.op("pe", lambda e, ps=ps, k0=k0, n=n, g=g, c=c, pl=pl, ph=ph: e.matmul(ps[0:Q, 0:n], lhsT=A[pl:ph, 2 * g + c, 0:Q], rhs=KT_sb[pl:ph, c, k0:k0 + n], start=True, stop=True),
                             r=[A, KT_sb], w=[ps])
                        k.op("dve", lambda e, ps=ps, k0=k0, n=n, mcol0=mcol0: e.tensor_tensor(out=ps[0:Q, 0:n], in0=ps[0:Q, 0:n], in1=maskb[0:Q, mcol0 + k0:mcol0 + k0 + n], op=ALU.add),
                             r=[ps, maskb], w=[ps])
                        k.op("dve", lambda e, ps=ps, n=n, ci=ci: e.reduce_max(out=ast[0:Q, 56 + ci:57 + ci], in_=ps[0:Q, 0:n], axis=AX.X), r=[ps], w=[ast])
                    nch = len(chunks)
                    k.op("dve", lambda e, hd=hd, nch=nch: e.reduce_max(out=ast[0:Q, hd:hd + 1], in_=ast[0:Q, 56:56 + nch], axis=AX.X), r=[ast], w=[ast])
                    k.op("dve", lambda e, hd=hd: e.tensor_scalar(out=ast[0:Q, 63:64], in0=ast[0:Q, hd:hd + 1], scalar1=-0.125, scalar2=None, op0=ALU.mult), r=[ast], w=[ast])
                    k.op("dve", lambda e: e.memset(ast[0:Q, 56:62], 0.0), r=[ast], w=[ast])
                    for ci, (k0, n) in enumerate(chunks):
                        ps = pss[ci]
                        k.op("act", lambda e, ps=ps, k0=k0, n=n, ci=ci: e.activation(out=Pb[0:Q, k0:k0 + n], in_=ps[0:Q, 0:n], func=AF.Exp, bias=ast[0:Q, 63:64], scale=0.125,
                                                                                   accum_out=ast[0:Q, 56 + ci:57 + ci]), r=[ps, ast], w=[Pb, ast])
                    k.op("dve", lambda e, hd=hd, nch=nch: e.reduce_sum(out=ast[0:Q, 12 + hd:13 + hd], in_=ast[0:Q, 56:56 + nch], axis=AX.X), r=[ast], w=[ast])
                    ups = self.PS.next()
                    for b0 in range(0, nkb, 4):
                        nb = min(4, nkb - b0)
                        pt = self.PS.next()
                        ptb = pt[:, :].bitcast(BF16)
                        for bb in range(nb):
                            b = b0 + bb
                            kn = min(128, nk - b * 128)
                            k.op("pe", lambda e, ptb=ptb, bb=bb, b=b, kn=kn: e.transpose(ptb[0:kn, bb * Q:(bb + 1) * Q], Pb[0:Q, b * 128:b * 128 + kn], ident_b[0:Q, 0:Q]), r=[Pb, ident_b], w=[pt])
                        MT = self.MT.next()
                        kmax = min(128, nk - b0 * 128)
                        k.op("act", lambda e, ptb=ptb, MT=MT, nb=nb, kmax=kmax: e.copy(out=MT[0:kmax, 0:nb * Q], in_=ptb[0:kmax, 0:nb * Q]), r=[pt], w=[MT])
                        for bb in range(nb):
                            b = b0 + bb
                            kn = min(128, nk - b * 128)
                            k.op("pe", lambda e, ups=ups, MT=MT, bb=bb, b=b, kn=kn, j=j, nkb=nkb: e.matmul(ups[0:Q, 0:64], lhsT=MT[0:kn, bb * Q:(bb + 1) * Q], rhs=V_sb[0:kn, b, j * 64:(j + 1) * 64],
                                                                                                 start=(b == 0), stop=(b == nkb - 1)), r=[MT, V_sb], w=[ups])
                    k.op("act", lambda e, ups=ups, hd=hd: e.copy(out=U[0:Q, hd * 64:(hd + 1) * 64], in_=ups[0:Q, 0:64]), r=[ups], w=[U])
            m3 = ast[0:Q, 0:12].rearrange("p (g j) -> p g j", g=3)
            d3 = ast[0:Q, 12:24].rearrange("p (g j) -> p g j", g=3)
            w3 = ast[0:Q, 24:36].rearrange("p (g j) -> p g j", g=3)
            f3 = ast[0:Q, 44:56].rearrange("p (g j) -> p g j", g=3)
            Mx, Zs = ast[0:Q, 36:40], ast[0:Q, 40:44]
            k.op("dve", lambda e: e.tensor_tensor(out=Mx, in0=m3[:, 0, :], in1=m3[:, 1, :], op=ALU.max), r=[ast], w=[ast])
            k.op("dve", lambda e: e.tensor_tensor(out=Mx, in0=Mx, in1=m3[:, 2, :], op=ALU.max), r=[ast], w=[ast])
            k.op("dve", lambda e: e.tensor_tensor(out=w3, in0=m3, in1=Mx.unsqueeze(1).to_broadcast([Q, 3, 4]), op=ALU.subtract), r=[ast], w=[ast])
            k.op("act", lambda e: e.activation(out=w3, in_=w3, func=AF.Exp, scale=0.125), r=[ast], w=[ast])
            k.op("dve", lambda e: e.tensor_tensor(out=f3, in0=w3, in1=d3, op=ALU.mult), r=[ast], w=[ast])
            k.op("dve", lambda e: e.tensor_tensor(out=Zs, in0=f3[:, 0, :], in1=f3[:, 1, :], op=ALU.add), r=[ast], w=[ast])
            k.op("dve", lambda e: e.tensor_tensor(out=Zs, in0=Zs, in1=f3[:, 2, :], op=ALU.add), r=[ast], w=[ast])
            k.op("dve", lambda e: e.reciprocal(out=Zs, in_=Zs), r=[ast], w=[ast])
            k.op("dve", lambda e: e.tensor_tensor(out=f3, in0=w3, in1=Zs.unsqueeze(1).to_broadcast([Q, 3, 4]), op=ALU.mult), r=[ast], w=[ast])
            k.op("dve", lambda e: e.tensor_tensor(out=ygn[0:Q, 0:768].rearrange("p (h d) -> p h d", d=64), in0=U[0:Q, 0:768].rearrange("p (h d) -> p h d", d=64),
                                                  in1=ast[0:Q, 44:56].unsqueeze(2).to_broadcast([Q, 12, 64]), op=ALU.mult), r=[U, ast], w=[ygn])
            ps = self.PS.next()
            pb = ps[:, :].bitcast(BF16)
            for c in range(6):
                k.op("pe", lambda e, pb=pb, c=c: e.transpose(pb[:, c * Q:(c + 1) * Q], ygn[0:Q, c * 128:(c + 1) * 128], ident_b[0:Q, 0:Q]), r=[ygn, ident_b], w=[ps])
            k.op("act", lambda e, pb=pb: e.copy(out=act[:, 0:6, o:o + Q], in_=pb[:, 0:6 * Q].rearrange("p (c q) -> p c q", c=6)), r=[ps], w=[act])

        for u_ in units:
            unit_body(*u_)
        self.proj_fm(("att_out", i), act, 6, ntok, self.add_to_x(ntok))

    def mixer_ssd(self, i, ntok, tile):
        k, cfg = self.k, self.cfg
        kind, ti = tile
        n_ssm = sorted(self.HT).index(i)
        NSC = cfg.NSC
        if kind == "p":
            nseq, L = 1, ntok
            units = [(u * 128, 128, 0) for u in range(ntok // 128)]
        else:
            nseq, L = NSC, cfg.DEC
            units = [(s * L, L, s) for s in range(nseq)]
        self.rmsnorm(("nmix", i), ntok, self.h)
        A, h, zt = self.A, self.h, self.zt
        cm, cmr = self.cmat, self.cmat_r
        ident_b = self.ident_b
        tri_f = lambda Q: cm[0:Q, 128:128 + Q]
        negm = lambda Q: cm[0:Q, 384:384 + Q]
        tri_r = lambda Q: cmr[0:Q, 128:128 + Q]
        upp_r = lambda Q: cmr[0:Q, 256:256 + Q]
        blocks = self.wp.blocks[("ssm_in", i)]
        for b in range(4):
            buf = self.wload(blocks[b])
            for ui, (o, Q, s) in enumerate(units):
                ps = self.PS.next()
                for kc in range(8):
                    k.op("pe", lambda e, ps=ps, kc=kc, o=o, Q=Q, buf=buf: e.matmul(ps[0:Q, 0:512], lhsT=h[:, kc, o:o + Q], rhs=buf[:, kc * 512:(kc + 1) * 512],
                                                                                   start=(kc == 0), stop=(kc == 7)), r=[h, buf], w=[ps])
                k.op("act", lambda e, ps=ps, ui=ui, Q=Q, b=b: e.activation(out=zt[0:Q, ui, b * 512:(b + 1) * 512], in_=ps[0:Q, 0:512], func=AF.Silu), r=[ps], w=[zt])
        prm_cw, off_cw, _ = self.P(("ssm_convw", i))
        prm_cb, off_cb, _ = self.P(("ssm_convb", i))
        prm_dtb, off_dtb, _ = self.P(("ssm_dtb", i))
        hist = self.hist[i]
        convst, convout = self.convst, self.convout
        W3 = 3 + L

        def cons(col, ncols, ps):
            if col < 2048 + 3072:
                cc = (col - 2048) // 128
                pre = self.pre.next()
                pv = pre[:, 0:nseq * W3].rearrange("p (s t) -> p s t", s=nseq)
                if kind == "p":
                    k.op("act", lambda e: e.copy(out=pre[:, 0:3], in_=hist[:, cc, :]), r=[hist], w=[pre])
                else:
                    k.op("act", lambda e: e.copy(out=pv[:, :, 0:3], in_=convst[:, n_ssm, cc * nseq * 3:(cc + 1) * nseq * 3].rearrange("p (s t) -> p s t", s=nseq)),
                         r=[convst], w=[pre])
                k.op("act", lambda e: e.copy(out=pv[:, :, 3:3 + L], in_=ps[:, 0:ntok].rearrange("p (s t) -> p s t", s=nseq)), r=[ps], w=[pre])
                tmp = self.ctmp.next()
                tv = tmp[:, 0:ntok].rearrange("p (s t) -> p s t", s=nseq)
                k.op("dve", lambda e: e.tensor_scalar(out=tv, in0=pv[:, :, 0:L], scalar1=prm_cw[:, off_cw + cc * 4:off_cw + cc * 4 + 1],
                                                      scalar2=prm_cb[:, off_cb + cc:off_cb + cc + 1], op0=ALU.mult, op1=ALU.add), r=[pre, prm_cw], w=[tmp])
                for t in range(1, 4):
                    k.op("dve", lambda e, t=t: e.scalar_tensor_tensor(out=tv, in0=pv[:, :, t:t + L], scalar=prm_cw[:, off_cw + cc * 4 + t:off_cw + cc * 4 + t + 1],
                                                                      in1=tv, op0=ALU.mult, op1=ALU.add), r=[pre, tmp, prm_cw], w=[tmp])
                k.op("act", lambda e: e.activation(out=A[:, cc, 0:ntok], in_=tmp[:, 0:ntok], func=AF.Silu), r=[tmp], w=[A])
                if kind == "p":
                    k.op("act", lambda e: e.copy(out=hist[:, cc, :], in_=pre[:, L:L + 3]), r=[pre], w=[hist])
                else:
                    k.op("act", lambda e: e.copy(out=convout[:, cc * nseq * 3:(cc + 1) * nseq * 3].rearrange("p (s t) -> p s t", s=nseq), in_=pv[:, :, L:L + 3]),
                         r=[pre], w=[convout])
            else:
                dtx, dta, dtl, dtT, aT = self.ctmp.next(), self.pre.next(), self.ctmp.next(), self.dtT, self.aT
                k.op("dve", lambda e: e.tensor_scalar(out=dtx[0:32, 0:ntok], in0=ps[0:32, 0:ntok], scalar1=prm_dtb[0:32, off_dtb:off_dtb + 1], scalar2=None, op0=ALU.add),
                     r=[ps, prm_dtb], w=[dtx])
                k.op("dve", lambda e: e.scalar_tensor_tensor(out=dta[0:32, 0:ntok], in0=dtx[0:32, 0:ntok], scalar=-1.0, in1=dtx[0:32, 0:ntok], op0=ALU.mult, op1=ALU.max), r=[dtx], w=[dta])
                k.op("act", lambda e: e.activation(out=dta[0:32, 0:ntok], in_=dta[0:32, 0:ntok], func=AF.Exp, scale=-1.0), r=[dta], w=[dta])
                k.op("act", lambda e: e.activation(out=dtl[0:32, 0:ntok], in_=dta[0:32, 0:ntok], func=AF.Ln, bias=self.one_t[0:32, 0:1], scale=1.0), r=[dta, self.one_t], w=[dtl])
                k.op("dve", lambda e: e.scalar_tensor_tensor(out=dtT[:, 0:ntok], in0=dtx[0:32, 0:ntok], scalar=0.0, in1=dtl[0:32, 0:ntok], op0=ALU.max, op1=ALU.add),
                     r=[dtx, dtl], w=[dtT])
                k.op("dve", lambda e: e.tensor_scalar(out=aT[:, 0:ntok], in0=dtT[:, 0:ntok], scalar1=self.acol[0:32, n_ssm:n_ssm + 1], scalar2=None, op0=ALU.mult),
                     r=[dtT, self.acol], w=[aT])

        self.proj_fm(("ssm_in", i), h, 8, ntok, cons, col_lo=2048)
        if kind == "p" and ti == cfg.NT - 1:
            k.dma("pool", self.st_sem, self.conv_p_o[n_ssm], hist[:, :, :].rearrange("p c t -> p (c t)"), r=[hist])
        if kind == "s":
            k.dma("pool", self.st_sem, self.conv_s_o[n_ssm], convout[:, :], r=[convout])
        HT, HTb = self.HT[i], self.HTb
        prm_d, off_d, _ = self.P(("ssm_dbc", i))
        prm_nw, off_nw, _ = self.P(("ssm_normw", i))
        xs_tok, xdt, xdd, xsD, B_tok = self.xs_tok, self.xdt, self.xdd, self.xsD, self.B_tok
        a_tok, dt_tok, acs, eacs, dst, edec = self.a_tok, self.dt_tok, self.acs, self.eacs, self.dst, self.edec
        ysb, ygn, act = self.ysb, self.ygn, self.act
        v3 = lambda ap, a: ap.rearrange("p (a b) -> p a b", a=a)
        bc3 = lambda ap, n: ap.unsqueeze(2).to_broadcast([ap.shape[0], ap.shape[1], n])
        for ui, (o, Q, s) in enumerate(units):
            if kind == "s":
                k.dma("pool", self.st_sem, HT[:, :], self.ssm_in[n_ssm, s], w=[HT])
            if kind == "s" or (ti == 0 and ui == 0) or True:
                k.op("act", lambda e: e.copy(out=HTb[:, :], in_=HT[:, :]), r=[HT], w=[HTb])
            for half in range(2):
                ps = self.PS.next()
                pb = ps[:, :].bitcast(BF16)
                for c in range(8):
                    k.op("pe", lambda e, pb=pb, c=c, half=half, o=o, Q=Q: e.transpose(pb[0:Q, c * 128:(c + 1) * 128], A[:, half * 8 + c, o:o + Q], ident_b[:, :]),
                         r=[A, ident_b], w=[ps])
                k.op("act", lambda e, pb=pb, half=half, Q=Q: e.copy(out=xs_tok[0:Q, half * 1024:(half + 1) * 1024], in_=pb[0:Q, :]), r=[ps], w=[xs_tok])
            ps = self.PS.next()
            pb = ps[:, :].bitcast(BF16)
            for g in range(4):
                k.op("pe", lambda e, pb=pb, g=g, o=o, Q=Q: e.transpose(pb[0:Q, g * 128:(g + 1) * 128], A[:, 16 + g, o:o + Q], ident_b[:, :]), r=[A, ident_b], w=[ps])
            k.op("act", lambda e, pb=pb, Q=Q: e.copy(out=B_tok[0:Q, :], in_=pb[0:Q, 0:512]), r=[ps], w=[B_tok])
            ps = self.PS.next()
            k.op("pe", lambda e, ps=ps, o=o, Q=Q: e.transpose(ps[0:Q, 0:32], self.aT[:, o:o + Q], cm[0:32, 0:32]), r=[self.aT, cm], w=[ps])
            k.op("pe", lambda e, ps=ps, o=o, Q=Q: e.transpose(ps[0:Q, 32:64], self.dtT[:, o:o + Q], cm[0:32, 0:32]), r=[self.dtT, cm], w=[ps])
            k.op("dve", lambda e, ps=ps, Q=Q: e.tensor_copy(out=a_tok[0:Q, :], in_=ps[0:Q, 0:32]), r=[ps], w=[a_tok])
            k.op("dve", lambda e, ps=ps, Q=Q: e.tensor_copy(out=dt_tok[0:Q, :], in_=ps[0:Q, 32:64]), r=[ps], w=[dt_tok])
            ps = self.PS.next()
            k.op("pe", lambda e, ps=ps, Q=Q: e.matmul(ps[0:Q, 0:32], lhsT=tri_r(Q), rhs=a_tok[0:Q, :], start=True, stop=True), r=[cmr, a_tok], w=[ps])
            k.op("pe", lambda e, ps=ps, Q=Q: e.matmul(ps[0:Q, 32:64], lhsT=upp_r(Q), rhs=a_tok[0:Q, :], start=True, stop=True), r=[cmr, a_tok], w=[ps])
            k.op("pe", lambda e, ps=ps, Q=Q: e.matmul(ps[:, 64:96], lhsT=cmr[0:Q, 512:640], rhs=a_tok[0:Q, :], start=True, stop=True), r=[cmr, a_tok], w=[ps])
            k.op("dve", lambda e, ps=ps, Q=Q: e.tensor_copy(out=acs[0:Q, :], in_=ps[0:Q, 0:32]), r=[ps], w=[acs])
            k.op("act", lambda e, ps=ps, Q=Q: e.activation(out=eacs[0:Q, :], in_=ps[0:Q, 0:32], func=AF.Exp), r=[ps], w=[eacs])
            k.op("act", lambda e, ps=ps, Q=Q: e.activation(out=dst[0:Q, :], in_=ps[0:Q, 32:64], func=AF.Exp), r=[ps], w=[dst])
            k.op("act", lambda e, ps=ps: e.activation(out=edec[:, :], in_=ps[:, 64:96], func=AF.Exp), r=[ps], w=[edec])
            k.op("dve", lambda e, Q=Q: e.tensor_tensor(out=v3(xdt[0:Q, :], 32), in0=v3(xs_tok[0:Q, :], 32), in1=bc3(dt_tok[0:Q, :], 64), op=ALU.mult), r=[xs_tok, dt_tok], w=[xdt])
            k.op("dve", lambda e, Q=Q: e.tensor_tensor(out=v3(xdd[0:Q, :], 32), in0=v3(xdt[0:Q, :], 32), in1=bc3(dst[0:Q, :], 64), op=ALU.mult), r=[xdt, dst], w=[xdd])
            k.op("dve", lambda e, Q=Q: e.tensor_tensor(out=v3(xsD[0:Q, :], 32), in0=v3(xs_tok[0:Q, :], 32), in1=bc3(prm_d[0:Q, off_d:off_d + 32], 64), op=ALU.mult),
                 r=[xs_tok, prm_d], w=[xsD])
            for g in range(4):
                psc = self.PS.next()
                k.op("pe", lambda e, psc=psc, g=g, o=o, Q=Q: e.matmul(psc[0:Q, 0:Q], lhsT=A[:, 16 + g, o:o + Q], rhs=A[:, 20 + g, o:o + Q], start=True, stop=True), r=[A], w=[psc])
                CBT = self.CBT.next()
                k.op("act", lambda e, psc=psc, CBT=CBT, Q=Q: e.copy(out=CBT[0:Q, 0:Q], in_=psc[0:Q, 0:Q]), r=[psc], w=[CBT])
                yps = self.PS.next()
                for half in range(2):
                    hs = g * 8 + half * 4
                    Xd = self.XD.next()
                    k.op("dve", lambda e, Xd=Xd, hs=hs, Q=Q: e.tensor_tensor(out=v3(Xd[0:Q, 0:4 * Q], 4), in0=tri_f(Q).unsqueeze(1).to_broadcast([Q, 4, Q]),
                                                                            in1=bc3(a_tok[0:Q, hs:hs + 4], Q), op=ALU.mult), r=[cm, a_tok], w=[Xd])
                    psb = self.PS.next()
                    k.op("pe", lambda e, psb=psb, Xd=Xd, Q=Q: e.matmul(psb[0:Q, 0:4 * Q], lhsT=cmr[0:Q, 512:512 + Q], rhs=Xd[0:Q, 0:4 * Q], start=True, stop=True), r=[cmr, Xd], w=[psb])
                    tmp = self.TMP.next()
                    for hh in range(4):
                        k.op("dve", lambda e, psb=psb, tmp=tmp, hh=hh, hs=hs, Q=Q: e.scalar_tensor_tensor(
                            out=tmp[0:Q, hh * Q:(hh + 1) * Q], in0=psb[0:Q, hh * Q:(hh + 1) * Q], scalar=acs[0:Q, hs + hh:hs + hh + 1], in1=negm(Q),
                            op0=ALU.subtract, op1=ALU.add), r=[psb, acs, cm], w=[tmp])
                    Lx = self.LX.next()
                    k.op("act", lambda e, tmp=tmp, Lx=Lx, Q=Q: e.activation(out=Lx[0:Q, 0:4 * Q], in_=tmp[0:Q, 0:4 * Q], func=AF.Exp), r=[tmp], w=[Lx])
                    MT = self.MT.next()
                    k.op("dve", lambda e, Lx=Lx, MT=MT, CBT=CBT, Q=Q: e.tensor_tensor(out=v3(MT[0:Q, 0:4 * Q], 4), in0=v3(Lx[0:Q, 0:4 * Q], 4),
                                                                                    in1=CBT[0:Q, 0:Q].unsqueeze(1).to_broadcast([Q, 4, Q]), op=ALU.mult), r=[Lx, CBT], w=[MT])
                    for hh in range(4):
                        h8 = half * 4 + hh
                        hg = hs + hh
                        k.op("pe", lambda e, yps=yps, h8=h8, hg=hg, Q=Q: e.matmul(yps[0:Q, h8 * 64:(h8 + 1) * 64], lhsT=ident_b[0:Q, 0:Q], rhs=xsD[0:Q, hg * 64:(hg + 1) * 64],
                                                                                 start=True, stop=False), r=[ident_b, xsD], w=[yps])
                        k.op("pe", lambda e, yps=yps, h8=h8, hg=hg, hh=hh, MT=MT, Q=Q: e.matmul(yps[0:Q, h8 * 64:(h8 + 1) * 64], lhsT=MT[0:Q, hh * Q:(hh + 1) * Q],
                                                                                               rhs=xdt[0:Q, hg * 64:(hg + 1) * 64], start=False, stop=True), r=[MT, xdt], w=[yps])
                pso = self.PS.next()
                k.op("pe", lambda e, pso=pso, g=g, o=o, Q=Q: e.matmul(pso[0:Q, 0:512], lhsT=A[:, 20 + g, o:o + Q], rhs=HTb[:, g * 512:(g + 1) * 512], start=True, stop=True), r=[A, HTb], w=[pso])
                k.op("dve", lambda e, pso=pso, g=g, Q=Q: e.tensor_tensor(out=v3(ysb[0:Q, g * 512:(g + 1) * 512], 8), in0=v3(pso[0:Q, 0:512], 8), in1=bc3(eacs[0:Q, g * 8:(g + 1) * 8], 64), op=ALU.mult),
                     r=[pso, eacs], w=[ysb])
                k.op("dve", lambda e, yps=yps, g=g, Q=Q: e.tensor_tensor(out=ysb[0:Q, g * 512:(g + 1) * 512], in0=ysb[0:Q, g * 512:(g + 1) * 512], in1=yps[0:Q, 0:512], op=ALU.add),
                     r=[ysb, yps], w=[ysb])
                psu = self.PS.next()
                k.op("pe", lambda e, psu=psu, g=g, Q=Q: e.matmul(psu[:, 0:512], lhsT=B_tok[0:Q, g * 128:(g + 1) * 128], rhs=xdd[0:Q, g * 512:(g + 1) * 512], start=True, stop=True), r=[B_tok, xdd], w=[psu])
                k.op("dve", lambda e, g=g: e.tensor_tensor(out=v3(HT[:, g * 512:(g + 1) * 512], 8), in0=v3(HT[:, g * 512:(g + 1) * 512], 8), in1=bc3(edec[:, g * 8:(g + 1) * 8], 64), op=ALU.mult),
                     r=[HT, edec], w=[HT])
                k.op("dve", lambda e, psu=psu, g=g: e.tensor_tensor(out=HT[:, g * 512:(g + 1) * 512], in0=HT[:, g * 512:(g + 1) * 512], in1=psu[:, 0:512], op=ALU.add), r=[HT, psu], w=[HT])
            k.op("dve", lambda e, ui=ui, Q=Q: e.tensor_tensor(out=ysb[0:Q, :], in0=ysb[0:Q, :], in1=zt[0:Q, ui, :], op=ALU.mult), r=[ysb, zt], w=[ysb])
            k.op("dve", lambda e: e.memset(self.ss[:, :], 0.0), w=[self.ss])
            for g in range(4):
                k.op("act", lambda e, g=g, Q=Q: e.activation(out=self.junk[0:Q, :], in_=ysb[0:Q, g * 512:(g + 1) * 512], func=AF.Square, accum_out=self.ss[0:Q, g:g + 1]),
                     r=[ysb], w=[self.junk, self.ss])
            k.op("act", lambda e, Q=Q: e.activation(out=self.rs[0:Q, 0:4], in_=self.ss[0:Q, 0:4], func=AF.Sqrt, bias=self.eps_t[0:Q, 0:1], scale=1.0 / 512), r=[self.ss, self.eps_t], w=[self.rs])
            k.op("dve", lambda e, Q=Q: e.reciprocal(out=self.rs[0:Q, 0:4], in_=self.rs[0:Q, 0:4]), r=[self.rs], w=[self.rs])
            k.op("dve", lambda e, Q=Q: e.tensor_tensor(out=v3(ygn[0:Q, :], 4), in0=v3(ysb[0:Q, :], 4), in1=bc3(self.rs[0:Q, 0:4], 512), op=ALU.mult), r=[ysb, self.rs], w=[ygn])
            for half in range(2):
                ps = self.PS.next()
                pb = ps[:, :].bitcast(BF16)
                for c in range(8):
                    cc = half * 8 + c
                    k.op("pe", lambda e, pb=pb, c=c, cc=cc, Q=Q: e.transpose(pb[:, c * Q:(c + 1) * Q], ygn[0:Q, cc * 128:(cc + 1) * 128], ident_b[0:Q, 0:Q]), r=[ygn, ident_b], w=[ps])
                for c in range(8):
                    cc = half * 8 + c
                    k.op("act", lambda e, pb=pb, c=c, cc=cc, o=o, Q=Q: e.activation(out=act[:, cc, o:o + Q], in_=pb[:, c * Q:(c + 1) * Q], func=AF.Copy, scale=prm_nw[:, off_nw + cc:off_nw + cc + 1]),
                         r=[ps, prm_nw], w=[act])
            if kind == "s":
                k.dma("pool", self.st_sem, self.ssm_s_o[n_ssm, s], HT[:, :], r=[HT])
        if kind == "p" and ti == cfg.NT - 1:
            k.dma("pool", self.st_sem, self.ssm_p_o[n_ssm], HT[:, :], r=[HT])
        self.proj_fm(("ssm_out", i), act, 16, ntok, self.add_to_x(ntok))

    def tile_prompt(self, ti):
        cfg = self.cfg
        TP = cfg.TP
        self.load_x(self.xp[ti * TP:(ti + 1) * TP, :], TP)
        self.layers(TP, ("p", ti))
        self.store_y(self.y_p[ti * TP:(ti + 1) * TP, :], TP)

    def tile_sample(self):
        self.load_x(self.xs, self.TS)
        self.layers(self.TS, ("s", 0))
        self.store_y(self.y_s, self.TS)

    def epilogue(self):
        self.k.wait_all("pool", self.out_sems)


def make_consts(cfg=None):
    i = np.arange(128)
    tri = (i[:, None] <= i[None, :]).astype(np.float32)
    upper = (i[:, None] > i[None, :]).astype(np.float32)
    negmask = np.where(i[None, :] >= i[:, None], 0.0, -1e30).astype(np.float32)
    cm = np.concatenate([np.eye(128, dtype=np.float32), tri, upper, negmask, np.ones((128, 128), np.float32)], axis=1)
    out = {"cmat": np.ascontiguousarray(cm)}
    if cfg is not None and cfg.DEPTH > 2:
        ms = []
        for (W, d) in AGROUPS:
            c = np.arange(W + 128)[None, :]
            diff = W + i[:, None] - c
            ok = (diff >= 0) & (diff <= W) & (diff % d == 0)
            ms.append(np.where(ok, 0.0, -30000.0).astype(np.float32))
        out["amask"] = np.ascontiguousarray(np.concatenate(ms, axis=1))
        NU = cfg.SEQ // 128
        pos = np.concatenate([np.arange(cfg.SEQ), cfg.PAST + np.arange(128)]).astype(np.float32)
        inv_freq = (np.float32(ROPE_THETA) ** (-np.arange(0, AROT, 2, dtype=np.float32) / np.float32(AROT))).astype(np.float32)
        ang = (pos[:, None] * inv_freq[None, :]).astype(np.float32)
        tab = np.concatenate([np.cos(ang), np.sin(ang)], axis=1).astype(np.float32)
        out["rope"] = np.ascontiguousarray(tab.reshape(NU + 1, 128, 16).transpose(1, 0, 2)).reshape(128, (NU + 1) * 16)
    return out


def kernel(**inputs):
    cfg = Cfg()
    prog = Prog(cfg)
    nc = prog.build()
    wp = weight_plan(cfg.DEPTH, inputs)
    pp = param_plan(cfg.DEPTH, inputs)
    wcat, pcat = wp.cat(), pp.cat()
    consts = make_consts(cfg)
    f32 = lambda n: np.asarray(inputs[n], np.float32)
    xp, xs = f32("x_prompt"), f32("x_sample")
    B = xp.shape[0]
    NSC = cfg.NSC
    st_ssm, st_conv, st_gla = f32("state_ssm"), f32("state_ssm_conv"), f32("state_gla")
    caches = [f32(f"cache_kv_g{g}") for g in range(3)]
    in_maps = []
    for c in range(cfg.NCORES):
        sl = slice(c * NSC, (c + 1) * NSC)
        m = {"xp": xp[c % B], "xs": np.ascontiguousarray(xs[sl].reshape(-1, D)), "wcat": wcat, "pcat": pcat,
             "ssm_in_t": lay_ssm_in(st_ssm[:, sl]), "conv_in_fm": lay_conv_in(st_conv[:, sl]),
             "gla_in": np.ascontiguousarray(st_gla[0, sl])}
        for g in range(3):
            m[f"kvc{g}"] = np.ascontiguousarray(caches[g][0, sl].reshape(NSC, -1, 512))
        m.update(consts)
        in_maps.append(m)
    res = run_bass_kernel_spmd(nc, in_maps, core_ids=list(range(cfg.NCORES)))
    R = res.results
    NC = cfg.NCORES
    y_p = np.stack([R[c]["y_p"] for c in range(B)])
    y_s = np.concatenate([R[c]["y_s"].reshape(NSC, cfg.DEC, D) for c in range(NC)])
    ssm_p = np.stack([unlay_ssm(R[c]["ssm_p_t"]) for c in range(B)], axis=1)
    ssm_s = np.concatenate([unlay_ssm(R[c]["ssm_s_t"]) for c in range(NC)], axis=1)
    conv_p = np.concatenate([unlay_conv(R[c]["conv_p_fm"], 1) for c in range(B)], axis=1)
    conv_s = np.concatenate([unlay_conv(R[c]["conv_s_fm"], NSC) for c in range(NC)], axis=1)
    gla_p = np.stack([R[c]["gla_p"] for c in range(B)])[None]
    gla_s = np.concatenate([R[c]["gla_s"] for c in range(NC)])[None]
    outs = [y_p, y_s, ssm_p, ssm_s, conv_p, conv_s, gla_p, gla_s]
    for g in range(3):
        kp = np.stack([R[c][f"kvp{g}"].reshape(-1, 2, AHPG, ADH) for c in range(B)])[None]
        ks = np.concatenate([R[c][f"kvs{g}"].reshape(NSC, -1, 2, AHPG, ADH) for c in range(NC)])[None]
        outs += [kp, ks]
    return tuple(np.ascontiguousarray(o, dtype=np.float32) for o in outs)


def lay_ssm_in(st):
    n, S = st.shape[:2]
    return np.ascontiguousarray(st.reshape(n, S, 2048, 128).transpose(0, 1, 3, 2))


def unlay_ssm(o):
    return np.ascontiguousarray(np.swapaxes(o, -1, -2)).reshape(o.shape[:-2] + (32, 64, 128))


def lay_conv_in(cv):
    n, S = cv.shape[:2]
    return np.ascontiguousarray(cv.reshape(n, S, 3, 24, 128).transpose(0, 4, 3, 1, 2)).reshape(n, 128, 24 * S * 3)


def unlay_conv(o, S):
    n = o.shape[0]
    return np.ascontiguousarray(o.reshape(n, 128, 24, S, 3).transpose(0, 3, 4, 2, 1)).reshape(n, S, 3, 3072)
```

```python
import contextlib
import numpy as np
import concourse.bass as bass
import concourse.mybir as mybir
from concourse.bass_utils import run_bass_kernel_spmd

F32 = mybir.dt.float32
F32R = mybir.dt.float32r
BF16 = mybir.dt.bfloat16
AF = mybir.ActivationFunctionType
ALU = mybir.AluOpType
AX = mybir.AxisListType

ENGS = ("pe", "act", "dve", "pool", "sp")
SAME_ENGINE_SYNC = True

D = 1024
DFF = 2816
EPS = 1e-6
DI, HD, NH, NG, NS, CONVK, CD, SIN = 2048, 64, 32, 4, 128, 4, 3072, 5152
GH, GDK, GDV, GHK, GHV, GRANK, GIN = 4, 512, 1024, 128, 256, 16, 3088
AGROUPS = ((128, 1), (512, 4), (2048, 16))
AHPG, ADH, ANH, AROT = 4, 64, 12, 16
ROPE_THETA = 500000.0


class Cfg:
    def __init__(self, seq=2048, tp=256, ns_core=4, dec=4, past=8192, depth=4, ncores=8, nseg=4):
        self.SEQ, self.TP, self.NSC, self.DEC, self.PAST, self.DEPTH, self.NCORES = seq, tp, ns_core, dec, past, depth, ncores
        self.NSEG = nseg
        self.NT = seq // tp


class T:
    def __init__(self, t, name):
        self.t, self.name, self.w, self.r = t, name, {}, {}

    def __getitem__(self, k):
        return self.t[k]


def TV(parent, ap, name="view"):
    v = T(ap, name)
    v.w, v.r = parent.w, parent.r
    return v


class DSem:
    def __init__(self, sem, name, inc=16):
        self.sem, self.name, self.total, self.inc = sem, name, 0, inc


class KB:
    def __init__(self, nc, stack):
        self.nc, self.stack = nc, stack
        self.q = {e: [] for e in ENGS}
        self.cnt = {e: 0 for e in ENGS}
        self.sem = {e: stack.enter_context(nc.semaphore("s_" + e)) for e in ENGS if e != "sp"}
        self.waited = {}
        self.n = 0
        self.dyn = {}
        self.dyn_setup = None

    def sbuf(self, shape, dtype, name=None):
        self.n += 1
        name = name or f"sb{self.n}"
        return T(self.stack.enter_context(self.nc.sbuf_tensor(name, list(shape), dtype)), name)

    def psum(self, shape, dtype=F32, name=None):
        self.n += 1
        name = name or f"ps{self.n}"
        return T(self.stack.enter_context(self.nc.psum_tensor(name, list(shape), dtype)), name)

    def dram(self, name, shape, dtype, kind="Internal"):
        return T(self.nc.dram_tensor(name, list(shape), dtype, kind=kind).ap(), name)

    def dsem(self, name=None, inc=16):
        self.n += 1
        name = name or f"d{self.n}"
        return DSem(self.stack.enter_context(self.nc.semaphore("ds_" + name)), name, inc)

    def _deps(self, eng, reads, writes):
        deps = {}
        for t in reads:
            for k, c in t.w.items():
                if deps.get(k, 0) < c:
                    deps[k] = c
        for t in writes:
            for dd in (t.w, t.r):
                for k, c in dd.items():
                    if deps.get(k, 0) < c:
                        deps[k] = c
        out = []
        for k, c in deps.items():
            if k == eng and (eng == "pe" or not SAME_ENGINE_SYNC):
                continue
            if self.waited.get((eng, k), 0) >= c:
                continue
            self.waited[(eng, k)] = c
            out.append(((k.sem if isinstance(k, DSem) else self.sem[k]), c))
        return out

    def op(self, eng, fn, r=(), w=()):
        sems = self._deps(eng, r, w)
        self.cnt[eng] += 1
        c = self.cnt[eng]
        mysem = self.sem[eng]

        def emit(e, sems=sems, fn=fn, mysem=mysem):
            for s, v in sems:
                e.wait_ge(s, v)
            fn(e).then_inc(mysem, 1)

        self.q[eng].append(emit)
        for t in r:
            t.r[eng] = c
        for t in w:
            t.w[eng] = c

    def dma(self, qeng, dsem, out, in_, r=(), w=(), **kw):
        sems = self._deps(qeng, r, w)
        if dsem.total > 0 and self.waited.get((qeng, dsem), 0) < dsem.total:
            self.waited[(qeng, dsem)] = dsem.total
            sems = sems + [(dsem.sem, dsem.total)]
        dsem.total += 16
        c = dsem.total

        def emit(e, sems=sems, out=out, in_=in_, kw=kw, ds=dsem.sem):
            for s, v in sems:
                e.wait_ge(s, v)
            o_ = out(self.dyn) if callable(out) else out
            i_ = in_(self.dyn) if callable(in_) else in_
            e.dma_start(out=o_, in_=i_, **kw).then_inc(ds, 16)

        self.q[qeng].append(emit)
        for t in r:
            t.r[dsem] = c
        for t in w:
            t.w[dsem] = c

    def coll(self, dsem, kind, in_ap, out_ap, ncores, r=(), w=()):
        sems = self._deps("pool", r, w)
        dsem.total += 1
        c = dsem.total

        def emit(e, sems=sems, ds=dsem.sem):
            for s, v in sems:
                e.wait_ge(s, v)
            e.collective_compute(kind, ALU.bypass, replica_groups=[list(range(ncores))], ins=[in_ap], outs=[out_ap]).then_inc(ds)

        self.q["pool"].append(emit)
        for t in r:
            t.r[dsem] = c
        for t in w:
            t.w[dsem] = c

    def wait_all(self, eng, dsems):
        sems = [(d.sem, d.total) for d in dsems if d.total > 0]

        def emit(e, sems=sems):
            for s, v in sems:
                e.wait_ge(s, v)

        self.q[eng].append(emit)

    def finish(self):
        with self.nc.Block() as block:
            @block.tensor
            def _(e):
                for f in self.q["pe"]:
                    f(e)

            @block.scalar
            def _(e):
                for f in self.q["act"]:
                    f(e)

            @block.vector
            def _(e):
                for f in self.q["dve"]:
                    f(e)

            @block.gpsimd
            def _(e):
                with contextlib.ExitStack() as rs:
                    if self.dyn_setup is not None:
                        self.dyn_setup(e, rs)
                    for f in self.q["pool"]:
                        f(e)

            @block.sync
            def _(e):
                for f in self.q["sp"]:
                    f(e)


class Ring:
    def __init__(self, bufs):
        self.bufs, self.i = bufs, 0

    def next(self):
        b = self.bufs[self.i % len(self.bufs)]
        self.i += 1
        return b


def _bw(K):
    kc = K // 128
    return 512 if kc <= 8 else (256 if kc <= 16 else 128)


class WPack:
    def __init__(self):
        self.blocks = {}
        self.tot = 0
        self.parts = []

    def add(self, key, W):
        K, N = W.shape
        kc, bw = K // 128, _bw(K)
        nblk = -(-N // bw)
        Wp = np.zeros((K, nblk * bw), np.float32)
        Wp[:, :N] = W
        til = Wp.reshape(kc, 128, nblk, bw).transpose(2, 1, 0, 3).reshape(nblk, 128, kc * bw)
        lst = []
        for b in range(nblk):
            lst.append((self.tot, kc, bw, min(bw, N - b * bw)))
            self.parts.append(til[b])
            self.tot += kc * bw
        self.blocks[key] = lst

    def add_meta(self, key, K, N):
        kc, bw = K // 128, _bw(K)
        nblk = -(-N // bw)
        lst = []
        for b in range(nblk):
            lst.append((self.tot, kc, bw, min(bw, N - b * bw)))
            self.tot += kc * bw
        self.blocks[key] = lst

    def cat(self):
        pad = (-self.tot) % 2048
        arr = np.concatenate(self.parts + [np.zeros((128, pad), np.float32)], axis=1)
        return np.ascontiguousarray(arr)


W_SHAPES = {"ssm_in": (D, SIN), "ssm_out": (DI, D), "gla_in": (D, GIN), "gla_out": (GDV, D),
            "att_qkv": (D, 3 * ANH * ADH), "att_out": (ANH * ADH, D),
            "ffn_gate": (D, DFF), "ffn_up": (D, DFF), "ffn_down": (DFF, D)}


def weight_plan(depth, inputs=None):
    wp = WPack()

    def add(key, name, j):
        if inputs is None:
            wp.add_meta(key, *W_SHAPES[key[0]])
        else:
            wp.add(key, np.asarray(inputs[name][j], np.float32))

    for i in range(depth):
        m, j = i % 3, i // 3
        if m == 0:
            add(("ssm_in", i), "ssm_w_in", j)
            add(("ssm_out", i), "ssm_w_out", j)
        elif m == 1:
            add(("gla_in", i), "gla_w_in", j)
            add(("gla_out", i), "gla_w_out", j)
        else:
            add(("att_qkv", i), "att_w_qkv", j)
            add(("att_out", i), "att_w_out", j)
        add(("ffn_gate", i), "ffn_gate", i)
        add(("ffn_up", i), "ffn_up", i)
        add(("ffn_down", i), "ffn_down", i)
    return wp


def fm(v, nchunk):
    return np.ascontiguousarray(np.asarray(v, np.float32).reshape(nchunk, 128).T)


class PPack:
    def __init__(self):
        self.off, self.tot, self.parts = {}, 0, []

    def add(self, key, arr=None, width=None):
        if arr is not None:
            arr = np.asarray(arr, np.float32)
            assert arr.shape[0] == 128
            arr = arr.reshape(128, -1)
            width = arr.shape[1]
            self.parts.append(arr)
        self.off[key] = (self.tot, width)
        self.tot += width

    def cat(self):
        return np.ascontiguousarray(np.concatenate(self.parts, axis=1))


def param_plan(depth, inputs=None):
    pp = PPack()
    g = (lambda name: np.asarray(inputs[name], np.float32)) if inputs is not None else None

    def add(key, fn, width):
        pp.add(key, fn() if inputs is not None else None, width)

    for i in range(depth):
        add(("nmix", i), lambda: fm(g("norm_mix")[i], 8), 8)
        add(("nffn", i), lambda: fm(g("norm_ffn")[i], 8), 8)
    add(("nfin",), lambda: fm(g("norm_final"), 8), 8)
    add(("core",), lambda: np.zeros((128, 8), np.float32), 8)
    for i in range(depth):
        m, j = i % 3, i // 3
        if m == 0:
            add(("ssm_convw", i), lambda: np.ascontiguousarray(g("ssm_conv_w")[j].reshape(4, 24, 128).transpose(2, 1, 0)).reshape(128, 96), 96)
            add(("ssm_convb", i), lambda: fm(g("ssm_conv_b")[j], 24), 24)
            add(("ssm_dtb", i), lambda: np.pad(g("ssm_dt_bias")[j].reshape(32, 1), ((0, 96), (0, 0))), 1)
            add(("ssm_alog", i), lambda: np.pad(g("ssm_a_log")[j].reshape(32, 1), ((0, 96), (0, 0))), 1)
            add(("ssm_dbc", i), lambda: np.broadcast_to(g("ssm_d")[j].reshape(1, 32), (128, 32)).copy(), 32)
            add(("ssm_normw", i), lambda: fm(g("ssm_norm")[j], 16), 16)
        elif m == 1:
            add(("gla_wg", i), lambda: np.pad(g("gla_w_gate")[j], ((0, 112), (0, 0))), 512)
            add(("gla_gb", i), lambda: fm(g("gla_gate_bias")[j], 4), 4)
            add(("gla_nw", i), lambda: np.broadcast_to(g("gla_norm")[j].reshape(1, 256), (128, 256)).copy(), 256)
    return pp


class Prog:
    def __init__(self, cfg):
        self.cfg = cfg
        self.wp = weight_plan(cfg.DEPTH)
        self.pp = param_plan(cfg.DEPTH)

    def build(self):
        cfg = self.cfg
        nc = bass.Bass("TRN2", target_bir_lowering=False)
        self.nc = nc
        with contextlib.ExitStack() as st:
            k = KB(nc, st)
            self.k = k
            self.declare_io()
            self.alloc()
            self.prologue()
            self.prepass()
            for i in range(cfg.DEPTH):
                m = i % 3
                if getattr(cfg, "stop", None):
                    if cfg.stop != "prepass":
                        self.layer_ssd(i)
                    break
                if m == 0:
                    self.layer_ssd(i)
                elif m == 1:
                    self.layer_gla(i)
                else:
                    self.layer_att(i)
                self.sample_layer(i)
            self.epilogue()
            k.finish()
        return nc

    def declare_io(self):
        nc, cfg = self.nc, self.cfg
        I = lambda n, s: nc.dram_tensor(n, list(s), F32, kind="ExternalInput").ap()
        O = lambda n, s: nc.dram_tensor(n, list(s), F32, kind="ExternalOutput").ap()
        TS = cfg.NSC * cfg.DEC
        self.TS = TS
        self.xp = I("xp", (cfg.SEQ, D))
        self.xs = I("xs", (TS, D))
        self.wcat = I("wcat", (128, self.wp.tot + ((-self.wp.tot) % 2048)))
        self.pcat = I("pcat", (128, self.pp.tot))
        self.cmat_in = I("cmat", (128, 640))
        self.rank_in = nc.dram_tensor("rankinfo", [1, 2], mybir.dt.int32, kind="ExternalInput").ap()
        self.y_p = O("y_p", (cfg.SEQ, D))
        self.y_s = O("y_s", (TS, D))

    def alloc(self):
        k, cfg = self.k, self.cfg
        TP = cfg.TP
        self.TMAX = TP
        self.out_sems = []
        self.wtot = self.wp.tot + ((-self.wp.tot) % 2048)
        self.wb = k.dram("wcat_b", (128, self.wtot), BF16)
        self.params = k.sbuf((128, self.pp.tot), F32, "params")
        self.cmat = k.sbuf((128, 640), F32, "cmat_s")
        self.cmat_r = k.sbuf((128, 640), F32R, "cmat_r")
        self.ident_b = k.sbuf((128, 128), BF16, "ident_b")
        self.ones_b = k.sbuf((128, 128), BF16, "ones_b")
        self.eps_t = k.sbuf((128, 1), F32, "eps_t")
        self.one_t = k.sbuf((128, 1), F32, "one_t")
        self.x = k.sbuf((128, 8, TP), F32, "x")
        self.h = k.sbuf((128, 8, TP), BF16, "h")
        self.sq = Ring([k.sbuf((128, TP), BF16, f"sq{i}") for i in range(2)])
        self.rstd = k.sbuf((128, TP), F32, "rstd")
        self.act = k.sbuf((128, 22, TP), BF16, "act")
        self.sil = Ring([k.sbuf((128, TP), F32, f"sil{i}") for i in range(2)])
        self.wring = Ring([k.sbuf((128, 4096), BF16, f"wr{i}") for i in range(3)])
        self.wsem = [k.dsem(f"w{i}") for i in range(3)]
        self.PS = Ring([k.psum((128, 512), F32, f"psb{i}") for i in range(8)])
        self.pp_ps = self.PS
        self.pt_ps = self.PS
        self.alloc_mixers()
        self.xin = Ring([k.sbuf((128, D), F32, f"xin{i}") for i in range(2)])
        self.xin_sem = [k.dsem(f"xin{i}") for i in range(2)]
        self.yout = Ring(self.xin.bufs)
        self.yout_sem = [k.dsem(f"yo{i}") for i in range(2)]
        self.misc_sem = k.dsem("misc")
        NT = cfg.NT
        self.xscr = k.dram("xscr", (NT + 1, 128, 8 * TP), F32)
        self.Ascr = k.dram("Ascr", (NT, 128, 24 * TP), BF16)
        self.zscr = k.dram("zscr", (NT, 128, 2 * 2048), BF16)
        self.dscr = k.dram("dscr", (NT, 32, 2 * TP), F32)
        self.escr = k.dram("escr", (NT, 128, 16), F32)
        self.qscr = k.dram("qscr", (NT, 2, 128, 768), BF16)
        self.xs_sem = [k.dsem(f"xs{i}") for i in range(2)]
        self.xs_i = 0
        self.sc_sems = [k.dsem(f"sc{i}") for i in range(4)]
        self.sc_i = 0
        self.out_sems += self.xs_sem + self.sc_sems
        self.small = k.sbuf((128, 8, 32), F32, "small")
        self.atsum = k.sbuf((128, 32), F32, "atsum")
        self.histsave = k.sbuf((128, 24, 3), F32, "histsave")
        self.blsum = k.sbuf((128, 4), F32, "blsum")

        def dyn_setup(e, rs):
            rp = rs.enter_context(e.register("r_prev"))
            rb = rs.enter_context(e.register("r_base"))
            e.reg_load(rp, self.rank_in[0:1, 0:1])
            e.reg_load(rb, self.rank_in[0:1, 1:2])
            k.dyn["prev"] = e.snap(rp, min_val=0, max_val=cfg.NCORES - 1)
            k.dyn["base"] = e.snap(rb, min_val=0, max_val=cfg.NCORES - cfg.NSEG)
        k.dyn_setup = dyn_setup
        self.out_sems += list(self.yout_sem)

    def P(self, key):
        off, w = self.pp.off[key]
        return self.params, off, w

    def prologue(self):
        k = self.k
        k.dma("pool", self.misc_sem, self.params[:], self.pcat, w=[self.params])
        k.dma("pool", self.misc_sem, self.cmat[:], self.cmat_in, w=[self.cmat])
        k.op("dve", lambda e: e.tensor_copy(out=self.ident_b[:], in_=self.cmat[:, 0:128]), r=[self.cmat], w=[self.ident_b])
        k.op("dve", lambda e: e.tensor_copy(out=self.cmat_r[:], in_=self.cmat[:]), r=[self.cmat], w=[self.cmat_r])
        k.op("dve", lambda e: e.memset(self.one_t[:], 1.0), w=[self.one_t])
        self.prologue_mixers()
        k.op("dve", lambda e: e.memset(self.ones_b[:], 1.0), w=[self.ones_b])
        k.op("dve", lambda e: e.memset(self.eps_t[:], EPS), w=[self.eps_t])
        CW = 1024
        stf = self.xin.bufs
        stb = [self.xs_tok, self.xdt]
        sin = [k.dsem(f"ci{i}") for i in range(2)]
        sout = [k.dsem(f"co{i}") for i in range(2)]
        engs = ["act", "dve", "pool"]
        n = self.wtot // CW
        for i in range(n):
            s = i % 2
            k.dma("sp", sin[s], stf[s][:], self.wcat[:, i * CW:(i + 1) * CW], w=[stf[s]])
            eng = engs[i % 3]
            if eng == "act":
                k.op("act", lambda e, s=s: e.copy(out=stb[s][:, 0:CW], in_=stf[s][:]), r=[stf[s]], w=[stb[s]])
            else:
                k.op(eng, lambda e, s=s: e.tensor_copy(out=stb[s][:, 0:CW], in_=stf[s][:]), r=[stf[s]], w=[stb[s]])
            k.dma("sp", sout[s], self.wb[:, i * CW:(i + 1) * CW], stb[s][:, 0:CW], r=[stb[s]], w=[self.wb])

    def wload(self, blk):
        off, kc, bw, nv = blk
        i = self.wring.i % 3
        buf = self.wring.next()
        k = self.k
        k.dma("sp", self.wsem[i], buf[:, 0:kc * bw], self.wb[:, off:off + kc * bw], r=[self.wb], w=[buf])
        return buf

    def proj_fm(self, wkey, src, kc_n, ntok, consumer, col_lo=0, col_hi=None):
        k = self.k
        blocks = self.wp.blocks[wkey]
        for b, blk in enumerate(blocks):
            off, kc, bw, nv = blk
            assert kc == kc_n
            c0 = b * bw
            if col_hi is not None and c0 >= col_hi:
                break
            if c0 + nv <= col_lo:
                continue
            buf = self.wload(blk)
            for j in range(0, nv, 128):
                col = c0 + j
                if col < col_lo or (col_hi is not None and col >= col_hi):
                    continue
                ncols = min(128, nv - j)
                ps = self.pp_ps.next()
                for kc_i in range(kc):
                    k.op("pe", lambda e, ps=ps, buf=buf, kc_i=kc_i, j=j, bw=bw, ncols=ncols:
                         e.matmul(ps[0:ncols, 0:ntok], lhsT=buf[:, kc_i * bw + j:kc_i * bw + j + ncols],
                                  rhs=src[:, kc_i, 0:ntok], start=(kc_i == 0), stop=(kc_i == kc - 1)),
                         r=[buf, src], w=[ps])
                consumer(col, ncols, ps)

    def rmsnorm(self, gkey, ntok, out, out_is_f32=False):
        k = self.k
        x = self.x
        ps = self.pp_ps.next()
        for c in range(8):
            sq = self.sq.next()
            k.op("act", lambda e, sq=sq, c=c: e.activation(out=sq[:, 0:ntok], in_=x[:, c, 0:ntok], func=AF.Square), r=[x], w=[sq])
            k.op("pe", lambda e, sq=sq, c=c, ps=ps: e.matmul(ps[:, 0:ntok], lhsT=self.ones_b[:], rhs=sq[:, 0:ntok], start=(c == 0), stop=(c == 7)),
                 r=[sq, self.ones_b], w=[ps])
        rstd = self.rstd
        k.op("act", lambda e: e.activation(out=rstd[:, 0:ntok], in_=ps[:, 0:ntok], func=AF.Sqrt, bias=self.eps_t[:, 0:1], scale=1.0 / D), r=[ps, self.eps_t], w=[rstd])
        k.op("dve", lambda e: e.reciprocal(out=rstd[:, 0:ntok], in_=rstd[:, 0:ntok]), r=[rstd], w=[rstd])
        prm, off, _ = self.P(gkey)
        for c in range(8):
            k.op("dve", lambda e, c=c: e.scalar_tensor_tensor(out=out[:, c, 0:ntok], in0=x[:, c, 0:ntok], scalar=prm[:, off + c:off + c + 1],
                                                              in1=rstd[:, 0:ntok], op0=ALU.mult, op1=ALU.mult), r=[x, rstd, prm], w=[out])

    def ffn(self, i, ntok):
        k = self.k
        self.rmsnorm(("nffn", i), ntok, self.h)
        act = self.act
        gb, ub = self.wp.blocks[("ffn_gate", i)], self.wp.blocks[("ffn_up", i)]
        for b in range(len(gb)):
            off, kc, bw, nv = gb[b]
            bufg = self.wload(gb[b])
            bufu = self.wload(ub[b])
            for j in range(0, nv, 128):
                ch = (b * bw + j) // 128
                psg, psu = self.pp_ps.next(), self.pp_ps.next()
                for buf, ps in ((bufg, psg), (bufu, psu)):
                    for kc_i in range(kc):
                        k.op("pe", lambda e, ps=ps, buf=buf, kc_i=kc_i, j=j, bw=bw:
                             e.matmul(ps[:, 0:ntok], lhsT=buf[:, kc_i * bw + j:kc_i * bw + j + 128],
                                      rhs=self.h[:, kc_i, 0:ntok], start=(kc_i == 0), stop=(kc_i == kc - 1)),
                             r=[buf, self.h], w=[ps])
                sil = self.sil.next()
                k.op("act", lambda e, sil=sil, psg=psg: e.activation(out=sil[:, 0:ntok], in_=psg[:, 0:ntok], func=AF.Silu), r=[psg], w=[sil])
                k.op("dve", lambda e, sil=sil, psu=psu, ch=ch: e.tensor_tensor(out=act[:, ch, 0:ntok], in0=sil[:, 0:ntok], in1=psu[:, 0:ntok], op=ALU.mult),
                     r=[sil, psu], w=[act])

        x = self.x

        def cons_down(col, ncols, ps):
            c = col // 128
            k.op("dve", lambda e: e.tensor_tensor(out=x[:, c, 0:ntok], in0=x[:, c, 0:ntok], in1=ps[:, 0:ntok], op=ALU.add), r=[x, ps], w=[x])

        self.proj_fm(("ffn_down", i), act, 22, ntok, cons_down)

    def load_x(self, src_ap, ntok):
        k = self.k
        for t0 in range(0, ntok, 128):
            n = min(128, ntok - t0)
            i = self.xin.i % 2
            xin = self.xin.next()
            k.dma("pool", self.xin_sem[i], xin[0:n, :], src_ap[t0:t0 + n, :], w=[xin])
            for c4 in range(2):
                ps = self.pt_ps.next()
                for cc in range(4):
                    c = c4 * 4 + cc
                    k.op("pe", lambda e, ps=ps, cc=cc, c=c, xin=xin, n=n: e.transpose(ps[:, cc * 128:cc * 128 + n], xin[0:n, c * 128:(c + 1) * 128], self.cmat[0:n, 0:n]),
                         r=[xin, self.cmat], w=[ps])
                k.op("act", lambda e, ps=ps, c4=c4, n=n, t0=t0: e.copy(out=self.x[:, c4 * 4:c4 * 4 + 4, t0:t0 + n],
                                                                         in_=ps[:, :].rearrange("p (c t) -> p c t", c=4)[:, :, 0:n]), r=[ps], w=[self.x])

    def store_y(self, dst_ap, ntok):
        k = self.k
        self.hf = T(self.ysb.t[:, :].rearrange("p (c t) -> p c t", c=8), "hfview")
        self.hf.w, self.hf.r = self.ysb.w, self.ysb.r
        self.rmsnorm(("nfin",), ntok, self.hf)
        for t0 in range(0, ntok, 128):
            n = min(128, ntok - t0)
            i = self.yout.i % 2
            yo = self.yout.next()
            for c4 in range(2):
                ps = self.pt_ps.next()
                for cc in range(4):
                    c = c4 * 4 + cc
                    k.op("pe", lambda e, ps=ps, cc=cc, c=c, n=n, t0=t0: e.transpose(ps[0:n, cc * 128:(cc + 1) * 128], self.hf[:, c, t0:t0 + n], self.cmat[:, 0:128]),
                         r=[self.hf, self.cmat], w=[ps])
                k.op("act", lambda e, ps=ps, c4=c4, n=n, yo=yo: e.copy(out=yo[0:n, c4 * 512:(c4 + 1) * 512], in_=ps[0:n, :]), r=[ps], w=[yo])
            k.dma("pool", self.yout_sem[i], dst_ap[t0:t0 + n, :], yo[0:n, :], r=[yo])

    def layers(self, ntok, tile):
        cfg = self.cfg
        for i in range(cfg.DEPTH):
            if cfg.mixers:
                self.mixer(i, ntok, tile)
            self.ffn(i, ntok)

    def alloc_mixers(self):
        k, cfg = self.k, self.cfg
        TP = cfg.TP
        self.A = k.sbuf((128, 24, TP), BF16, "A")
        self.zt = k.sbuf((128, 4, 2048), BF16, "zt")
        self.pre = Ring([k.sbuf((128, TP + 16), F32, f"pre{i}") for i in range(2)])
        self.ctmp = Ring([k.sbuf((128, TP), F32, f"ctmp{i}") for i in range(2)])
        self.hist = {}
        self.HT = {}
        for i in range(cfg.DEPTH):
            if i % 3 == 0:
                self.hist[i] = k.sbuf((128, 24, 3), F32, f"hist{i}")
                self.HT[i] = k.sbuf((128, 2048), F32, f"HT{i}")
        self.HTb = k.sbuf((128, 2048), BF16, "HTb")
        self.acol = k.sbuf((128, 4), F32, "acol")
        self.dtT = k.sbuf((32, TP), F32, "dtT")
        self.aT = k.sbuf((32, TP), F32, "aT")
        self.xs_tok = k.sbuf((128, 2048), BF16, "xs_tok")
        self.xdt = k.sbuf((128, 2048), BF16, "xdt")
        self.xdd = k.sbuf((128, 2048), BF16, "xdd")
        self.xsD = k.sbuf((128, 2048), BF16, "xsD")
        self.B_tok = k.sbuf((128, 512), BF16, "B_tok")
        self.a_tok = k.sbuf((128, 32), F32R, "a_tok")
        self.dt_tok = k.sbuf((128, 32), F32, "dt_tok")
        self.acs = k.sbuf((128, 32), F32, "acs")
        self.eacs = k.sbuf((128, 32), F32, "eacs")
        self.dst = k.sbuf((128, 32), F32, "dst")
        self.edec = k.sbuf((128, 32), F32, "edec")
        self.CBT = Ring([k.sbuf((128, 128), F32, f"CBT{i}") for i in range(2)])
        self.XD = Ring([k.sbuf((128, 512), F32R, f"XD{i}") for i in range(2)])
        self.TMP = Ring([k.sbuf((128, 512), F32, f"TMP{i}") for i in range(2)])
        self.LX = Ring([k.sbuf((128, 512), F32, f"LX{i}") for i in range(2)])
        self.MT = Ring([k.sbuf((128, 512), BF16, f"MT{i}") for i in range(2)])
        self.ysb = k.sbuf((128, 2048), F32, "ysb")
        self.ygn = k.sbuf((128, 2048), BF16, "ygn")
        self.junk = k.sbuf((128, 512), F32, "junk")
        self.ss = k.sbuf((128, 8), F32, "ss")
        self.rs = k.sbuf((128, 8), F32, "rs")
        self.st_sem = k.dsem("st")
        self.out_sems.append(self.st_sem)
        nc = self.nc
        NSC = cfg.NSC
        nssm = len(self.HT)
        I = lambda n, s: nc.dram_tensor(n, list(s), F32, kind="ExternalInput").ap()
        O = lambda n, s: nc.dram_tensor(n, list(s), F32, kind="ExternalOutput").ap()
        self.ssm_in = I("ssm_in_t", (nssm, NSC, 128, 2048))
        self.conv_in = I("conv_in_fm", (nssm, 128, 24 * NSC * 3))
        self.ssm_p_o = O("ssm_p_t", (nssm, 128, 2048))
        self.ssm_s_o = O("ssm_s_t", (nssm, NSC, 128, 2048))
        self.conv_p_o = O("conv_p_fm", (nssm, 128, 72))
        self.conv_s_o = O("conv_s_fm", (nssm, 128, 24 * NSC * 3))
        self.convst = k.sbuf((128, nssm, 24 * NSC * 3), F32, "convst")
        self.convout = k.sbuf((128, 24 * NSC * 3), F32, "convout")
        if cfg.DEPTH > 1:
            self.alloc_gla()
        if cfg.DEPTH > 2:
            self.alloc_att()

    def prologue_mixers(self):
        k, cfg = self.k, self.cfg
        for n, i in enumerate(sorted(self.HT)):
            k.op("dve", lambda e, i=i: e.memset(self.hist[i][:], 0.0), w=[self.hist[i]])
            k.op("dve", lambda e, i=i: e.memset(self.HT[i][:], 0.0), w=[self.HT[i]])
            prm, off, _ = self.P(("ssm_alog", i))
            k.op("act", lambda e, n=n, off=off: e.activation(out=self.acol[:, n:n + 1], in_=prm[:, off:off + 1], func=AF.Exp), r=[prm], w=[self.acol])
            k.op("dve", lambda e, n=n: e.tensor_scalar(out=self.acol[:, n:n + 1], in0=self.acol[:, n:n + 1], scalar1=-1.0, scalar2=None, op0=ALU.mult), r=[self.acol], w=[self.acol])
            k.dma("pool", self.misc_sem, self.convst[:, n, :], self.conv_in[n], w=[self.convst])
        if cfg.DEPTH > 2:
            self.prologue_att()

    def mixer(self, i, ntok, tile, phase="all"):
        m = i % 3
        if m == 0:
            self.mixer_ssd(i, ntok, tile, phase)
        elif m == 1:
            self.mixer_gla(i, ntok, tile, phase)
        else:
            self.mixer_att(i, ntok, tile, phase)

    def add_to_x(self, ntok):
        k, x = self.k, self.x

        def cons(col, ncols, ps):
            c = col // 128
            k.op("dve", lambda e: e.tensor_tensor(out=x[:, c, 0:ntok], in0=x[:, c, 0:ntok], in1=ps[:, 0:ntok], op=ALU.add), r=[x, ps], w=[x])
        return cons

    def alloc_gla(self):
        k, cfg, nc = self.k, self.cfg, self.nc
        TP, NSC = cfg.TP, cfg.NSC
        self.glow = k.sbuf((16, TP), F32, "glow")
        self.G3 = k.sbuf((128, 3, 4, TP), F32, "G3")
        self.la = [TV(self.G3, self.G3[:, i], f"la{i}") for i in range(2)]
        self.ebx = TV(self.G3, self.G3[:, 2], "ebx")
        self.eblast = k.sbuf((128, 16), F32, "eblast")
        self.S = k.sbuf((128, 1024), F32, "S")
        self.Sb = k.sbuf((128, 1024), BF16, "Sb")
        I = lambda n, s: nc.dram_tensor(n, list(s), F32, kind="ExternalInput").ap()
        O = lambda n, s: nc.dram_tensor(n, list(s), F32, kind="ExternalOutput").ap()
        self.gla_in = I("gla_in", (NSC, 4, 128, 256))
        self.gla_p_o = O("gla_p", (4, 128, 256))
        self.gla_s_o = O("gla_s", (NSC, 4, 128, 256))

    def mixer_gla(self, i, ntok, tile, phase="all"):
        k, cfg = self.k, self.cfg
        kind, ti = tile
        do_proj, do_y = True, phase != "p1"
        NSC = cfg.NSC
        if kind == "p":
            units = [(u * 128, 128, 0) for u in range(ntok // 128)]
        else:
            units = [(s * cfg.DEC, cfg.DEC, s) for s in range(NSC)]
        nun, Q = len(units), units[0][1]
        A, h, zt, act = self.A, self.h, self.zt, self.act
        cm, ident_b = self.cmat, self.ident_b
        S, Sb = self.S, self.Sb
        glow, la, ebx, eblast = self.glow, self.la, self.ebx, self.eblast
        blocks = self.wp.blocks[("gla_in", i)]
        prm_wg, off_wg, _ = self.P(("gla_wg", i))
        prm_gb, off_gb, _ = self.P(("gla_gb", i))
        prm_nw, off_nw, _ = self.P(("gla_nw", i))
        if phase == "p1" and ti == 0:
            k.op("dve", lambda e: e.memset(S[:, :], 0.0), w=[S])
            k.op("dve", lambda e: e.memset(Sb[:, :], 0.0), w=[Sb])
            k.op("dve", lambda e: e.memset(self.blsum[:, :], 0.0), w=[self.blsum])

        if do_proj:
            self.rmsnorm(("nmix", i), ntok, self.h)
            def cons(col, ncols, ps):
                if col < 1024:
                    k.op("act", lambda e: e.copy(out=A[:, col // 128, 0:ntok], in_=ps[:, 0:ntok]), r=[ps], w=[A])
                else:
                    k.op("act", lambda e: e.copy(out=glow[:, 0:ntok], in_=ps[0:16, 0:ntok]), r=[ps], w=[glow])
            self.proj_fm(("gla_in", i), h, 8, ntok, cons, col_lo=0, col_hi=1024)
            self.proj_fm(("gla_in", i), h, 8, ntok, cons, col_lo=3072)
            for b in (2, 3, 4, 5):
                buf = self.wload(blocks[b])
                for ui, (o, Qu, s) in enumerate(units):
                    ps = self.PS.next()
                    for kc in range(8):
                        k.op("pe", lambda e, ps=ps, kc=kc, o=o, buf=buf: e.matmul(ps[0:Q, 0:512], lhsT=h[:, kc, o:o + Q], rhs=buf[:, kc * 512:(kc + 1) * 512],
                                                                                  start=(kc == 0), stop=(kc == 7)), r=[h, buf], w=[ps])
                    if b >= 4:
                        k.op("act", lambda e, ps=ps, ui=ui, b=b: e.activation(out=zt[0:Q, ui, (b - 4) * 512:(b - 3) * 512], in_=ps[0:Q, 0:512], func=AF.Silu), r=[ps], w=[zt])
                    else:
                        k.op("act", lambda e, ps=ps, ui=ui, b=b: e.copy(out=zt[0:Q, ui, 1024 + (b - 2) * 512:1024 + (b - 1) * 512], in_=ps[0:Q, 0:512]), r=[ps], w=[zt])
            for hh in range(4):
                ps = self.PS.next()
                k.op("pe", lambda e, ps=ps, hh=hh: e.matmul(ps[:, 0:ntok], lhsT=prm_wg[0:16, off_wg + hh * 128:off_wg + (hh + 1) * 128], rhs=glow[:, 0:ntok], start=True, stop=True),
                     r=[prm_wg, glow], w=[ps])
                xb, ax = self.ctmp.next(), self.pre.next()
                k.op("dve", lambda e, ps=ps, xb=xb, hh=hh: e.tensor_scalar(out=xb[:, 0:ntok], in0=ps[:, 0:ntok], scalar1=prm_gb[:, off_gb + hh:off_gb + hh + 1], scalar2=None, op0=ALU.add),
                     r=[ps, prm_gb], w=[xb])
                k.op("dve", lambda e, xb=xb, ax=ax: e.scalar_tensor_tensor(out=ax[:, 0:ntok], in0=xb[:, 0:ntok], scalar=-1.0, in1=xb[:, 0:ntok], op0=ALU.mult, op1=ALU.max), r=[xb], w=[ax])
                k.op("act", lambda e, ax=ax: e.activation(out=ax[:, 0:ntok], in_=ax[:, 0:ntok], func=AF.Exp, scale=-1.0), r=[ax], w=[ax])
                k.op("act", lambda e, ax=ax: e.activation(out=ax[:, 0:ntok], in_=ax[:, 0:ntok], func=AF.Ln, bias=self.one_t[:, 0:1], scale=1.0), r=[ax, self.one_t], w=[ax])
                k.op("dve", lambda e, xb=xb, ax=ax, hh=hh: e.scalar_tensor_tensor(out=la[0][:, hh, 0:ntok], in0=xb[:, 0:ntok], scalar=0.0, in1=ax[:, 0:ntok], op0=ALU.min, op1=ALU.subtract),
                     r=[xb, ax], w=[la[0]])
            cur = 0
            sh = 1
            vw = lambda t: t[:, :, 0:ntok].rearrange("p h (u q) -> p h u q", q=Q)
            while sh < Q:
                src, dst_ = la[cur], la[1 - cur]
                k.op("dve", lambda e, src=src, dst_=dst_, sh=sh: e.tensor_copy(out=vw(dst_)[:, :, :, 0:sh], in_=vw(src)[:, :, :, 0:sh]), r=[src], w=[dst_])
                k.op("dve", lambda e, src=src, dst_=dst_, sh=sh: e.tensor_tensor(out=vw(dst_)[:, :, :, sh:Q], in0=vw(src)[:, :, :, sh:Q], in1=vw(src)[:, :, :, 0:Q - sh], op=ALU.add), r=[src], w=[dst_])
                cur = 1 - cur
                sh *= 2
            bc_ = la[cur]
            k.op("act", lambda e: e.activation(out=ebx[:, :, 0:ntok], in_=bc_[:, :, 0:ntok], func=AF.Exp, scale=1.0 / 16), r=[bc_], w=[ebx])
            k.op("dve", lambda e: e.scalar_tensor_tensor(out=A[:, 8:12, 0:ntok], in0=A[:, 0:4, 0:ntok], scalar=float(GHK) ** -0.5, in1=ebx[:, :, 0:ntok], op0=ALU.mult, op1=ALU.mult),
                 r=[A, ebx], w=[A])
            k.op("act", lambda e: e.activation(out=eblast[:, 0:4 * nun].rearrange("p (h u) -> p h u", h=4), in_=vw(bc_)[:, :, :, Q - 1], func=AF.Exp, scale=1.0 / 16), r=[bc_], w=[eblast])
            k.op("act", lambda e: e.activation(out=ebx[:, :, 0:ntok], in_=bc_[:, :, 0:ntok], func=AF.Exp, scale=-1.0 / 16), r=[bc_], w=[ebx])
            k.op("dve", lambda e: e.tensor_tensor(out=A[:, 12:16, 0:ntok], in0=A[:, 4:8, 0:ntok], in1=ebx[:, :, 0:ntok], op=ALU.mult), r=[A, ebx], w=[A])
            k.op("dve", lambda e: e.tensor_tensor(out=vw(A[:, 16:20, :]), in0=vw(A[:, 12:16, :]), in1=eblast[:, 0:4 * nun].rearrange("p (h u) -> p h u", h=4).unsqueeze(3).to_broadcast([128, 4, nun, Q]), op=ALU.mult),
                 r=[A, eblast], w=[A])
            if phase == "p1":
                for u_ in range(nun):
                    k.op("dve", lambda e, u_=u_: e.tensor_tensor(out=self.blsum[:, 0:4], in0=self.blsum[:, 0:4], in1=vw(bc_)[:, :, u_, Q - 1], op=ALU.add), r=[self.blsum, bc_], w=[self.blsum])
        ysb, ygn, kd_tok = self.ysb, self.ygn, self.B_tok
        for ui, (o, Qu, s) in enumerate(units):
            if kind == "s":
                k.dma("pool", self.st_sem, S[:, :].rearrange("p (h v) -> p h v", h=4), self.gla_in[s].rearrange("h k v -> k h v"), w=[S])
                k.op("act", lambda e: e.copy(out=Sb[:, :], in_=S[:, :]), r=[S], w=[Sb])
            ps = self.PS.next()
            pb = ps[:, :].bitcast(BF16)
            for hh in range(4):
                k.op("pe", lambda e, pb=pb, hh=hh, o=o: e.transpose(pb[0:Q, hh * 128:(hh + 1) * 128], A[:, 16 + hh, o:o + Q], ident_b[:, :]), r=[A, ident_b], w=[ps])
            k.op("act", lambda e, pb=pb: e.copy(out=kd_tok[0:Q, :], in_=pb[0:Q, 0:512]), r=[ps], w=[kd_tok])
            MT = self.MT.next()
            k.op("dve", lambda e: e.memset(self.ss[:, :], 0.0), w=[self.ss])
            ops = []
            for hh in range(4):
                vt = zt[0:Q, ui, 1024 + hh * 256:1024 + (hh + 1) * 256]
                if do_y:
                    psa = self.PS.next()
                    k.op("pe", lambda e, psa=psa, hh=hh, o=o: e.matmul(psa[0:Q, 0:Q], lhsT=A[:, 12 + hh, o:o + Q], rhs=A[:, 8 + hh, o:o + Q], start=True, stop=True), r=[A], w=[psa])
                    k.op("dve", lambda e, psa=psa, hh=hh, MT=MT: e.tensor_tensor(out=MT[0:Q, hh * Q:(hh + 1) * Q], in0=psa[0:Q, 0:Q], in1=cm[0:Q, 128:128 + Q], op=ALU.mult), r=[psa, cm], w=[MT])
                    pso = self.PS.next()
                    k.op("pe", lambda e, pso=pso, hh=hh, MT=MT, vt=vt: e.matmul(pso[0:Q, 0:256], lhsT=MT[0:Q, hh * Q:(hh + 1) * Q], rhs=vt, start=True, stop=False), r=[MT, zt], w=[pso])
                    k.op("pe", lambda e, pso=pso, hh=hh, o=o: e.matmul(pso[0:Q, 0:256], lhsT=A[:, 8 + hh, o:o + Q], rhs=Sb[:, hh * 256:(hh + 1) * 256], start=False, stop=True), r=[A, Sb], w=[pso])
                    k.op("act", lambda e, pso=pso, hh=hh: e.activation(out=self.junk[0:Q, 0:256], in_=pso[0:Q, 0:256], func=AF.Square, accum_out=self.ss[0:Q, hh:hh + 1]),
                         r=[pso], w=[self.junk, self.ss])
                    k.op("act", lambda e, pso=pso, hh=hh: e.copy(out=ysb[0:Q, hh * 256:(hh + 1) * 256], in_=pso[0:Q, 0:256]), r=[pso], w=[ysb])
                psu = self.PS.next()
                k.op("pe", lambda e, psu=psu, hh=hh, vt=vt: e.matmul(psu[:, 0:256], lhsT=kd_tok[0:Q, hh * 128:(hh + 1) * 128], rhs=vt, start=True, stop=True), r=[kd_tok, zt], w=[psu])
                k.op("dve", lambda e, psu=psu, hh=hh, ui=ui: e.scalar_tensor_tensor(out=S[:, hh * 256:(hh + 1) * 256], in0=S[:, hh * 256:(hh + 1) * 256], scalar=eblast[:, hh * nun + ui:hh * nun + ui + 1],
                                                                                   in1=psu[:, 0:256], op0=ALU.mult, op1=ALU.add), r=[S, eblast, psu], w=[S])
            k.op("act", lambda e: e.copy(out=Sb[:, :], in_=S[:, :]), r=[S], w=[Sb])
            if do_y:
                k.op("act", lambda e: e.activation(out=self.rs[0:Q, 0:4], in_=self.ss[0:Q, 0:4], func=AF.Sqrt, bias=self.eps_t[0:Q, 0:1], scale=1.0 / GHV), r=[self.ss, self.eps_t], w=[self.rs])
                k.op("dve", lambda e: e.reciprocal(out=self.rs[0:Q, 0:4], in_=self.rs[0:Q, 0:4]), r=[self.rs], w=[self.rs])
                for hh in range(4):
                    k.op("dve", lambda e, hh=hh: e.scalar_tensor_tensor(out=ysb[0:Q, hh * 256:(hh + 1) * 256], in0=ysb[0:Q, hh * 256:(hh + 1) * 256], scalar=self.rs[0:Q, hh:hh + 1],
                                                                       in1=prm_nw[0:Q, off_nw:off_nw + 256], op0=ALU.mult, op1=ALU.mult), r=[ysb, self.rs, prm_nw], w=[ysb])
                k.op("dve", lambda e, ui=ui: e.tensor_tensor(out=ygn[0:Q, 0:1024], in0=ysb[0:Q, 0:1024], in1=zt[0:Q, ui, 0:1024], op=ALU.mult), r=[ysb, zt], w=[ygn])
                ps = self.PS.next()
                pb = ps[:, :].bitcast(BF16)
                for c in range(8):
                    k.op("pe", lambda e, pb=pb, c=c: e.transpose(pb[:, c * Q:(c + 1) * Q], ygn[0:Q, c * 128:(c + 1) * 128], ident_b[0:Q, 0:Q]), r=[ygn, ident_b], w=[ps])
                k.op("act", lambda e, pb=pb, o=o: e.copy(out=act[:, 0:8, o:o + Q], in_=pb[:, 0:8 * Q].rearrange("p (c q) -> p c q", c=8)), r=[ps], w=[act])
            if kind == "s":
                k.dma("pool", self.st_sem, self.gla_s_o[s].rearrange("h k v -> k h v"), S[:, :].rearrange("p (h v) -> p h v", h=4), r=[S])
        if kind == "p" and ti == cfg.NT - 1 and phase == "p2":
            k.dma("pool", self.st_sem, self.gla_p_o.rearrange("h k v -> k h v"), S[:, :].rearrange("p (h v) -> p h v", h=4), r=[S])
        if do_y:
            self.proj_fm(("gla_out", i), act, 8, ntok, self.add_to_x(ntok))

    def alloc_att(self):
        k, cfg, nc = self.k, self.cfg, self.nc
        NSC, SEQ = cfg.NSC, cfg.SEQ
        I = lambda n, s: nc.dram_tensor(n, list(s), F32, kind="ExternalInput").ap()
        O = lambda n, s: nc.dram_tensor(n, list(s), F32, kind="ExternalOutput").ap()
        self.NU = SEQ // 128
        self.vtok = k.sbuf((128, 768), F32, "vtok")
        self.maskb = k.sbuf((128, 3072), BF16, "maskb")
        self.ropeT = k.sbuf((128, (self.NU + 1) * 16), F32, "ropeT")
        self.ast = k.sbuf((128, 64), F32, "ast")
        self.rope_in = I("rope", (128, (self.NU + 1) * 16))
        self.mask_in = I("amask", (128, 3072))
        self.kt_hist = k.dram("kt_hist", (6, 128, SEQ), BF16)
        self.v_hist = k.dram("v_hist", (SEQ, 768), BF16)
        self.cache = [I(f"kvc{g}", (NSC, W, 512)) for g, (W, d) in enumerate(AGROUPS)]
        self.kvp_o = [O(f"kvp{g}", (min(W, SEQ), 512)) for g, (W, d) in enumerate(AGROUPS)]
        self.kvs_o = [O(f"kvs{g}", (NSC, W, 512)) for g, (W, d) in enumerate(AGROUPS)]
        self.att_sems = [k.dsem(f"att{i}") for i in range(4)]
        self.att_i = 0
        self.out_sems += self.att_sems
        ztf = self.zt.t[:, :, :].rearrange("p a b -> p (a b)")
        self.KT_sb = TV(self.zt, ztf[:, 0:4608].rearrange("p (c t) -> p c t", c=2), "KT_sb")
        self.Pb = TV(self.zt, ztf[:, 4608:4608 + 2304], "Pb")
        g3f = self.G3.t[:, :, :, :].rearrange("p a b c -> p (a b c)").bitcast(BF16)
        self.V_sb = TV(self.G3, g3f[:, 0:4608].rearrange("p (b f) -> p b f", f=256), "V_sb")

    def prologue_att(self):
        k = self.k
        k.dma("pool", self.misc_sem, self.ropeT[:, :], self.rope_in, w=[self.ropeT])
        for i in range(3):
            st = self.xin.bufs[i % 2]
            k.dma("pool", self.misc_sem, st[:, :], self.mask_in[:, i * 1024:(i + 1) * 1024], w=[st])
            k.op("dve", lambda e, st=st, i=i: e.tensor_copy(out=self.maskb[:, i * 1024:(i + 1) * 1024], in_=st[:, :]), r=[st], w=[self.maskb])

    def adma(self, out, in_, r=(), w=()):
        s = self.att_sems[self.att_i % 4]
        self.att_i += 1
        self.k.dma("pool", s, out, in_, r=r, w=w)

    def mixer_att(self, i, ntok, tile, phase="all"):
        k, cfg = self.k, self.cfg
        kind, ti = tile
        SEQ, NSC, TP = cfg.SEQ, cfg.NSC, cfg.TP
        if kind == "p":
            units = [(u * 128, 128, 0, ti * TP + u * 128, (ti * TP) // 128 + u) for u in range(ntok // 128)]
        else:
            units = [(s * cfg.DEC, cfg.DEC, s, cfg.PAST, self.NU) for s in range(NSC)]
        self.rmsnorm(("nmix", i), ntok, self.h)
        A, h, act, cm, ident_b = self.A, self.h, self.act, self.cmat, self.ident_b
        ysb, ygn, vtok, vb, ast = self.ysb, self.ygn, self.vtok, self.xs_tok, self.ast
        KT_sb, V_sb, Pb, U = self.KT_sb, self.V_sb, self.Pb, self.xin.bufs[0]
        maskb, ropeT = self.maskb, self.ropeT
        blocks = self.wp.blocks[("att_qkv", i)]
        def unit_body(o, Q, s, q0, ru):
            if True:
                for b, blk in enumerate(blocks):
                    buf = self.wload(blk)
                    nv = blk[3]
                    ps = self.PS.next()
                    for kc in range(8):
                        k.op("pe", lambda e, ps=ps, kc=kc, buf=buf, nv=nv: e.matmul(ps[0:Q, 0:nv], lhsT=h[:, kc, o:o + Q], rhs=buf[:, kc * 512:kc * 512 + nv],
                                                                                   start=(kc == 0), stop=(kc == 7)), r=[h, buf], w=[ps])
                    c0 = b * 512
                    if c0 < 1536:
                        k.op("act", lambda e, ps=ps, c0=c0, nv=nv: e.copy(out=ysb[0:Q, c0:c0 + nv], in_=ps[0:Q, 0:nv]), r=[ps], w=[ysb])
                    else:
                        k.op("act", lambda e, ps=ps, c0=c0, nv=nv: e.copy(out=vtok[0:Q, c0 - 1536:c0 - 1536 + nv], in_=ps[0:Q, 0:nv]), r=[ps], w=[vtok])
                qk = ysb[0:Q, 0:1536].rearrange("p (h d) -> p h d", d=64)
                cosb = ropeT[0:Q, ru * 16:ru * 16 + 8].unsqueeze(1).to_broadcast([Q, 24, 8])
                sinb = ropeT[0:Q, ru * 16 + 8:ru * 16 + 16].unsqueeze(1).to_broadcast([Q, 24, 8])
                t0_, t1_ = self.TMP.next(), self.LX.next()
                ta = t0_[0:Q, 0:192].rearrange("p (h d) -> p h d", d=8)
                tb = t0_[0:Q, 192:384].rearrange("p (h d) -> p h d", d=8)
                tc = t1_[0:Q, 0:192].rearrange("p (h d) -> p h d", d=8)
                td = t1_[0:Q, 192:384].rearrange("p (h d) -> p h d", d=8)
                x1, x2 = qk[:, :, 0:8], qk[:, :, 8:16]
                k.op("dve", lambda e: e.tensor_tensor(out=ta, in0=x1, in1=cosb, op=ALU.mult), r=[ysb, ropeT], w=[t0_])
                k.op("dve", lambda e: e.tensor_tensor(out=tb, in0=x2, in1=sinb, op=ALU.mult), r=[ysb, ropeT], w=[t0_])
                k.op("dve", lambda e: e.tensor_tensor(out=tc, in0=x2, in1=cosb, op=ALU.mult), r=[ysb, ropeT], w=[t1_])
                k.op("dve", lambda e: e.tensor_tensor(out=td, in0=x1, in1=sinb, op=ALU.mult), r=[ysb, ropeT], w=[t1_])
                k.op("dve", lambda e: e.tensor_tensor(out=x1, in0=ta, in1=tb, op=ALU.subtract), r=[t0_], w=[ysb])
                k.op("dve", lambda e: e.tensor_tensor(out=x2, in0=tc, in1=td, op=ALU.add), r=[t1_], w=[ysb])
                k.op("dve", lambda e: e.tensor_copy(out=ygn[0:Q, 0:1536], in_=ysb[0:Q, 0:1536]), r=[ysb], w=[ygn])
                k.op("act", lambda e: e.copy(out=vb[0:Q, 0:768], in_=vtok[0:Q, :]), r=[vtok], w=[vb])
                for part in range(2):
                    ps = self.PS.next()
                    pb = ps[:, :].bitcast(BF16)
                    for c in range(6):
                        k.op("pe", lambda e, pb=pb, c=c, part=part: e.transpose(pb[:, c * Q:(c + 1) * Q], ygn[0:Q, part * 768 + c * 128:part * 768 + (c + 1) * 128], ident_b[0:Q, 0:Q]),
                             r=[ygn, ident_b], w=[ps])
                    k.op("act", lambda e, pb=pb, part=part: e.copy(out=A[:, part * 6:(part + 1) * 6, 0:Q], in_=pb[:, 0:6 * Q].rearrange("p (c q) -> p c q", c=6)), r=[ps], w=[A])
                if phase == "p2":
                    pass
                elif kind == "p":
                    self.adma(self.kt_hist[:, :, q0:q0 + Q].rearrange("c p t -> p c t"), A[:, 6:12, 0:Q], r=[A], w=[self.kt_hist])
                    self.adma(self.v_hist[q0:q0 + Q, :], vb[0:Q, 0:768], r=[vb], w=[self.v_hist])
                    for g, (W, d) in enumerate(AGROUPS):
                        Wc = min(W, SEQ)
                        if q0 + Q > SEQ - Wc:
                            r0 = q0 - (SEQ - Wc)
                            self.adma(self.kvp_o[g][r0:r0 + Q, 0:256], ysb[0:Q, 768 + g * 256:768 + (g + 1) * 256], r=[ysb])
                            self.adma(self.kvp_o[g][r0:r0 + Q, 256:512], vtok[0:Q, g * 256:(g + 1) * 256], r=[vtok])
                else:
                    for g, (W, d) in enumerate(AGROUPS):
                        self.adma(self.kvs_o[g][s, W - Q:W, 0:256], ysb[0:Q, 768 + g * 256:768 + (g + 1) * 256], r=[ysb])
                        self.adma(self.kvs_o[g][s, W - Q:W, 256:512], vtok[0:Q, g * 256:(g + 1) * 256], r=[vtok])
                        self.adma(self.kvs_o[g][s, 0:W - Q, :], self.cache[g][s, Q:W, :])
                if phase == "p1":
                    return
            for g, (W, d) in enumerate(AGROUPS):
                nkp = 0
                if kind == "p":
                    nkp = max(0, W - q0)
                    lo = max(q0 - W, 0)
                    nkl = q0 + Q - lo
                    nk, moff = nkp + nkl, 0
                    if nkp > 0:
                        lk, lv = self.loc_k, self.loc_v
                        self.adma(KT_sb[:, :, 0:nkp], lk[2 * g * 128:(2 * g + 2) * 128, SEQ - nkp:SEQ].rearrange("(c p) t -> p c t", p=128), r=[lk], w=[KT_sb])
                        self.adma(V_sb[:, 0:nkp // 128, :], lv[SEQ - nkp:SEQ, g * 256:(g + 1) * 256].rearrange("(b p) f -> p b f", p=128), r=[lv], w=[V_sb])
                    self.adma(KT_sb[:, :, nkp:nk], self.kt_hist[2 * g:2 * g + 2, :, lo:lo + nkl].rearrange("c p t -> p c t"), r=[self.kt_hist], w=[KT_sb])
                    self.adma(V_sb[:, nkp // 128:nk // 128, :], self.v_hist[lo:lo + nkl, g * 256:(g + 1) * 256].rearrange("(b p) f -> p b f", p=128), r=[self.v_hist], w=[V_sb])
                else:
                    nk, moff = W + Q, 0
                    stg = self.xin.bufs[1]
                    for blk in range(W // 128):
                        self.adma(stg[:, 0:512], self.cache[g][s, blk * 128:(blk + 1) * 128, :], w=[stg])
                        k.op("act", lambda e, blk=blk: e.copy(out=V_sb[:, blk, :], in_=stg[:, 256:512]), r=[stg], w=[V_sb])
                        ps = self.PS.next()
                        for c in range(2):
                            k.op("pe", lambda e, ps=ps, c=c: e.transpose(ps[:, c * 128:(c + 1) * 128], stg[:, c * 128:(c + 1) * 128], cm[:, 0:128]), r=[stg, cm], w=[ps])
                        k.op("act", lambda e, ps=ps, blk=blk: e.copy(out=KT_sb[:, :, blk * 128:(blk + 1) * 128], in_=ps[:, 0:256].rearrange("p (c t) -> p c t", c=2)), r=[ps], w=[KT_sb])
                    k.op("act", lambda e, g=g, W=W: e.copy(out=KT_sb[:, :, W:W + Q], in_=A[:, 6 + 2 * g:8 + 2 * g, 0:Q]), r=[A], w=[KT_sb])
                    k.op("act", lambda e, g=g, W=W: e.copy(out=V_sb[0:Q, W // 128, :], in_=vb[0:Q, g * 256:(g + 1) * 256]), r=[vb], w=[V_sb])
                nkb = -(-nk // 128)
                mcol0 = (0, 256, 896)[g] + moff
                for j in range(4):
                    hd = 4 * g + j
                    c, half = j // 2, j % 2
                    pl, ph = half * 64, half * 64 + 64
                    chunks = [(k0, min(512, nk - k0)) for k0 in range(0, nk, 512)]
                    pss = []
                    for ci, (k0, n) in enumerate(chunks):
                        ps = self.PS.next()
                        pss.append(ps)
                        k.op("pe", lambda e, ps=ps, k0=k0, n=n, g=g, c=c, pl=pl, ph=ph: e.matmul(ps[0:Q, 0:n], lhsT=A[pl:ph, 2 * g + c, 0:Q], rhs=KT_sb[pl:ph, c, k0:k0 + n], start=True, stop=True),
                             r=[A, KT_sb], w=[ps])
                        k.op("dve", lambda e, ps=ps, k0=k0, n=n, mcol0=mcol0: e.tensor_tensor(out=ps[0:Q, 0:n], in0=ps[0:Q, 0:n], in1=maskb[0:Q, mcol0 + k0:mcol0 + k0 + n], op=ALU.add),
                             r=[ps, maskb], w=[ps])
                        if k0 < nkp:
                            n2 = min(n, nkp - k0)
                            prm_c, off_c, _ = self.P(("core",))
                            k.op("dve", lambda e, ps=ps, n2=n2: e.tensor_scalar(out=ps[0:Q, 0:n2], in0=ps[0:Q, 0:n2], scalar1=prm_c[0:Q, off_c + 5:off_c + 6], scalar2=None, op0=ALU.add),
                                 r=[ps, prm_c], w=[ps])
                        k.op("dve", lambda e, ps=ps, n=n, ci=ci: e.reduce_max(out=ast[0:Q, 56 + ci:57 + ci], in_=ps[0:Q, 0:n], axis=AX.X), r=[ps], w=[ast])
                    nch = len(chunks)
                    k.op("dve", lambda e, hd=hd, nch=nch: e.reduce_max(out=ast[0:Q, hd:hd + 1], in_=ast[0:Q, 56:56 + nch], axis=AX.X), r=[ast], w=[ast])
                    k.op("dve", lambda e, hd=hd: e.tensor_scalar(out=ast[0:Q, 63:64], in0=ast[0:Q, hd:hd + 1], scalar1=-0.125, scalar2=None, op0=ALU.mult), r=[ast], w=[ast])
                    k.op("dve", lambda e: e.memset(ast[0:Q, 56:62], 0.0), r=[ast], w=[ast])
                    for ci, (k0, n) in enumerate(chunks):
                        ps = pss[ci]
                        k.op("act", lambda e, ps=ps, k0=k0, n=n, ci=ci: e.activation(out=Pb[0:Q, k0:k0 + n], in_=ps[0:Q, 0:n], func=AF.Exp, bias=ast[0:Q, 63:64], scale=0.125,
                                                                                   accum_out=ast[0:Q, 56 + ci:57 + ci]), r=[ps, ast], w=[Pb, ast])
                    k.op("dve", lambda e, hd=hd, nch=nch: e.reduce_sum(out=ast[0:Q, 12 + hd:13 + hd], in_=ast[0:Q, 56:56 + nch], axis=AX.X), r=[ast], w=[ast])
                    ups = self.PS.next()
                    for b0 in range(0, nkb, 4):
                        nb = min(4, nkb - b0)
                        pt = self.PS.next()
                        ptb = pt[:, :].bitcast(BF16)
                        for bb in range(nb):
                            b = b0 + bb
                            kn = min(128, nk - b * 128)
                            k.op("pe", lambda e, ptb=ptb, bb=bb, b=b, kn=kn: e.transpose(ptb[0:kn, bb * Q:(bb + 1) * Q], Pb[0:Q, b * 128:b * 128 + kn], ident_b[0:Q, 0:Q]), r=[Pb, ident_b], w=[pt])
                        MT = self.MT.next()
                        kmax = min(128, nk - b0 * 128)
                        k.op("act", lambda e, ptb=ptb, MT=MT, nb=nb, kmax=kmax: e.copy(out=MT[0:kmax, 0:nb * Q], in_=ptb[0:kmax, 0:nb * Q]), r=[pt], w=[MT])
                        for bb in range(nb):
                            b = b0 + bb
                            kn = min(128, nk - b * 128)
                            k.op("pe", lambda e, ups=ups, MT=MT, bb=bb, b=b, kn=kn, j=j, nkb=nkb: e.matmul(ups[0:Q, 0:64], lhsT=MT[0:kn, bb * Q:(bb + 1) * Q], rhs=V_sb[0:kn, b, j * 64:(j + 1) * 64],
                                                                                                 start=(b == 0), stop=(b == nkb - 1)), r=[MT, V_sb], w=[ups])
                    k.op("act", lambda e, ups=ups, hd=hd: e.copy(out=U[0:Q, hd * 64:(hd + 1) * 64], in_=ups[0:Q, 0:64]), r=[ups], w=[U])
            m3 = ast[0:Q, 0:12].rearrange("p (g j) -> p g j", g=3)
            d3 = ast[0:Q, 12:24].rearrange("p (g j) -> p g j", g=3)
            w3 = ast[0:Q, 24:36].rearrange("p (g j) -> p g j", g=3)
            f3 = ast[0:Q, 44:56].rearrange("p (g j) -> p g j", g=3)
            Mx, Zs = ast[0:Q, 36:40], ast[0:Q, 40:44]
            k.op("dve", lambda e: e.tensor_tensor(out=Mx, in0=m3[:, 0, :], in1=m3[:, 1, :], op=ALU.max), r=[ast], w=[ast])
            k.op("dve", lambda e: e.tensor_tensor(out=Mx, in0=Mx, in1=m3[:, 2, :], op=ALU.max), r=[ast], w=[ast])
            k.op("dve", lambda e: e.tensor_tensor(out=w3, in0=m3, in1=Mx.unsqueeze(1).to_broadcast([Q, 3, 4]), op=ALU.subtract), r=[ast], w=[ast])
            k.op("act", lambda e: e.activation(out=w3, in_=w3, func=AF.Exp, scale=0.125), r=[ast], w=[ast])
            k.op("dve", lambda e: e.tensor_tensor(out=f3, in0=w3, in1=d3, op=ALU.mult), r=[ast], w=[ast])
            k.op("dve", lambda e: e.tensor_tensor(out=Zs, in0=f3[:, 0, :], in1=f3[:, 1, :], op=ALU.add), r=[ast], w=[ast])
            k.op("dve", lambda e: e.tensor_tensor(out=Zs, in0=Zs, in1=f3[:, 2, :], op=ALU.add), r=[ast], w=[ast])
            k.op("dve", lambda e: e.reciprocal(out=Zs, in_=Zs), r=[ast], w=[ast])
            k.op("dve", lambda e: e.tensor_tensor(out=f3, in0=w3, in1=Zs.unsqueeze(1).to_broadcast([Q, 3, 4]), op=ALU.mult), r=[ast], w=[ast])
            k.op("dve", lambda e: e.tensor_tensor(out=ygn[0:Q, 0:768].rearrange("p (h d) -> p h d", d=64), in0=U[0:Q, 0:768].rearrange("p (h d) -> p h d", d=64),
                                                  in1=ast[0:Q, 44:56].unsqueeze(2).to_broadcast([Q, 12, 64]), op=ALU.mult), r=[U, ast], w=[ygn])
            ps = self.PS.next()
            pb = ps[:, :].bitcast(BF16)
            for c in range(6):
                k.op("pe", lambda e, pb=pb, c=c: e.transpose(pb[:, c * Q:(c + 1) * Q], ygn[0:Q, c * 128:(c + 1) * 128], ident_b[0:Q, 0:Q]), r=[ygn, ident_b], w=[ps])
            k.op("act", lambda e, pb=pb: e.copy(out=act[:, 0:6, o:o + Q], in_=pb[:, 0:6 * Q].rearrange("p (c q) -> p c q", c=6)), r=[ps], w=[act])

        for u_ in units:
            unit_body(*u_)
        if phase != "p1":
            self.proj_fm(("att_out", i), act, 6, ntok, self.add_to_x(ntok))

    def mixer_ssd(self, i, ntok, tile, phase="all"):
        k, cfg = self.k, self.cfg
        kind, ti = tile
        do_proj, do_y = True, phase != "p1"
        n_ssm = sorted(self.HT).index(i)
        NSC = cfg.NSC
        if kind == "p":
            nseq, L = 1, ntok
            units = [(u * 128, 128, 0) for u in range(ntok // 128)]
        else:
            nseq, L = NSC, cfg.DEC
            units = [(s * L, L, s) for s in range(nseq)]
        A, h, zt = self.A, self.h, self.zt
        cm, cmr = self.cmat, self.cmat_r
        ident_b = self.ident_b
        tri_f = lambda Q: cm[0:Q, 128:128 + Q]
        negm = lambda Q: cm[0:Q, 384:384 + Q]
        tri_r = lambda Q: cmr[0:Q, 128:128 + Q]
        upp_r = lambda Q: cmr[0:Q, 256:256 + Q]
        blocks = self.wp.blocks[("ssm_in", i)]
        if do_proj:
            self.rmsnorm(("nmix", i), ntok, self.h)
            for b in range(4):
                buf = self.wload(blocks[b])
                for ui, (o, Q, s) in enumerate(units):
                    ps = self.PS.next()
                    for kc in range(8):
                        k.op("pe", lambda e, ps=ps, kc=kc, o=o, Q=Q, buf=buf: e.matmul(ps[0:Q, 0:512], lhsT=h[:, kc, o:o + Q], rhs=buf[:, kc * 512:(kc + 1) * 512],
                                                                                       start=(kc == 0), stop=(kc == 7)), r=[h, buf], w=[ps])
                    k.op("act", lambda e, ps=ps, ui=ui, Q=Q, b=b: e.activation(out=zt[0:Q, ui, b * 512:(b + 1) * 512], in_=ps[0:Q, 0:512], func=AF.Silu), r=[ps], w=[zt])
        prm_cw, off_cw, _ = self.P(("ssm_convw", i))
        prm_cb, off_cb, _ = self.P(("ssm_convb", i))
        prm_dtb, off_dtb, _ = self.P(("ssm_dtb", i))
        hist = self.hist[i]
        convst, convout = self.convst, self.convout
        W3 = 3 + L

        def cons(col, ncols, ps):
            if col < 2048 + 3072:
                cc = (col - 2048) // 128
                pre = self.pre.next()
                pv = pre[:, 0:nseq * W3].rearrange("p (s t) -> p s t", s=nseq)
                if kind == "p":
                    k.op("act", lambda e: e.copy(out=pre[:, 0:3], in_=hist[:, cc, :]), r=[hist], w=[pre])
                else:
                    k.op("act", lambda e: e.copy(out=pv[:, :, 0:3], in_=convst[:, n_ssm, cc * nseq * 3:(cc + 1) * nseq * 3].rearrange("p (s t) -> p s t", s=nseq)),
                         r=[convst], w=[pre])
                k.op("act", lambda e: e.copy(out=pv[:, :, 3:3 + L], in_=ps[:, 0:ntok].rearrange("p (s t) -> p s t", s=nseq)), r=[ps], w=[pre])
                tmp = self.ctmp.next()
                tv = tmp[:, 0:ntok].rearrange("p (s t) -> p s t", s=nseq)
                k.op("dve", lambda e: e.tensor_scalar(out=tv, in0=pv[:, :, 0:L], scalar1=prm_cw[:, off_cw + cc * 4:off_cw + cc * 4 + 1],
                                                      scalar2=prm_cb[:, off_cb + cc:off_cb + cc + 1], op0=ALU.mult, op1=ALU.add), r=[pre, prm_cw], w=[tmp])
                for t in range(1, 4):
                    k.op("dve", lambda e, t=t: e.scalar_tensor_tensor(out=tv, in0=pv[:, :, t:t + L], scalar=prm_cw[:, off_cw + cc * 4 + t:off_cw + cc * 4 + t + 1],
                                                                      in1=tv, op0=ALU.mult, op1=ALU.add), r=[pre, tmp, prm_cw], w=[tmp])
                k.op("act", lambda e: e.activation(out=A[:, cc, 0:ntok], in_=tmp[:, 0:ntok], func=AF.Silu), r=[tmp], w=[A])
                if kind == "p":
                    k.op("act", lambda e: e.copy(out=hist[:, cc, :], in_=pre[:, L:L + 3]), r=[pre], w=[hist])
                else:
                    k.op("act", lambda e: e.copy(out=convout[:, cc * nseq * 3:(cc + 1) * nseq * 3].rearrange("p (s t) -> p s t", s=nseq), in_=pv[:, :, L:L + 3]),
                         r=[pre], w=[convout])
            else:
                dtx, dta, dtl, dtT, aT = self.ctmp.next(), self.pre.next(), self.ctmp.next(), self.dtT, self.aT
                k.op("dve", lambda e: e.tensor_scalar(out=dtx[0:32, 0:ntok], in0=ps[0:32, 0:ntok], scalar1=prm_dtb[0:32, off_dtb:off_dtb + 1], scalar2=None, op0=ALU.add),
                     r=[ps, prm_dtb], w=[dtx])
                k.op("dve", lambda e: e.scalar_tensor_tensor(out=dta[0:32, 0:ntok], in0=dtx[0:32, 0:ntok], scalar=-1.0, in1=dtx[0:32, 0:ntok], op0=ALU.mult, op1=ALU.max), r=[dtx], w=[dta])
                k.op("act", lambda e: e.activation(out=dta[0:32, 0:ntok], in_=dta[0:32, 0:ntok], func=AF.Exp, scale=-1.0), r=[dta], w=[dta])
                k.op("act", lambda e: e.activation(out=dtl[0:32, 0:ntok], in_=dta[0:32, 0:ntok], func=AF.Ln, bias=self.one_t[0:32, 0:1], scale=1.0), r=[dta, self.one_t], w=[dtl])
                k.op("dve", lambda e: e.scalar_tensor_tensor(out=dtT[:, 0:ntok], in0=dtx[0:32, 0:ntok], scalar=0.0, in1=dtl[0:32, 0:ntok], op0=ALU.max, op1=ALU.add),
                     r=[dtx, dtl], w=[dtT])
                k.op("dve", lambda e: e.tensor_scalar(out=aT[:, 0:ntok], in0=dtT[:, 0:ntok], scalar1=self.acol[0:32, n_ssm:n_ssm + 1], scalar2=None, op0=ALU.mult),
                     r=[dtT, self.acol], w=[aT])

        if do_proj:
            self.proj_fm(("ssm_in", i), h, 8, ntok, cons, col_lo=2048)
            if kind == "p" and ti == cfg.NT - 1 and phase == "p2":
                k.dma("pool", self.st_sem, self.conv_p_o[n_ssm], hist[:, :, :].rearrange("p c t -> p (c t)"), r=[hist])
            if kind == "s":
                k.dma("pool", self.st_sem, self.conv_s_o[n_ssm], convout[:, :], r=[convout])
        if phase == "p1" and ti == 0:
            k.op("dve", lambda e: e.memset(self.HT[i][:, :], 0.0), w=[self.HT[i]])
            k.op("dve", lambda e: e.memset(self.atsum[:, :], 0.0), w=[self.atsum])
        HT, HTb = self.HT[i], self.HTb
        prm_d, off_d, _ = self.P(("ssm_dbc", i))
        prm_nw, off_nw, _ = self.P(("ssm_normw", i))
        xs_tok, xdt, xdd, xsD, B_tok = self.xs_tok, self.xdt, self.xdd, self.xsD, self.B_tok
        a_tok, dt_tok, acs, eacs, dst, edec = self.a_tok, self.dt_tok, self.acs, self.eacs, self.dst, self.edec
        ysb, ygn, act = self.ysb, self.ygn, self.act
        v3 = lambda ap, a: ap.rearrange("p (a b) -> p a b", a=a)
        bc3 = lambda ap, n: ap.unsqueeze(2).to_broadcast([ap.shape[0], ap.shape[1], n])
        for ui, (o, Q, s) in enumerate(units):
            if kind == "s":
                k.dma("pool", self.st_sem, HT[:, :], self.ssm_in[n_ssm, s], w=[HT])
            if kind == "s" or (ti == 0 and ui == 0) or True:
                k.op("act", lambda e: e.copy(out=HTb[:, :], in_=HT[:, :]), r=[HT], w=[HTb])
            for half in range(2):
                ps = self.PS.next()
                pb = ps[:, :].bitcast(BF16)
                for c in range(8):
                    k.op("pe", lambda e, pb=pb, c=c, half=half, o=o, Q=Q: e.transpose(pb[0:Q, c * 128:(c + 1) * 128], A[:, half * 8 + c, o:o + Q], ident_b[:, :]),
                         r=[A, ident_b], w=[ps])
                k.op("act", lambda e, pb=pb, half=half, Q=Q: e.copy(out=xs_tok[0:Q, half * 1024:(half + 1) * 1024], in_=pb[0:Q, :]), r=[ps], w=[xs_tok])
            ps = self.PS.next()
            pb = ps[:, :].bitcast(BF16)
            for g in range(4):
                k.op("pe", lambda e, pb=pb, g=g, o=o, Q=Q: e.transpose(pb[0:Q, g * 128:(g + 1) * 128], A[:, 16 + g, o:o + Q], ident_b[:, :]), r=[A, ident_b], w=[ps])
            k.op("act", lambda e, pb=pb, Q=Q: e.copy(out=B_tok[0:Q, :], in_=pb[0:Q, 0:512]), r=[ps], w=[B_tok])
            ps = self.PS.next()
            k.op("pe", lambda e, ps=ps, o=o, Q=Q: e.transpose(ps[0:Q, 0:32], self.aT[:, o:o + Q], cm[0:32, 0:32]), r=[self.aT, cm], w=[ps])
            k.op("pe", lambda e, ps=ps, o=o, Q=Q: e.transpose(ps[0:Q, 32:64], self.dtT[:, o:o + Q], cm[0:32, 0:32]), r=[self.dtT, cm], w=[ps])
            k.op("dve", lambda e, ps=ps, Q=Q: e.tensor_copy(out=a_tok[0:Q, :], in_=ps[0:Q, 0:32]), r=[ps], w=[a_tok])
            k.op("dve", lambda e, ps=ps, Q=Q: e.tensor_copy(out=dt_tok[0:Q, :], in_=ps[0:Q, 32:64]), r=[ps], w=[dt_tok])
            ps = self.PS.next()
            k.op("pe", lambda e, ps=ps, Q=Q: e.matmul(ps[0:Q, 0:32], lhsT=tri_r(Q), rhs=a_tok[0:Q, :], start=True, stop=True), r=[cmr, a_tok], w=[ps])
            k.op("pe", lambda e, ps=ps, Q=Q: e.matmul(ps[0:Q, 32:64], lhsT=upp_r(Q), rhs=a_tok[0:Q, :], start=True, stop=True), r=[cmr, a_tok], w=[ps])
            k.op("pe", lambda e, ps=ps, Q=Q: e.matmul(ps[:, 64:96], lhsT=cmr[0:Q, 512:640], rhs=a_tok[0:Q, :], start=True, stop=True), r=[cmr, a_tok], w=[ps])
            k.op("dve", lambda e, ps=ps, Q=Q: e.tensor_copy(out=acs[0:Q, :], in_=ps[0:Q, 0:32]), r=[ps], w=[acs])
            k.op("act", lambda e, ps=ps, Q=Q: e.activation(out=eacs[0:Q, :], in_=ps[0:Q, 0:32], func=AF.Exp), r=[ps], w=[eacs])
            k.op("act", lambda e, ps=ps, Q=Q: e.activation(out=dst[0:Q, :], in_=ps[0:Q, 32:64], func=AF.Exp), r=[ps], w=[dst])
            k.op("act", lambda e, ps=ps: e.activation(out=edec[:, :], in_=ps[:, 64:96], func=AF.Exp), r=[ps], w=[edec])
            if phase == "p1":
                k.op("dve", lambda e, ps=ps: e.tensor_tensor(out=self.atsum[:, :], in0=self.atsum[:, :], in1=ps[:, 64:96], op=ALU.add), r=[self.atsum, ps], w=[self.atsum])
            k.op("dve", lambda e, Q=Q: e.tensor_tensor(out=v3(xdt[0:Q, :], 32), in0=v3(xs_tok[0:Q, :], 32), in1=bc3(dt_tok[0:Q, :], 64), op=ALU.mult), r=[xs_tok, dt_tok], w=[xdt])
            k.op("dve", lambda e, Q=Q: e.tensor_tensor(out=v3(xdd[0:Q, :], 32), in0=v3(xdt[0:Q, :], 32), in1=bc3(dst[0:Q, :], 64), op=ALU.mult), r=[xdt, dst], w=[xdd])
            k.op("dve", lambda e, Q=Q: e.tensor_tensor(out=v3(xsD[0:Q, :], 32), in0=v3(xs_tok[0:Q, :], 32), in1=bc3(prm_d[0:Q, off_d:off_d + 32], 64), op=ALU.mult),
                 r=[xs_tok, prm_d], w=[xsD])
            for g in range(4):
                if do_y:
                    psc = self.PS.next()
                    k.op("pe", lambda e, psc=psc, g=g, o=o, Q=Q: e.matmul(psc[0:Q, 0:Q], lhsT=A[:, 16 + g, o:o + Q], rhs=A[:, 20 + g, o:o + Q], start=True, stop=True), r=[A], w=[psc])
                    CBT = self.CBT.next()
                    k.op("act", lambda e, psc=psc, CBT=CBT, Q=Q: e.copy(out=CBT[0:Q, 0:Q], in_=psc[0:Q, 0:Q]), r=[psc], w=[CBT])
                    yps = self.PS.next()
                    for half in range(2):
                        hs = g * 8 + half * 4
                        Xd = self.XD.next()
                        k.op("dve", lambda e, Xd=Xd, hs=hs, Q=Q: e.tensor_tensor(out=v3(Xd[0:Q, 0:4 * Q], 4), in0=tri_f(Q).unsqueeze(1).to_broadcast([Q, 4, Q]),
                                                                                in1=bc3(a_tok[0:Q, hs:hs + 4], Q), op=ALU.mult), r=[cm, a_tok], w=[Xd])
                        psb = self.PS.next()
                        k.op("pe", lambda e, psb=psb, Xd=Xd, Q=Q: e.matmul(psb[0:Q, 0:4 * Q], lhsT=cmr[0:Q, 512:512 + Q], rhs=Xd[0:Q, 0:4 * Q], start=True, stop=True), r=[cmr, Xd], w=[psb])
                        tmp = self.TMP.next()
                        for hh in range(4):
                            k.op("dve", lambda e, psb=psb, tmp=tmp, hh=hh, hs=hs, Q=Q: e.scalar_tensor_tensor(
                                out=tmp[0:Q, hh * Q:(hh + 1) * Q], in0=psb[0:Q, hh * Q:(hh + 1) * Q], scalar=acs[0:Q, hs + hh:hs + hh + 1], in1=negm(Q),
                                op0=ALU.subtract, op1=ALU.add), r=[psb, acs, cm], w=[tmp])
                        Lx = self.LX.next()
                        k.op("act", lambda e, tmp=tmp, Lx=Lx, Q=Q: e.activation(out=Lx[0:Q, 0:4 * Q], in_=tmp[0:Q, 0:4 * Q], func=AF.Exp), r=[tmp], w=[Lx])
                        MT = self.MT.next()
                        k.op("dve", lambda e, Lx=Lx, MT=MT, CBT=CBT, Q=Q: e.tensor_tensor(out=v3(MT[0:Q, 0:4 * Q], 4), in0=v3(Lx[0:Q, 0:4 * Q], 4),
                                                                                        in1=CBT[0:Q, 0:Q].unsqueeze(1).to_broadcast([Q, 4, Q]), op=ALU.mult), r=[Lx, CBT], w=[MT])
                        for hh in range(4):
                            h8 = half * 4 + hh
                            hg = hs + hh
                            k.op("pe", lambda e, yps=yps, h8=h8, hg=hg, Q=Q: e.matmul(yps[0:Q, h8 * 64:(h8 + 1) * 64], lhsT=ident_b[0:Q, 0:Q], rhs=xsD[0:Q, hg * 64:(hg + 1) * 64],
                                                                                     start=True, stop=False), r=[ident_b, xsD], w=[yps])
                            k.op("pe", lambda e, yps=yps, h8=h8, hg=hg, hh=hh, MT=MT, Q=Q: e.matmul(yps[0:Q, h8 * 64:(h8 + 1) * 64], lhsT=MT[0:Q, hh * Q:(hh + 1) * Q],
                                                                                                   rhs=xdt[0:Q, hg * 64:(hg + 1) * 64], start=False, stop=True), r=[MT, xdt], w=[yps])
                    pso = self.PS.next()
                    k.op("pe", lambda e, pso=pso, g=g, o=o, Q=Q: e.matmul(pso[0:Q, 0:512], lhsT=A[:, 20 + g, o:o + Q], rhs=HTb[:, g * 512:(g + 1) * 512], start=True, stop=True), r=[A, HTb], w=[pso])
                    k.op("dve", lambda e, pso=pso, g=g, Q=Q: e.tensor_tensor(out=v3(ysb[0:Q, g * 512:(g + 1) * 512], 8), in0=v3(pso[0:Q, 0:512], 8), in1=bc3(eacs[0:Q, g * 8:(g + 1) * 8], 64), op=ALU.mult),
                         r=[pso, eacs], w=[ysb])
                    k.op("dve", lambda e, yps=yps, g=g, Q=Q: e.tensor_tensor(out=ysb[0:Q, g * 512:(g + 1) * 512], in0=ysb[0:Q, g * 512:(g + 1) * 512], in1=yps[0:Q, 0:512], op=ALU.add),
                         r=[ysb, yps], w=[ysb])
                psu = self.PS.next()
                k.op("pe", lambda e, psu=psu, g=g, Q=Q: e.matmul(psu[:, 0:512], lhsT=B_tok[0:Q, g * 128:(g + 1) * 128], rhs=xdd[0:Q, g * 512:(g + 1) * 512], start=True, stop=True), r=[B_tok, xdd], w=[psu])
                k.op("dve", lambda e, g=g: e.tensor_tensor(out=v3(HT[:, g * 512:(g + 1) * 512], 8), in0=v3(HT[:, g * 512:(g + 1) * 512], 8), in1=bc3(edec[:, g * 8:(g + 1) * 8], 64), op=ALU.mult),
                     r=[HT, edec], w=[HT])
                k.op("dve", lambda e, psu=psu, g=g: e.tensor_tensor(out=HT[:, g * 512:(g + 1) * 512], in0=HT[:, g * 512:(g + 1) * 512], in1=psu[:, 0:512], op=ALU.add), r=[HT, psu], w=[HT])
            if do_y:
                k.op("dve", lambda e, ui=ui, Q=Q: e.tensor_tensor(out=ysb[0:Q, :], in0=ysb[0:Q, :], in1=zt[0:Q, ui, :], op=ALU.mult), r=[ysb, zt], w=[ysb])
                k.op("dve", lambda e: e.memset(self.ss[:, :], 0.0), w=[self.ss])
                for g in range(4):
                    k.op("act", lambda e, g=g, Q=Q: e.activation(out=self.junk[0:Q, :], in_=ysb[0:Q, g * 512:(g + 1) * 512], func=AF.Square, accum_out=self.ss[0:Q, g:g + 1]),
                         r=[ysb], w=[self.junk, self.ss])
                k.op("act", lambda e, Q=Q: e.activation(out=self.rs[0:Q, 0:4], in_=self.ss[0:Q, 0:4], func=AF.Sqrt, bias=self.eps_t[0:Q, 0:1], scale=1.0 / 512), r=[self.ss, self.eps_t], w=[self.rs])
                k.op("dve", lambda e, Q=Q: e.reciprocal(out=self.rs[0:Q, 0:4], in_=self.rs[0:Q, 0:4]), r=[self.rs], w=[self.rs])
                k.op("dve", lambda e, Q=Q: e.tensor_tensor(out=v3(ygn[0:Q, :], 4), in0=v3(ysb[0:Q, :], 4), in1=bc3(self.rs[0:Q, 0:4], 512), op=ALU.mult), r=[ysb, self.rs], w=[ygn])
                for half in range(2):
                    ps = self.PS.next()
                    pb = ps[:, :].bitcast(BF16)
                    for c in range(8):
                        cc = half * 8 + c
                        k.op("pe", lambda e, pb=pb, c=c, cc=cc, Q=Q: e.transpose(pb[:, c * Q:(c + 1) * Q], ygn[0:Q, cc * 128:(cc + 1) * 128], ident_b[0:Q, 0:Q]), r=[ygn, ident_b], w=[ps])
                    for c in range(8):
                        cc = half * 8 + c
                        k.op("act", lambda e, pb=pb, c=c, cc=cc, o=o, Q=Q: e.activation(out=act[:, cc, o:o + Q], in_=pb[:, c * Q:(c + 1) * Q], func=AF.Copy, scale=prm_nw[:, off_nw + cc:off_nw + cc + 1]),
                             r=[ps, prm_nw], w=[act])
            if kind == "s":
                k.dma("pool", self.st_sem, self.ssm_s_o[n_ssm, s], HT[:, :], r=[HT])
        if kind == "p" and ti == cfg.NT - 1 and phase == "p2":
            k.dma("pool", self.st_sem, self.ssm_p_o[n_ssm], HT[:, :], r=[HT])
        if do_y:
            self.proj_fm(("ssm_out", i), act, 16, ntok, self.add_to_x(ntok))

    def sdma(self, out, in_, r=(), w=()):
        d = self.sc_sems[self.sc_i % 4]
        self.sc_i += 1
        self.k.dma("pool", d, out, in_, r=r, w=w)

    def xdma(self, out, in_, r=(), w=()):
        d = self.xs_sem[self.xs_i % 2]
        self.xs_i += 1
        self.k.dma("pool", d, out, in_, r=r, w=w)

    def ntok_of(self, ti):
        return self.cfg.TP if ti < self.cfg.NT else self.TS

    def load_xs(self, ti):
        n = self.ntok_of(ti)
        src = self.xscr[ti].rearrange("p (c t) -> p c t", c=8)[:, :, 0:n]
        self.xdma(self.x[:, :, 0:n], src, r=[self.xscr], w=[self.x])

    def store_xs(self, ti):
        n = self.ntok_of(ti)
        dst = self.xscr[ti].rearrange("p (c t) -> p c t", c=8)[:, :, 0:n]
        self.xdma(dst, self.x[:, :, 0:n], r=[self.x], w=[self.xscr])

    def prepass(self):
        cfg = self.cfg
        for ti in range(cfg.NT):
            self.load_x(self.xp[ti * cfg.TP:(ti + 1) * cfg.TP, :], cfg.TP)
            self.store_xs(ti)
        self.load_x(self.xs, self.TS)
        self.store_xs(cfg.NT)

    def finish_tile(self, i, ti):
        cfg = self.cfg
        n = self.ntok_of(ti)
        self.ffn(i, n)
        if i == cfg.DEPTH - 1:
            if ti < cfg.NT:
                self.store_y(self.y_p[ti * cfg.TP:(ti + 1) * cfg.TP, :], n)
            else:
                self.store_y(self.y_s, n)
        else:
            self.store_xs(ti)

    def sample_layer(self, i):
        cfg = self.cfg
        self.load_xs(cfg.NT)
        self.mixer(i, self.TS, ("s", 0))
        self.finish_tile(i, cfg.NT)

    def allgather(self, name, src_T, rows, cols, dtype, src_ap=None):
        k, cfg = self.k, self.cfg
        g = k.dram("g_" + name, (cfg.NCORES * rows, cols), dtype)
        d = k.dsem("cc_" + name, inc=1)
        sap = src_ap if src_ap is not None else src_T.t
        if getattr(cfg, "fake_cc", False):
            for r_ in range(cfg.NCORES):
                self.sdma(g[r_ * rows:(r_ + 1) * rows, :], sap, r=[src_T], w=[g])
            return g
        k.coll(d, "AllGather", sap.opt(), g.t.opt(), cfg.NCORES, r=[src_T], w=[g])
        return g

    def corecol(self, j):
        prm, off, _ = self.P(("core",))
        return prm[:, off + j:off + j + 1]

    def seg_weights(self, src_T, src_ap, n, scale):
        k = self.k
        sm = self.small
        prm = self.params
        self.sdma(sm[:, 0:4, 0:n], src_ap.rearrange("(q p) c -> p q c", p=128), r=[src_T], w=[sm])
        for m_ in range(4):
            k.op("dve", lambda e, m_=m_: e.tensor_scalar(out=sm[:, m_, 0:n], in0=sm[:, m_, 0:n], scalar1=self.corecol(m_), scalar2=None, op0=ALU.mult), r=[sm, prm], w=[sm])
        k.op("dve", lambda e: e.memset(sm[:, 7, 0:n], 0.0), w=[sm])
        k.op("dve", lambda e: e.tensor_copy(out=sm[:, 6, 0:n], in_=sm[:, 3, 0:n]), r=[sm], w=[sm])
        k.op("dve", lambda e: e.tensor_tensor(out=sm[:, 5, 0:n], in0=sm[:, 2, 0:n], in1=sm[:, 6, 0:n], op=ALU.add), r=[sm], w=[sm])
        k.op("dve", lambda e: e.tensor_tensor(out=sm[:, 4, 0:n], in0=sm[:, 1, 0:n], in1=sm[:, 5, 0:n], op=ALU.add), r=[sm], w=[sm])
        k.op("act", lambda e: e.activation(out=sm[:, 4:8, 0:n], in_=sm[:, 4:8, 0:n], func=AF.Exp, scale=scale), r=[sm], w=[sm])
        for q in range(4):
            k.op("dve", lambda e, q=q: e.tensor_scalar(out=sm[:, 4 + q, 0:n], in0=sm[:, 4 + q, 0:n], scalar1=self.corecol(q), scalar2=None, op0=ALU.mult), r=[sm, prm], w=[sm])

    def layer_ssd(self, i):
        k, cfg = self.k, self.cfg
        NT, TP = cfg.NT, cfg.TP
        n_ssm = sorted(self.HT).index(i)
        HT, hist = self.HT[i], self.hist[i]
        src = self.xscr[NT - 1].rearrange("p (c t) -> p c t", c=8)[:, :, TP - 4:TP]
        self.xdma(self.x[:, :, 0:4], src, r=[self.xscr], w=[self.x])
        self.rmsnorm(("nmix", i), 4, self.h)
        tail = self.convout

        def cons(col, ncols, ps):
            cc = (col - 2048) // 128
            k.op("act", lambda e: e.copy(out=tail[:, cc * 3:cc * 3 + 3], in_=ps[:, 1:4]), r=[ps], w=[tail])
        self.proj_fm(("ssm_in", i), self.h, 8, 4, cons, col_lo=2048, col_hi=2048 + 3072)
        bt = k.dram(f"bnc_tail{i}", (128, 72), F32)
        self.sdma(bt[:, :], tail[:, 0:72], r=[tail], w=[bt])
        gt = self.allgather(f"tail{i}", bt, 128, 72, F32)
        self.sdma(hist[:, :, :].rearrange("p c t -> p (c t)"), lambda dyn: gt.t[bass.ds(dyn["prev"] * 128, 128), :], r=[gt], w=[hist])
        k.op("dve", lambda e: e.tensor_scalar(out=hist[:, :, :], in0=hist[:, :, :], scalar1=self.corecol(4), scalar2=None, op0=ALU.mult), r=[hist, self.params], w=[hist])
        hsave = self.histsave
        k.op("dve", lambda e: e.tensor_copy(out=hsave[:, :, :], in_=hist[:, :, :]), r=[hist], w=[hsave])
        if getattr(cfg, "stop", None) == "phase0":
            return
        for ti in range(NT):
            self.load_xs(ti)
            self.mixer_ssd(i, TP, ("p", ti), phase="p1")
        if getattr(cfg, "stop", None) == "p1":
            return
        bh = k.dram(f"bnc_h{i}", (128, 2080), F32)
        self.sdma(bh[:, 0:2048], HT[:, :], r=[HT], w=[bh])
        self.sdma(bh[:, 2048:2080], self.atsum[:, :], r=[self.atsum], w=[bh])
        gh = self.allgather(f"h{i}", bh, 128, 2080, F32)
        lh = k.dram(f"loc_h{i}", (512, 2080), F32)
        self.sdma(lh[:, :], lambda dyn: gh.t[bass.ds(dyn["base"] * 128, 512), :], r=[gh], w=[lh])
        self.seg_weights(lh, lh[:, 2048:2080], 32, 1.0)
        k.op("dve", lambda e: e.memset(HT[:, :], 0.0), w=[HT])
        ysb, sm = self.ysb, self.small
        v3 = lambda ap, a: ap.rearrange("p (a b) -> p a b", a=a)
        for q in range(4):
            self.sdma(ysb[:, :], lh[q * 128:(q + 1) * 128, 0:2048], r=[lh], w=[ysb])
            k.op("dve", lambda e, q=q: e.tensor_tensor(out=v3(ysb[:, :], 32), in0=v3(ysb[:, :], 32), in1=sm[:, 4 + q, 0:32].unsqueeze(2).to_broadcast([128, 32, 64]), op=ALU.mult), r=[ysb, sm], w=[ysb])
            k.op("dve", lambda e: e.tensor_tensor(out=HT[:, :], in0=HT[:, :], in1=ysb[:, :], op=ALU.add), r=[HT, ysb], w=[HT])
        k.op("dve", lambda e: e.tensor_copy(out=hist[:, :, :], in_=hsave[:, :, :]), r=[hsave], w=[hist])
        if getattr(cfg, "stop", None) == "exch":
            return
        for ti in range(NT):
            self.load_xs(ti)
            self.mixer_ssd(i, TP, ("p", ti), phase="p2")
            if getattr(cfg, "stop", None) in ("p2", "p2a", "p2b", "p2c"):
                continue
            self.finish_tile(i, ti)

    def layer_gla(self, i):
        k, cfg = self.k, self.cfg
        NT, TP = cfg.NT, cfg.TP
        S = self.S
        for ti in range(NT):
            self.load_xs(ti)
            self.mixer_gla(i, TP, ("p", ti), phase="p1")
        bs = k.dram(f"bnc_s{i}", (128, 1028), F32)
        self.sdma(bs[:, 0:1024], S[:, :], r=[S], w=[bs])
        self.sdma(bs[:, 1024:1028], self.blsum[:, :], r=[self.blsum], w=[bs])
        gs = self.allgather(f"s{i}", bs, 128, 1028, F32)
        ls = k.dram(f"loc_s{i}", (512, 1028), F32)
        self.sdma(ls[:, :], lambda dyn: gs.t[bass.ds(dyn["base"] * 128, 512), :], r=[gs], w=[ls])
        self.seg_weights(ls, ls[:, 1024:1028], 4, 1.0 / 16)
        k.op("dve", lambda e: e.memset(S[:, :], 0.0), w=[S])
        ysb, sm = self.ysb, self.small
        v3 = lambda ap, a: ap.rearrange("p (a b) -> p a b", a=a)
        for q in range(4):
            self.sdma(ysb[:, 0:1024], ls[q * 128:(q + 1) * 128, 0:1024], r=[ls], w=[ysb])
            k.op("dve", lambda e, q=q: e.tensor_tensor(out=v3(ysb[:, 0:1024], 4), in0=v3(ysb[:, 0:1024], 4), in1=sm[:, 4 + q, 0:4].unsqueeze(2).to_broadcast([128, 4, 256]), op=ALU.mult), r=[ysb, sm], w=[ysb])
            k.op("dve", lambda e: e.tensor_tensor(out=S[:, :], in0=S[:, :], in1=ysb[:, 0:1024], op=ALU.add), r=[S, ysb], w=[S])
        k.op("act", lambda e: e.copy(out=self.Sb[:, :], in_=S[:, :]), r=[S], w=[self.Sb])
        for ti in range(NT):
            self.load_xs(ti)
            self.mixer_gla(i, TP, ("p", ti), phase="p2")
            self.finish_tile(i, ti)

    def layer_att(self, i):
        k, cfg = self.k, self.cfg
        NT, TP = cfg.NT, cfg.TP
        for ti in range(NT):
            self.load_xs(ti)
            self.mixer_att(i, TP, ("p", ti), phase="p1")
        self.Gk = self.allgather("kt", self.kt_hist, 6 * 128, cfg.SEQ, BF16, src_ap=self.kt_hist.t.rearrange("c p t -> (c p) t"))
        self.Gv = self.allgather("v", self.v_hist, cfg.SEQ, 768, BF16)
        self.loc_k = k.dram("loc_k", (768, cfg.SEQ), BF16)
        self.loc_v = k.dram("loc_v", (cfg.SEQ, 768), BF16)
        Gk, Gv, SEQ_ = self.Gk, self.Gv, cfg.SEQ
        self.sdma(self.loc_k[:, :], lambda dyn: Gk.t[bass.ds(dyn["prev"] * 768, 768), :], r=[Gk], w=[self.loc_k])
        self.sdma(self.loc_v[:, :], lambda dyn: Gv.t[bass.ds(dyn["prev"] * SEQ_, SEQ_), :], r=[Gv], w=[self.loc_v])
        for ti in range(NT):
            self.load_xs(ti)
            self.mixer_att(i, TP, ("p", ti), phase="p2")
            self.finish_tile(i, ti)

    def epilogue(self):
        self.k.wait_all("pool", self.out_sems)


def make_consts(cfg=None, sgi=0):
    i = np.arange(128)
    tri = (i[:, None] <= i[None, :]).astype(np.float32)
    upper = (i[:, None] > i[None, :]).astype(np.float32)
    negmask = np.where(i[None, :] >= i[:, None], 0.0, -1e30).astype(np.float32)
    cm = np.concatenate([np.eye(128, dtype=np.float32), tri, upper, negmask, np.ones((128, 128), np.float32)], axis=1)
    out = {"cmat": np.ascontiguousarray(cm)}
    if cfg is not None and cfg.DEPTH > 2:
        ms = []
        for (W, d) in AGROUPS:
            c = np.arange(W + 128)[None, :]
            diff = W + i[:, None] - c
            ok = (diff >= 0) & (diff <= W) & (diff % d == 0)
            ms.append(np.where(ok, 0.0, -30000.0).astype(np.float32))
        out["amask"] = np.ascontiguousarray(np.concatenate(ms, axis=1))
        NU = cfg.SEQ // 128
        pos = np.concatenate([sgi * cfg.SEQ + np.arange(cfg.SEQ), cfg.PAST + np.arange(128)]).astype(np.float32)
        inv_freq = (np.float32(ROPE_THETA) ** (-np.arange(0, AROT, 2, dtype=np.float32) / np.float32(AROT))).astype(np.float32)
        ang = (pos[:, None] * inv_freq[None, :]).astype(np.float32)
        tab = np.concatenate([np.cos(ang), np.sin(ang)], axis=1).astype(np.float32)
        out["rope"] = np.ascontiguousarray(tab.reshape(NU + 1, 128, 16).transpose(1, 0, 2)).reshape(128, (NU + 1) * 16)
    return out


def kernel(**inputs):
    return run(Cfg(), inputs)


def run(cfg, inputs, trace=False):
    prog = Prog(cfg)
    nc = prog.build()
    in_maps = make_in_maps(cfg, inputs)
    return launch(cfg, nc, in_maps, np.asarray(inputs["x_prompt"]).shape[0], trace)


def make_in_maps(cfg, inputs):
    wp = weight_plan(cfg.DEPTH, inputs)
    pp = param_plan(cfg.DEPTH, inputs)
    wcat, pcat0 = wp.cat(), pp.cat()
    f32 = lambda n: np.asarray(inputs[n], np.float32)
    xp, xs = f32("x_prompt"), f32("x_sample")
    B = xp.shape[0]
    NSC, NSEG, SEQ, NC = cfg.NSC, cfg.NSEG, cfg.SEQ, cfg.NCORES
    st_ssm, st_conv, st_gla = f32("state_ssm"), f32("state_ssm_conv"), f32("state_gla")
    caches = [f32(f"cache_kv_g{g}") for g in range(3)]
    coff = pp.off[("core",)][0]
    consts = [make_consts(cfg, sgi) for sgi in range(NSEG)]
    in_maps = []
    for c in range(NC):
        b_, sgi = c // NSEG, c % NSEG
        sl = slice(c * NSC, (c + 1) * NSC)
        pc = pcat0.copy()
        for m_ in range(4):
            pc[:, coff + m_] = 1.0 if m_ < sgi else 0.0
        pc[:, coff + 4] = 1.0 if sgi > 0 else 0.0
        pc[:, coff + 5] = 0.0 if sgi > 0 else -30000.0
        m = {"xp": np.ascontiguousarray(xp[b_, sgi * SEQ:(sgi + 1) * SEQ]), "xs": np.ascontiguousarray(xs[sl].reshape(-1, D)), "wcat": wcat, "pcat": pc,
             "rankinfo": np.array([[c - 1 if sgi > 0 else c, b_ * NSEG]], np.int32),
             "ssm_in_t": lay_ssm_in(st_ssm[:, sl]), "conv_in_fm": lay_conv_in(st_conv[:, sl]),
             "gla_in": np.ascontiguousarray(st_gla[0, sl])}
        nssm = len([i_ for i_ in range(cfg.DEPTH) if i_ % 3 == 0])
        m["ssm_in_t"], m["conv_in_fm"] = m["ssm_in_t"][:nssm], m["conv_in_fm"][:nssm]
        if cfg.DEPTH < 2:
            del m["gla_in"]
        for g in range(3 if cfg.DEPTH > 2 else 0):
            m[f"kvc{g}"] = np.ascontiguousarray(caches[g][0, sl].reshape(NSC, -1, 512))
        m.update(consts[sgi])
        in_maps.append(m)
    return in_maps


def launch(cfg, nc, in_maps, B, trace=False):
    NSC, NSEG, SEQ, NC = cfg.NSC, cfg.NSEG, cfg.SEQ, cfg.NCORES
    res = run_bass_kernel_spmd(nc, in_maps, core_ids=list(range(NC)), **({"trace": True} if trace else {}))
    if trace:
        print("exec_ns", res.exec_time_ns)
    R = res.results
    last = [b_ * NSEG + NSEG - 1 for b_ in range(B)]
    y_p = np.stack([np.concatenate([R[b_ * NSEG + sg]["y_p"] for sg in range(NSEG)]) for b_ in range(B)])
    y_s = np.concatenate([R[c]["y_s"].reshape(NSC, cfg.DEC, D) for c in range(NC)])
    ssm_p = np.stack([unlay_ssm(R[c]["ssm_p_t"]) for c in last], axis=1)
    ssm_s = np.concatenate([unlay_ssm(R[c]["ssm_s_t"]) for c in range(NC)], axis=1)
    conv_p = np.concatenate([unlay_conv(R[c]["conv_p_fm"], 1) for c in last], axis=1)
    conv_s = np.concatenate([unlay_conv(R[c]["conv_s_fm"], NSC) for c in range(NC)], axis=1)
    outs = [y_p, y_s, ssm_p, ssm_s, conv_p, conv_s]
    if cfg.DEPTH > 1:
        outs += [np.stack([R[c]["gla_p"] for c in last])[None], np.concatenate([R[c]["gla_s"] for c in range(NC)])[None]]
    for g in range(3 if cfg.DEPTH > 2 else 0):
        kp = np.stack([R[c][f"kvp{g}"].reshape(-1, 2, AHPG, ADH) for c in last])[None]
        ks = np.concatenate([R[c][f"kvs{g}"].reshape(NSC, -1, 2, AHPG, ADH) for c in range(NC)])[None]
        outs += [kp, ks]
    return tuple(np.ascontiguousarray(o, dtype=np.float32) for o in outs)


def lay_ssm_in(st):
    n, S = st.shape[:2]
    return np.ascontiguousarray(st.reshape(n, S, 2048, 128).transpose(0, 1, 3, 2))


def unlay_ssm(o):
    return np.ascontiguousarray(np.swapaxes(o, -1, -2)).reshape(o.shape[:-2] + (32, 64, 128))


def lay_conv_in(cv):
    n, S = cv.shape[:2]
    return np.ascontiguousarray(cv.reshape(n, S, 3, 24, 128).transpose(0, 4, 3, 1, 2)).reshape(n, 128, 24 * S * 3)


def unlay_conv(o, S):
    n = o.shape[0]
    return np.ascontiguousarray(o.reshape(n, 128, 24, S, 3).transpose(0, 3, 4, 2, 1)).reshape(n, S, 3, 3072)
```

```python
import contextlib
import numpy as np
import concourse.bass as bass
import concourse.mybir as mybir
from concourse.bass_utils import run_bass_kernel_spmd

F32 = mybir.dt.float32
F32R = mybir.dt.float32r
BF16 = mybir.dt.bfloat16
AF = mybir.ActivationFunctionType
ALU = mybir.AluOpType
AX = mybir.AxisListType

ENGS = ("pe", "act", "dve", "pool", "sp")
SAME_ENGINE_SYNC = True

D = 1024
DFF = 2816
EPS = 1e-6
DI, HD, NH, NG, NS, CONVK, CD, SIN = 2048, 64, 32, 4, 128, 4, 3072, 5152
GH, GDK, GDV, GHK, GHV, GRANK, GIN = 4, 512, 1024, 128, 256, 16, 3088
AGROUPS = ((128, 1), (512, 4), (2048, 16))
AHPG, ADH, ANH, AROT = 4, 64, 12, 16
ROPE_THETA = 500000.0


class Cfg:
    def __init__(self, seq=2048, tp=256, ns_core=4, dec=4, past=8192, depth=4, ncores=8, nseg=4):
        self.SEQ, self.TP, self.NSC, self.DEC, self.PAST, self.DEPTH, self.NCORES = seq, tp, ns_core, dec, past, depth, ncores
        self.NSEG = nseg
        self.NT = seq // tp


class T:
    def __init__(self, t, name):
        self.t, self.name, self.w, self.r = t, name, {}, {}

    def __getitem__(self, k):
        return self.t[k]


def TV(parent, ap, name="view"):
    v = T(ap, name)
    v.w, v.r = parent.w, parent.r
    return v


class DSem:
    def __init__(self, sem, name, inc=16):
        self.sem, self.name, self.total, self.inc = sem, name, 0, inc


class KB:
    def __init__(self, nc, stack):
        self.nc, self.stack = nc, stack
        self.q = {e: [] for e in ENGS}
        self.cnt = {e: 0 for e in ENGS}
        self.sem = {e: stack.enter_context(nc.semaphore("s_" + e)) for e in ENGS if e != "sp"}
        self.waited = {}
        self.n = 0
        self.dyn = {}
        self.dyn_setup = None

    def sbuf(self, shape, dtype, name=None):
        self.n += 1
        name = name or f"sb{self.n}"
        return T(self.stack.enter_context(self.nc.sbuf_tensor(name, list(shape), dtype)), name)

    def psum(self, shape, dtype=F32, name=None):
        self.n += 1
        name = name or f"ps{self.n}"
        return T(self.stack.enter_context(self.nc.psum_tensor(name, list(shape), dtype)), name)

    def dram(self, name, shape, dtype, kind="Internal"):
        return T(self.nc.dram_tensor(name, list(shape), dtype, kind=kind).ap(), name)

    def dsem(self, name=None, inc=16):
        self.n += 1
        name = name or f"d{self.n}"
        return DSem(self.stack.enter_context(self.nc.semaphore("ds_" + name)), name, inc)

    def _deps(self, eng, reads, writes):
        deps = {}
        for t in reads:
            for k, c in t.w.items():
                if deps.get(k, 0) < c:
                    deps[k] = c
        for t in writes:
            for dd in (t.w, t.r):
                for k, c in dd.items():
                    if deps.get(k, 0) < c:
                        deps[k] = c
        out = []
        for k, c in deps.items():
            if k == eng and (eng == "pe" or not SAME_ENGINE_SYNC):
                continue
            if self.waited.get((eng, k), 0) >= c:
                continue
            self.waited[(eng, k)] = c
            out.append(((k.sem if isinstance(k, DSem) else self.sem[k]), c))
        return out

    def op(self, eng, fn, r=(), w=()):
        sems = self._deps(eng, r, w)
        self.cnt[eng] += 1
        c = self.cnt[eng]
        mysem = self.sem[eng]

        def emit(e, sems=sems, fn=fn, mysem=mysem):
            for s, v in sems:
                e.wait_ge(s, v)
            fn(e).then_inc(mysem, 1)

        self.q[eng].append(emit)
        for t in r:
            t.r[eng] = c
        for t in w:
            t.w[eng] = c

    def dma(self, qeng, dsem, out, in_, r=(), w=(), **kw):
        sems = self._deps(qeng, r, w)
        if dsem.total > 0 and self.waited.get((qeng, dsem), 0) < dsem.total:
            self.waited[(qeng, dsem)] = dsem.total
            sems = sems + [(dsem.sem, dsem.total)]
        dsem.total += 16
        c = dsem.total

        def emit(e, sems=sems, out=out, in_=in_, kw=kw, ds=dsem.sem):
            for s, v in sems:
                e.wait_ge(s, v)
            o_ = out(self.dyn) if callable(out) else out
            i_ = in_(self.dyn) if callable(in_) else in_
            e.dma_start(out=o_, in_=i_, **kw).then_inc(ds, 16)

        self.q[qeng].append(emit)
        for t in r:
            t.r[dsem] = c
        for t in w:
            t.w[dsem] = c

    def coll(self, dsem, kind, in_ap, out_ap, ncores, r=(), w=()):
        sems = self._deps("pool", r, w)
        dsem.total += 1
        c = dsem.total

        def emit(e, sems=sems, ds=dsem.sem):
            for s, v in sems:
                e.wait_ge(s, v)
            e.collective_compute(kind, ALU.bypass, replica_groups=[list(range(ncores))], ins=[in_ap], outs=[out_ap]).then_inc(ds)

        self.q["pool"].append(emit)
        for t in r:
            t.r[dsem] = c
        for t in w:
            t.w[dsem] = c

    def wait_all(self, eng, dsems):
        sems = [(d.sem, d.total) for d in dsems if d.total > 0]

        def emit(e, sems=sems):
            for s, v in sems:
                e.wait_ge(s, v)

        self.q[eng].append(emit)

    def finish(self):
        with self.nc.Block() as block:
            @block.tensor
            def _(e):
                for f in self.q["pe"]:
                    f(e)

            @block.scalar
            def _(e):
                for f in self.q["act"]:
                    f(e)

            @block.vector
            def _(e):
                for f in self.q["dve"]:
                    f(e)

            @block.gpsimd
            def _(e):
                with contextlib.ExitStack() as rs:
                    if self.dyn_setup is not None:
                        self.dyn_setup(e, rs)
                    for f in self.q["pool"]:
                        f(e)

            @block.sync
            def _(e):
                for f in self.q["sp"]:
                    f(e)


class Ring:
    def __init__(self, bufs):
        self.bufs, self.i = bufs, 0

    def next(self):
        b = self.bufs[self.i % len(self.bufs)]
        self.i += 1
        return b


def _bw(K):
    kc = K // 128
    return 512 if kc <= 8 else (256 if kc <= 16 else 128)


class WPack:
    def __init__(self):
        self.blocks = {}
        self.tot = 0
        self.parts = []

    def add(self, key, W):
        K, N = W.shape
        kc, bw = K // 128, _bw(K)
        nblk = -(-N // bw)
        Wp = np.zeros((K, nblk * bw), np.float32)
        Wp[:, :N] = W
        til = Wp.reshape(kc, 128, nblk, bw).transpose(2, 1, 0, 3).reshape(nblk, 128, kc * bw)
        lst = []
        for b in range(nblk):
            lst.append((self.tot, kc, bw, min(bw, N - b * bw)))
            self.parts.append(til[b])
            self.tot += kc * bw
        self.blocks[key] = lst

    def add_meta(self, key, K, N):
        kc, bw = K // 128, _bw(K)
        nblk = -(-N // bw)
        lst = []
        for b in range(nblk):
            lst.append((self.tot, kc, bw, min(bw, N - b * bw)))
            self.tot += kc * bw
        self.blocks[key] = lst

    def cat(self):
        pad = (-self.tot) % 2048
        arr = np.concatenate(self.parts + [np.zeros((128, pad), np.float32)], axis=1)
        return np.ascontiguousarray(arr)


W_SHAPES = {"ssm_in": (D, SIN), "ssm_out": (DI, D), "gla_in": (D, GIN), "gla_out": (GDV, D),
            "att_qkv": (D, 3 * ANH * ADH), "att_out": (ANH * ADH, D),
            "ffn_gate": (D, DFF), "ffn_up": (D, DFF), "ffn_down": (DFF, D)}


def weight_plan(depth, inputs=None):
    wp = WPack()

    def add(key, name, j):
        if inputs is None:
            wp.add_meta(key, *W_SHAPES[key[0]])
        else:
            wp.add(key, np.asarray(inputs[name][j], np.float32))

    for i in range(depth):
        m, j = i % 3, i // 3
        if m == 0:
            add(("ssm_in", i), "ssm_w_in", j)
            add(("ssm_out", i), "ssm_w_out", j)
        elif m == 1:
            add(("gla_in", i), "gla_w_in", j)
            add(("gla_out", i), "gla_w_out", j)
        else:
            add(("att_qkv", i), "att_w_qkv", j)
            add(("att_out", i), "att_w_out", j)
        add(("ffn_gate", i), "ffn_gate", i)
        add(("ffn_up", i), "ffn_up", i)
        add(("ffn_down", i), "ffn_down", i)
    return wp


def fm(v, nchunk):
    return np.ascontiguousarray(np.asarray(v, np.float32).reshape(nchunk, 128).T)


class PPack:
    def __init__(self):
        self.off, self.tot, self.parts = {}, 0, []

    def add(self, key, arr=None, width=None):
        if arr is not None:
            arr = np.asarray(arr, np.float32)
            assert arr.shape[0] == 128
            arr = arr.reshape(128, -1)
            width = arr.shape[1]
            self.parts.append(arr)
        self.off[key] = (self.tot, width)
        self.tot += width

    def cat(self):
        return np.ascontiguousarray(np.concatenate(self.parts, axis=1))


def param_plan(depth, inputs=None):
    pp = PPack()
    g = (lambda name: np.asarray(inputs[name], np.float32)) if inputs is not None else None

    def add(key, fn, width):
        pp.add(key, fn() if inputs is not None else None, width)

    for i in range(depth):
        add(("nmix", i), lambda: fm(g("norm_mix")[i], 8), 8)
        add(("nffn", i), lambda: fm(g("norm_ffn")[i], 8), 8)
    add(("nfin",), lambda: fm(g("norm_final"), 8), 8)
    add(("core",), lambda: np.zeros((128, 8), np.float32), 8)
    for i in range(depth):
        m, j = i % 3, i // 3
        if m == 0:
            add(("ssm_convw", i), lambda: np.ascontiguousarray(g("ssm_conv_w")[j].reshape(4, 24, 128).transpose(2, 1, 0)).reshape(128, 96), 96)
            add(("ssm_convb", i), lambda: fm(g("ssm_conv_b")[j], 24), 24)
            add(("ssm_dtb", i), lambda: np.pad(g("ssm_dt_bias")[j].reshape(32, 1), ((0, 96), (0, 0))), 1)
            add(("ssm_alog", i), lambda: np.pad(g("ssm_a_log")[j].reshape(32, 1), ((0, 96), (0, 0))), 1)
            add(("ssm_dbc", i), lambda: np.broadcast_to(g("ssm_d")[j].reshape(1, 32), (128, 32)).copy(), 32)
            add(("ssm_normw", i), lambda: fm(g("ssm_norm")[j], 16), 16)
        elif m == 1:
            add(("gla_wg", i), lambda: np.pad(g("gla_w_gate")[j], ((0, 112), (0, 0))), 512)
            add(("gla_gb", i), lambda: fm(g("gla_gate_bias")[j], 4), 4)
            add(("gla_nw", i), lambda: np.broadcast_to(g("gla_norm")[j].reshape(1, 256), (128, 256)).copy(), 256)
    return pp


class Prog:
    def __init__(self, cfg):
        self.cfg = cfg
        self.wp = weight_plan(cfg.DEPTH)
        self.pp = param_plan(cfg.DEPTH)

    def build(self):
        cfg = self.cfg
        nc = bass.Bass("TRN2", target_bir_lowering=False)
        self.nc = nc
        with contextlib.ExitStack() as st:
            k = KB(nc, st)
            self.k = k
            self.declare_io()
            self.alloc()
            self.prologue()
            self.prepass()
            for i in range(cfg.DEPTH):
                m = i % 3
                if getattr(cfg, "stop", None):
                    if cfg.stop != "prepass":
                        self.layer_ssd(i)
                    break
                if m == 0:
                    self.layer_ssd(i)
                elif m == 1:
                    self.layer_gla(i)
                else:
                    self.layer_att(i)
                self.sample_layer(i)
            self.epilogue()
            k.finish()
        return nc

    def declare_io(self):
        nc, cfg = self.nc, self.cfg
        I = lambda n, s: nc.dram_tensor(n, list(s), F32, kind="ExternalInput").ap()
        O = lambda n, s: nc.dram_tensor(n, list(s), F32, kind="ExternalOutput").ap()
        TS = cfg.NSC * cfg.DEC
        self.TS = TS
        self.xp = I("xp", (cfg.SEQ, D))
        self.xs = I("xs", (TS, D))
        self.wcat = I("wcat", (128, self.wp.tot + ((-self.wp.tot) % 2048)))
        self.pcat = I("pcat", (128, self.pp.tot))
        self.cmat_in = I("cmat", (128, 640))
        self.rank_in = nc.dram_tensor("rankinfo", [1, 2], mybir.dt.int32, kind="ExternalInput").ap()
        self.y_p = O("y_p", (cfg.SEQ, D))
        self.y_s = O("y_s", (TS, D))

    def alloc(self):
        k, cfg = self.k, self.cfg
        TP = cfg.TP
        self.TMAX = TP
        self.out_sems = []
        self.wtot = self.wp.tot + ((-self.wp.tot) % 2048)
        self.wb = k.dram("wcat_b", (128, self.wtot), BF16)
        self.params = k.sbuf((128, self.pp.tot), F32, "params")
        self.cmat = k.sbuf((128, 640), F32, "cmat_s")
        self.cmat_r = k.sbuf((128, 640), F32R, "cmat_r")
        self.ident_b = k.sbuf((128, 128), BF16, "ident_b")
        self.ones_b = k.sbuf((128, 128), BF16, "ones_b")
        self.eps_t = k.sbuf((128, 1), F32, "eps_t")
        self.one_t = k.sbuf((128, 1), F32, "one_t")
        self.x = k.sbuf((128, 8, TP), F32, "x")
        self.h = k.sbuf((128, 8, TP), BF16, "h")
        self.sq = Ring([k.sbuf((128, TP), BF16, f"sq{i}") for i in range(2)])
        self.rstd = k.sbuf((128, TP), F32, "rstd")
        self.act = k.sbuf((128, 22, TP), BF16, "act")
        self.sil = Ring([k.sbuf((128, TP), F32, f"sil{i}") for i in range(2)])
        self.wring = Ring([k.sbuf((128, 4096), BF16, f"wr{i}") for i in range(3)])
        self.wsem = [k.dsem(f"w{i}") for i in range(3)]
        self.PS = Ring([k.psum((128, 512), F32, f"psb{i}") for i in range(8)])
        self.pp_ps = self.PS
        self.pt_ps = self.PS
        self.alloc_mixers()
        self.xin = Ring([k.sbuf((128, D), F32, f"xin{i}") for i in range(2)])
        self.xin_sem = [k.dsem(f"xin{i}") for i in range(2)]
        self.yout = Ring(self.xin.bufs)
        self.yout_sem = [k.dsem(f"yo{i}") for i in range(2)]
        self.misc_sem = k.dsem("misc")
        NT = cfg.NT
        self.xscr = [k.dram(f"xscr{t_}", (128, 8 * TP), F32) for t_ in range(NT + 1)]
        self.Ascr = k.dram("Ascr", (NT, 128, 24 * TP), BF16)
        self.zscr = k.dram("zscr", (NT, 128, 2 * 2048), BF16)
        self.dscr = k.dram("dscr", (NT, 32, 2 * TP), F32)
        self.escr = k.dram("escr", (NT, 128, 16), F32)
        self.qscr = k.dram("qscr", (NT, 2, 128, 768), BF16)
        self.xs_sem = [k.dsem(f"xs{i}") for i in range(2)]
        self.xs_i = 0
        self.sc_sems = [k.dsem(f"sc{i}") for i in range(4)]
        self.sc_i = 0
        self.out_sems += self.xs_sem + self.sc_sems
        self.small = k.sbuf((128, 8, 32), F32, "small")
        self.atsum = k.sbuf((128, 32), F32, "atsum")
        self.histsave = k.sbuf((128, 24, 3), F32, "histsave")
        self.blsum = k.sbuf((128, 4), F32, "blsum")

        def dyn_setup(e, rs):
            rp = rs.enter_context(e.register("r_prev"))
            rb = rs.enter_context(e.register("r_base"))
            e.reg_load(rp, self.rank_in[0:1, 0:1])
            e.reg_load(rb, self.rank_in[0:1, 1:2])
            k.dyn["prev"] = e.snap(rp, min_val=0, max_val=cfg.NCORES - 1)
            k.dyn["base"] = e.snap(rb, min_val=0, max_val=cfg.NCORES - cfg.NSEG)
        k.dyn_setup = dyn_setup
        self.out_sems += list(self.yout_sem)

    def P(self, key):
        off, w = self.pp.off[key]
        return self.params, off, w

    def prologue(self):
        k = self.k
        k.dma("pool", self.misc_sem, self.params[:], self.pcat, w=[self.params])
        k.dma("pool", self.misc_sem, self.cmat[:], self.cmat_in, w=[self.cmat])
        k.op("dve", lambda e: e.tensor_copy(out=self.ident_b[:], in_=self.cmat[:, 0:128]), r=[self.cmat], w=[self.ident_b])
        k.op("dve", lambda e: e.tensor_copy(out=self.cmat_r[:], in_=self.cmat[:]), r=[self.cmat], w=[self.cmat_r])
        k.op("dve", lambda e: e.memset(self.one_t[:], 1.0), w=[self.one_t])
        self.prologue_mixers()
        k.op("dve", lambda e: e.memset(self.ones_b[:], 1.0), w=[self.ones_b])
        k.op("dve", lambda e: e.memset(self.eps_t[:], EPS), w=[self.eps_t])
        CW = 1024

        def slots(parent, n, dt_bytes):
            flat = parent.t
            out = []
            for j in range(n):
                out.append((T(flat[:, j * CW:(j + 1) * CW], f"{parent.name}_s{j}"), parent))
            return out
        ysbf = T(self.ysb.t[:, :], "ysbf")
        fsl = [(self.xin.bufs[0], None), (self.xin.bufs[1], None)] + slots(self.ysb, 2, 4) + slots(self.HT[min(self.HT)], 2, 4)
        bsl = slots(self.xs_tok, 2, 2) + slots(self.xdt, 2, 2) + slots(self.xdd, 2, 2)
        NS = len(fsl)
        sin = [k.dsem(f"ci{i}") for i in range(NS)]
        sout = [k.dsem(f"co{i}") for i in range(NS)]
        engs = ["act", "dve"]
        n = self.wtot // CW
        DPT = NS - 1

        def ld(i):
            s_ = i % NS
            stf = fsl[s_][0]
            stf_ap = stf[:, :] if fsl[s_][1] is None else stf.t
            k.dma("sp", sin[s_], stf_ap, self.wcat[:, i * CW:(i + 1) * CW], w=[stf])

        for i in range(min(DPT, n)):
            ld(i)
        for i in range(n):
            s_ = i % NS
            stf, stb = fsl[s_][0], bsl[s_][0]
            stf_ap = stf[:, :] if fsl[s_][1] is None else stf.t
            stb_ap = stb.t
            eng = engs[i % 2]
            if eng == "act":
                k.op("act", lambda e, stf_ap=stf_ap, stb_ap=stb_ap: e.copy(out=stb_ap, in_=stf_ap), r=[stf], w=[stb])
            else:
                k.op(eng, lambda e, stf_ap=stf_ap, stb_ap=stb_ap: e.tensor_copy(out=stb_ap, in_=stf_ap), r=[stf], w=[stb])
            k.dma("pool", sout[s_], self.wb[:, i * CW:(i + 1) * CW], stb_ap, r=[stb], w=[self.wb])
            if i + DPT < n:
                ld(i + DPT)
        for (slot, parent) in fsl + bsl:
            if parent is not None:
                for dd_s, dd_p in ((slot.w, parent.w), (slot.r, parent.r)):
                    for key, c in dd_s.items():
                        if dd_p.get(key, 0) < c:
                            dd_p[key] = c

    def wload(self, blk):
        off, kc, bw, nv = blk
        i = self.wring.i % 3
        buf = self.wring.next()
        k = self.k
        k.dma("sp", self.wsem[i], buf[:, 0:kc * bw], self.wb[:, off:off + kc * bw], r=[self.wb], w=[buf])
        return buf

    def proj_fm(self, wkey, src, kc_n, ntok, consumer, col_lo=0, col_hi=None):
        k = self.k
        blocks = self.wp.blocks[wkey]
        for b, blk in enumerate(blocks):
            off, kc, bw, nv = blk
            assert kc == kc_n
            c0 = b * bw
            if col_hi is not None and c0 >= col_hi:
                break
            if c0 + nv <= col_lo:
                continue
            buf = self.wload(blk)
            for j in range(0, nv, 128):
                col = c0 + j
                if col < col_lo or (col_hi is not None and col >= col_hi):
                    continue
                ncols = min(128, nv - j)
                ps = self.pp_ps.next()
                for kc_i in range(kc):
                    k.op("pe", lambda e, ps=ps, buf=buf, kc_i=kc_i, j=j, bw=bw, ncols=ncols:
                         e.matmul(ps[0:ncols, 0:ntok], lhsT=buf[:, kc_i * bw + j:kc_i * bw + j + ncols],
                                  rhs=src[:, kc_i, 0:ntok], start=(kc_i == 0), stop=(kc_i == kc - 1)),
                         r=[buf, src], w=[ps])
                consumer(col, ncols, ps)

    def rmsnorm(self, gkey, ntok, out, out_is_f32=False):
        k = self.k
        x = self.x
        ps = self.pp_ps.next()
        for c in range(8):
            sq = self.sq.next()
            k.op("act", lambda e, sq=sq, c=c: e.activation(out=sq[:, 0:ntok], in_=x[:, c, 0:ntok], func=AF.Square), r=[x], w=[sq])
            k.op("pe", lambda e, sq=sq, c=c, ps=ps: e.matmul(ps[:, 0:ntok], lhsT=self.ones_b[:], rhs=sq[:, 0:ntok], start=(c == 0), stop=(c == 7)),
                 r=[sq, self.ones_b], w=[ps])
        rstd = self.rstd
        k.op("act", lambda e: e.activation(out=rstd[:, 0:ntok], in_=ps[:, 0:ntok], func=AF.Sqrt, bias=self.eps_t[:, 0:1], scale=1.0 / D), r=[ps, self.eps_t], w=[rstd])
        k.op("dve", lambda e: e.reciprocal(out=rstd[:, 0:ntok], in_=rstd[:, 0:ntok]), r=[rstd], w=[rstd])
        prm, off, _ = self.P(gkey)
        for c in range(8):
            k.op("dve", lambda e, c=c: e.scalar_tensor_tensor(out=out[:, c, 0:ntok], in0=x[:, c, 0:ntok], scalar=prm[:, off + c:off + c + 1],
                                                              in1=rstd[:, 0:ntok], op0=ALU.mult, op1=ALU.mult), r=[x, rstd, prm], w=[out])

    def ffn(self, i, ntok):
        k = self.k
        self.rmsnorm(("nffn", i), ntok, self.h)
        act = self.act
        gb, ub = self.wp.blocks[("ffn_gate", i)], self.wp.blocks[("ffn_up", i)]
        for b in range(len(gb)):
            off, kc, bw, nv = gb[b]
            bufg = self.wload(gb[b])
            bufu = self.wload(ub[b])
            for j in range(0, nv, 128):
                ch = (b * bw + j) // 128
                psg, psu = self.pp_ps.next(), self.pp_ps.next()
                for buf, ps in ((bufg, psg), (bufu, psu)):
                    for kc_i in range(kc):
                        k.op("pe", lambda e, ps=ps, buf=buf, kc_i=kc_i, j=j, bw=bw:
                             e.matmul(ps[:, 0:ntok], lhsT=buf[:, kc_i * bw + j:kc_i * bw + j + 128],
                                      rhs=self.h[:, kc_i, 0:ntok], start=(kc_i == 0), stop=(kc_i == kc - 1)),
                             r=[buf, self.h], w=[ps])
                sil = self.sil.next()
                k.op("act", lambda e, sil=sil, psg=psg: e.activation(out=sil[:, 0:ntok], in_=psg[:, 0:ntok], func=AF.Silu), r=[psg], w=[sil])
                k.op("dve", lambda e, sil=sil, psu=psu, ch=ch: e.tensor_tensor(out=act[:, ch, 0:ntok], in0=sil[:, 0:ntok], in1=psu[:, 0:ntok], op=ALU.mult),
                     r=[sil, psu], w=[act])

        x = self.x

        def cons_down(col, ncols, ps):
            c = col // 128
            k.op("dve", lambda e: e.tensor_tensor(out=x[:, c, 0:ntok], in0=x[:, c, 0:ntok], in1=ps[:, 0:ntok], op=ALU.add), r=[x, ps], w=[x])

        self.proj_fm(("ffn_down", i), act, 22, ntok, cons_down)

    def load_x(self, src_ap, ntok):
        k = self.k
        for t0 in range(0, ntok, 128):
            n = min(128, ntok - t0)
            i = self.xin.i % 2
            xin = self.xin.next()
            k.dma("pool", self.xin_sem[i], xin[0:n, :], src_ap[t0:t0 + n, :], w=[xin])
            for c4 in range(2):
                ps = self.pt_ps.next()
                for cc in range(4):
                    c = c4 * 4 + cc
                    k.op("pe", lambda e, ps=ps, cc=cc, c=c, xin=xin, n=n: e.transpose(ps[:, cc * 128:cc * 128 + n], xin[0:n, c * 128:(c + 1) * 128], self.cmat[0:n, 0:n]),
                         r=[xin, self.cmat], w=[ps])
                k.op("act", lambda e, ps=ps, c4=c4, n=n, t0=t0: e.copy(out=self.x[:, c4 * 4:c4 * 4 + 4, t0:t0 + n],
                                                                         in_=ps[:, :].rearrange("p (c t) -> p c t", c=4)[:, :, 0:n]), r=[ps], w=[self.x])

    def store_y(self, dst_ap, ntok):
        k = self.k
        self.hf = T(self.ysb.t[:, :].rearrange("p (c t) -> p c t", c=8), "hfview")
        self.hf.w, self.hf.r = self.ysb.w, self.ysb.r
        self.rmsnorm(("nfin",), ntok, self.hf)
        for t0 in range(0, ntok, 128):
            n = min(128, ntok - t0)
            i = self.yout.i % 2
            yo = self.yout.next()
            for c4 in range(2):
                ps = self.pt_ps.next()
                for cc in range(4):
                    c = c4 * 4 + cc
                    k.op("pe", lambda e, ps=ps, cc=cc, c=c, n=n, t0=t0: e.transpose(ps[0:n, cc * 128:(cc + 1) * 128], self.hf[:, c, t0:t0 + n], self.cmat[:, 0:128]),
                         r=[self.hf, self.cmat], w=[ps])
                k.op("act", lambda e, ps=ps, c4=c4, n=n, yo=yo: e.copy(out=yo[0:n, c4 * 512:(c4 + 1) * 512], in_=ps[0:n, :]), r=[ps], w=[yo])
            k.dma("pool", self.yout_sem[i], dst_ap[t0:t0 + n, :], yo[0:n, :], r=[yo])

    def layers(self, ntok, tile):
        cfg = self.cfg
        for i in range(cfg.DEPTH):
            if cfg.mixers:
                self.mixer(i, ntok, tile)
            self.ffn(i, ntok)

    def alloc_mixers(self):
        k, cfg = self.k, self.cfg
        TP = cfg.TP
        self.A = k.sbuf((128, 24, TP), BF16, "A")
        self.zt = k.sbuf((128, 4, 2048), BF16, "zt")
        self.pre = Ring([k.sbuf((128, TP + 16), F32, f"pre{i}") for i in range(2)])
        self.ctmp = Ring([k.sbuf((128, TP), F32, f"ctmp{i}") for i in range(2)])
        self.hist = {}
        self.HT = {}
        for i in range(cfg.DEPTH):
            if i % 3 == 0:
                self.hist[i] = k.sbuf((128, 24, 3), F32, f"hist{i}")
                self.HT[i] = k.sbuf((128, 2048), F32, f"HT{i}")
        self.HTb = k.sbuf((128, 2048), BF16, "HTb")
        self.acol = k.sbuf((128, 4), F32, "acol")
        self.dtT = k.sbuf((32, TP), F32, "dtT")
        self.aT = k.sbuf((32, TP), F32, "aT")
        self.xs_tok = k.sbuf((128, 2048), BF16, "xs_tok")
        self.xdt = k.sbuf((128, 2048), BF16, "xdt")
        self.xdd = k.sbuf((128, 2048), BF16, "xdd")
        self.xsD = k.sbuf((128, 2048), BF16, "xsD")
        self.B_tok = k.sbuf((128, 512), BF16, "B_tok")
        self.a_tok = k.sbuf((128, 32), F32R, "a_tok")
        self.dt_tok = k.sbuf((128, 32), F32, "dt_tok")
        self.acs = k.sbuf((128, 32), F32, "acs")
        self.eacs = k.sbuf((128, 32), F32, "eacs")
        self.dst = k.sbuf((128, 32), F32, "dst")
        self.edec = k.sbuf((128, 32), F32, "edec")
        self.CBT = Ring([k.sbuf((128, 128), F32, f"CBT{i}") for i in range(2)])
        self.XD = Ring([k.sbuf((128, 512), F32R, f"XD{i}") for i in range(2)])
        self.TMP = Ring([k.sbuf((128, 512), F32, f"TMP{i}") for i in range(2)])
        self.LX = Ring([k.sbuf((128, 512), F32, f"LX{i}") for i in range(2)])
        self.MT = Ring([k.sbuf((128, 512), BF16, f"MT{i}") for i in range(2)])
        self.ysb = k.sbuf((128, 2048), F32, "ysb")
        self.ygn = k.sbuf((128, 2048), BF16, "ygn")
        self.junk = k.sbuf((128, 512), F32, "junk")
        self.ss = k.sbuf((128, 8), F32, "ss")
        self.rs = k.sbuf((128, 8), F32, "rs")
        self.st_sem = k.dsem("st")
        self.out_sems.append(self.st_sem)
        nc = self.nc
        NSC = cfg.NSC
        nssm = len(self.HT)
        I = lambda n, s: nc.dram_tensor(n, list(s), F32, kind="ExternalInput").ap()
        O = lambda n, s: nc.dram_tensor(n, list(s), F32, kind="ExternalOutput").ap()
        self.ssm_in = I("ssm_in_t", (nssm, NSC, 128, 2048))
        self.conv_in = I("conv_in_fm", (nssm, 128, 24 * NSC * 3))
        self.ssm_p_o = O("ssm_p_t", (nssm, 128, 2048))
        self.ssm_s_o = O("ssm_s_t", (nssm, NSC, 128, 2048))
        self.conv_p_o = O("conv_p_fm", (nssm, 128, 72))
        self.conv_s_o = O("conv_s_fm", (nssm, 128, 24 * NSC * 3))
        self.convst = k.sbuf((128, nssm, 24 * NSC * 3), F32, "convst")
        self.convout = k.sbuf((128, 24 * NSC * 3), F32, "convout")
        if cfg.DEPTH > 1:
            self.alloc_gla()
        if cfg.DEPTH > 2:
            self.alloc_att()

    def prologue_mixers(self):
        k, cfg = self.k, self.cfg
        for n, i in enumerate(sorted(self.HT)):
            k.op("dve", lambda e, i=i: e.memset(self.hist[i][:], 0.0), w=[self.hist[i]])
            k.op("dve", lambda e, i=i: e.memset(self.HT[i][:], 0.0), w=[self.HT[i]])
            prm, off, _ = self.P(("ssm_alog", i))
            k.op("act", lambda e, n=n, off=off: e.activation(out=self.acol[:, n:n + 1], in_=prm[:, off:off + 1], func=AF.Exp), r=[prm], w=[self.acol])
            k.op("dve", lambda e, n=n: e.tensor_scalar(out=self.acol[:, n:n + 1], in0=self.acol[:, n:n + 1], scalar1=-1.0, scalar2=None, op0=ALU.mult), r=[self.acol], w=[self.acol])
            k.dma("pool", self.misc_sem, self.convst[:, n, :], self.conv_in[n], w=[self.convst])
        if cfg.DEPTH > 2:
            self.prologue_att()

    def mixer(self, i, ntok, tile, phase="all"):
        m = i % 3
        if m == 0:
            self.mixer_ssd(i, ntok, tile, phase)
        elif m == 1:
            self.mixer_gla(i, ntok, tile, phase)
        else:
            self.mixer_att(i, ntok, tile, phase)

    def add_to_x(self, ntok):
        k, x = self.k, self.x

        def cons(col, ncols, ps):
            c = col // 128
            k.op("dve", lambda e: e.tensor_tensor(out=x[:, c, 0:ntok], in0=x[:, c, 0:ntok], in1=ps[:, 0:ntok], op=ALU.add), r=[x, ps], w=[x])
        return cons

    def alloc_gla(self):
        k, cfg, nc = self.k, self.cfg, self.nc
        TP, NSC = cfg.TP, cfg.NSC
        self.glow = k.sbuf((16, TP), F32, "glow")
        self.G3 = k.sbuf((128, 3, 4, TP), F32, "G3")
        self.la = [TV(self.G3, self.G3[:, i], f"la{i}") for i in range(2)]
        self.ebx = TV(self.G3, self.G3[:, 2], "ebx")
        self.eblast = k.sbuf((128, 16), F32, "eblast")
        self.S = k.sbuf((128, 1024), F32, "S")
        self.Sb = k.sbuf((128, 1024), BF16, "Sb")
        I = lambda n, s: nc.dram_tensor(n, list(s), F32, kind="ExternalInput").ap()
        O = lambda n, s: nc.dram_tensor(n, list(s), F32, kind="ExternalOutput").ap()
        self.gla_in = I("gla_in", (NSC, 4, 128, 256))
        self.gla_p_o = O("gla_p", (4, 128, 256))
        self.gla_s_o = O("gla_s", (NSC, 4, 128, 256))

    def mixer_gla(self, i, ntok, tile, phase="all"):
        k, cfg = self.k, self.cfg
        kind, ti = tile
        do_proj, do_y = True, phase != "p1"
        NSC = cfg.NSC
        if kind == "p":
            units = [(u * 128, 128, 0) for u in range(ntok // 128)]
        else:
            units = [(s * cfg.DEC, cfg.DEC, s) for s in range(NSC)]
        nun, Q = len(units), units[0][1]
        A, h, zt, act = self.A, self.h, self.zt, self.act
        cm, ident_b = self.cmat, self.ident_b
        S, Sb = self.S, self.Sb
        glow, la, ebx, eblast = self.glow, self.la, self.ebx, self.eblast
        blocks = self.wp.blocks[("gla_in", i)]
        prm_wg, off_wg, _ = self.P(("gla_wg", i))
        prm_gb, off_gb, _ = self.P(("gla_gb", i))
        prm_nw, off_nw, _ = self.P(("gla_nw", i))
        if phase == "p1" and ti == 0:
            k.op("dve", lambda e: e.memset(S[:, :], 0.0), w=[S])
            k.op("dve", lambda e: e.memset(Sb[:, :], 0.0), w=[Sb])
            k.op("dve", lambda e: e.memset(self.blsum[:, :], 0.0), w=[self.blsum])

        if do_proj:
            self.rmsnorm(("nmix", i), ntok, self.h)
            def cons(col, ncols, ps):
                if col < 1024:
                    k.op("act", lambda e: e.copy(out=A[:, col // 128, 0:ntok], in_=ps[:, 0:ntok]), r=[ps], w=[A])
                else:
                    k.op("act", lambda e: e.copy(out=glow[:, 0:ntok], in_=ps[0:16, 0:ntok]), r=[ps], w=[glow])
            self.proj_fm(("gla_in", i), h, 8, ntok, cons, col_lo=(0 if do_y else 512), col_hi=1024)
            self.proj_fm(("gla_in", i), h, 8, ntok, cons, col_lo=3072)
            for b in ((2, 3, 4, 5) if do_y else (2, 3)):
                buf = self.wload(blocks[b])
                for ui, (o, Qu, s) in enumerate(units):
                    ps = self.PS.next()
                    for kc in range(8):
                        k.op("pe", lambda e, ps=ps, kc=kc, o=o, buf=buf: e.matmul(ps[0:Q, 0:512], lhsT=h[:, kc, o:o + Q], rhs=buf[:, kc * 512:(kc + 1) * 512],
                                                                                  start=(kc == 0), stop=(kc == 7)), r=[h, buf], w=[ps])
                    if b >= 4:
                        k.op("act", lambda e, ps=ps, ui=ui, b=b: e.activation(out=zt[0:Q, ui, (b - 4) * 512:(b - 3) * 512], in_=ps[0:Q, 0:512], func=AF.Silu), r=[ps], w=[zt])
                    else:
                        k.op("act", lambda e, ps=ps, ui=ui, b=b: e.copy(out=zt[0:Q, ui, 1024 + (b - 2) * 512:1024 + (b - 1) * 512], in_=ps[0:Q, 0:512]), r=[ps], w=[zt])
            for hh in range(4):
                ps = self.PS.next()
                k.op("pe", lambda e, ps=ps, hh=hh: e.matmul(ps[:, 0:ntok], lhsT=prm_wg[0:16, off_wg + hh * 128:off_wg + (hh + 1) * 128], rhs=glow[:, 0:ntok], start=True, stop=True),
                     r=[prm_wg, glow], w=[ps])
                xb, ax = self.ctmp.next(), self.pre.next()
                k.op("dve", lambda e, ps=ps, xb=xb, hh=hh: e.tensor_scalar(out=xb[:, 0:ntok], in0=ps[:, 0:ntok], scalar1=prm_gb[:, off_gb + hh:off_gb + hh + 1], scalar2=None, op0=ALU.add),
                     r=[ps, prm_gb], w=[xb])
                k.op("dve", lambda e, xb=xb, ax=ax: e.scalar_tensor_tensor(out=ax[:, 0:ntok], in0=xb[:, 0:ntok], scalar=-1.0, in1=xb[:, 0:ntok], op0=ALU.mult, op1=ALU.max), r=[xb], w=[ax])
                k.op("act", lambda e, ax=ax: e.activation(out=ax[:, 0:ntok], in_=ax[:, 0:ntok], func=AF.Exp, scale=-1.0), r=[ax], w=[ax])
                k.op("act", lambda e, ax=ax: e.activation(out=ax[:, 0:ntok], in_=ax[:, 0:ntok], func=AF.Ln, bias=self.one_t[:, 0:1], scale=1.0), r=[ax, self.one_t], w=[ax])
                k.op("dve", lambda e, xb=xb, ax=ax, hh=hh: e.scalar_tensor_tensor(out=la[0][:, hh, 0:ntok], in0=xb[:, 0:ntok], scalar=0.0, in1=ax[:, 0:ntok], op0=ALU.min, op1=ALU.subtract),
                     r=[xb, ax], w=[la[0]])
            cur = 0
            sh = 1
            vw = lambda t: t[:, :, 0:ntok].rearrange("p h (u q) -> p h u q", q=Q)
            while sh < Q:
                src, dst_ = la[cur], la[1 - cur]
                k.op("dve", lambda e, src=src, dst_=dst_, sh=sh: e.tensor_copy(out=vw(dst_)[:, :, :, 0:sh], in_=vw(src)[:, :, :, 0:sh]), r=[src], w=[dst_])
                k.op("dve", lambda e, src=src, dst_=dst_, sh=sh: e.tensor_tensor(out=vw(dst_)[:, :, :, sh:Q], in0=vw(src)[:, :, :, sh:Q], in1=vw(src)[:, :, :, 0:Q - sh], op=ALU.add), r=[src], w=[dst_])
                cur = 1 - cur
                sh *= 2
            bc_ = la[cur]
            k.op("act", lambda e: e.activation(out=ebx[:, :, 0:ntok], in_=bc_[:, :, 0:ntok], func=AF.Exp, scale=1.0 / 16), r=[bc_], w=[ebx])
            k.op("dve", lambda e: e.scalar_tensor_tensor(out=A[:, 8:12, 0:ntok], in0=A[:, 0:4, 0:ntok], scalar=float(GHK) ** -0.5, in1=ebx[:, :, 0:ntok], op0=ALU.mult, op1=ALU.mult),
                 r=[A, ebx], w=[A])
            k.op("act", lambda e: e.activation(out=eblast[:, 0:4 * nun].rearrange("p (h u) -> p h u", h=4), in_=vw(bc_)[:, :, :, Q - 1], func=AF.Exp, scale=1.0 / 16), r=[bc_], w=[eblast])
            k.op("act", lambda e: e.activation(out=ebx[:, :, 0:ntok], in_=bc_[:, :, 0:ntok], func=AF.Exp, scale=-1.0 / 16), r=[bc_], w=[ebx])
            k.op("dve", lambda e: e.tensor_tensor(out=A[:, 12:16, 0:ntok], in0=A[:, 4:8, 0:ntok], in1=ebx[:, :, 0:ntok], op=ALU.mult), r=[A, ebx], w=[A])
            k.op("dve", lambda e: e.tensor_tensor(out=vw(A[:, 16:20, :]), in0=vw(A[:, 12:16, :]), in1=eblast[:, 0:4 * nun].rearrange("p (h u) -> p h u", h=4).unsqueeze(3).to_broadcast([128, 4, nun, Q]), op=ALU.mult),
                 r=[A, eblast], w=[A])
            if phase == "p1":
                for u_ in range(nun):
                    k.op("dve", lambda e, u_=u_: e.tensor_tensor(out=self.blsum[:, 0:4], in0=self.blsum[:, 0:4], in1=vw(bc_)[:, :, u_, Q - 1], op=ALU.add), r=[self.blsum, bc_], w=[self.blsum])
        ysb, ygn, kd_tok = self.ysb, self.ygn, self.B_tok
        for ui, (o, Qu, s) in enumerate(units):
            if kind == "s":
                k.dma("pool", self.st_sem, S[:, :].rearrange("p (h v) -> p h v", h=4), self.gla_in[s].rearrange("h k v -> k h v"), w=[S])
                k.op("act", lambda e: e.copy(out=Sb[:, :], in_=S[:, :]), r=[S], w=[Sb])
            ps = self.PS.next()
            pb = ps[:, :].bitcast(BF16)
            for hh in range(4):
                k.op("pe", lambda e, pb=pb, hh=hh, o=o: e.transpose(pb[0:Q, hh * 128:(hh + 1) * 128], A[:, 16 + hh, o:o + Q], ident_b[:, :]), r=[A, ident_b], w=[ps])
            k.op("act", lambda e, pb=pb: e.copy(out=kd_tok[0:Q, :], in_=pb[0:Q, 0:512]), r=[ps], w=[kd_tok])
            MT = self.MT.next()
            k.op("dve", lambda e: e.memset(self.ss[:, :], 0.0), w=[self.ss])
            ops = []
            for hh in range(4):
                vt = zt[0:Q, ui, 1024 + hh * 256:1024 + (hh + 1) * 256]
                if do_y:
                    psa = self.PS.next()
                    k.op("pe", lambda e, psa=psa, hh=hh, o=o: e.matmul(psa[0:Q, 0:Q], lhsT=A[:, 12 + hh, o:o + Q], rhs=A[:, 8 + hh, o:o + Q], start=True, stop=True), r=[A], w=[psa])
                    k.op("dve", lambda e, psa=psa, hh=hh, MT=MT: e.tensor_tensor(out=MT[0:Q, hh * Q:(hh + 1) * Q], in0=psa[0:Q, 0:Q], in1=cm[0:Q, 128:128 + Q], op=ALU.mult), r=[psa, cm], w=[MT])
                    pso = self.PS.next()
                    k.op("pe", lambda e, pso=pso, hh=hh, MT=MT, vt=vt: e.matmul(pso[0:Q, 0:256], lhsT=MT[0:Q, hh * Q:(hh + 1) * Q], rhs=vt, start=True, stop=False), r=[MT, zt], w=[pso])
                    k.op("pe", lambda e, pso=pso, hh=hh, o=o: e.matmul(pso[0:Q, 0:256], lhsT=A[:, 8 + hh, o:o + Q], rhs=Sb[:, hh * 256:(hh + 1) * 256], start=False, stop=True), r=[A, Sb], w=[pso])
                    k.op("act", lambda e, pso=pso, hh=hh: e.activation(out=self.junk[0:Q, 0:256], in_=pso[0:Q, 0:256], func=AF.Square, accum_out=self.ss[0:Q, hh:hh + 1]),
                         r=[pso], w=[self.junk, self.ss])
                    k.op("act", lambda e, pso=pso, hh=hh: e.copy(out=ysb[0:Q, hh * 256:(hh + 1) * 256], in_=pso[0:Q, 0:256]), r=[pso], w=[ysb])
                psu = self.PS.next()
                k.op("pe", lambda e, psu=psu, hh=hh, vt=vt: e.matmul(psu[:, 0:256], lhsT=kd_tok[0:Q, hh * 128:(hh + 1) * 128], rhs=vt, start=True, stop=True), r=[kd_tok, zt], w=[psu])
                k.op("dve", lambda e, psu=psu, hh=hh, ui=ui: e.scalar_tensor_tensor(out=S[:, hh * 256:(hh + 1) * 256], in0=S[:, hh * 256:(hh + 1) * 256], scalar=eblast[:, hh * nun + ui:hh * nun + ui + 1],
                                                                                   in1=psu[:, 0:256], op0=ALU.mult, op1=ALU.add), r=[S, eblast, psu], w=[S])
            k.op("act", lambda e: e.copy(out=Sb[:, :], in_=S[:, :]), r=[S], w=[Sb])
            if do_y:
                k.op("act", lambda e: e.activation(out=self.rs[0:Q, 0:4], in_=self.ss[0:Q, 0:4], func=AF.Sqrt, bias=self.eps_t[0:Q, 0:1], scale=1.0 / GHV), r=[self.ss, self.eps_t], w=[self.rs])
                k.op("dve", lambda e: e.reciprocal(out=self.rs[0:Q, 0:4], in_=self.rs[0:Q, 0:4]), r=[self.rs], w=[self.rs])
                for hh in range(4):
                    k.op("dve", lambda e, hh=hh: e.scalar_tensor_tensor(out=ysb[0:Q, hh * 256:(hh + 1) * 256], in0=ysb[0:Q, hh * 256:(hh + 1) * 256], scalar=self.rs[0:Q, hh:hh + 1],
                                                                       in1=prm_nw[0:Q, off_nw:off_nw + 256], op0=ALU.mult, op1=ALU.mult), r=[ysb, self.rs, prm_nw], w=[ysb])
                k.op("dve", lambda e, ui=ui: e.tensor_tensor(out=ygn[0:Q, 0:1024], in0=ysb[0:Q, 0:1024], in1=zt[0:Q, ui, 0:1024], op=ALU.mult), r=[ysb, zt], w=[ygn])
                ps = self.PS.next()
                pb = ps[:, :].bitcast(BF16)
                for c in range(8):
                    k.op("pe", lambda e, pb=pb, c=c: e.transpose(pb[:, c * Q:(c + 1) * Q], ygn[0:Q, c * 128:(c + 1) * 128], ident_b[0:Q, 0:Q]), r=[ygn, ident_b], w=[ps])
                k.op("act", lambda e, pb=pb, o=o: e.copy(out=act[:, 0:8, o:o + Q], in_=pb[:, 0:8 * Q].rearrange("p (c q) -> p c q", c=8)), r=[ps], w=[act])
            if kind == "s":
                k.dma("pool", self.st_sem, self.gla_s_o[s].rearrange("h k v -> k h v"), S[:, :].rearrange("p (h v) -> p h v", h=4), r=[S])
        if kind == "p" and ti == cfg.NT - 1 and phase == "p2":
            k.dma("pool", self.st_sem, self.gla_p_o.rearrange("h k v -> k h v"), S[:, :].rearrange("p (h v) -> p h v", h=4), r=[S])
        if do_y:
            self.proj_fm(("gla_out", i), act, 8, ntok, self.add_to_x(ntok))

    def alloc_att(self):
        k, cfg, nc = self.k, self.cfg, self.nc
        NSC, SEQ = cfg.NSC, cfg.SEQ
        I = lambda n, s: nc.dram_tensor(n, list(s), F32, kind="ExternalInput").ap()
        O = lambda n, s: nc.dram_tensor(n, list(s), F32, kind="ExternalOutput").ap()
        self.NU = SEQ // 128
        self.vtok = k.sbuf((128, 768), F32, "vtok")
        self.maskb = k.sbuf((128, 3072), BF16, "maskb")
        self.ropeT = k.sbuf((128, (self.NU + 1) * 16), F32, "ropeT")
        self.ast = k.sbuf((128, 64), F32, "ast")
        self.rope_in = I("rope", (128, (self.NU + 1) * 16))
        self.mask_in = I("amask", (128, 3072))
        self.kt_hist = k.dram("kt_hist", (6, 128, SEQ), BF16)
        self.v_hist = k.dram("v_hist", (SEQ, 768), BF16)
        self.cache = [I(f"kvc{g}", (NSC, W, 512)) for g, (W, d) in enumerate(AGROUPS)]
        self.kvp_o = [O(f"kvp{g}", (min(W, SEQ), 512)) for g, (W, d) in enumerate(AGROUPS)]
        self.kvs_o = [O(f"kvs{g}", (NSC, W, 512)) for g, (W, d) in enumerate(AGROUPS)]
        self.att_sems = [k.dsem(f"att{i}") for i in range(4)]
        self.att_i = 0
        self.out_sems += self.att_sems
        ztf = self.zt.t[:, :, :].rearrange("p a b -> p (a b)")
        self.KT_sb = TV(self.zt, ztf[:, 0:4608].rearrange("p (c t) -> p c t", c=2), "KT_sb")
        self.Pb = TV(self.zt, ztf[:, 4608:4608 + 2304], "Pb")
        g3f = self.G3.t[:, :, :, :].rearrange("p a b c -> p (a b c)").bitcast(BF16)
        self.V_sb = TV(self.G3, g3f[:, 0:4608].rearrange("p (b f) -> p b f", f=256), "V_sb")

    def prologue_att(self):
        k = self.k
        k.dma("pool", self.misc_sem, self.ropeT[:, :], self.rope_in, w=[self.ropeT])
        for i in range(3):
            st = self.xin.bufs[i % 2]
            k.dma("pool", self.misc_sem, st[:, :], self.mask_in[:, i * 1024:(i + 1) * 1024], w=[st])
            k.op("dve", lambda e, st=st, i=i: e.tensor_copy(out=self.maskb[:, i * 1024:(i + 1) * 1024], in_=st[:, :]), r=[st], w=[self.maskb])

    def adma(self, out, in_, r=(), w=()):
        s = self.att_sems[self.att_i % 4]
        self.att_i += 1
        self.k.dma("pool", s, out, in_, r=r, w=w)

    def mixer_att(self, i, ntok, tile, phase="all"):
        k, cfg = self.k, self.cfg
        kind, ti = tile
        SEQ, NSC, TP = cfg.SEQ, cfg.NSC, cfg.TP
        if kind == "p":
            units = [(u * 128, 128, 0, ti * TP + u * 128, (ti * TP) // 128 + u) for u in range(ntok // 128)]
        else:
            units = [(s * cfg.DEC, cfg.DEC, s, cfg.PAST, self.NU) for s in range(NSC)]
        self.rmsnorm(("nmix", i), ntok, self.h)
        A, h, act, cm, ident_b = self.A, self.h, self.act, self.cmat, self.ident_b
        ysb, ygn, vtok, vb, ast = self.ysb, self.ygn, self.vtok, self.xs_tok, self.ast
        KT_sb, V_sb, Pb, U = self.KT_sb, self.V_sb, self.Pb, self.xin.bufs[0]
        maskb, ropeT = self.maskb, self.ropeT
        blocks = self.wp.blocks[("att_qkv", i)]
        def unit_body(o, Q, s, q0, ru):
            if True:
                for b, blk in enumerate(blocks):
                    if (phase == "p1" and b == 0) or (phase == "p2" and b >= 2):
                        continue
                    buf = self.wload(blk)
                    nv = blk[3]
                    ps = self.PS.next()
                    for kc in range(8):
                        k.op("pe", lambda e, ps=ps, kc=kc, buf=buf, nv=nv: e.matmul(ps[0:Q, 0:nv], lhsT=h[:, kc, o:o + Q], rhs=buf[:, kc * 512:kc * 512 + nv],
                                                                                   start=(kc == 0), stop=(kc == 7)), r=[h, buf], w=[ps])
                    c0 = b * 512
                    if c0 < 1536:
                        k.op("act", lambda e, ps=ps, c0=c0, nv=nv: e.copy(out=ysb[0:Q, c0:c0 + nv], in_=ps[0:Q, 0:nv]), r=[ps], w=[ysb])
                    else:
                        k.op("act", lambda e, ps=ps, c0=c0, nv=nv: e.copy(out=vtok[0:Q, c0 - 1536:c0 - 1536 + nv], in_=ps[0:Q, 0:nv]), r=[ps], w=[vtok])
                qk = ysb[0:Q, 0:1536].rearrange("p (h d) -> p h d", d=64)
                cosb = ropeT[0:Q, ru * 16:ru * 16 + 8].unsqueeze(1).to_broadcast([Q, 24, 8])
                sinb = ropeT[0:Q, ru * 16 + 8:ru * 16 + 16].unsqueeze(1).to_broadcast([Q, 24, 8])
                t0_, t1_ = self.TMP.next(), self.LX.next()
                ta = t0_[0:Q, 0:192].rearrange("p (h d) -> p h d", d=8)
                tb = t0_[0:Q, 192:384].rearrange("p (h d) -> p h d", d=8)
                tc = t1_[0:Q, 0:192].rearrange("p (h d) -> p h d", d=8)
                td = t1_[0:Q, 192:384].rearrange("p (h d) -> p h d", d=8)
                x1, x2 = qk[:, :, 0:8], qk[:, :, 8:16]
                k.op("dve", lambda e: e.tensor_tensor(out=ta, in0=x1, in1=cosb, op=ALU.mult), r=[ysb, ropeT], w=[t0_])
                k.op("dve", lambda e: e.tensor_tensor(out=tb, in0=x2, in1=sinb, op=ALU.mult), r=[ysb, ropeT], w=[t0_])
                k.op("dve", lambda e: e.tensor_tensor(out=tc, in0=x2, in1=cosb, op=ALU.mult), r=[ysb, ropeT], w=[t1_])
                k.op("dve", lambda e: e.tensor_tensor(out=td, in0=x1, in1=sinb, op=ALU.mult), r=[ysb, ropeT], w=[t1_])
                k.op("dve", lambda e: e.tensor_tensor(out=x1, in0=ta, in1=tb, op=ALU.subtract), r=[t0_], w=[ysb])
                k.op("dve", lambda e: e.tensor_tensor(out=x2, in0=tc, in1=td, op=ALU.add), r=[t1_], w=[ysb])
                k.op("dve", lambda e: e.tensor_copy(out=ygn[0:Q, 0:1536], in_=ysb[0:Q, 0:1536]), r=[ysb], w=[ygn])
                k.op("act", lambda e: e.copy(out=vb[0:Q, 0:768], in_=vtok[0:Q, :]), r=[vtok], w=[vb])
                for part in range(2):
                    if (phase == "p1" and part == 0) or (phase == "p2" and part == 1):
                        continue
                    ps = self.PS.next()
                    pb = ps[:, :].bitcast(BF16)
                    for c in range(6):
                        k.op("pe", lambda e, pb=pb, c=c, part=part: e.transpose(pb[:, c * Q:(c + 1) * Q], ygn[0:Q, part * 768 + c * 128:part * 768 + (c + 1) * 128], ident_b[0:Q, 0:Q]),
                             r=[ygn, ident_b], w=[ps])
                    k.op("act", lambda e, pb=pb, part=part: e.copy(out=A[:, part * 6:(part + 1) * 6, 0:Q], in_=pb[:, 0:6 * Q].rearrange("p (c q) -> p c q", c=6)), r=[ps], w=[A])
                if phase == "p2":
                    pass
                elif kind == "p":
                    self.adma(self.kt_hist[:, :, q0:q0 + Q].rearrange("c p t -> p c t"), A[:, 6:12, 0:Q], r=[A], w=[self.kt_hist])
                    self.adma(self.v_hist[q0:q0 + Q, :], vb[0:Q, 0:768], r=[vb], w=[self.v_hist])
                    for g, (W, d) in enumerate(AGROUPS):
                        Wc = min(W, SEQ)
                        if q0 + Q > SEQ - Wc:
                            r0 = q0 - (SEQ - Wc)
                            self.adma(self.kvp_o[g][r0:r0 + Q, 0:256], ysb[0:Q, 768 + g * 256:768 + (g + 1) * 256], r=[ysb])
                            self.adma(self.kvp_o[g][r0:r0 + Q, 256:512], vtok[0:Q, g * 256:(g + 1) * 256], r=[vtok])
                else:
                    for g, (W, d) in enumerate(AGROUPS):
                        self.adma(self.kvs_o[g][s, W - Q:W, 0:256], ysb[0:Q, 768 + g * 256:768 + (g + 1) * 256], r=[ysb])
                        self.adma(self.kvs_o[g][s, W - Q:W, 256:512], vtok[0:Q, g * 256:(g + 1) * 256], r=[vtok])
                        self.adma(self.kvs_o[g][s, 0:W - Q, :], self.cache[g][s, Q:W, :])
                if phase == "p1":
                    return
            for g, (W, d) in enumerate(AGROUPS):
                nkp = 0
                if kind == "p":
                    nkp = max(0, W - q0)
                    lo = max(q0 - W, 0)
                    nkl = q0 + Q - lo
                    nk, moff = nkp + nkl, 0
                    if nkp > 0:
                        lk, lv = self.loc_k, self.loc_v
                        self.adma(KT_sb[:, :, 0:nkp], lk[2 * g * 128:(2 * g + 2) * 128, SEQ - nkp:SEQ].rearrange("(c p) t -> p c t", p=128), r=[lk], w=[KT_sb])
                        self.adma(V_sb[:, 0:nkp // 128, :], lv[SEQ - nkp:SEQ, g * 256:(g + 1) * 256].rearrange("(b p) f -> p b f", p=128), r=[lv], w=[V_sb])
                    self.adma(KT_sb[:, :, nkp:nk], self.kt_hist[2 * g:2 * g + 2, :, lo:lo + nkl].rearrange("c p t -> p c t"), r=[self.kt_hist], w=[KT_sb])
                    self.adma(V_sb[:, nkp // 128:nk // 128, :], self.v_hist[lo:lo + nkl, g * 256:(g + 1) * 256].rearrange("(b p) f -> p b f", p=128), r=[self.v_hist], w=[V_sb])
                else:
                    nk, moff = W + Q, 0
                    stg = self.xin.bufs[1]
                    for blk in range(W // 128):
                        self.adma(stg[:, 0:512], self.cache[g][s, blk * 128:(blk + 1) * 128, :], w=[stg])
                        k.op("act", lambda e, blk=blk: e.copy(out=V_sb[:, blk, :], in_=stg[:, 256:512]), r=[stg], w=[V_sb])
                        ps = self.PS.next()
                        for c in range(2):
                            k.op("pe", lambda e, ps=ps, c=c: e.transpose(ps[:, c * 128:(c + 1) * 128], stg[:, c * 128:(c + 1) * 128], cm[:, 0:128]), r=[stg, cm], w=[ps])
                        k.op("act", lambda e, ps=ps, blk=blk: e.copy(out=KT_sb[:, :, blk * 128:(blk + 1) * 128], in_=ps[:, 0:256].rearrange("p (c t) -> p c t", c=2)), r=[ps], w=[KT_sb])
                    k.op("act", lambda e, g=g, W=W: e.copy(out=KT_sb[:, :, W:W + Q], in_=A[:, 6 + 2 * g:8 + 2 * g, 0:Q]), r=[A], w=[KT_sb])
                    k.op("act", lambda e, g=g, W=W: e.copy(out=V_sb[0:Q, W // 128, :], in_=vb[0:Q, g * 256:(g + 1) * 256]), r=[vb], w=[V_sb])
                nkb = -(-nk // 128)
                mcol0 = (0, 256, 896)[g] + moff
                for j in range(4):
                    hd = 4 * g + j
                    c, half = j // 2, j % 2
                    pl, ph = half * 64, half * 64 + 64
                    chunks = [(k0, min(512, nk - k0)) for k0 in range(0, nk, 512)]
                    pss = []
                    for ci, (k0, n) in enumerate(chunks):
                        ps = self.PS.next()
                        pss.append(ps)
                        k.op("pe", lambda e, ps=ps, k0=k0, n=n, g=g, c=c, pl=pl, ph=ph: e.matmul(ps[0:Q, 0:n], lhsT=A[pl:ph, 2 * g + c, 0:Q], rhs=KT_sb[pl:ph, c, k0:k0 + n], start=True, stop=True),
                             r=[A, KT_sb], w=[ps])
                        k.op("dve", lambda e, ps=ps, k0=k0, n=n, mcol0=mcol0: e.tensor_tensor(out=ps[0:Q, 0:n], in0=ps[0:Q, 0:n], in1=maskb[0:Q, mcol0 + k0:mcol0 + k0 + n], op=ALU.add),
                             r=[ps, maskb], w=[ps])
                        if k0 < nkp:
                            n2 = min(n, nkp - k0)
                            prm_c, off_c, _ = self.P(("core",))
                            k.op("dve", lambda e, ps=ps, n2=n2: e.tensor_scalar(out=ps[0:Q, 0:n2], in0=ps[0:Q, 0:n2], scalar1=prm_c[0:Q, off_c + 5:off_c + 6], scalar2=None, op0=ALU.add),
                                 r=[ps, prm_c], w=[ps])
                        k.op("dve", lambda e, ps=ps, n=n, ci=ci: e.reduce_max(out=ast[0:Q, 56 + ci:57 + ci], in_=ps[0:Q, 0:n], axis=AX.X), r=[ps], w=[ast])
                    nch = len(chunks)
                    k.op("dve", lambda e, hd=hd, nch=nch: e.reduce_max(out=ast[0:Q, hd:hd + 1], in_=ast[0:Q, 56:56 + nch], axis=AX.X), r=[ast], w=[ast])
                    k.op("dve", lambda e, hd=hd: e.tensor_scalar(out=ast[0:Q, 63:64], in0=ast[0:Q, hd:hd + 1], scalar1=-0.125, scalar2=None, op0=ALU.mult), r=[ast], w=[ast])
                    k.op("dve", lambda e: e.memset(ast[0:Q, 56:62], 0.0), r=[ast], w=[ast])
                    for ci, (k0, n) in enumerate(chunks):
                        ps = pss[ci]
                        k.op("act", lambda e, ps=ps, k0=k0, n=n, ci=ci: e.activation(out=Pb[0:Q, k0:k0 + n], in_=ps[0:Q, 0:n], func=AF.Exp, bias=ast[0:Q, 63:64], scale=0.125,
                                                                                   accum_out=ast[0:Q, 56 + ci:57 + ci]), r=[ps, ast], w=[Pb, ast])
                    k.op("dve", lambda e, hd=hd, nch=nch: e.reduce_sum(out=ast[0:Q, 12 + hd:13 + hd], in_=ast[0:Q, 56:56 + nch], axis=AX.X), r=[ast], w=[ast])
                    ups = self.PS.next()
                    for b0 in range(0, nkb, 4):
                        nb = min(4, nkb - b0)
                        pt = self.PS.next()
                        ptb = pt[:, :].bitcast(BF16)
                        for bb in range(nb):
                            b = b0 + bb
                            kn = min(128, nk - b * 128)
                            k.op("pe", lambda e, ptb=ptb, bb=bb, b=b, kn=kn: e.transpose(ptb[0:kn, bb * Q:(bb + 1) * Q], Pb[0:Q, b * 128:b * 128 + kn], ident_b[0:Q, 0:Q]), r=[Pb, ident_b], w=[pt])
                        MT = self.MT.next()
                        kmax = min(128, nk - b0 * 128)
                        k.op("act", lambda e, ptb=ptb, MT=MT, nb=nb, kmax=kmax: e.copy(out=MT[0:kmax, 0:nb * Q], in_=ptb[0:kmax, 0:nb * Q]), r=[pt], w=[MT])
                        for bb in range(nb):
                            b = b0 + bb
                            kn = min(128, nk - b * 128)
                            k.op("pe", lambda e, ups=ups, MT=MT, bb=bb, b=b, kn=kn, j=j, nkb=nkb: e.matmul(ups[0:Q, 0:64], lhsT=MT[0:kn, bb * Q:(bb + 1) * Q], rhs=V_sb[0:kn, b, j * 64:(j + 1) * 64],
                                                                                                 start=(b == 0), stop=(b == nkb - 1)), r=[MT, V_sb], w=[ups])
                    k.op("act", lambda e, ups=ups, hd=hd: e.copy(out=U[0:Q, hd * 64:(hd + 1) * 64], in_=ups[0:Q, 0:64]), r=[ups], w=[U])
            m3 = ast[0:Q, 0:12].rearrange("p (g j) -> p g j", g=3)
            d3 = ast[0:Q, 12:24].rearrange("p (g j) -> p g j", g=3)
            w3 = ast[0:Q, 24:36].rearrange("p (g j) -> p g j", g=3)
            f3 = ast[0:Q, 44:56].rearrange("p (g j) -> p g j", g=3)
            Mx, Zs = ast[0:Q, 36:40], ast[0:Q, 40:44]
            k.op("dve", lambda e: e.tensor_tensor(out=Mx, in0=m3[:, 0, :], in1=m3[:, 1, :], op=ALU.max), r=[ast], w=[ast])
            k.op("dve", lambda e: e.tensor_tensor(out=Mx, in0=Mx, in1=m3[:, 2, :], op=ALU.max), r=[ast], w=[ast])
            k.op("dve", lambda e: e.tensor_tensor(out=w3, in0=m3, in1=Mx.unsqueeze(1).to_broadcast([Q, 3, 4]), op=ALU.subtract), r=[ast], w=[ast])
            k.op("act", lambda e: e.activation(out=w3, in_=w3, func=AF.Exp, scale=0.125), r=[ast], w=[ast])
            k.op("dve", lambda e: e.tensor_tensor(out=f3, in0=w3, in1=d3, op=ALU.mult), r=[ast], w=[ast])
            k.op("dve", lambda e: e.tensor_tensor(out=Zs, in0=f3[:, 0, :], in1=f3[:, 1, :], op=ALU.add), r=[ast], w=[ast])
            k.op("dve", lambda e: e.tensor_tensor(out=Zs, in0=Zs, in1=f3[:, 2, :], op=ALU.add), r=[ast], w=[ast])
            k.op("dve", lambda e: e.reciprocal(out=Zs, in_=Zs), r=[ast], w=[ast])
            k.op("dve", lambda e: e.tensor_tensor(out=f3, in0=w3, in1=Zs.unsqueeze(1).to_broadcast([Q, 3, 4]), op=ALU.mult), r=[ast], w=[ast])
            k.op("dve", lambda e: e.tensor_tensor(out=ygn[0:Q, 0:768].rearrange("p (h d) -> p h d", d=64), in0=U[0:Q, 0:768].rearrange("p (h d) -> p h d", d=64),
                                                  in1=ast[0:Q, 44:56].unsqueeze(2).to_broadcast([Q, 12, 64]), op=ALU.mult), r=[U, ast], w=[ygn])
            ps = self.PS.next()
            pb = ps[:, :].bitcast(BF16)
            for c in range(6):
                k.op("pe", lambda e, pb=pb, c=c: e.transpose(pb[:, c * Q:(c + 1) * Q], ygn[0:Q, c * 128:(c + 1) * 128], ident_b[0:Q, 0:Q]), r=[ygn, ident_b], w=[ps])
            k.op("act", lambda e, pb=pb: e.copy(out=act[:, 0:6, o:o + Q], in_=pb[:, 0:6 * Q].rearrange("p (c q) -> p c q", c=6)), r=[ps], w=[act])

        for u_ in units:
            unit_body(*u_)
        if phase != "p1":
            self.proj_fm(("att_out", i), act, 6, ntok, self.add_to_x(ntok))

    def mixer_ssd(self, i, ntok, tile, phase="all"):
        k, cfg = self.k, self.cfg
        kind, ti = tile
        do_proj, do_y = True, phase != "p1"
        n_ssm = sorted(self.HT).index(i)
        NSC = cfg.NSC
        if kind == "p":
            nseq, L = 1, ntok
            units = [(u * 128, 128, 0) for u in range(ntok // 128)]
        else:
            nseq, L = NSC, cfg.DEC
            units = [(s * L, L, s) for s in range(nseq)]
        A, h, zt = self.A, self.h, self.zt
        cm, cmr = self.cmat, self.cmat_r
        ident_b = self.ident_b
        tri_f = lambda Q: cm[0:Q, 128:128 + Q]
        negm = lambda Q: cm[0:Q, 384:384 + Q]
        tri_r = lambda Q: cmr[0:Q, 128:128 + Q]
        upp_r = lambda Q: cmr[0:Q, 256:256 + Q]
        blocks = self.wp.blocks[("ssm_in", i)]
        if do_proj:
            self.rmsnorm(("nmix", i), ntok, self.h)
            for b in (range(4) if do_y else ()):
                buf = self.wload(blocks[b])
                for ui, (o, Q, s) in enumerate(units):
                    ps = self.PS.next()
                    for kc in range(8):
                        k.op("pe", lambda e, ps=ps, kc=kc, o=o, Q=Q, buf=buf: e.matmul(ps[0:Q, 0:512], lhsT=h[:, kc, o:o + Q], rhs=buf[:, kc * 512:(kc + 1) * 512],
                                                                                       start=(kc == 0), stop=(kc == 7)), r=[h, buf], w=[ps])
                    k.op("act", lambda e, ps=ps, ui=ui, Q=Q, b=b: e.activation(out=zt[0:Q, ui, b * 512:(b + 1) * 512], in_=ps[0:Q, 0:512], func=AF.Silu), r=[ps], w=[zt])
        prm_cw, off_cw, _ = self.P(("ssm_convw", i))
        prm_cb, off_cb, _ = self.P(("ssm_convb", i))
        prm_dtb, off_dtb, _ = self.P(("ssm_dtb", i))
        hist = self.hist[i]
        convst, convout = self.convst, self.convout
        W3 = 3 + L

        def cons(col, ncols, ps):
            if col < 2048 + 3072:
                cc = (col - 2048) // 128
                pre = self.pre.next()
                pv = pre[:, 0:nseq * W3].rearrange("p (s t) -> p s t", s=nseq)
                if kind == "p":
                    k.op("act", lambda e: e.copy(out=pre[:, 0:3], in_=hist[:, cc, :]), r=[hist], w=[pre])
                else:
                    k.op("act", lambda e: e.copy(out=pv[:, :, 0:3], in_=convst[:, n_ssm, cc * nseq * 3:(cc + 1) * nseq * 3].rearrange("p (s t) -> p s t", s=nseq)),
                         r=[convst], w=[pre])
                k.op("act", lambda e: e.copy(out=pv[:, :, 3:3 + L], in_=ps[:, 0:ntok].rearrange("p (s t) -> p s t", s=nseq)), r=[ps], w=[pre])
                tmp = self.ctmp.next()
                tv = tmp[:, 0:ntok].rearrange("p (s t) -> p s t", s=nseq)
                k.op("dve", lambda e: e.tensor_scalar(out=tv, in0=pv[:, :, 0:L], scalar1=prm_cw[:, off_cw + cc * 4:off_cw + cc * 4 + 1],
                                                      scalar2=prm_cb[:, off_cb + cc:off_cb + cc + 1], op0=ALU.mult, op1=ALU.add), r=[pre, prm_cw], w=[tmp])
                for t in range(1, 4):
                    k.op("dve", lambda e, t=t: e.scalar_tensor_tensor(out=tv, in0=pv[:, :, t:t + L], scalar=prm_cw[:, off_cw + cc * 4 + t:off_cw + cc * 4 + t + 1],
                                                                      in1=tv, op0=ALU.mult, op1=ALU.add), r=[pre, tmp, prm_cw], w=[tmp])
                k.op("act", lambda e: e.activation(out=A[:, cc, 0:ntok], in_=tmp[:, 0:ntok], func=AF.Silu), r=[tmp], w=[A])
                if kind == "p":
                    k.op("act", lambda e: e.copy(out=hist[:, cc, :], in_=pre[:, L:L + 3]), r=[pre], w=[hist])
                else:
                    k.op("act", lambda e: e.copy(out=convout[:, cc * nseq * 3:(cc + 1) * nseq * 3].rearrange("p (s t) -> p s t", s=nseq), in_=pv[:, :, L:L + 3]),
                         r=[pre], w=[convout])
            else:
                dtx, dta, dtl, dtT, aT = self.ctmp.next(), self.pre.next(), self.ctmp.next(), self.dtT, self.aT
                k.op("dve", lambda e: e.tensor_scalar(out=dtx[0:32, 0:ntok], in0=ps[0:32, 0:ntok], scalar1=prm_dtb[0:32, off_dtb:off_dtb + 1], scalar2=None, op0=ALU.add),
                     r=[ps, prm_dtb], w=[dtx])
                k.op("dve", lambda e: e.scalar_tensor_tensor(out=dta[0:32, 0:ntok], in0=dtx[0:32, 0:ntok], scalar=-1.0, in1=dtx[0:32, 0:ntok], op0=ALU.mult, op1=ALU.max), r=[dtx], w=[dta])
                k.op("act", lambda e: e.activation(out=dta[0:32, 0:ntok], in_=dta[0:32, 0:ntok], func=AF.Exp, scale=-1.0), r=[dta], w=[dta])
                k.op("act", lambda e: e.activation(out=dtl[0:32, 0:ntok], in_=dta[0:32, 0:ntok], func=AF.Ln, bias=self.one_t[0:32, 0:1], scale=1.0), r=[dta, self.one_t], w=[dtl])
                k.op("dve", lambda e: e.scalar_tensor_tensor(out=dtT[:, 0:ntok], in0=dtx[0:32, 0:ntok], scalar=0.0, in1=dtl[0:32, 0:ntok], op0=ALU.max, op1=ALU.add),
                     r=[dtx, dtl], w=[dtT])
                k.op("dve", lambda e: e.tensor_scalar(out=aT[:, 0:ntok], in0=dtT[:, 0:ntok], scalar1=self.acol[0:32, n_ssm:n_ssm + 1], scalar2=None, op0=ALU.mult),
                     r=[dtT, self.acol], w=[aT])

        if do_proj:
            self.proj_fm(("ssm_in", i), h, 8, ntok, cons, col_lo=2048)
            if kind == "p" and ti == cfg.NT - 1 and phase == "p2":
                k.dma("pool", self.st_sem, self.conv_p_o[n_ssm], hist[:, :, :].rearrange("p c t -> p (c t)"), r=[hist])
            if kind == "s":
                k.dma("pool", self.st_sem, self.conv_s_o[n_ssm], convout[:, :], r=[convout])
        if phase == "p1" and ti == 0:
            k.op("dve", lambda e: e.memset(self.HT[i][:, :], 0.0), w=[self.HT[i]])
            k.op("dve", lambda e: e.memset(self.atsum[:, :], 0.0), w=[self.atsum])
        HT, HTb = self.HT[i], self.HTb
        prm_d, off_d, _ = self.P(("ssm_dbc", i))
        prm_nw, off_nw, _ = self.P(("ssm_normw", i))
        xs_tok, xdt, xdd, xsD, B_tok = self.xs_tok, self.xdt, self.xdd, self.xsD, self.B_tok
        a_tok, dt_tok, acs, eacs, dst, edec = self.a_tok, self.dt_tok, self.acs, self.eacs, self.dst, self.edec
        ysb, ygn, act = self.ysb, self.ygn, self.act
        v3 = lambda ap, a: ap.rearrange("p (a b) -> p a b", a=a)
        bc3 = lambda ap, n: ap.unsqueeze(2).to_broadcast([ap.shape[0], ap.shape[1], n])
        for ui, (o, Q, s) in enumerate(units):
            if kind == "s":
                k.dma("pool", self.st_sem, HT[:, :], self.ssm_in[n_ssm, s], w=[HT])
            if kind == "s" or (ti == 0 and ui == 0) or True:
                k.op("act", lambda e: e.copy(out=HTb[:, :], in_=HT[:, :]), r=[HT], w=[HTb])
            for half in range(2):
                ps = self.PS.next()
                pb = ps[:, :].bitcast(BF16)
                for c in range(8):
                    k.op("pe", lambda e, pb=pb, c=c, half=half, o=o, Q=Q: e.transpose(pb[0:Q, c * 128:(c + 1) * 128], A[:, half * 8 + c, o:o + Q], ident_b[:, :]),
                         r=[A, ident_b], w=[ps])
                k.op("act", lambda e, pb=pb, half=half, Q=Q: e.copy(out=xs_tok[0:Q, half * 1024:(half + 1) * 1024], in_=pb[0:Q, :]), r=[ps], w=[xs_tok])
            ps = self.PS.next()
            pb = ps[:, :].bitcast(BF16)
            for g in range(4):
                k.op("pe", lambda e, pb=pb, g=g, o=o, Q=Q: e.transpose(pb[0:Q, g * 128:(g + 1) * 128], A[:, 16 + g, o:o + Q], ident_b[:, :]), r=[A, ident_b], w=[ps])
            k.op("act", lambda e, pb=pb, Q=Q: e.copy(out=B_tok[0:Q, :], in_=pb[0:Q, 0:512]), r=[ps], w=[B_tok])
            ps = self.PS.next()
            k.op("pe", lambda e, ps=ps, o=o, Q=Q: e.transpose(ps[0:Q, 0:32], self.aT[:, o:o + Q], cm[0:32, 0:32]), r=[self.aT, cm], w=[ps])
            k.op("pe", lambda e, ps=ps, o=o, Q=Q: e.transpose(ps[0:Q, 32:64], self.dtT[:, o:o + Q], cm[0:32, 0:32]), r=[self.dtT, cm], w=[ps])
            k.op("dve", lambda e, ps=ps, Q=Q: e.tensor_copy(out=a_tok[0:Q, :], in_=ps[0:Q, 0:32]), r=[ps], w=[a_tok])
            k.op("dve", lambda e, ps=ps, Q=Q: e.tensor_copy(out=dt_tok[0:Q, :], in_=ps[0:Q, 32:64]), r=[ps], w=[dt_tok])
            ps = self.PS.next()
            k.op("pe", lambda e, ps=ps, Q=Q: e.matmul(ps[0:Q, 0:32], lhsT=tri_r(Q), rhs=a_tok[0:Q, :], start=True, stop=True), r=[cmr, a_tok], w=[ps])
            k.op("pe", lambda e, ps=ps, Q=Q: e.matmul(ps[0:Q, 32:64], lhsT=upp_r(Q), rhs=a_tok[0:Q, :], start=True, stop=True), r=[cmr, a_tok], w=[ps])
            k.op("pe", lambda e, ps=ps, Q=Q: e.matmul(ps[:, 64:96], lhsT=cmr[0:Q, 512:640], rhs=a_tok[0:Q, :], start=True, stop=True), r=[cmr, a_tok], w=[ps])
            k.op("dve", lambda e, ps=ps, Q=Q: e.tensor_copy(out=acs[0:Q, :], in_=ps[0:Q, 0:32]), r=[ps], w=[acs])
            k.op("act", lambda e, ps=ps, Q=Q: e.activation(out=eacs[0:Q, :], in_=ps[0:Q, 0:32], func=AF.Exp), r=[ps], w=[eacs])
            k.op("act", lambda e, ps=ps, Q=Q: e.activation(out=dst[0:Q, :], in_=ps[0:Q, 32:64], func=AF.Exp), r=[ps], w=[dst])
            k.op("act", lambda e, ps=ps: e.activation(out=edec[:, :], in_=ps[:, 64:96], func=AF.Exp), r=[ps], w=[edec])
            if phase == "p1":
                k.op("dve", lambda e, ps=ps: e.tensor_tensor(out=self.atsum[:, :], in0=self.atsum[:, :], in1=ps[:, 64:96], op=ALU.add), r=[self.atsum, ps], w=[self.atsum])
            k.op("dve", lambda e, Q=Q: e.tensor_tensor(out=v3(xdt[0:Q, :], 32), in0=v3(xs_tok[0:Q, :], 32), in1=bc3(dt_tok[0:Q, :], 64), op=ALU.mult), r=[xs_tok, dt_tok], w=[xdt])
            k.op("dve", lambda e, Q=Q: e.tensor_tensor(out=v3(xdd[0:Q, :], 32), in0=v3(xdt[0:Q, :], 32), in1=bc3(dst[0:Q, :], 64), op=ALU.mult), r=[xdt, dst], w=[xdd])
            k.op("dve", lambda e, Q=Q: e.tensor_tensor(out=v3(xsD[0:Q, :], 32), in0=v3(xs_tok[0:Q, :], 32), in1=bc3(prm_d[0:Q, off_d:off_d + 32], 64), op=ALU.mult),
                 r=[xs_tok, prm_d], w=[xsD])
            for g in range(4):
                if do_y:
                    psc = self.PS.next()
                    k.op("pe", lambda e, psc=psc, g=g, o=o, Q=Q: e.matmul(psc[0:Q, 0:Q], lhsT=A[:, 16 + g, o:o + Q], rhs=A[:, 20 + g, o:o + Q], start=True, stop=True), r=[A], w=[psc])
                    CBT = self.CBT.next()
                    k.op("act", lambda e, psc=psc, CBT=CBT, Q=Q: e.copy(out=CBT[0:Q, 0:Q], in_=psc[0:Q, 0:Q]), r=[psc], w=[CBT])
                    yps = self.PS.next()
                    for half in range(2):
                        hs = g * 8 + half * 4
                        Xd = self.XD.next()
                        k.op("dve", lambda e, Xd=Xd, hs=hs, Q=Q: e.tensor_tensor(out=v3(Xd[0:Q, 0:4 * Q], 4), in0=tri_f(Q).unsqueeze(1).to_broadcast([Q, 4, Q]),
                                                                                in1=bc3(a_tok[0:Q, hs:hs + 4], Q), op=ALU.mult), r=[cm, a_tok], w=[Xd])
                        psb = self.PS.next()
                        k.op("pe", lambda e, psb=psb, Xd=Xd, Q=Q: e.matmul(psb[0:Q, 0:4 * Q], lhsT=cmr[0:Q, 512:512 + Q], rhs=Xd[0:Q, 0:4 * Q], start=True, stop=True), r=[cmr, Xd], w=[psb])
                        tmp = self.TMP.next()
                        for hh in range(4):
                            k.op("dve", lambda e, psb=psb, tmp=tmp, hh=hh, hs=hs, Q=Q: e.scalar_tensor_tensor(
                                out=tmp[0:Q, hh * Q:(hh + 1) * Q], in0=psb[0:Q, hh * Q:(hh + 1) * Q], scalar=acs[0:Q, hs + hh:hs + hh + 1], in1=negm(Q),
                                op0=ALU.subtract, op1=ALU.add), r=[psb, acs, cm], w=[tmp])
                        Lx = self.LX.next()
                        k.op("act", lambda e, tmp=tmp, Lx=Lx, Q=Q: e.activation(out=Lx[0:Q, 0:4 * Q], in_=tmp[0:Q, 0:4 * Q], func=AF.Exp), r=[tmp], w=[Lx])
                        MT = self.MT.next()
                        k.op("dve", lambda e, Lx=Lx, MT=MT, CBT=CBT, Q=Q: e.tensor_tensor(out=v3(MT[0:Q, 0:4 * Q], 4), in0=v3(Lx[0:Q, 0:4 * Q], 4),
                                                                                        in1=CBT[0:Q, 0:Q].unsqueeze(1).to_broadcast([Q, 4, Q]), op=ALU.mult), r=[Lx, CBT], w=[MT])
                        for hh in range(4):
                            h8 = half * 4 + hh
                            hg = hs + hh
                            k.op("pe", lambda e, yps=yps, h8=h8, hg=hg, Q=Q: e.matmul(yps[0:Q, h8 * 64:(h8 + 1) * 64], lhsT=ident_b[0:Q, 0:Q], rhs=xsD[0:Q, hg * 64:(hg + 1) * 64],
                                                                                     start=True, stop=False), r=[ident_b, xsD], w=[yps])
                            k.op("pe", lambda e, yps=yps, h8=h8, hg=hg, hh=hh, MT=MT, Q=Q: e.matmul(yps[0:Q, h8 * 64:(h8 + 1) * 64], lhsT=MT[0:Q, hh * Q:(hh + 1) * Q],
                                                                                                   rhs=xdt[0:Q, hg * 64:(hg + 1) * 64], start=False, stop=True), r=[MT, xdt], w=[yps])
                    pso = self.PS.next()
                    k.op("pe", lambda e, pso=pso, g=g, o=o, Q=Q: e.matmul(pso[0:Q, 0:512], lhsT=A[:, 20 + g, o:o + Q], rhs=HTb[:, g * 512:(g + 1) * 512], start=True, stop=True), r=[A, HTb], w=[pso])
                    k.op("dve", lambda e, pso=pso, g=g, Q=Q: e.tensor_tensor(out=v3(ysb[0:Q, g * 512:(g + 1) * 512], 8), in0=v3(pso[0:Q, 0:512], 8), in1=bc3(eacs[0:Q, g * 8:(g + 1) * 8], 64), op=ALU.mult),
                         r=[pso, eacs], w=[ysb])
                    k.op("dve", lambda e, yps=yps, g=g, Q=Q: e.tensor_tensor(out=ysb[0:Q, g * 512:(g + 1) * 512], in0=ysb[0:Q, g * 512:(g + 1) * 512], in1=yps[0:Q, 0:512], op=ALU.add),
                         r=[ysb, yps], w=[ysb])
                psu = self.PS.next()
                k.op("pe", lambda e, psu=psu, g=g, Q=Q: e.matmul(psu[:, 0:512], lhsT=B_tok[0:Q, g * 128:(g + 1) * 128], rhs=xdd[0:Q, g * 512:(g + 1) * 512], start=True, stop=True), r=[B_tok, xdd], w=[psu])
                k.op("dve", lambda e, g=g: e.tensor_tensor(out=v3(HT[:, g * 512:(g + 1) * 512], 8), in0=v3(HT[:, g * 512:(g + 1) * 512], 8), in1=bc3(edec[:, g * 8:(g + 1) * 8], 64), op=ALU.mult),
                     r=[HT, edec], w=[HT])
                k.op("dve", lambda e, psu=psu, g=g: e.tensor_tensor(out=HT[:, g * 512:(g + 1) * 512], in0=HT[:, g * 512:(g + 1) * 512], in1=psu[:, 0:512], op=ALU.add), r=[HT, psu], w=[HT])
            if do_y:
                k.op("dve", lambda e, ui=ui, Q=Q: e.tensor_tensor(out=ysb[0:Q, :], in0=ysb[0:Q, :], in1=zt[0:Q, ui, :], op=ALU.mult), r=[ysb, zt], w=[ysb])
                k.op("dve", lambda e: e.memset(self.ss[:, :], 0.0), w=[self.ss])
                for g in range(4):
                    k.op("act", lambda e, g=g, Q=Q: e.activation(out=self.junk[0:Q, :], in_=ysb[0:Q, g * 512:(g + 1) * 512], func=AF.Square, accum_out=self.ss[0:Q, g:g + 1]),
                         r=[ysb], w=[self.junk, self.ss])
                k.op("act", lambda e, Q=Q: e.activation(out=self.rs[0:Q, 0:4], in_=self.ss[0:Q, 0:4], func=AF.Sqrt, bias=self.eps_t[0:Q, 0:1], scale=1.0 / 512), r=[self.ss, self.eps_t], w=[self.rs])
                k.op("dve", lambda e, Q=Q: e.reciprocal(out=self.rs[0:Q, 0:4], in_=self.rs[0:Q, 0:4]), r=[self.rs], w=[self.rs])
                k.op("dve", lambda e, Q=Q: e.tensor_tensor(out=v3(ygn[0:Q, :], 4), in0=v3(ysb[0:Q, :], 4), in1=bc3(self.rs[0:Q, 0:4], 512), op=ALU.mult), r=[ysb, self.rs], w=[ygn])
                for half in range(2):
                    ps = self.PS.next()
                    pb = ps[:, :].bitcast(BF16)
                    for c in range(8):
                        cc = half * 8 + c
                        k.op("pe", lambda e, pb=pb, c=c, cc=cc, Q=Q: e.transpose(pb[:, c * Q:(c + 1) * Q], ygn[0:Q, cc * 128:(cc + 1) * 128], ident_b[0:Q, 0:Q]), r=[ygn, ident_b], w=[ps])
                    for c in range(8):
                        cc = half * 8 + c
                        k.op("act", lambda e, pb=pb, c=c, cc=cc, o=o, Q=Q: e.activation(out=act[:, cc, o:o + Q], in_=pb[:, c * Q:(c + 1) * Q], func=AF.Copy, scale=prm_nw[:, off_nw + cc:off_nw + cc + 1]),
                             r=[ps, prm_nw], w=[act])
            if kind == "s":
                k.dma("pool", self.st_sem, self.ssm_s_o[n_ssm, s], HT[:, :], r=[HT])
        if kind == "p" and ti == cfg.NT - 1 and phase == "p2":
            k.dma("pool", self.st_sem, self.ssm_p_o[n_ssm], HT[:, :], r=[HT])
        if do_y:
            self.proj_fm(("ssm_out", i), act, 16, ntok, self.add_to_x(ntok))

    def sdma(self, out, in_, r=(), w=()):
        d = self.sc_sems[self.sc_i % 4]
        self.sc_i += 1
        self.k.dma("pool", d, out, in_, r=r, w=w)

    def xdma(self, out, in_, r=(), w=()):
        d = self.xs_sem[self.xs_i % 2]
        self.xs_i += 1
        self.k.dma("pool", d, out, in_, r=r, w=w)

    def ntok_of(self, ti):
        return self.cfg.TP if ti < self.cfg.NT else self.TS

    def load_xs(self, ti):
        n = self.ntok_of(ti)
        src = self.xscr[ti].t.rearrange("p (c t) -> p c t", c=8)[:, :, 0:n]
        self.xdma(self.x[:, :, 0:n], src, r=[self.xscr[ti]], w=[self.x])

    def store_xs(self, ti):
        n = self.ntok_of(ti)
        dst = self.xscr[ti].t.rearrange("p (c t) -> p c t", c=8)[:, :, 0:n]
        self.xdma(dst, self.x[:, :, 0:n], r=[self.x], w=[self.xscr[ti]])

    def prepass(self):
        cfg = self.cfg
        for ti in range(cfg.NT):
            self.load_x(self.xp[ti * cfg.TP:(ti + 1) * cfg.TP, :], cfg.TP)
            self.store_xs(ti)
        self.load_x(self.xs, self.TS)
        self.store_xs(cfg.NT)

    def finish_tile(self, i, ti):
        cfg = self.cfg
        n = self.ntok_of(ti)
        self.ffn(i, n)
        if i == cfg.DEPTH - 1:
            if ti < cfg.NT:
                self.store_y(self.y_p[ti * cfg.TP:(ti + 1) * cfg.TP, :], n)
            else:
                self.store_y(self.y_s, n)
        else:
            self.store_xs(ti)

    def sample_layer(self, i):
        cfg = self.cfg
        self.load_xs(cfg.NT)
        self.mixer(i, self.TS, ("s", 0))
        self.finish_tile(i, cfg.NT)

    def allgather(self, name, src_T, rows, cols, dtype, src_ap=None):
        k, cfg = self.k, self.cfg
        g = k.dram("g_" + name, (cfg.NCORES * rows, cols), dtype)
        d = k.dsem("cc_" + name, inc=1)
        sap = src_ap if src_ap is not None else src_T.t
        if getattr(cfg, "fake_cc", False):
            for r_ in range(cfg.NCORES):
                self.sdma(g[r_ * rows:(r_ + 1) * rows, :], sap, r=[src_T], w=[g])
            return g
        k.coll(d, "AllGather", sap.opt(), g.t.opt(), cfg.NCORES, r=[src_T], w=[g])
        return g

    def corecol(self, j):
        prm, off, _ = self.P(("core",))
        return prm[:, off + j:off + j + 1]

    def seg_weights(self, src_T, src_ap, n, scale):
        k = self.k
        sm = self.small
        prm = self.params
        self.sdma(sm[:, 0:4, 0:n], src_ap.rearrange("(q p) c -> p q c", p=128), r=[src_T], w=[sm])
        for m_ in range(4):
            k.op("dve", lambda e, m_=m_: e.tensor_scalar(out=sm[:, m_, 0:n], in0=sm[:, m_, 0:n], scalar1=self.corecol(m_), scalar2=None, op0=ALU.mult), r=[sm, prm], w=[sm])
        k.op("dve", lambda e: e.memset(sm[:, 7, 0:n], 0.0), w=[sm])
        k.op("dve", lambda e: e.tensor_copy(out=sm[:, 6, 0:n], in_=sm[:, 3, 0:n]), r=[sm], w=[sm])
        k.op("dve", lambda e: e.tensor_tensor(out=sm[:, 5, 0:n], in0=sm[:, 2, 0:n], in1=sm[:, 6, 0:n], op=ALU.add), r=[sm], w=[sm])
        k.op("dve", lambda e: e.tensor_tensor(out=sm[:, 4, 0:n], in0=sm[:, 1, 0:n], in1=sm[:, 5, 0:n], op=ALU.add), r=[sm], w=[sm])
        k.op("act", lambda e: e.activation(out=sm[:, 4:8, 0:n], in_=sm[:, 4:8, 0:n], func=AF.Exp, scale=scale), r=[sm], w=[sm])
        for q in range(4):
            k.op("dve", lambda e, q=q: e.tensor_scalar(out=sm[:, 4 + q, 0:n], in0=sm[:, 4 + q, 0:n], scalar1=self.corecol(q), scalar2=None, op0=ALU.mult), r=[sm, prm], w=[sm])

    def layer_ssd(self, i):
        k, cfg = self.k, self.cfg
        NT, TP = cfg.NT, cfg.TP
        n_ssm = sorted(self.HT).index(i)
        HT, hist = self.HT[i], self.hist[i]
        src = self.xscr[NT - 1].t.rearrange("p (c t) -> p c t", c=8)[:, :, TP - 4:TP]
        self.xdma(self.x[:, :, 0:4], src, r=[self.xscr[NT - 1]], w=[self.x])
        self.rmsnorm(("nmix", i), 4, self.h)
        tail = self.convout

        def cons(col, ncols, ps):
            cc = (col - 2048) // 128
            k.op("act", lambda e: e.copy(out=tail[:, cc * 3:cc * 3 + 3], in_=ps[:, 1:4]), r=[ps], w=[tail])
        self.proj_fm(("ssm_in", i), self.h, 8, 4, cons, col_lo=2048, col_hi=2048 + 3072)
        bt = k.dram(f"bnc_tail{i}", (128, 72), F32)
        self.sdma(bt[:, :], tail[:, 0:72], r=[tail], w=[bt])
        gt = self.allgather(f"tail{i}", bt, 128, 72, F32)
        self.sdma(hist[:, :, :].rearrange("p c t -> p (c t)"), lambda dyn: gt.t[bass.ds(dyn["prev"] * 128, 128), :], r=[gt], w=[hist])
        k.op("dve", lambda e: e.tensor_scalar(out=hist[:, :, :], in0=hist[:, :, :], scalar1=self.corecol(4), scalar2=None, op0=ALU.mult), r=[hist, self.params], w=[hist])
        hsave = self.histsave
        k.op("dve", lambda e: e.tensor_copy(out=hsave[:, :, :], in_=hist[:, :, :]), r=[hist], w=[hsave])
        if getattr(cfg, "stop", None) == "phase0":
            return
        for ti in range(NT):
            self.load_xs(ti)
            self.mixer_ssd(i, TP, ("p", ti), phase="p1")
        if getattr(cfg, "stop", None) == "p1":
            return
        bh = k.dram(f"bnc_h{i}", (128, 2080), F32)
        self.sdma(bh[:, 0:2048], HT[:, :], r=[HT], w=[bh])
        self.sdma(bh[:, 2048:2080], self.atsum[:, :], r=[self.atsum], w=[bh])
        gh = self.allgather(f"h{i}", bh, 128, 2080, F32)
        lh = k.dram(f"loc_h{i}", (512, 2080), F32)
        self.sdma(lh[:, :], lambda dyn: gh.t[bass.ds(dyn["base"] * 128, 512), :], r=[gh], w=[lh])
        self.seg_weights(lh, lh[:, 2048:2080], 32, 1.0)
        k.op("dve", lambda e: e.memset(HT[:, :], 0.0), w=[HT])
        ysb, sm = self.ysb, self.small
        v3 = lambda ap, a: ap.rearrange("p (a b) -> p a b", a=a)
        for q in range(4):
            self.sdma(ysb[:, :], lh[q * 128:(q + 1) * 128, 0:2048], r=[lh], w=[ysb])
            k.op("dve", lambda e, q=q: e.tensor_tensor(out=v3(ysb[:, :], 32), in0=v3(ysb[:, :], 32), in1=sm[:, 4 + q, 0:32].unsqueeze(2).to_broadcast([128, 32, 64]), op=ALU.mult), r=[ysb, sm], w=[ysb])
            k.op("dve", lambda e: e.tensor_tensor(out=HT[:, :], in0=HT[:, :], in1=ysb[:, :], op=ALU.add), r=[HT, ysb], w=[HT])
        k.op("dve", lambda e: e.tensor_copy(out=hist[:, :, :], in_=hsave[:, :, :]), r=[hsave], w=[hist])
        if getattr(cfg, "stop", None) == "exch":
            return
        for ti in range(NT):
            self.load_xs(ti)
            self.mixer_ssd(i, TP, ("p", ti), phase="p2")
            if getattr(cfg, "stop", None) in ("p2", "p2a", "p2b", "p2c"):
                continue
            self.finish_tile(i, ti)

    def layer_gla(self, i):
        k, cfg = self.k, self.cfg
        NT, TP = cfg.NT, cfg.TP
        S = self.S
        for ti in range(NT):
            self.load_xs(ti)
            self.mixer_gla(i, TP, ("p", ti), phase="p1")
        bs = k.dram(f"bnc_s{i}", (128, 1028), F32)
        self.sdma(bs[:, 0:1024], S[:, :], r=[S], w=[bs])
        self.sdma(bs[:, 1024:1028], self.blsum[:, :], r=[self.blsum], w=[bs])
        gs = self.allgather(f"s{i}", bs, 128, 1028, F32)
        ls = k.dram(f"loc_s{i}", (512, 1028), F32)
        self.sdma(ls[:, :], lambda dyn: gs.t[bass.ds(dyn["base"] * 128, 512), :], r=[gs], w=[ls])
        self.seg_weights(ls, ls[:, 1024:1028], 4, 1.0 / 16)
        k.op("dve", lambda e: e.memset(S[:, :], 0.0), w=[S])
        ysb, sm = self.ysb, self.small
        v3 = lambda ap, a: ap.rearrange("p (a b) -> p a b", a=a)
        for q in range(4):
            self.sdma(ysb[:, 0:1024], ls[q * 128:(q + 1) * 128, 0:1024], r=[ls], w=[ysb])
            k.op("dve", lambda e, q=q: e.tensor_tensor(out=v3(ysb[:, 0:1024], 4), in0=v3(ysb[:, 0:1024], 4), in1=sm[:, 4 + q, 0:4].unsqueeze(2).to_broadcast([128, 4, 256]), op=ALU.mult), r=[ysb, sm], w=[ysb])
            k.op("dve", lambda e: e.tensor_tensor(out=S[:, :], in0=S[:, :], in1=ysb[:, 0:1024], op=ALU.add), r=[S, ysb], w=[S])
        k.op("act", lambda e: e.copy(out=self.Sb[:, :], in_=S[:, :]), r=[S], w=[self.Sb])
        for ti in range(NT):
            self.load_xs(ti)
            self.mixer_gla(i, TP, ("p", ti), phase="p2")
            self.finish_tile(i, ti)

    def layer_att(self, i):
        k, cfg = self.k, self.cfg
        NT, TP = cfg.NT, cfg.TP
        for ti in range(NT):
            self.load_xs(ti)
            self.mixer_att(i, TP, ("p", ti), phase="p1")
        self.Gk = self.allgather("kt", self.kt_hist, 6 * 128, cfg.SEQ, BF16, src_ap=self.kt_hist.t.rearrange("c p t -> (c p) t"))
        self.Gv = self.allgather("v", self.v_hist, cfg.SEQ, 768, BF16)
        self.loc_k = k.dram("loc_k", (768, cfg.SEQ), BF16)
        self.loc_v = k.dram("loc_v", (cfg.SEQ, 768), BF16)
        Gk, Gv, SEQ_ = self.Gk, self.Gv, cfg.SEQ
        self.sdma(self.loc_k[:, :], lambda dyn: Gk.t[bass.ds(dyn["prev"] * 768, 768), :], r=[Gk], w=[self.loc_k])
        self.sdma(self.loc_v[:, :], lambda dyn: Gv.t[bass.ds(dyn["prev"] * SEQ_, SEQ_), :], r=[Gv], w=[self.loc_v])
        for ti in range(NT):
            self.load_xs(ti)
            self.mixer_att(i, TP, ("p", ti), phase="p2")
            self.finish_tile(i, ti)

    def epilogue(self):
        self.k.wait_all("pool", self.out_sems)


def make_consts(cfg=None, sgi=0):
    i = np.arange(128)
    tri = (i[:, None] <= i[None, :]).astype(np.float32)
    upper = (i[:, None] > i[None, :]).astype(np.float32)
    negmask = np.where(i[None, :] >= i[:, None], 0.0, -1e30).astype(np.float32)
    cm = np.concatenate([np.eye(128, dtype=np.float32), tri, upper, negmask, np.ones((128, 128), np.float32)], axis=1)
    out = {"cmat": np.ascontiguousarray(cm)}
    if cfg is not None and cfg.DEPTH > 2:
        ms = []
        for (W, d) in AGROUPS:
            c = np.arange(W + 128)[None, :]
            diff = W + i[:, None] - c
            ok = (diff >= 0) & (diff <= W) & (diff % d == 0)
            ms.append(np.where(ok, 0.0, -30000.0).astype(np.float32))
        out["amask"] = np.ascontiguousarray(np.concatenate(ms, axis=1))
        NU = cfg.SEQ // 128
        pos = np.concatenate([sgi * cfg.SEQ + np.arange(cfg.SEQ), cfg.PAST + np.arange(128)]).astype(np.float32)
        inv_freq = (np.float32(ROPE_THETA) ** (-np.arange(0, AROT, 2, dtype=np.float32) / np.float32(AROT))).astype(np.float32)
        ang = (pos[:, None] * inv_freq[None, :]).astype(np.float32)
        tab = np.concatenate([np.cos(ang), np.sin(ang)], axis=1).astype(np.float32)
        out["rope"] = np.ascontiguousarray(tab.reshape(NU + 1, 128, 16).transpose(1, 0, 2)).reshape(128, (NU + 1) * 16)
    return out


def kernel(**inputs):
    return run(Cfg(), inputs)


def run(cfg, inputs, trace=False):
    prog = Prog(cfg)
    nc = prog.build()
    in_maps = make_in_maps(cfg, inputs)
    return launch(cfg, nc, in_maps, np.asarray(inputs["x_prompt"]).shape[0], trace)


def make_in_maps(cfg, inputs):
    wp = weight_plan(cfg.DEPTH, inputs)
    pp = param_plan(cfg.DEPTH, inputs)
    wcat, pcat0 = wp.cat(), pp.cat()
    f32 = lambda n: np.asarray(inputs[n], np.float32)
    xp, xs = f32("x_prompt"), f32("x_sample")
    B = xp.shape[0]
    NSC, NSEG, SEQ, NC = cfg.NSC, cfg.NSEG, cfg.SEQ, cfg.NCORES
    st_ssm, st_conv, st_gla = f32("state_ssm"), f32("state_ssm_conv"), f32("state_gla")
    caches = [f32(f"cache_kv_g{g}") for g in range(3)]
    coff = pp.off[("core",)][0]
    consts = [make_consts(cfg, sgi) for sgi in range(NSEG)]
    in_maps = []
    for c in range(NC):
        b_, sgi = c // NSEG, c % NSEG
        sl = slice(c * NSC, (c + 1) * NSC)
        pc = pcat0.copy()
        for m_ in range(4):
            pc[:, coff + m_] = 1.0 if m_ < sgi else 0.0
        pc[:, coff + 4] = 1.0 if sgi > 0 else 0.0
        pc[:, coff + 5] = 0.0 if sgi > 0 else -30000.0
        m = {"xp": np.ascontiguousarray(xp[b_, sgi * SEQ:(sgi + 1) * SEQ]), "xs": np.ascontiguousarray(xs[sl].reshape(-1, D)), "wcat": wcat, "pcat": pc,
             "rankinfo": np.array([[c - 1 if sgi > 0 else c, b_ * NSEG]], np.int32),
             "ssm_in_t": lay_ssm_in(st_ssm[:, sl]), "conv_in_fm": lay_conv_in(st_conv[:, sl]),
             "gla_in": np.ascontiguousarray(st_gla[0, sl])}
        nssm = len([i_ for i_ in range(cfg.DEPTH) if i_ % 3 == 0])
        m["ssm_in_t"], m["conv_in_fm"] = m["ssm_in_t"][:nssm], m["conv_in_fm"][:nssm]
        if cfg.DEPTH < 2:
            del m["gla_in"]
        for g in range(3 if cfg.DEPTH > 2 else 0):
            m[f"kvc{g}"] = np.ascontiguousarray(caches[g][0, sl].reshape(NSC, -1, 512))
        m.update(consts[sgi])
        in_maps.append(m)
    return in_maps


def launch(cfg, nc, in_maps, B, trace=False):
    NSC, NSEG, SEQ, NC = cfg.NSC, cfg.NSEG, cfg.SEQ, cfg.NCORES
    res = run_bass_kernel_spmd(nc, in_maps, core_ids=list(range(NC)), **({"trace": True} if trace else {}))
    if trace:
        print("exec_ns", res.exec_time_ns)
    R = res.results
    last = [b_ * NSEG + NSEG - 1 for b_ in range(B)]
    y_p = np.stack([np.concatenate([R[b_ * NSEG + sg]["y_p"] for sg in range(NSEG)]) for b_ in range(B)])
    y_s = np.concatenate([R[c]["y_s"].reshape(NSC, cfg.DEC, D) for c in range(NC)])
    ssm_p = np.stack([unlay_ssm(R[c]["ssm_p_t"]) for c in last], axis=1)
    ssm_s = np.concatenate([unlay_ssm(R[c]["ssm_s_t"]) for c in range(NC)], axis=1)
    conv_p = np.concatenate([unlay_conv(R[c]["conv_p_fm"], 1) for c in last], axis=1)
    conv_s = np.concatenate([unlay_conv(R[c]["conv_s_fm"], NSC) for c in range(NC)], axis=1)
    outs = [y_p, y_s, ssm_p, ssm_s, conv_p, conv_s]
    if cfg.DEPTH > 1:
        outs += [np.stack([R[c]["gla_p"] for c in last])[None], np.concatenate([R[c]["gla_s"] for c in range(NC)])[None]]
    for g in range(3 if cfg.DEPTH > 2 else 0):
        kp = np.stack([R[c][f"kvp{g}"].reshape(-1, 2, AHPG, ADH) for c in last])[None]
        ks = np.concatenate([R[c][f"kvs{g}"].reshape(NSC, -1, 2, AHPG, ADH) for c in range(NC)])[None]
        outs += [kp, ks]
    return tuple(np.ascontiguousarray(o, dtype=np.float32) for o in outs)


def lay_ssm_in(st):
    n, S = st.shape[:2]
    return np.ascontiguousarray(st.reshape(n, S, 2048, 128).transpose(0, 1, 3, 2))


def unlay_ssm(o):
    return np.ascontiguousarray(np.swapaxes(o, -1, -2)).reshape(o.shape[:-2] + (32, 64, 128))


def lay_conv_in(cv):
    n, S = cv.shape[:2]
    return np.ascontiguousarray(cv.reshape(n, S, 3, 24, 128).transpose(0, 4, 3, 1, 2)).reshape(n, 128, 24 * S * 3)


def unlay_conv(o, S):
    n = o.shape[0]
    return np.ascontiguousarray(o.reshape(n, 128, 24, S, 3).transpose(0, 3, 4, 2, 1)).reshape(n, S, 3, 3072)
```

```python
import contextlib
import numpy as np
import concourse.bass as bass
import concourse.mybir as mybir
from concourse.bass_utils import run_bass_kernel_spmd

F32 = mybir.dt.float32
F32R = mybir.dt.float32r
BF16 = mybir.dt.bfloat16
AF = mybir.ActivationFunctionType
ALU = mybir.AluOpType
AX = mybir.AxisListType

ENGS = ("pe", "act", "dve", "pool", "sp")
SAME_ENGINE_SYNC = True

D = 1024
DFF = 2816
EPS = 1e-6
DI, HD, NH, NG, NS, CONVK, CD, SIN = 2048, 64, 32, 4, 128, 4, 3072, 5152
GH, GDK, GDV, GHK, GHV, GRANK, GIN = 4, 512, 1024, 128, 256, 16, 3088
AGROUPS = ((128, 1), (512, 4), (2048, 16))
AHPG, ADH, ANH, AROT = 4, 64, 12, 16
ROPE_THETA = 500000.0


class Cfg:
    def __init__(self, seq=2048, tp=256, ns_core=4, dec=4, past=8192, depth=4, ncores=8, nseg=4):
        self.SEQ, self.TP, self.NSC, self.DEC, self.PAST, self.DEPTH, self.NCORES = seq, tp, ns_core, dec, past, depth, ncores
        self.NSEG = nseg
        self.NT = seq // tp


class T:
    def __init__(self, t, name):
        self.t, self.name, self.w, self.r = t, name, {}, {}

    def __getitem__(self, k):
        return self.t[k]


def TV(parent, ap, name="view"):
    v = T(ap, name)
    v.w, v.r = parent.w, parent.r
    return v


class DSem:
    def __init__(self, sem, name, inc=16):
        self.sem, self.name, self.total, self.inc = sem, name, 0, inc


class KB:
    def __init__(self, nc, stack):
        self.nc, self.stack = nc, stack
        self.q = {e: [] for e in ENGS}
        self.cnt = {e: 0 for e in ENGS}
        self.sem = {e: stack.enter_context(nc.semaphore("s_" + e)) for e in ENGS if e != "sp"}
        self.waited = {}
        self.n = 0
        self.dyn = {}
        self.dyn_setup = None

    def sbuf(self, shape, dtype, name=None):
        self.n += 1
        name = name or f"sb{self.n}"
        return T(self.stack.enter_context(self.nc.sbuf_tensor(name, list(shape), dtype)), name)

    def psum(self, shape, dtype=F32, name=None):
        self.n += 1
        name = name or f"ps{self.n}"
        return T(self.stack.enter_context(self.nc.psum_tensor(name, list(shape), dtype)), name)

    def dram(self, name, shape, dtype, kind="Internal"):
        return T(self.nc.dram_tensor(name, list(shape), dtype, kind=kind).ap(), name)

    def dsem(self, name=None, inc=16):
        self.n += 1
        name = name or f"d{self.n}"
        return DSem(self.stack.enter_context(self.nc.semaphore("ds_" + name)), name, inc)

    def _deps(self, eng, reads, writes):
        deps = {}
        for t in reads:
            for k, c in t.w.items():
                if deps.get(k, 0) < c:
                    deps[k] = c
        for t in writes:
            for dd in (t.w, t.r):
                for k, c in dd.items():
                    if deps.get(k, 0) < c:
                        deps[k] = c
        out = []
        for k, c in deps.items():
            if k == eng and (eng == "pe" or not SAME_ENGINE_SYNC):
                continue
            if self.waited.get((eng, k), 0) >= c:
                continue
            self.waited[(eng, k)] = c
            out.append(((k.sem if isinstance(k, DSem) else self.sem[k]), c))
        return out

    def op(self, eng, fn, r=(), w=()):
        sems = self._deps(eng, r, w)
        self.cnt[eng] += 1
        c = self.cnt[eng]
        mysem = self.sem[eng]

        def emit(e, sems=sems, fn=fn, mysem=mysem):
            for s, v in sems:
                e.wait_ge(s, v)
            fn(e).then_inc(mysem, 1)

        self.q[eng].append(emit)
        for t in r:
            t.r[eng] = c
        for t in w:
            t.w[eng] = c

    def dma(self, qeng, dsem, out, in_, r=(), w=(), **kw):
        sems = self._deps(qeng, r, w)
        if dsem.total > 0 and self.waited.get((qeng, dsem), 0) < dsem.total:
            self.waited[(qeng, dsem)] = dsem.total
            sems = sems + [(dsem.sem, dsem.total)]
        dsem.total += 16
        c = dsem.total

        def emit(e, sems=sems, out=out, in_=in_, kw=kw, ds=dsem.sem):
            for s, v in sems:
                e.wait_ge(s, v)
            o_ = out(self.dyn) if callable(out) else out
            i_ = in_(self.dyn) if callable(in_) else in_
            e.dma_start(out=o_, in_=i_, **kw).then_inc(ds, 16)

        self.q[qeng].append(emit)
        for t in r:
            t.r[dsem] = c
        for t in w:
            t.w[dsem] = c

    def coll(self, dsem, kind, in_ap, out_ap, ncores, r=(), w=()):
        sems = self._deps("pool", r, w)
        dsem.total += 1
        c = dsem.total

        def emit(e, sems=sems, ds=dsem.sem):
            for s, v in sems:
                e.wait_ge(s, v)
            e.collective_compute(kind, ALU.bypass, replica_groups=[list(range(ncores))], ins=[in_ap], outs=[out_ap]).then_inc(ds)

        self.q["pool"].append(emit)
        for t in r:
            t.r[dsem] = c
        for t in w:
            t.w[dsem] = c

    def wait_all(self, eng, dsems):
        sems = [(d.sem, d.total) for d in dsems if d.total > 0]

        def emit(e, sems=sems):
            for s, v in sems:
                e.wait_ge(s, v)

        self.q[eng].append(emit)

    def finish(self):
        with self.nc.Block() as block:
            @block.tensor
            def _(e):
                for f in self.q["pe"]:
                    f(e)

            @block.scalar
            def _(e):
                for f in self.q["act"]:
                    f(e)

            @block.vector
            def _(e):
                for f in self.q["dve"]:
                    f(e)

            @block.gpsimd
            def _(e):
                with contextlib.ExitStack() as rs:
                    if self.dyn_setup is not None:
                        self.dyn_setup(e, rs)
                    for f in self.q["pool"]:
                        f(e)

            @block.sync
            def _(e):
                for f in self.q["sp"]:
                    f(e)


class Ring:
    def __init__(self, bufs):
        self.bufs, self.i = bufs, 0

    def next(self):
        b = self.bufs[self.i % len(self.bufs)]
        self.i += 1
        return b


def _bw(K):
    kc = K // 128
    return 512 if kc <= 8 else (256 if kc <= 16 else 128)


class WPack:
    def __init__(self):
        self.blocks = {}
        self.tot = 0
        self.parts = []

    def add(self, key, W):
        K, N = W.shape
        kc, bw = K // 128, _bw(K)
        nblk = -(-N // bw)
        Wp = np.zeros((K, nblk * bw), np.float32)
        Wp[:, :N] = W
        til = Wp.reshape(kc, 128, nblk, bw).transpose(2, 1, 0, 3).reshape(nblk, 128, kc * bw)
        lst = []
        for b in range(nblk):
            lst.append((self.tot, kc, bw, min(bw, N - b * bw)))
            self.parts.append(til[b])
            self.tot += kc * bw
        self.blocks[key] = lst

    def add_meta(self, key, K, N):
        kc, bw = K // 128, _bw(K)
        nblk = -(-N // bw)
        lst = []
        for b in range(nblk):
            lst.append((self.tot, kc, bw, min(bw, N - b * bw)))
            self.tot += kc * bw
        self.blocks[key] = lst

    def cat(self):
        pad = (-self.tot) % 2048
        arr = np.concatenate(self.parts + [np.zeros((128, pad), np.float32)], axis=1)
        return np.ascontiguousarray(arr)


W_SHAPES = {"ssm_in": (D, SIN), "ssm_out": (DI, D), "gla_in": (D, GIN), "gla_out": (GDV, D),
            "att_qkv": (D, 3 * ANH * ADH), "att_out": (ANH * ADH, D),
            "ffn_gate": (D, DFF), "ffn_up": (D, DFF), "ffn_down": (DFF, D)}


def weight_plan(depth, inputs=None):
    wp = WPack()

    def add(key, name, j):
        if inputs is None:
            wp.add_meta(key, *W_SHAPES[key[0]])
        else:
            wp.add(key, np.asarray(inputs[name][j], np.float32))

    for i in range(depth):
        m, j = i % 3, i // 3
        if m == 0:
            add(("ssm_in", i), "ssm_w_in", j)
            add(("ssm_out", i), "ssm_w_out", j)
        elif m == 1:
            add(("gla_in", i), "gla_w_in", j)
            add(("gla_out", i), "gla_w_out", j)
        else:
            add(("att_qkv", i), "att_w_qkv", j)
            add(("att_out", i), "att_w_out", j)
        add(("ffn_gate", i), "ffn_gate", i)
        add(("ffn_up", i), "ffn_up", i)
        add(("ffn_down", i), "ffn_down", i)
    return wp


def fm(v, nchunk):
    return np.ascontiguousarray(np.asarray(v, np.float32).reshape(nchunk, 128).T)


class PPack:
    def __init__(self):
        self.off, self.tot, self.parts = {}, 0, []

    def add(self, key, arr=None, width=None):
        if arr is not None:
            arr = np.asarray(arr, np.float32)
            assert arr.shape[0] == 128
            arr = arr.reshape(128, -1)
            width = arr.shape[1]
            self.parts.append(arr)
        self.off[key] = (self.tot, width)
        self.tot += width

    def cat(self):
        return np.ascontiguousarray(np.concatenate(self.parts, axis=1))


def param_plan(depth, inputs=None):
    pp = PPack()
    g = (lambda name: np.asarray(inputs[name], np.float32)) if inputs is not None else None

    def add(key, fn, width):
        pp.add(key, fn() if inputs is not None else None, width)

    for i in range(depth):
        add(("nmix", i), lambda: fm(g("norm_mix")[i], 8), 8)
        add(("nffn", i), lambda: fm(g("norm_ffn")[i], 8), 8)
    add(("nfin",), lambda: fm(g("norm_final"), 8), 8)
    add(("core",), lambda: np.zeros((128, 8), np.float32), 8)
    for i in range(depth):
        m, j = i % 3, i // 3
        if m == 0:
            add(("ssm_convw", i), lambda: np.ascontiguousarray(g("ssm_conv_w")[j].reshape(4, 24, 128).transpose(2, 1, 0)).reshape(128, 96), 96)
            add(("ssm_convb", i), lambda: fm(g("ssm_conv_b")[j], 24), 24)
            add(("ssm_dtb", i), lambda: np.pad(g("ssm_dt_bias")[j].reshape(32, 1), ((0, 96), (0, 0))), 1)
            add(("ssm_alog", i), lambda: np.pad(g("ssm_a_log")[j].reshape(32, 1), ((0, 96), (0, 0))), 1)
            add(("ssm_dbc", i), lambda: np.broadcast_to(g("ssm_d")[j].reshape(1, 32), (128, 32)).copy(), 32)
            add(("ssm_normw", i), lambda: fm(g("ssm_norm")[j], 16), 16)
        elif m == 1:
            add(("gla_wg", i), lambda: np.pad(g("gla_w_gate")[j], ((0, 112), (0, 0))), 512)
            add(("gla_gb", i), lambda: fm(g("gla_gate_bias")[j], 4), 4)
            add(("gla_nw", i), lambda: np.broadcast_to(g("gla_norm")[j].reshape(1, 256), (128, 256)).copy(), 256)
    return pp


class Prog:
    def __init__(self, cfg):
        self.cfg = cfg
        self.wp = weight_plan(cfg.DEPTH)
        self.pp = param_plan(cfg.DEPTH)

    def build(self):
        cfg = self.cfg
        nc = bass.Bass("TRN2", target_bir_lowering=False)
        self.nc = nc
        with contextlib.ExitStack() as st:
            k = KB(nc, st)
            self.k = k
            self.declare_io()
            self.alloc()
            self.prologue()
            self.prepass()
            for i in range(cfg.DEPTH):
                m = i % 3
                if m == 0:
                    self.layer_ssd(i)
                elif m == 1:
                    self.layer_gla(i)
                else:
                    self.layer_att(i)
            self.epilogue()
            k.finish()
        return nc

    def declare_io(self):
        nc, cfg = self.nc, self.cfg
        I = lambda n, s: nc.dram_tensor(n, list(s), F32, kind="ExternalInput").ap()
        O = lambda n, s: nc.dram_tensor(n, list(s), F32, kind="ExternalOutput").ap()
        TS = cfg.NSC * cfg.DEC
        self.TS = TS
        self.xp = I("xp", (cfg.SEQ, D))
        self.xs = I("xs", (TS, D))
        self.wcat = I("wcat", (128, self.wp.tot + ((-self.wp.tot) % 2048)))
        self.pcat = I("pcat", (128, self.pp.tot))
        self.cmat_in = I("cmat", (128, 640))
        self.rank_in = nc.dram_tensor("rankinfo", [1, 2], mybir.dt.int32, kind="ExternalInput").ap()
        self.y_p = O("y_p", (cfg.SEQ, D))
        self.y_s = O("y_s", (TS, D))

    def alloc(self):
        k, cfg = self.k, self.cfg
        TP = cfg.TP
        self.TMAX = TP
        self.out_sems = []
        self.wtot = self.wp.tot + ((-self.wp.tot) % 2048)
        self.wb = k.dram("wcat_b", (128, self.wtot), BF16)
        self.params = k.sbuf((128, self.pp.tot), F32, "params")
        self.cmat = k.sbuf((128, 640), F32, "cmat_s")
        self.cmat_r = k.sbuf((128, 640), F32R, "cmat_r")
        self.ident_b = k.sbuf((128, 128), BF16, "ident_b")
        self.ones_b = k.sbuf((128, 128), BF16, "ones_b")
        self.eps_t = k.sbuf((128, 1), F32, "eps_t")
        self.one_t = k.sbuf((128, 1), F32, "one_t")
        self.x = k.sbuf((128, 8, TP), F32, "x")
        self.h = k.sbuf((128, 8, TP), BF16, "h")
        self.sq = Ring([k.sbuf((128, TP), BF16, f"sq{i}") for i in range(2)])
        self.rstd = k.sbuf((128, TP), F32, "rstd")
        self.act = k.sbuf((128, 22, TP), BF16, "act")
        self.sil = Ring([k.sbuf((128, TP), F32, f"sil{i}") for i in range(2)])
        self.wring = Ring([k.sbuf((128, 4096), BF16, f"wr{i}") for i in range(3)])
        self.wsem = [k.dsem(f"w{i}") for i in range(3)]
        self.PS = Ring([k.psum((128, 512), F32, f"psb{i}") for i in range(8)])
        self.pp_ps = self.PS
        self.pt_ps = self.PS
        self.alloc_mixers()
        self.xin = Ring([k.sbuf((128, D), F32, f"xin{i}") for i in range(2)])
        self.xin_sem = [k.dsem(f"xin{i}") for i in range(2)]
        self.yout = Ring(self.xin.bufs)
        self.yout_sem = [k.dsem(f"yo{i}") for i in range(2)]
        self.misc_sem = k.dsem("misc")
        NT = cfg.NT
        self.xscr = [k.dram(f"xscr{t_}", (128, 8 * TP), F32) for t_ in range(NT + 1)]
        self.xs_sem = [k.dsem(f"xs{i}") for i in range(2)]
        self.xs_i = 0
        self.sc_sems = [k.dsem(f"sc{i}") for i in range(4)]
        self.sc_i = 0
        self.out_sems += self.xs_sem + self.sc_sems
        self.small = k.sbuf((128, 8, 32), F32, "small")
        self.atsum = k.sbuf((128, 32), F32, "atsum")
        self.histsave = k.sbuf((128, 24, 3), F32, "histsave")
        self.blsum = k.sbuf((128, 4), F32, "blsum")

        def dyn_setup(e, rs):
            rp = rs.enter_context(e.register("r_prev"))
            rb = rs.enter_context(e.register("r_base"))
            e.reg_load(rp, self.rank_in[0:1, 0:1])
            e.reg_load(rb, self.rank_in[0:1, 1:2])
            k.dyn["prev"] = e.snap(rp, min_val=0, max_val=cfg.NCORES - 1)
            k.dyn["base"] = e.snap(rb, min_val=0, max_val=cfg.NCORES - cfg.NSEG)
        k.dyn_setup = dyn_setup
        self.out_sems += list(self.yout_sem)

    def P(self, key):
        off, w = self.pp.off[key]
        return self.params, off, w

    def prologue(self):
        k = self.k
        k.dma("pool", self.misc_sem, self.params[:], self.pcat, w=[self.params])
        k.dma("pool", self.misc_sem, self.cmat[:], self.cmat_in, w=[self.cmat])
        k.op("dve", lambda e: e.tensor_copy(out=self.ident_b[:], in_=self.cmat[:, 0:128]), r=[self.cmat], w=[self.ident_b])
        k.op("dve", lambda e: e.tensor_copy(out=self.cmat_r[:], in_=self.cmat[:]), r=[self.cmat], w=[self.cmat_r])
        k.op("dve", lambda e: e.memset(self.one_t[:], 1.0), w=[self.one_t])
        self.prologue_mixers()
        k.op("dve", lambda e: e.memset(self.ones_b[:], 1.0), w=[self.ones_b])
        k.op("dve", lambda e: e.memset(self.eps_t[:], EPS), w=[self.eps_t])
        CW = 1024

        def slots(parent, n, dt_bytes):
            flat = parent.t
            out = []
            for j in range(n):
                out.append((T(flat[:, j * CW:(j + 1) * CW], f"{parent.name}_s{j}"), parent))
            return out
        ysbf = T(self.ysb.t[:, :], "ysbf")
        fsl = [(self.xin.bufs[0], None), (self.xin.bufs[1], None)] + slots(self.ysb, 2, 4) + slots(self.HT[min(self.HT)], 2, 4)
        bsl = slots(self.xs_tok, 2, 2) + slots(self.xdt, 2, 2) + slots(self.xdd, 2, 2)
        NS = len(fsl)
        sin = [k.dsem(f"ci{i}") for i in range(NS)]
        sout = [k.dsem(f"co{i}") for i in range(NS)]
        engs = ["act", "dve"]
        n = self.wtot // CW
        DPT = NS - 1

        def ld(i):
            s_ = i % NS
            stf = fsl[s_][0]
            stf_ap = stf[:, :] if fsl[s_][1] is None else stf.t
            k.dma("sp", sin[s_], stf_ap, self.wcat[:, i * CW:(i + 1) * CW], w=[stf])

        for i in range(min(DPT, n)):
            ld(i)
        for i in range(n):
            s_ = i % NS
            stf, stb = fsl[s_][0], bsl[s_][0]
            stf_ap = stf[:, :] if fsl[s_][1] is None else stf.t
            stb_ap = stb.t
            eng = engs[i % 2]
            if eng == "act":
                k.op("act", lambda e, stf_ap=stf_ap, stb_ap=stb_ap: e.copy(out=stb_ap, in_=stf_ap), r=[stf], w=[stb])
            else:
                k.op(eng, lambda e, stf_ap=stf_ap, stb_ap=stb_ap: e.tensor_copy(out=stb_ap, in_=stf_ap), r=[stf], w=[stb])
            k.dma("pool", sout[s_], self.wb[:, i * CW:(i + 1) * CW], stb_ap, r=[stb], w=[self.wb])
            if i + DPT < n:
                ld(i + DPT)
        for (slot, parent) in fsl + bsl:
            if parent is not None:
                for dd_s, dd_p in ((slot.w, parent.w), (slot.r, parent.r)):
                    for key, c in dd_s.items():
                        if dd_p.get(key, 0) < c:
                            dd_p[key] = c

    def wload(self, blk):
        off, kc, bw, nv = blk
        i = self.wring.i % 3
        buf = self.wring.next()
        k = self.k
        k.dma("sp", self.wsem[i], buf[:, 0:kc * bw], self.wb[:, off:off + kc * bw], r=[self.wb], w=[buf])
        return buf

    def proj_fm(self, wkey, src, kc_n, ntok, consumer, col_lo=0, col_hi=None):
        k = self.k
        blocks = self.wp.blocks[wkey]
        for b, blk in enumerate(blocks):
            off, kc, bw, nv = blk
            assert kc == kc_n
            c0 = b * bw
            if col_hi is not None and c0 >= col_hi:
                break
            if c0 + nv <= col_lo:
                continue
            buf = self.wload(blk)
            for j in range(0, nv, 128):
                col = c0 + j
                if col < col_lo or (col_hi is not None and col >= col_hi):
                    continue
                ncols = min(128, nv - j)
                ps = self.pp_ps.next()
                for kc_i in range(kc):
                    k.op("pe", lambda e, ps=ps, buf=buf, kc_i=kc_i, j=j, bw=bw, ncols=ncols:
                         e.matmul(ps[0:ncols, 0:ntok], lhsT=buf[:, kc_i * bw + j:kc_i * bw + j + ncols],
                                  rhs=src[:, kc_i, 0:ntok], start=(kc_i == 0), stop=(kc_i == kc - 1)),
                         r=[buf, src], w=[ps])
                consumer(col, ncols, ps)

    def rmsnorm(self, gkey, ntok, out, out_is_f32=False):
        k = self.k
        x = self.x
        ps = self.pp_ps.next()
        for c in range(8):
            sq = self.sq.next()
            k.op("act", lambda e, sq=sq, c=c: e.activation(out=sq[:, 0:ntok], in_=x[:, c, 0:ntok], func=AF.Square), r=[x], w=[sq])
            k.op("pe", lambda e, sq=sq, c=c, ps=ps: e.matmul(ps[:, 0:ntok], lhsT=self.ones_b[:], rhs=sq[:, 0:ntok], start=(c == 0), stop=(c == 7)),
                 r=[sq, self.ones_b], w=[ps])
        rstd = self.rstd
        k.op("act", lambda e: e.activation(out=rstd[:, 0:ntok], in_=ps[:, 0:ntok], func=AF.Sqrt, bias=self.eps_t[:, 0:1], scale=1.0 / D), r=[ps, self.eps_t], w=[rstd])
        k.op("dve", lambda e: e.reciprocal(out=rstd[:, 0:ntok], in_=rstd[:, 0:ntok]), r=[rstd], w=[rstd])
        prm, off, _ = self.P(gkey)
        for c in range(8):
            k.op("dve", lambda e, c=c: e.scalar_tensor_tensor(out=out[:, c, 0:ntok], in0=x[:, c, 0:ntok], scalar=prm[:, off + c:off + c + 1],
                                                              in1=rstd[:, 0:ntok], op0=ALU.mult, op1=ALU.mult), r=[x, rstd, prm], w=[out])

    def ffn(self, i, ntok):
        k = self.k
        self.rmsnorm(("nffn", i), ntok, self.h)
        act = self.act
        gb, ub = self.wp.blocks[("ffn_gate", i)], self.wp.blocks[("ffn_up", i)]
        for b in range(len(gb)):
            off, kc, bw, nv = gb[b]
            bufg = self.wload(gb[b])
            bufu = self.wload(ub[b])
            for j in range(0, nv, 128):
                ch = (b * bw + j) // 128
                psg, psu = self.pp_ps.next(), self.pp_ps.next()
                for buf, ps in ((bufg, psg), (bufu, psu)):
                    for kc_i in range(kc):
                        k.op("pe", lambda e, ps=ps, buf=buf, kc_i=kc_i, j=j, bw=bw:
                             e.matmul(ps[:, 0:ntok], lhsT=buf[:, kc_i * bw + j:kc_i * bw + j + 128],
                                      rhs=self.h[:, kc_i, 0:ntok], start=(kc_i == 0), stop=(kc_i == kc - 1)),
                             r=[buf, self.h], w=[ps])
                sil = self.sil.next()
                k.op("act", lambda e, sil=sil, psg=psg: e.activation(out=sil[:, 0:ntok], in_=psg[:, 0:ntok], func=AF.Silu), r=[psg], w=[sil])
                k.op("dve", lambda e, sil=sil, psu=psu, ch=ch: e.tensor_tensor(out=act[:, ch, 0:ntok], in0=sil[:, 0:ntok], in1=psu[:, 0:ntok], op=ALU.mult),
                     r=[sil, psu], w=[act])

        x = self.x

        def cons_down(col, ncols, ps):
            c = col // 128
            k.op("dve", lambda e: e.tensor_tensor(out=x[:, c, 0:ntok], in0=x[:, c, 0:ntok], in1=ps[:, 0:ntok], op=ALU.add), r=[x, ps], w=[x])

        self.proj_fm(("ffn_down", i), act, 22, ntok, cons_down)

    def load_x(self, src_ap, ntok):
        k = self.k
        for t0 in range(0, ntok, 128):
            n = min(128, ntok - t0)
            i = self.xin.i % 2
            xin = self.xin.next()
            k.dma("pool", self.xin_sem[i], xin[0:n, :], src_ap[t0:t0 + n, :], w=[xin])
            for c4 in range(2):
                ps = self.pt_ps.next()
                for cc in range(4):
                    c = c4 * 4 + cc
                    k.op("pe", lambda e, ps=ps, cc=cc, c=c, xin=xin, n=n: e.transpose(ps[:, cc * 128:cc * 128 + n], xin[0:n, c * 128:(c + 1) * 128], self.cmat[0:n, 0:n]),
                         r=[xin, self.cmat], w=[ps])
                k.op("act", lambda e, ps=ps, c4=c4, n=n, t0=t0: e.copy(out=self.x[:, c4 * 4:c4 * 4 + 4, t0:t0 + n],
                                                                         in_=ps[:, :].rearrange("p (c t) -> p c t", c=4)[:, :, 0:n]), r=[ps], w=[self.x])

    def store_y(self, dst_ap, ntok):
        k = self.k
        self.hf = T(self.ysb.t[:, :].rearrange("p (c t) -> p c t", c=8), "hfview")
        self.hf.w, self.hf.r = self.ysb.w, self.ysb.r
        self.rmsnorm(("nfin",), ntok, self.hf)
        for t0 in range(0, ntok, 128):
            n = min(128, ntok - t0)
            i = self.yout.i % 2
            yo = self.yout.next()
            for c4 in range(2):
                ps = self.pt_ps.next()
                for cc in range(4):
                    c = c4 * 4 + cc
                    k.op("pe", lambda e, ps=ps, cc=cc, c=c, n=n, t0=t0: e.transpose(ps[0:n, cc * 128:(cc + 1) * 128], self.hf[:, c, t0:t0 + n], self.cmat[:, 0:128]),
                         r=[self.hf, self.cmat], w=[ps])
                k.op("act", lambda e, ps=ps, c4=c4, n=n, yo=yo: e.copy(out=yo[0:n, c4 * 512:(c4 + 1) * 512], in_=ps[0:n, :]), r=[ps], w=[yo])
            k.dma("pool", self.yout_sem[i], dst_ap[t0:t0 + n, :], yo[0:n, :], r=[yo])

    def layers(self, ntok, tile):
        cfg = self.cfg
        for i in range(cfg.DEPTH):
            if cfg.mixers:
                self.mixer(i, ntok, tile)
            self.ffn(i, ntok)

    def alloc_mixers(self):
        k, cfg = self.k, self.cfg
        TP = cfg.TP
        self.A = k.sbuf((128, 24, TP), BF16, "A")
        self.zt = k.sbuf((128, 4, 2048), BF16, "zt")
        self.pre = Ring([k.sbuf((128, TP + 16), F32, f"pre{i}") for i in range(2)])
        self.ctmp = Ring([k.sbuf((128, TP), F32, f"ctmp{i}") for i in range(2)])
        self.hist = {}
        self.HT = {}
        for i in range(cfg.DEPTH):
            if i % 3 == 0:
                self.hist[i] = k.sbuf((128, 24, 3), F32, f"hist{i}")
                self.HT[i] = k.sbuf((128, 2048), F32, f"HT{i}")
        self.HTb = k.sbuf((128, 2048), BF16, "HTb")
        self.acol = k.sbuf((128, 4), F32, "acol")
        self.dtT = k.sbuf((32, TP), F32, "dtT")
        self.aT = k.sbuf((32, TP), F32, "aT")
        self.xs_tok = k.sbuf((128, 2048), BF16, "xs_tok")
        self.xdt = k.sbuf((128, 2048), BF16, "xdt")
        self.xdd = k.sbuf((128, 2048), BF16, "xdd")
        self.xsD = k.sbuf((128, 2048), BF16, "xsD")
        self.B_tok = k.sbuf((128, 512), BF16, "B_tok")
        self.a_tok = k.sbuf((128, 32), F32R, "a_tok")
        self.dt_tok = k.sbuf((128, 32), F32, "dt_tok")
        self.acs = k.sbuf((128, 32), F32, "acs")
        self.eacs = k.sbuf((128, 32), F32, "eacs")
        self.dst = k.sbuf((128, 32), F32, "dst")
        self.edec = k.sbuf((128, 32), F32, "edec")
        self.CBT = Ring([k.sbuf((128, 128), F32, f"CBT{i}") for i in range(2)])
        self.XD = Ring([k.sbuf((128, 512), F32R, f"XD{i}") for i in range(2)])
        self.TMP = Ring([k.sbuf((128, 512), F32, f"TMP{i}") for i in range(2)])
        self.LX = Ring([k.sbuf((128, 512), F32, f"LX{i}") for i in range(2)])
        self.MT = Ring([k.sbuf((128, 512), BF16, f"MT{i}") for i in range(2)])
        self.ysb = k.sbuf((128, 2048), F32, "ysb")
        self.ygn = k.sbuf((128, 2048), BF16, "ygn")
        self.junk = k.sbuf((128, 512), F32, "junk")
        self.ss = k.sbuf((128, 8), F32, "ss")
        self.rs = k.sbuf((128, 8), F32, "rs")
        self.st_sem = k.dsem("st")
        self.out_sems.append(self.st_sem)
        nc = self.nc
        NSC = cfg.NSC
        nssm = len(self.HT)
        I = lambda n, s: nc.dram_tensor(n, list(s), F32, kind="ExternalInput").ap()
        O = lambda n, s: nc.dram_tensor(n, list(s), F32, kind="ExternalOutput").ap()
        self.ssm_in = I("ssm_in_t", (nssm, NSC, 128, 2048))
        self.conv_in = I("conv_in_fm", (nssm, 128, 24 * NSC * 3))
        self.ssm_p_o = O("ssm_p_t", (nssm, 128, 2048))
        self.ssm_s_o = O("ssm_s_t", (nssm, NSC, 128, 2048))
        self.conv_p_o = O("conv_p_fm", (nssm, 128, 72))
        self.conv_s_o = O("conv_s_fm", (nssm, 128, 24 * NSC * 3))
        self.convst = k.sbuf((128, nssm, 24 * NSC * 3), F32, "convst")
        self.convout = k.sbuf((128, 24 * NSC * 3), F32, "convout")
        if cfg.DEPTH > 1:
            self.alloc_gla()
        if cfg.DEPTH > 2:
            self.alloc_att()

    def prologue_mixers(self):
        k, cfg = self.k, self.cfg
        for n, i in enumerate(sorted(self.HT)):
            k.op("dve", lambda e, i=i: e.memset(self.hist[i][:], 0.0), w=[self.hist[i]])
            k.op("dve", lambda e, i=i: e.memset(self.HT[i][:], 0.0), w=[self.HT[i]])
            prm, off, _ = self.P(("ssm_alog", i))
            k.op("act", lambda e, n=n, off=off: e.activation(out=self.acol[:, n:n + 1], in_=prm[:, off:off + 1], func=AF.Exp), r=[prm], w=[self.acol])
            k.op("dve", lambda e, n=n: e.tensor_scalar(out=self.acol[:, n:n + 1], in0=self.acol[:, n:n + 1], scalar1=-1.0, scalar2=None, op0=ALU.mult), r=[self.acol], w=[self.acol])
            k.dma("pool", self.misc_sem, self.convst[:, n, :], self.conv_in[n], w=[self.convst])
        if cfg.DEPTH > 2:
            self.prologue_att()

    def mixer(self, i, ntok, tile, phase="all"):
        m = i % 3
        if m == 0:
            self.mixer_ssd(i, ntok, tile, phase)
        elif m == 1:
            self.mixer_gla(i, ntok, tile, phase)
        else:
            self.mixer_att(i, ntok, tile, phase)

    def add_to_x(self, ntok):
        k, x = self.k, self.x

        def cons(col, ncols, ps):
            c = col // 128
            k.op("dve", lambda e: e.tensor_tensor(out=x[:, c, 0:ntok], in0=x[:, c, 0:ntok], in1=ps[:, 0:ntok], op=ALU.add), r=[x, ps], w=[x])
        return cons

    def alloc_gla(self):
        k, cfg, nc = self.k, self.cfg, self.nc
        TP, NSC = cfg.TP, cfg.NSC
        self.glow = k.sbuf((16, TP), F32, "glow")
        self.G3 = k.sbuf((128, 3, 4, TP), F32, "G3")
        self.la = [TV(self.G3, self.G3[:, i], f"la{i}") for i in range(2)]
        self.ebx = TV(self.G3, self.G3[:, 2], "ebx")
        self.eblast = k.sbuf((128, 16), F32, "eblast")
        self.S = k.sbuf((128, 1024), F32, "S")
        self.Sb = k.sbuf((128, 1024), BF16, "Sb")
        I = lambda n, s: nc.dram_tensor(n, list(s), F32, kind="ExternalInput").ap()
        O = lambda n, s: nc.dram_tensor(n, list(s), F32, kind="ExternalOutput").ap()
        self.gla_in = I("gla_in", (NSC, 4, 128, 256))
        self.gla_p_o = O("gla_p", (4, 128, 256))
        self.gla_s_o = O("gla_s", (NSC, 4, 128, 256))

    def mixer_gla(self, i, ntok, tile, phase="all"):
        k, cfg = self.k, self.cfg
        kind, ti = tile
        do_proj, do_y = True, phase != "p1"
        NSC = cfg.NSC
        if kind == "p":
            units = [(u * 128, 128, 0) for u in range(ntok // 128)]
        else:
            units = [(s * cfg.DEC, cfg.DEC, s) for s in range(NSC)]
        nun, Q = len(units), units[0][1]
        A, h, zt, act = self.A, self.h, self.zt, self.act
        cm, ident_b = self.cmat, self.ident_b
        S, Sb = self.S, self.Sb
        glow, la, ebx, eblast = self.glow, self.la, self.ebx, self.eblast
        blocks = self.wp.blocks[("gla_in", i)]
        prm_wg, off_wg, _ = self.P(("gla_wg", i))
        prm_gb, off_gb, _ = self.P(("gla_gb", i))
        prm_nw, off_nw, _ = self.P(("gla_nw", i))
        if phase == "p1" and ti == 0:
            k.op("dve", lambda e: e.memset(S[:, :], 0.0), w=[S])
            k.op("dve", lambda e: e.memset(Sb[:, :], 0.0), w=[Sb])
            k.op("dve", lambda e: e.memset(self.blsum[:, :], 0.0), w=[self.blsum])

        if do_proj:
            self.rmsnorm(("nmix", i), ntok, self.h)
            def cons(col, ncols, ps):
                if col < 1024:
                    k.op("act", lambda e: e.copy(out=A[:, col // 128, 0:ntok], in_=ps[:, 0:ntok]), r=[ps], w=[A])
                else:
                    k.op("act", lambda e: e.copy(out=glow[:, 0:ntok], in_=ps[0:16, 0:ntok]), r=[ps], w=[glow])
            self.proj_fm(("gla_in", i), h, 8, ntok, cons, col_lo=(0 if do_y else 512), col_hi=1024)
            self.proj_fm(("gla_in", i), h, 8, ntok, cons, col_lo=3072)
            for b in ((2, 3, 4, 5) if do_y else (2, 3)):
                buf = self.wload(blocks[b])
                for ui, (o, Qu, s) in enumerate(units):
                    ps = self.PS.next()
                    for kc in range(8):
                        k.op("pe", lambda e, ps=ps, kc=kc, o=o, buf=buf: e.matmul(ps[0:Q, 0:512], lhsT=h[:, kc, o:o + Q], rhs=buf[:, kc * 512:(kc + 1) * 512],
                                                                                  start=(kc == 0), stop=(kc == 7)), r=[h, buf], w=[ps])
                    if b >= 4:
                        k.op("act", lambda e, ps=ps, ui=ui, b=b: e.activation(out=zt[0:Q, ui, (b - 4) * 512:(b - 3) * 512], in_=ps[0:Q, 0:512], func=AF.Silu), r=[ps], w=[zt])
                    else:
                        k.op("act", lambda e, ps=ps, ui=ui, b=b: e.copy(out=zt[0:Q, ui, 1024 + (b - 2) * 512:1024 + (b - 1) * 512], in_=ps[0:Q, 0:512]), r=[ps], w=[zt])
            for hh in range(4):
                ps = self.PS.next()
                k.op("pe", lambda e, ps=ps, hh=hh: e.matmul(ps[:, 0:ntok], lhsT=prm_wg[0:16, off_wg + hh * 128:off_wg + (hh + 1) * 128], rhs=glow[:, 0:ntok], start=True, stop=True),
                     r=[prm_wg, glow], w=[ps])
                xb, ax = self.ctmp.next(), self.pre.next()
                k.op("dve", lambda e, ps=ps, xb=xb, hh=hh: e.tensor_scalar(out=xb[:, 0:ntok], in0=ps[:, 0:ntok], scalar1=prm_gb[:, off_gb + hh:off_gb + hh + 1], scalar2=None, op0=ALU.add),
                     r=[ps, prm_gb], w=[xb])
                k.op("dve", lambda e, xb=xb, ax=ax: e.scalar_tensor_tensor(out=ax[:, 0:ntok], in0=xb[:, 0:ntok], scalar=-1.0, in1=xb[:, 0:ntok], op0=ALU.mult, op1=ALU.max), r=[xb], w=[ax])
                k.op("act", lambda e, ax=ax: e.activation(out=ax[:, 0:ntok], in_=ax[:, 0:ntok], func=AF.Exp, scale=-1.0), r=[ax], w=[ax])
                k.op("act", lambda e, ax=ax: e.activation(out=ax[:, 0:ntok], in_=ax[:, 0:ntok], func=AF.Ln, bias=self.one_t[:, 0:1], scale=1.0), r=[ax, self.one_t], w=[ax])
                k.op("dve", lambda e, xb=xb, ax=ax, hh=hh: e.scalar_tensor_tensor(out=la[0][:, hh, 0:ntok], in0=xb[:, 0:ntok], scalar=0.0, in1=ax[:, 0:ntok], op0=ALU.min, op1=ALU.subtract),
                     r=[xb, ax], w=[la[0]])
            cur = 0
            sh = 1
            vw = lambda t: t[:, :, 0:ntok].rearrange("p h (u q) -> p h u q", q=Q)
            while sh < Q:
                src, dst_ = la[cur], la[1 - cur]
                k.op("dve", lambda e, src=src, dst_=dst_, sh=sh: e.tensor_copy(out=vw(dst_)[:, :, :, 0:sh], in_=vw(src)[:, :, :, 0:sh]), r=[src], w=[dst_])
                k.op("dve", lambda e, src=src, dst_=dst_, sh=sh: e.tensor_tensor(out=vw(dst_)[:, :, :, sh:Q], in0=vw(src)[:, :, :, sh:Q], in1=vw(src)[:, :, :, 0:Q - sh], op=ALU.add), r=[src], w=[dst_])
                cur = 1 - cur
                sh *= 2
            bc_ = la[cur]
            k.op("act", lambda e: e.activation(out=ebx[:, :, 0:ntok], in_=bc_[:, :, 0:ntok], func=AF.Exp, scale=1.0 / 16), r=[bc_], w=[ebx])
            k.op("dve", lambda e: e.scalar_tensor_tensor(out=A[:, 8:12, 0:ntok], in0=A[:, 0:4, 0:ntok], scalar=float(GHK) ** -0.5, in1=ebx[:, :, 0:ntok], op0=ALU.mult, op1=ALU.mult),
                 r=[A, ebx], w=[A])
            k.op("act", lambda e: e.activation(out=eblast[:, 0:4 * nun].rearrange("p (h u) -> p h u", h=4), in_=vw(bc_)[:, :, :, Q - 1], func=AF.Exp, scale=1.0 / 16), r=[bc_], w=[eblast])
            k.op("act", lambda e: e.activation(out=ebx[:, :, 0:ntok], in_=bc_[:, :, 0:ntok], func=AF.Exp, scale=-1.0 / 16), r=[bc_], w=[ebx])
            k.op("dve", lambda e: e.tensor_tensor(out=A[:, 12:16, 0:ntok], in0=A[:, 4:8, 0:ntok], in1=ebx[:, :, 0:ntok], op=ALU.mult), r=[A, ebx], w=[A])
            k.op("dve", lambda e: e.tensor_tensor(out=vw(A[:, 16:20, :]), in0=vw(A[:, 12:16, :]), in1=eblast[:, 0:4 * nun].rearrange("p (h u) -> p h u", h=4).unsqueeze(3).to_broadcast([128, 4, nun, Q]), op=ALU.mult),
                 r=[A, eblast], w=[A])
            if phase == "p1":
                for u_ in range(nun):
                    k.op("dve", lambda e, u_=u_: e.tensor_tensor(out=self.blsum[:, 0:4], in0=self.blsum[:, 0:4], in1=vw(bc_)[:, :, u_, Q - 1], op=ALU.add), r=[self.blsum, bc_], w=[self.blsum])
        ysb, ygn, kd_tok = self.ysb, self.ygn, self.B_tok
        for ui, (o, Qu, s) in enumerate(units):
            if kind == "s":
                k.dma("pool", self.st_sem, S[:, :].rearrange("p (h v) -> p h v", h=4), self.gla_in[s].rearrange("h k v -> k h v"), w=[S])
                k.op("act", lambda e: e.copy(out=Sb[:, :], in_=S[:, :]), r=[S], w=[Sb])
            ps = self.PS.next()
            pb = ps[:, :].bitcast(BF16)
            for hh in range(4):
                k.op("pe", lambda e, pb=pb, hh=hh, o=o: e.transpose(pb[0:Q, hh * 128:(hh + 1) * 128], A[:, 16 + hh, o:o + Q], ident_b[:, :]), r=[A, ident_b], w=[ps])
            k.op("act", lambda e, pb=pb: e.copy(out=kd_tok[0:Q, :], in_=pb[0:Q, 0:512]), r=[ps], w=[kd_tok])
            MT = self.MT.next()
            k.op("dve", lambda e: e.memset(self.ss[:, :], 0.0), w=[self.ss])
            ops = []
            for hh in range(4):
                vt = zt[0:Q, ui, 1024 + hh * 256:1024 + (hh + 1) * 256]
                if do_y:
                    psa = self.PS.next()
                    k.op("pe", lambda e, psa=psa, hh=hh, o=o: e.matmul(psa[0:Q, 0:Q], lhsT=A[:, 12 + hh, o:o + Q], rhs=A[:, 8 + hh, o:o + Q], start=True, stop=True), r=[A], w=[psa])
                    k.op("dve", lambda e, psa=psa, hh=hh, MT=MT: e.tensor_tensor(out=MT[0:Q, hh * Q:(hh + 1) * Q], in0=psa[0:Q, 0:Q], in1=cm[0:Q, 128:128 + Q], op=ALU.mult), r=[psa, cm], w=[MT])
                    pso = self.PS.next()
                    k.op("pe", lambda e, pso=pso, hh=hh, MT=MT, vt=vt: e.matmul(pso[0:Q, 0:256], lhsT=MT[0:Q, hh * Q:(hh + 1) * Q], rhs=vt, start=True, stop=False), r=[MT, zt], w=[pso])
                    k.op("pe", lambda e, pso=pso, hh=hh, o=o: e.matmul(pso[0:Q, 0:256], lhsT=A[:, 8 + hh, o:o + Q], rhs=Sb[:, hh * 256:(hh + 1) * 256], start=False, stop=True), r=[A, Sb], w=[pso])
                    k.op("act", lambda e, pso=pso, hh=hh: e.activation(out=self.junk[0:Q, 0:256], in_=pso[0:Q, 0:256], func=AF.Square, accum_out=self.ss[0:Q, hh:hh + 1]),
                         r=[pso], w=[self.junk, self.ss])
                    k.op("act", lambda e, pso=pso, hh=hh: e.copy(out=ysb[0:Q, hh * 256:(hh + 1) * 256], in_=pso[0:Q, 0:256]), r=[pso], w=[ysb])
                psu = self.PS.next()
                k.op("pe", lambda e, psu=psu, hh=hh, vt=vt: e.matmul(psu[:, 0:256], lhsT=kd_tok[0:Q, hh * 128:(hh + 1) * 128], rhs=vt, start=True, stop=True), r=[kd_tok, zt], w=[psu])
                k.op("dve", lambda e, psu=psu, hh=hh, ui=ui: e.scalar_tensor_tensor(out=S[:, hh * 256:(hh + 1) * 256], in0=S[:, hh * 256:(hh + 1) * 256], scalar=eblast[:, hh * nun + ui:hh * nun + ui + 1],
                                                                                   in1=psu[:, 0:256], op0=ALU.mult, op1=ALU.add), r=[S, eblast, psu], w=[S])
            k.op("act", lambda e: e.copy(out=Sb[:, :], in_=S[:, :]), r=[S], w=[Sb])
            if do_y:
                k.op("act", lambda e: e.activation(out=self.rs[0:Q, 0:4], in_=self.ss[0:Q, 0:4], func=AF.Sqrt, bias=self.eps_t[0:Q, 0:1], scale=1.0 / GHV), r=[self.ss, self.eps_t], w=[self.rs])
                k.op("dve", lambda e: e.reciprocal(out=self.rs[0:Q, 0:4], in_=self.rs[0:Q, 0:4]), r=[self.rs], w=[self.rs])
                for hh in range(4):
                    k.op("dve", lambda e, hh=hh: e.scalar_tensor_tensor(out=ysb[0:Q, hh * 256:(hh + 1) * 256], in0=ysb[0:Q, hh * 256:(hh + 1) * 256], scalar=self.rs[0:Q, hh:hh + 1],
                                                                       in1=prm_nw[0:Q, off_nw:off_nw + 256], op0=ALU.mult, op1=ALU.mult), r=[ysb, self.rs, prm_nw], w=[ysb])
                k.op("dve", lambda e, ui=ui: e.tensor_tensor(out=ygn[0:Q, 0:1024], in0=ysb[0:Q, 0:1024], in1=zt[0:Q, ui, 0:1024], op=ALU.mult), r=[ysb, zt], w=[ygn])
                ps = self.PS.next()
                pb = ps[:, :].bitcast(BF16)
                for c in range(8):
                    k.op("pe", lambda e, pb=pb, c=c: e.transpose(pb[:, c * Q:(c + 1) * Q], ygn[0:Q, c * 128:(c + 1) * 128], ident_b[0:Q, 0:Q]), r=[ygn, ident_b], w=[ps])
                k.op("act", lambda e, pb=pb, o=o: e.copy(out=act[:, 0:8, o:o + Q], in_=pb[:, 0:8 * Q].rearrange("p (c q) -> p c q", c=8)), r=[ps], w=[act])
            if kind == "s":
                k.dma("pool", self.st_sem, self.gla_s_o[s].rearrange("h k v -> k h v"), S[:, :].rearrange("p (h v) -> p h v", h=4), r=[S])
        if kind == "p" and ti == cfg.NT - 1 and phase == "p2":
            k.dma("pool", self.st_sem, self.gla_p_o.rearrange("h k v -> k h v"), S[:, :].rearrange("p (h v) -> p h v", h=4), r=[S])
        if do_y:
            self.proj_fm(("gla_out", i), act, 8, ntok, self.add_to_x(ntok))

    def alloc_att(self):
        k, cfg, nc = self.k, self.cfg, self.nc
        NSC, SEQ = cfg.NSC, cfg.SEQ
        I = lambda n, s: nc.dram_tensor(n, list(s), F32, kind="ExternalInput").ap()
        O = lambda n, s: nc.dram_tensor(n, list(s), F32, kind="ExternalOutput").ap()
        self.NU = SEQ // 128
        self.vtok = k.sbuf((128, 768), F32, "vtok")
        self.maskb = k.sbuf((128, 3072), BF16, "maskb")
        self.ropeT = k.sbuf((128, (self.NU + 1) * 16), F32, "ropeT")
        self.ast = k.sbuf((128, 64), F32, "ast")
        self.rope_in = I("rope", (128, (self.NU + 1) * 16))
        self.mask_in = I("amask", (128, 3072))
        self.kt_hist = k.dram("kt_hist", (6, 128, SEQ), BF16)
        self.v_hist = k.dram("v_hist", (SEQ, 768), BF16)
        self.cache = [I(f"kvc{g}", (NSC, W, 512)) for g, (W, d) in enumerate(AGROUPS)]
        self.kvp_o = [O(f"kvp{g}", (min(W, SEQ), 512)) for g, (W, d) in enumerate(AGROUPS)]
        self.kvs_o = [O(f"kvs{g}", (NSC, W, 512)) for g, (W, d) in enumerate(AGROUPS)]
        self.att_sems = [k.dsem(f"att{i}") for i in range(4)]
        self.att_i = 0
        self.out_sems += self.att_sems
        ztf = self.zt.t[:, :, :].rearrange("p a b -> p (a b)")
        self.KT_sb = TV(self.zt, ztf[:, 0:4608].rearrange("p (c t) -> p c t", c=2), "KT_sb")
        self.Pb = TV(self.zt, ztf[:, 4608:4608 + 2304], "Pb")
        g3f = self.G3.t[:, :, :, :].rearrange("p a b c -> p (a b c)").bitcast(BF16)
        self.V_sb = TV(self.G3, g3f[:, 0:4608].rearrange("p (b f) -> p b f", f=256), "V_sb")
        actf = self.act.t[:, :, :].rearrange("p a b -> p (a b)")
        self.Pbs = [self.Pb, TV(self.act, actf[:, 1536:1536 + 2304], "Pb1")]
        self.astw = [k.sbuf((128, 16), F32, f"astw{i_}") for i_ in range(2)]

    def prologue_att(self):
        k = self.k
        k.dma("pool", self.misc_sem, self.ropeT[:, :], self.rope_in, w=[self.ropeT])
        for i in range(3):
            st = self.xin.bufs[i % 2]
            k.dma("pool", self.misc_sem, st[:, :], self.mask_in[:, i * 1024:(i + 1) * 1024], w=[st])
            k.op("dve", lambda e, st=st, i=i: e.tensor_copy(out=self.maskb[:, i * 1024:(i + 1) * 1024], in_=st[:, :]), r=[st], w=[self.maskb])

    def adma(self, out, in_, r=(), w=()):
        s = self.att_sems[self.att_i % 4]
        self.att_i += 1
        self.k.dma("pool", s, out, in_, r=r, w=w)

    def mixer_att(self, i, ntok, tile, phase="all"):
        k, cfg = self.k, self.cfg
        kind, ti = tile
        SEQ, NSC, TP = cfg.SEQ, cfg.NSC, cfg.TP
        if kind == "p":
            units = [(u * 128, 128, 0, ti * TP + u * 128, (ti * TP) // 128 + u) for u in range(ntok // 128)]
        else:
            units = [(s * cfg.DEC, cfg.DEC, s, cfg.PAST, self.NU) for s in range(NSC)]
        self.rmsnorm(("nmix", i), ntok, self.h)
        A, h, act, cm, ident_b = self.A, self.h, self.act, self.cmat, self.ident_b
        ysb, ygn, vtok, vb, ast = self.ysb, self.ygn, self.vtok, self.xs_tok, self.ast
        KT_sb, V_sb, Pb, U = self.KT_sb, self.V_sb, self.Pb, self.xin.bufs[0]
        maskb, ropeT = self.maskb, self.ropeT
        blocks = self.wp.blocks[("att_qkv", i)]
        def unit_body(o, Q, s, q0, ru):
            if True:
                for b, blk in enumerate(blocks):
                    if (phase == "p1" and b == 0) or (phase == "p2" and b >= 2):
                        continue
                    buf = self.wload(blk)
                    nv = blk[3]
                    ps = self.PS.next()
                    for kc in range(8):
                        k.op("pe", lambda e, ps=ps, kc=kc, buf=buf, nv=nv: e.matmul(ps[0:Q, 0:nv], lhsT=h[:, kc, o:o + Q], rhs=buf[:, kc * 512:kc * 512 + nv],
                                                                                   start=(kc == 0), stop=(kc == 7)), r=[h, buf], w=[ps])
                    c0 = b * 512
                    if c0 < 1536:
                        k.op("act", lambda e, ps=ps, c0=c0, nv=nv: e.copy(out=ysb[0:Q, c0:c0 + nv], in_=ps[0:Q, 0:nv]), r=[ps], w=[ysb])
                    else:
                        k.op("act", lambda e, ps=ps, c0=c0, nv=nv: e.copy(out=vtok[0:Q, c0 - 1536:c0 - 1536 + nv], in_=ps[0:Q, 0:nv]), r=[ps], w=[vtok])
                qk = ysb[0:Q, 0:1536].rearrange("p (h d) -> p h d", d=64)
                cosb = ropeT[0:Q, ru * 16:ru * 16 + 8].unsqueeze(1).to_broadcast([Q, 24, 8])
                sinb = ropeT[0:Q, ru * 16 + 8:ru * 16 + 16].unsqueeze(1).to_broadcast([Q, 24, 8])
                t0_, t1_ = self.TMP.next(), self.LX.next()
                ta = t0_[0:Q, 0:192].rearrange("p (h d) -> p h d", d=8)
                tb = t0_[0:Q, 192:384].rearrange("p (h d) -> p h d", d=8)
                tc = t1_[0:Q, 0:192].rearrange("p (h d) -> p h d", d=8)
                td = t1_[0:Q, 192:384].rearrange("p (h d) -> p h d", d=8)
                x1, x2 = qk[:, :, 0:8], qk[:, :, 8:16]
                k.op("dve", lambda e: e.tensor_tensor(out=ta, in0=x1, in1=cosb, op=ALU.mult), r=[ysb, ropeT], w=[t0_])
                k.op("dve", lambda e: e.tensor_tensor(out=tb, in0=x2, in1=sinb, op=ALU.mult), r=[ysb, ropeT], w=[t0_])
                k.op("dve", lambda e: e.tensor_tensor(out=tc, in0=x2, in1=cosb, op=ALU.mult), r=[ysb, ropeT], w=[t1_])
                k.op("dve", lambda e: e.tensor_tensor(out=td, in0=x1, in1=sinb, op=ALU.mult), r=[ysb, ropeT], w=[t1_])
                k.op("dve", lambda e: e.tensor_tensor(out=x1, in0=ta, in1=tb, op=ALU.subtract), r=[t0_], w=[ysb])
                k.op("dve", lambda e: e.tensor_tensor(out=x2, in0=tc, in1=td, op=ALU.add), r=[t1_], w=[ysb])
                k.op("dve", lambda e: e.tensor_copy(out=ygn[0:Q, 0:1536], in_=ysb[0:Q, 0:1536]), r=[ysb], w=[ygn])
                k.op("act", lambda e: e.copy(out=vb[0:Q, 0:768], in_=vtok[0:Q, :]), r=[vtok], w=[vb])
                for part in range(2):
                    if (phase == "p1" and part == 0) or (phase == "p2" and part == 1):
                        continue
                    ps = self.PS.next()
                    pb = ps[:, :].bitcast(BF16)
                    for c in range(6):
                        k.op("pe", lambda e, pb=pb, c=c, part=part: e.transpose(pb[:, c * Q:(c + 1) * Q], ygn[0:Q, part * 768 + c * 128:part * 768 + (c + 1) * 128], ident_b[0:Q, 0:Q]),
                             r=[ygn, ident_b], w=[ps])
                    k.op("act", lambda e, pb=pb, part=part: e.copy(out=A[:, part * 6:(part + 1) * 6, 0:Q], in_=pb[:, 0:6 * Q].rearrange("p (c q) -> p c q", c=6)), r=[ps], w=[A])
                if phase == "p2":
                    pass
                elif kind == "p":
                    self.adma(self.kt_hist[:, :, q0:q0 + Q].rearrange("c p t -> p c t"), A[:, 6:12, 0:Q], r=[A], w=[self.kt_hist])
                    self.adma(self.v_hist[q0:q0 + Q, :], vb[0:Q, 0:768], r=[vb], w=[self.v_hist])
                    for g, (W, d) in enumerate(AGROUPS):
                        Wc = min(W, SEQ)
                        if q0 + Q > SEQ - Wc:
                            r0 = q0 - (SEQ - Wc)
                            self.adma(self.kvp_o[g][r0:r0 + Q, 0:256], ysb[0:Q, 768 + g * 256:768 + (g + 1) * 256], r=[ysb])
                            self.adma(self.kvp_o[g][r0:r0 + Q, 256:512], vtok[0:Q, g * 256:(g + 1) * 256], r=[vtok])
                else:
                    for g, (W, d) in enumerate(AGROUPS):
                        self.adma(self.kvs_o[g][s, W - Q:W, 0:256], ysb[0:Q, 768 + g * 256:768 + (g + 1) * 256], r=[ysb])
                        self.adma(self.kvs_o[g][s, W - Q:W, 256:512], vtok[0:Q, g * 256:(g + 1) * 256], r=[vtok])
                        self.adma(self.kvs_o[g][s, 0:W - Q, :], self.cache[g][s, Q:W, :])
                if phase == "p1":
                    return
            for g, (W, d) in enumerate(AGROUPS):
                nkp = 0
                if kind == "p":
                    nkp = max(0, W - q0)
                    lo = max(q0 - W, 0)
                    nkl = q0 + Q - lo
                    nk, moff = nkp + nkl, 0
                    if nkp > 0:
                        lk, lv = self.loc_k, self.loc_v
                        self.adma(KT_sb[:, :, 0:nkp], lk[2 * g * 128:(2 * g + 2) * 128, SEQ - nkp:SEQ].rearrange("(c p) t -> p c t", p=128), r=[lk], w=[KT_sb])
                        self.adma(V_sb[:, 0:nkp // 128, :], lv[SEQ - nkp:SEQ, g * 256:(g + 1) * 256].rearrange("(b p) f -> p b f", p=128), r=[lv], w=[V_sb])
                    self.adma(KT_sb[:, :, nkp:nk], self.kt_hist[2 * g:2 * g + 2, :, lo:lo + nkl].rearrange("c p t -> p c t"), r=[self.kt_hist], w=[KT_sb])
                    self.adma(V_sb[:, nkp // 128:nk // 128, :], self.v_hist[lo:lo + nkl, g * 256:(g + 1) * 256].rearrange("(b p) f -> p b f", p=128), r=[self.v_hist], w=[V_sb])
                else:
                    nk, moff = W + Q, 0
                    stg = self.xin.bufs[1]
                    for blk in range(W // 128):
                        self.adma(stg[:, 0:512], self.cache[g][s, blk * 128:(blk + 1) * 128, :], w=[stg])
                        k.op("act", lambda e, blk=blk: e.copy(out=V_sb[:, blk, :], in_=stg[:, 256:512]), r=[stg], w=[V_sb])
                        ps = self.PS.next()
                        for c in range(2):
                            k.op("pe", lambda e, ps=ps, c=c: e.transpose(ps[:, c * 128:(c + 1) * 128], stg[:, c * 128:(c + 1) * 128], cm[:, 0:128]), r=[stg, cm], w=[ps])
                        k.op("act", lambda e, ps=ps, blk=blk: e.copy(out=KT_sb[:, :, blk * 128:(blk + 1) * 128], in_=ps[:, 0:256].rearrange("p (c t) -> p c t", c=2)), r=[ps], w=[KT_sb])
                    k.op("act", lambda e, g=g, W=W: e.copy(out=KT_sb[:, :, W:W + Q], in_=A[:, 6 + 2 * g:8 + 2 * g, 0:Q]), r=[A], w=[KT_sb])
                    k.op("act", lambda e, g=g, W=W: e.copy(out=V_sb[0:Q, W // 128, :], in_=vb[0:Q, g * 256:(g + 1) * 256]), r=[vb], w=[V_sb])
                nkb = -(-nk // 128)
                mcol0 = (0, 256, 896)[g] + moff
                chunks = [(k0, min(512, nk - k0)) for k0 in range(0, nk, 512)]
                nch = len(chunks)
                prm_c, off_c, _ = self.P(("core",))

                def stage1(j, par, g=g, nk=nk, nkp=nkp, mcol0=mcol0, chunks=chunks, nch=nch):
                    hd = 4 * g + j
                    c, half = j // 2, j % 2
                    pl, ph = half * 64, half * 64 + 64
                    aw, Pp = self.astw[par], self.Pbs[par]
                    pss = []
                    for ci, (k0, n) in enumerate(chunks):
                        ps = self.PS.next()
                        pss.append(ps)
                        k.op("pe", lambda e, ps=ps, k0=k0, n=n, c=c, pl=pl, ph=ph: e.matmul(ps[0:Q, 0:n], lhsT=A[pl:ph, 2 * g + c, 0:Q], rhs=KT_sb[pl:ph, c, k0:k0 + n], start=True, stop=True),
                             r=[A, KT_sb], w=[ps])
                        k.op("dve", lambda e, ps=ps, k0=k0, n=n: e.tensor_tensor(out=ps[0:Q, 0:n], in0=ps[0:Q, 0:n], in1=maskb[0:Q, mcol0 + k0:mcol0 + k0 + n], op=ALU.add),
                             r=[ps, maskb], w=[ps])
                        if k0 < nkp:
                            n2 = min(n, nkp - k0)
                            k.op("dve", lambda e, ps=ps, n2=n2: e.tensor_scalar(out=ps[0:Q, 0:n2], in0=ps[0:Q, 0:n2], scalar1=prm_c[0:Q, off_c + 5:off_c + 6], scalar2=None, op0=ALU.add),
                                 r=[ps, prm_c], w=[ps])
                        k.op("dve", lambda e, ps=ps, n=n, ci=ci: e.reduce_max(out=aw[0:Q, ci:ci + 1], in_=ps[0:Q, 0:n], axis=AX.X), r=[ps], w=[aw])
                    k.op("dve", lambda e: e.reduce_max(out=ast[0:Q, hd:hd + 1], in_=aw[0:Q, 0:nch], axis=AX.X), r=[aw], w=[ast])
                    k.op("dve", lambda e: e.tensor_scalar(out=aw[0:Q, 7:8], in0=ast[0:Q, hd:hd + 1], scalar1=-0.125, scalar2=None, op0=ALU.mult), r=[ast], w=[aw])
                    k.op("dve", lambda e: e.memset(aw[0:Q, 0:6], 0.0), r=[aw], w=[aw])
                    for ci, (k0, n) in enumerate(chunks):
                        ps = pss[ci]
                        k.op("act", lambda e, ps=ps, k0=k0, n=n, ci=ci: e.activation(out=Pp[0:Q, k0:k0 + n], in_=ps[0:Q, 0:n], func=AF.Exp, bias=aw[0:Q, 7:8], scale=0.125,
                                                                                   accum_out=aw[0:Q, ci:ci + 1]), r=[ps, aw], w=[Pp, aw])
                    k.op("dve", lambda e: e.reduce_sum(out=ast[0:Q, 12 + hd:13 + hd], in_=aw[0:Q, 0:nch], axis=AX.X), r=[aw], w=[ast])

                def stage2(j, par, g=g, nk=nk, nkb=nkb):
                    hd = 4 * g + j
                    Pp = self.Pbs[par]
                    ups = self.PS.next()
                    for b0 in range(0, nkb, 4):
                        nb = min(4, nkb - b0)
                        pt = self.PS.next()
                        ptb = pt[:, :].bitcast(BF16)
                        for bb in range(nb):
                            b = b0 + bb
                            kn = min(128, nk - b * 128)
                            k.op("pe", lambda e, ptb=ptb, bb=bb, b=b, kn=kn: e.transpose(ptb[0:kn, bb * Q:(bb + 1) * Q], Pp[0:Q, b * 128:b * 128 + kn], ident_b[0:Q, 0:Q]), r=[Pp, ident_b], w=[pt])
                        MT = self.MT.next()
                        kmax = min(128, nk - b0 * 128)
                        k.op("act", lambda e, ptb=ptb, MT=MT, nb=nb, kmax=kmax: e.copy(out=MT[0:kmax, 0:nb * Q], in_=ptb[0:kmax, 0:nb * Q]), r=[pt], w=[MT])
                        for bb in range(nb):
                            b = b0 + bb
                            kn = min(128, nk - b * 128)
                            k.op("pe", lambda e, ups=ups, MT=MT, bb=bb, b=b, kn=kn: e.matmul(ups[0:Q, 0:64], lhsT=MT[0:kn, bb * Q:(bb + 1) * Q], rhs=V_sb[0:kn, b, j * 64:(j + 1) * 64],
                                                                                    start=(b == 0), stop=(b == nkb - 1)), r=[MT, V_sb], w=[ups])
                    k.op("act", lambda e, ups=ups: e.copy(out=U[0:Q, hd * 64:(hd + 1) * 64], in_=ups[0:Q, 0:64]), r=[ups], w=[U])

                for st_, j_ in (("s1", 0), ("s1", 1), ("s2", 0), ("s1", 2), ("s2", 1), ("s1", 3), ("s2", 2), ("s2", 3)):
                    (stage1 if st_ == "s1" else stage2)(j_, j_ % 2)
            m3 = ast[0:Q, 0:12].rearrange("p (g j) -> p g j", g=3)
            d3 = ast[0:Q, 12:24].rearrange("p (g j) -> p g j", g=3)
            w3 = ast[0:Q, 24:36].rearrange("p (g j) -> p g j", g=3)
            f3 = ast[0:Q, 44:56].rearrange("p (g j) -> p g j", g=3)
            Mx, Zs = ast[0:Q, 36:40], ast[0:Q, 40:44]
            k.op("dve", lambda e: e.tensor_tensor(out=Mx, in0=m3[:, 0, :], in1=m3[:, 1, :], op=ALU.max), r=[ast], w=[ast])
            k.op("dve", lambda e: e.tensor_tensor(out=Mx, in0=Mx, in1=m3[:, 2, :], op=ALU.max), r=[ast], w=[ast])
            k.op("dve", lambda e: e.tensor_tensor(out=w3, in0=m3, in1=Mx.unsqueeze(1).to_broadcast([Q, 3, 4]), op=ALU.subtract), r=[ast], w=[ast])
            k.op("act", lambda e: e.activation(out=w3, in_=w3, func=AF.Exp, scale=0.125), r=[ast], w=[ast])
            k.op("dve", lambda e: e.tensor_tensor(out=f3, in0=w3, in1=d3, op=ALU.mult), r=[ast], w=[ast])
            k.op("dve", lambda e: e.tensor_tensor(out=Zs, in0=f3[:, 0, :], in1=f3[:, 1, :], op=ALU.add), r=[ast], w=[ast])
            k.op("dve", lambda e: e.tensor_tensor(out=Zs, in0=Zs, in1=f3[:, 2, :], op=ALU.add), r=[ast], w=[ast])
            k.op("dve", lambda e: e.reciprocal(out=Zs, in_=Zs), r=[ast], w=[ast])
            k.op("dve", lambda e: e.tensor_tensor(out=f3, in0=w3, in1=Zs.unsqueeze(1).to_broadcast([Q, 3, 4]), op=ALU.mult), r=[ast], w=[ast])
            k.op("dve", lambda e: e.tensor_tensor(out=ygn[0:Q, 0:768].rearrange("p (h d) -> p h d", d=64), in0=U[0:Q, 0:768].rearrange("p (h d) -> p h d", d=64),
                                                  in1=ast[0:Q, 44:56].unsqueeze(2).to_broadcast([Q, 12, 64]), op=ALU.mult), r=[U, ast], w=[ygn])
            ps = self.PS.next()
            pb = ps[:, :].bitcast(BF16)
            for c in range(6):
                k.op("pe", lambda e, pb=pb, c=c: e.transpose(pb[:, c * Q:(c + 1) * Q], ygn[0:Q, c * 128:(c + 1) * 128], ident_b[0:Q, 0:Q]), r=[ygn, ident_b], w=[ps])
            k.op("act", lambda e, pb=pb: e.copy(out=act[:, 0:6, o:o + Q], in_=pb[:, 0:6 * Q].rearrange("p (c q) -> p c q", c=6)), r=[ps], w=[act])

        for u_ in units:
            unit_body(*u_)
        if phase != "p1":
            self.proj_fm(("att_out", i), act, 6, ntok, self.add_to_x(ntok))

    def mixer_ssd(self, i, ntok, tile, phase="all"):
        k, cfg = self.k, self.cfg
        kind, ti = tile
        do_proj, do_y = True, phase != "p1"
        n_ssm = sorted(self.HT).index(i)
        NSC = cfg.NSC
        if kind == "p":
            nseq, L = 1, ntok
            units = [(u * 128, 128, 0) for u in range(ntok // 128)]
        else:
            nseq, L = NSC, cfg.DEC
            units = [(s * L, L, s) for s in range(nseq)]
        A, h, zt = self.A, self.h, self.zt
        cm, cmr = self.cmat, self.cmat_r
        ident_b = self.ident_b
        tri_f = lambda Q: cm[0:Q, 128:128 + Q]
        negm = lambda Q: cm[0:Q, 384:384 + Q]
        tri_r = lambda Q: cmr[0:Q, 128:128 + Q]
        upp_r = lambda Q: cmr[0:Q, 256:256 + Q]
        blocks = self.wp.blocks[("ssm_in", i)]
        if do_proj:
            self.rmsnorm(("nmix", i), ntok, self.h)
            for b in (range(4) if do_y else ()):
                buf = self.wload(blocks[b])
                for ui, (o, Q, s) in enumerate(units):
                    ps = self.PS.next()
                    for kc in range(8):
                        k.op("pe", lambda e, ps=ps, kc=kc, o=o, Q=Q, buf=buf: e.matmul(ps[0:Q, 0:512], lhsT=h[:, kc, o:o + Q], rhs=buf[:, kc * 512:(kc + 1) * 512],
                                                                                       start=(kc == 0), stop=(kc == 7)), r=[h, buf], w=[ps])
                    k.op("act", lambda e, ps=ps, ui=ui, Q=Q, b=b: e.activation(out=zt[0:Q, ui, b * 512:(b + 1) * 512], in_=ps[0:Q, 0:512], func=AF.Silu), r=[ps], w=[zt])
        prm_cw, off_cw, _ = self.P(("ssm_convw", i))
        prm_cb, off_cb, _ = self.P(("ssm_convb", i))
        prm_dtb, off_dtb, _ = self.P(("ssm_dtb", i))
        hist = self.hist[i]
        convst, convout = self.convst, self.convout
        W3 = 3 + L

        def cons(col, ncols, ps):
            if col < 2048 + 3072:
                cc = (col - 2048) // 128
                pre = self.pre.next()
                pv = pre[:, 0:nseq * W3].rearrange("p (s t) -> p s t", s=nseq)
                if kind == "p":
                    k.op("act", lambda e: e.copy(out=pre[:, 0:3], in_=hist[:, cc, :]), r=[hist], w=[pre])
                else:
                    k.op("act", lambda e: e.copy(out=pv[:, :, 0:3], in_=convst[:, n_ssm, cc * nseq * 3:(cc + 1) * nseq * 3].rearrange("p (s t) -> p s t", s=nseq)),
                         r=[convst], w=[pre])
                k.op("act", lambda e: e.copy(out=pv[:, :, 3:3 + L], in_=ps[:, 0:ntok].rearrange("p (s t) -> p s t", s=nseq)), r=[ps], w=[pre])
                tmp = self.ctmp.next()
                tv = tmp[:, 0:ntok].rearrange("p (s t) -> p s t", s=nseq)
                k.op("dve", lambda e: e.tensor_scalar(out=tv, in0=pv[:, :, 0:L], scalar1=prm_cw[:, off_cw + cc * 4:off_cw + cc * 4 + 1],
                                                      scalar2=prm_cb[:, off_cb + cc:off_cb + cc + 1], op0=ALU.mult, op1=ALU.add), r=[pre, prm_cw], w=[tmp])
                for t in range(1, 4):
                    k.op("dve", lambda e, t=t: e.scalar_tensor_tensor(out=tv, in0=pv[:, :, t:t + L], scalar=prm_cw[:, off_cw + cc * 4 + t:off_cw + cc * 4 + t + 1],
                                                                      in1=tv, op0=ALU.mult, op1=ALU.add), r=[pre, tmp, prm_cw], w=[tmp])
                k.op("act", lambda e: e.activation(out=A[:, cc, 0:ntok], in_=tmp[:, 0:ntok], func=AF.Silu), r=[tmp], w=[A])
                if kind == "p":
                    k.op("act", lambda e: e.copy(out=hist[:, cc, :], in_=pre[:, L:L + 3]), r=[pre], w=[hist])
                else:
                    k.op("act", lambda e: e.copy(out=convout[:, cc * nseq * 3:(cc + 1) * nseq * 3].rearrange("p (s t) -> p s t", s=nseq), in_=pv[:, :, L:L + 3]),
                         r=[pre], w=[convout])
            else:
                dtx, dta, dtl, dtT, aT = self.ctmp.next(), self.pre.next(), self.ctmp.next(), self.dtT, self.aT
                k.op("dve", lambda e: e.tensor_scalar(out=dtx[0:32, 0:ntok], in0=ps[0:32, 0:ntok], scalar1=prm_dtb[0:32, off_dtb:off_dtb + 1], scalar2=None, op0=ALU.add),
                     r=[ps, prm_dtb], w=[dtx])
                k.op("dve", lambda e: e.scalar_tensor_tensor(out=dta[0:32, 0:ntok], in0=dtx[0:32, 0:ntok], scalar=-1.0, in1=dtx[0:32, 0:ntok], op0=ALU.mult, op1=ALU.max), r=[dtx], w=[dta])
                k.op("act", lambda e: e.activation(out=dta[0:32, 0:ntok], in_=dta[0:32, 0:ntok], func=AF.Exp, scale=-1.0), r=[dta], w=[dta])
                k.op("act", lambda e: e.activation(out=dtl[0:32, 0:ntok], in_=dta[0:32, 0:ntok], func=AF.Ln, bias=self.one_t[0:32, 0:1], scale=1.0), r=[dta, self.one_t], w=[dtl])
                k.op("dve", lambda e: e.scalar_tensor_tensor(out=dtT[:, 0:ntok], in0=dtx[0:32, 0:ntok], scalar=0.0, in1=dtl[0:32, 0:ntok], op0=ALU.max, op1=ALU.add),
                     r=[dtx, dtl], w=[dtT])
                k.op("dve", lambda e: e.tensor_scalar(out=aT[:, 0:ntok], in0=dtT[:, 0:ntok], scalar1=self.acol[0:32, n_ssm:n_ssm + 1], scalar2=None, op0=ALU.mult),
                     r=[dtT, self.acol], w=[aT])

        if do_proj:
            self.proj_fm(("ssm_in", i), h, 8, ntok, cons, col_lo=2048)
            if kind == "p" and ti == cfg.NT - 1 and phase == "p2":
                k.dma("pool", self.st_sem, self.conv_p_o[n_ssm], hist[:, :, :].rearrange("p c t -> p (c t)"), r=[hist])
            if kind == "s":
                k.dma("pool", self.st_sem, self.conv_s_o[n_ssm], convout[:, :], r=[convout])
        if phase == "p1" and ti == 0:
            k.op("dve", lambda e: e.memset(self.HT[i][:, :], 0.0), w=[self.HT[i]])
            k.op("dve", lambda e: e.memset(self.atsum[:, :], 0.0), w=[self.atsum])
        HT, HTb = self.HT[i], self.HTb
        prm_d, off_d, _ = self.P(("ssm_dbc", i))
        prm_nw, off_nw, _ = self.P(("ssm_normw", i))
        xs_tok, xdt, xdd, xsD, B_tok = self.xs_tok, self.xdt, self.xdd, self.xsD, self.B_tok
        a_tok, dt_tok, acs, eacs, dst, edec = self.a_tok, self.dt_tok, self.acs, self.eacs, self.dst, self.edec
        ysb, ygn, act = self.ysb, self.ygn, self.act
        v3 = lambda ap, a: ap.rearrange("p (a b) -> p a b", a=a)
        bc3 = lambda ap, n: ap.unsqueeze(2).to_broadcast([ap.shape[0], ap.shape[1], n])
        for ui, (o, Q, s) in enumerate(units):
            if kind == "s":
                k.dma("pool", self.st_sem, HT[:, :], self.ssm_in[n_ssm, s], w=[HT])
            if kind == "s" or (ti == 0 and ui == 0) or True:
                k.op("act", lambda e: e.copy(out=HTb[:, :], in_=HT[:, :]), r=[HT], w=[HTb])
            for half in range(2):
                ps = self.PS.next()
                pb = ps[:, :].bitcast(BF16)
                for c in range(8):
                    k.op("pe", lambda e, pb=pb, c=c, half=half, o=o, Q=Q: e.transpose(pb[0:Q, c * 128:(c + 1) * 128], A[:, half * 8 + c, o:o + Q], ident_b[:, :]),
                         r=[A, ident_b], w=[ps])
                k.op("act", lambda e, pb=pb, half=half, Q=Q: e.copy(out=xs_tok[0:Q, half * 1024:(half + 1) * 1024], in_=pb[0:Q, :]), r=[ps], w=[xs_tok])
            ps = self.PS.next()
            pb = ps[:, :].bitcast(BF16)
            for g in range(4):
                k.op("pe", lambda e, pb=pb, g=g, o=o, Q=Q: e.transpose(pb[0:Q, g * 128:(g + 1) * 128], A[:, 16 + g, o:o + Q], ident_b[:, :]), r=[A, ident_b], w=[ps])
            k.op("act", lambda e, pb=pb, Q=Q: e.copy(out=B_tok[0:Q, :], in_=pb[0:Q, 0:512]), r=[ps], w=[B_tok])
            ps = self.PS.next()
            k.op("pe", lambda e, ps=ps, o=o, Q=Q: e.transpose(ps[0:Q, 0:32], self.aT[:, o:o + Q], cm[0:32, 0:32]), r=[self.aT, cm], w=[ps])
            k.op("pe", lambda e, ps=ps, o=o, Q=Q: e.transpose(ps[0:Q, 32:64], self.dtT[:, o:o + Q], cm[0:32, 0:32]), r=[self.dtT, cm], w=[ps])
            k.op("dve", lambda e, ps=ps, Q=Q: e.tensor_copy(out=a_tok[0:Q, :], in_=ps[0:Q, 0:32]), r=[ps], w=[a_tok])
            k.op("dve", lambda e, ps=ps, Q=Q: e.tensor_copy(out=dt_tok[0:Q, :], in_=ps[0:Q, 32:64]), r=[ps], w=[dt_tok])
            ps = self.PS.next()
            k.op("pe", lambda e, ps=ps, Q=Q: e.matmul(ps[0:Q, 0:32], lhsT=tri_r(Q), rhs=a_tok[0:Q, :], start=True, stop=True), r=[cmr, a_tok], w=[ps])
            k.op("pe", lambda e, ps=ps, Q=Q: e.matmul(ps[0:Q, 32:64], lhsT=upp_r(Q), rhs=a_tok[0:Q, :], start=True, stop=True), r=[cmr, a_tok], w=[ps])
            k.op("pe", lambda e, ps=ps, Q=Q: e.matmul(ps[:, 64:96], lhsT=cmr[0:Q, 512:640], rhs=a_tok[0:Q, :], start=True, stop=True), r=[cmr, a_tok], w=[ps])
            k.op("dve", lambda e, ps=ps, Q=Q: e.tensor_copy(out=acs[0:Q, :], in_=ps[0:Q, 0:32]), r=[ps], w=[acs])
            k.op("act", lambda e, ps=ps, Q=Q: e.activation(out=eacs[0:Q, :], in_=ps[0:Q, 0:32], func=AF.Exp), r=[ps], w=[eacs])
            k.op("act", lambda e, ps=ps, Q=Q: e.activation(out=dst[0:Q, :], in_=ps[0:Q, 32:64], func=AF.Exp), r=[ps], w=[dst])
            k.op("act", lambda e, ps=ps: e.activation(out=edec[:, :], in_=ps[:, 64:96], func=AF.Exp), r=[ps], w=[edec])
            if phase == "p1":
                k.op("dve", lambda e, ps=ps: e.tensor_tensor(out=self.atsum[:, :], in0=self.atsum[:, :], in1=ps[:, 64:96], op=ALU.add), r=[self.atsum, ps], w=[self.atsum])
            k.op("dve", lambda e, Q=Q: e.tensor_tensor(out=v3(xdt[0:Q, :], 32), in0=v3(xs_tok[0:Q, :], 32), in1=bc3(dt_tok[0:Q, :], 64), op=ALU.mult), r=[xs_tok, dt_tok], w=[xdt])
            k.op("dve", lambda e, Q=Q: e.tensor_tensor(out=v3(xdd[0:Q, :], 32), in0=v3(xdt[0:Q, :], 32), in1=bc3(dst[0:Q, :], 64), op=ALU.mult), r=[xdt, dst], w=[xdd])
            k.op("dve", lambda e, Q=Q: e.tensor_tensor(out=v3(xsD[0:Q, :], 32), in0=v3(xs_tok[0:Q, :], 32), in1=bc3(prm_d[0:Q, off_d:off_d + 32], 64), op=ALU.mult),
                 r=[xs_tok, prm_d], w=[xsD])
            for g in range(4):
                if do_y:
                    psc = self.PS.next()
                    k.op("pe", lambda e, psc=psc, g=g, o=o, Q=Q: e.matmul(psc[0:Q, 0:Q], lhsT=A[:, 16 + g, o:o + Q], rhs=A[:, 20 + g, o:o + Q], start=True, stop=True), r=[A], w=[psc])
                    CBT = self.CBT.next()
                    k.op("act", lambda e, psc=psc, CBT=CBT, Q=Q: e.copy(out=CBT[0:Q, 0:Q], in_=psc[0:Q, 0:Q]), r=[psc], w=[CBT])
                    yps = self.PS.next()
                    for half in range(2):
                        hs = g * 8 + half * 4
                        Xd = self.XD.next()
                        k.op("dve", lambda e, Xd=Xd, hs=hs, Q=Q: e.tensor_tensor(out=v3(Xd[0:Q, 0:4 * Q], 4), in0=tri_f(Q).unsqueeze(1).to_broadcast([Q, 4, Q]),
                                                                                in1=bc3(a_tok[0:Q, hs:hs + 4], Q), op=ALU.mult), r=[cm, a_tok], w=[Xd])
                        psb = self.PS.next()
                        k.op("pe", lambda e, psb=psb, Xd=Xd, Q=Q: e.matmul(psb[0:Q, 0:4 * Q], lhsT=cmr[0:Q, 512:512 + Q], rhs=Xd[0:Q, 0:4 * Q], start=True, stop=True), r=[cmr, Xd], w=[psb])
                        tmp = self.TMP.next()
                        for hh in range(4):
                            k.op("dve", lambda e, psb=psb, tmp=tmp, hh=hh, hs=hs, Q=Q: e.scalar_tensor_tensor(
                                out=tmp[0:Q, hh * Q:(hh + 1) * Q], in0=psb[0:Q, hh * Q:(hh + 1) * Q], scalar=acs[0:Q, hs + hh:hs + hh + 1], in1=negm(Q),
                                op0=ALU.subtract, op1=ALU.add), r=[psb, acs, cm], w=[tmp])
                        Lx = self.LX.next()
                        k.op("act", lambda e, tmp=tmp, Lx=Lx, Q=Q: e.activation(out=Lx[0:Q, 0:4 * Q], in_=tmp[0:Q, 0:4 * Q], func=AF.Exp), r=[tmp], w=[Lx])
                        MT = self.MT.next()
                        k.op("dve", lambda e, Lx=Lx, MT=MT, CBT=CBT, Q=Q: e.tensor_tensor(out=v3(MT[0:Q, 0:4 * Q], 4), in0=v3(Lx[0:Q, 0:4 * Q], 4),
                                                                                        in1=CBT[0:Q, 0:Q].unsqueeze(1).to_broadcast([Q, 4, Q]), op=ALU.mult), r=[Lx, CBT], w=[MT])
                        for hh in range(4):
                            h8 = half * 4 + hh
                            hg = hs + hh
                            k.op("pe", lambda e, yps=yps, h8=h8, hg=hg, Q=Q: e.matmul(yps[0:Q, h8 * 64:(h8 + 1) * 64], lhsT=ident_b[0:Q, 0:Q], rhs=xsD[0:Q, hg * 64:(hg + 1) * 64],
                                                                                     start=True, stop=False), r=[ident_b, xsD], w=[yps])
                            k.op("pe", lambda e, yps=yps, h8=h8, hg=hg, hh=hh, MT=MT, Q=Q: e.matmul(yps[0:Q, h8 * 64:(h8 + 1) * 64], lhsT=MT[0:Q, hh * Q:(hh + 1) * Q],
                                                                                                   rhs=xdt[0:Q, hg * 64:(hg + 1) * 64], start=False, stop=True), r=[MT, xdt], w=[yps])
                    pso = self.PS.next()
                    k.op("pe", lambda e, pso=pso, g=g, o=o, Q=Q: e.matmul(pso[0:Q, 0:512], lhsT=A[:, 20 + g, o:o + Q], rhs=HTb[:, g * 512:(g + 1) * 512], start=True, stop=True), r=[A, HTb], w=[pso])
                    k.op("dve", lambda e, pso=pso, g=g, Q=Q: e.tensor_tensor(out=v3(ysb[0:Q, g * 512:(g + 1) * 512], 8), in0=v3(pso[0:Q, 0:512], 8), in1=bc3(eacs[0:Q, g * 8:(g + 1) * 8], 64), op=ALU.mult),
                         r=[pso, eacs], w=[ysb])
                    k.op("dve", lambda e, yps=yps, g=g, Q=Q: e.tensor_tensor(out=ysb[0:Q, g * 512:(g + 1) * 512], in0=ysb[0:Q, g * 512:(g + 1) * 512], in1=yps[0:Q, 0:512], op=ALU.add),
                         r=[ysb, yps], w=[ysb])
                psu = self.PS.next()
                k.op("pe", lambda e, psu=psu, g=g, Q=Q: e.matmul(psu[:, 0:512], lhsT=B_tok[0:Q, g * 128:(g + 1) * 128], rhs=xdd[0:Q, g * 512:(g + 1) * 512], start=True, stop=True), r=[B_tok, xdd], w=[psu])
                k.op("dve", lambda e, g=g: e.tensor_tensor(out=v3(HT[:, g * 512:(g + 1) * 512], 8), in0=v3(HT[:, g * 512:(g + 1) * 512], 8), in1=bc3(edec[:, g * 8:(g + 1) * 8], 64), op=ALU.mult),
                     r=[HT, edec], w=[HT])
                k.op("dve", lambda e, psu=psu, g=g: e.tensor_tensor(out=HT[:, g * 512:(g + 1) * 512], in0=HT[:, g * 512:(g + 1) * 512], in1=psu[:, 0:512], op=ALU.add), r=[HT, psu], w=[HT])
            if do_y:
                k.op("dve", lambda e, ui=ui, Q=Q: e.tensor_tensor(out=ysb[0:Q, :], in0=ysb[0:Q, :], in1=zt[0:Q, ui, :], op=ALU.mult), r=[ysb, zt], w=[ysb])
                k.op("dve", lambda e: e.memset(self.ss[:, :], 0.0), w=[self.ss])
                for g in range(4):
                    k.op("act", lambda e, g=g, Q=Q: e.activation(out=self.junk[0:Q, :], in_=ysb[0:Q, g * 512:(g + 1) * 512], func=AF.Square, accum_out=self.ss[0:Q, g:g + 1]),
                         r=[ysb], w=[self.junk, self.ss])
                k.op("act", lambda e, Q=Q: e.activation(out=self.rs[0:Q, 0:4], in_=self.ss[0:Q, 0:4], func=AF.Sqrt, bias=self.eps_t[0:Q, 0:1], scale=1.0 / 512), r=[self.ss, self.eps_t], w=[self.rs])
                k.op("dve", lambda e, Q=Q: e.reciprocal(out=self.rs[0:Q, 0:4], in_=self.rs[0:Q, 0:4]), r=[self.rs], w=[self.rs])
                k.op("dve", lambda e, Q=Q: e.tensor_tensor(out=v3(ygn[0:Q, :], 4), in0=v3(ysb[0:Q, :], 4), in1=bc3(self.rs[0:Q, 0:4], 512), op=ALU.mult), r=[ysb, self.rs], w=[ygn])
                for half in range(2):
                    ps = self.PS.next()
                    pb = ps[:, :].bitcast(BF16)
                    for c in range(8):
                        cc = half * 8 + c
                        k.op("pe", lambda e, pb=pb, c=c, cc=cc, Q=Q: e.transpose(pb[:, c * Q:(c + 1) * Q], ygn[0:Q, cc * 128:(cc + 1) * 128], ident_b[0:Q, 0:Q]), r=[ygn, ident_b], w=[ps])
                    for c in range(8):
                        cc = half * 8 + c
                        k.op("act", lambda e, pb=pb, c=c, cc=cc, o=o, Q=Q: e.activation(out=act[:, cc, o:o + Q], in_=pb[:, c * Q:(c + 1) * Q], func=AF.Copy, scale=prm_nw[:, off_nw + cc:off_nw + cc + 1]),
                             r=[ps, prm_nw], w=[act])
            if kind == "s":
                k.dma("pool", self.st_sem, self.ssm_s_o[n_ssm, s], HT[:, :], r=[HT])
        if kind == "p" and ti == cfg.NT - 1 and phase == "p2":
            k.dma("pool", self.st_sem, self.ssm_p_o[n_ssm], HT[:, :], r=[HT])
        if do_y:
            self.proj_fm(("ssm_out", i), act, 16, ntok, self.add_to_x(ntok))

    def sdma(self, out, in_, r=(), w=()):
        d = self.sc_sems[self.sc_i % 4]
        self.sc_i += 1
        self.k.dma("pool", d, out, in_, r=r, w=w)

    def xdma(self, out, in_, r=(), w=()):
        d = self.xs_sem[self.xs_i % 2]
        self.xs_i += 1
        self.k.dma("pool", d, out, in_, r=r, w=w)

    def ntok_of(self, ti):
        return self.cfg.TP if ti < self.cfg.NT else self.TS

    def load_xs(self, ti):
        n = self.ntok_of(ti)
        src = self.xscr[ti].t.rearrange("p (c t) -> p c t", c=8)[:, :, 0:n]
        self.xdma(self.x[:, :, 0:n], src, r=[self.xscr[ti]], w=[self.x])

    def store_xs(self, ti):
        n = self.ntok_of(ti)
        dst = self.xscr[ti].t.rearrange("p (c t) -> p c t", c=8)[:, :, 0:n]
        self.xdma(dst, self.x[:, :, 0:n], r=[self.x], w=[self.xscr[ti]])

    def prepass(self):
        cfg = self.cfg
        for ti in range(cfg.NT):
            self.load_x(self.xp[ti * cfg.TP:(ti + 1) * cfg.TP, :], cfg.TP)
            self.store_xs(ti)
        self.load_x(self.xs, self.TS)
        self.store_xs(cfg.NT)

    def finish_tile(self, i, ti):
        cfg = self.cfg
        n = self.ntok_of(ti)
        self.ffn(i, n)
        if i == cfg.DEPTH - 1:
            if ti < cfg.NT:
                self.store_y(self.y_p[ti * cfg.TP:(ti + 1) * cfg.TP, :], n)
            else:
                self.store_y(self.y_s, n)
        else:
            self.store_xs(ti)

    def sample_layer(self, i):
        cfg = self.cfg
        self.load_xs(cfg.NT)
        self.mixer(i, self.TS, ("s", 0))
        self.finish_tile(i, cfg.NT)

    def allgather(self, name, src_T, rows, cols, dtype, src_ap=None):
        k, cfg = self.k, self.cfg
        g = k.dram("g_" + name, (cfg.NCORES * rows, cols), dtype)
        d = k.dsem("cc_" + name, inc=1)
        sap = src_ap if src_ap is not None else src_T.t
        k.coll(d, "AllGather", sap.opt(), g.t.opt(), cfg.NCORES, r=[src_T], w=[g])
        return g

    def corecol(self, j):
        prm, off, _ = self.P(("core",))
        return prm[:, off + j:off + j + 1]

    def seg_weights(self, src_T, src_ap, n, scale):
        k = self.k
        sm = self.small
        prm = self.params
        self.sdma(sm[:, 0:4, 0:n], src_ap.rearrange("(q p) c -> p q c", p=128), r=[src_T], w=[sm])
        for m_ in range(4):
            k.op("dve", lambda e, m_=m_: e.tensor_scalar(out=sm[:, m_, 0:n], in0=sm[:, m_, 0:n], scalar1=self.corecol(m_), scalar2=None, op0=ALU.mult), r=[sm, prm], w=[sm])
        k.op("dve", lambda e: e.memset(sm[:, 7, 0:n], 0.0), w=[sm])
        k.op("dve", lambda e: e.tensor_copy(out=sm[:, 6, 0:n], in_=sm[:, 3, 0:n]), r=[sm], w=[sm])
        k.op("dve", lambda e: e.tensor_tensor(out=sm[:, 5, 0:n], in0=sm[:, 2, 0:n], in1=sm[:, 6, 0:n], op=ALU.add), r=[sm], w=[sm])
        k.op("dve", lambda e: e.tensor_tensor(out=sm[:, 4, 0:n], in0=sm[:, 1, 0:n], in1=sm[:, 5, 0:n], op=ALU.add), r=[sm], w=[sm])
        k.op("act", lambda e: e.activation(out=sm[:, 4:8, 0:n], in_=sm[:, 4:8, 0:n], func=AF.Exp, scale=scale), r=[sm], w=[sm])
        for q in range(4):
            k.op("dve", lambda e, q=q: e.tensor_scalar(out=sm[:, 4 + q, 0:n], in0=sm[:, 4 + q, 0:n], scalar1=self.corecol(q), scalar2=None, op0=ALU.mult), r=[sm, prm], w=[sm])

    def layer_ssd(self, i):
        k, cfg = self.k, self.cfg
        NT, TP = cfg.NT, cfg.TP
        n_ssm = sorted(self.HT).index(i)
        HT, hist = self.HT[i], self.hist[i]
        src = self.xscr[NT - 1].t.rearrange("p (c t) -> p c t", c=8)[:, :, TP - 4:TP]
        self.xdma(self.x[:, :, 0:4], src, r=[self.xscr[NT - 1]], w=[self.x])
        self.rmsnorm(("nmix", i), 4, self.h)
        tail = self.convout

        def cons(col, ncols, ps):
            cc = (col - 2048) // 128
            k.op("act", lambda e: e.copy(out=tail[:, cc * 3:cc * 3 + 3], in_=ps[:, 1:4]), r=[ps], w=[tail])
        self.proj_fm(("ssm_in", i), self.h, 8, 4, cons, col_lo=2048, col_hi=2048 + 3072)
        bt = k.dram(f"bnc_tail{i}", (128, 72), F32)
        self.sdma(bt[:, :], tail[:, 0:72], r=[tail], w=[bt])
        gt = self.allgather(f"tail{i}", bt, 128, 72, F32)
        self.sdma(hist[:, :, :].rearrange("p c t -> p (c t)"), lambda dyn: gt.t[bass.ds(dyn["prev"] * 128, 128), :], r=[gt], w=[hist])
        k.op("dve", lambda e: e.tensor_scalar(out=hist[:, :, :], in0=hist[:, :, :], scalar1=self.corecol(4), scalar2=None, op0=ALU.mult), r=[hist, self.params], w=[hist])
        hsave = self.histsave
        k.op("dve", lambda e: e.tensor_copy(out=hsave[:, :, :], in_=hist[:, :, :]), r=[hist], w=[hsave])
        for ti in range(NT):
            self.load_xs(ti)
            self.mixer_ssd(i, TP, ("p", ti), phase="p1")
        bh = k.dram(f"bnc_h{i}", (128, 2080), F32)
        self.sdma(bh[:, 0:2048], HT[:, :], r=[HT], w=[bh])
        self.sdma(bh[:, 2048:2080], self.atsum[:, :], r=[self.atsum], w=[bh])
        gh = self.allgather(f"h{i}", bh, 128, 2080, F32)
        self.sample_layer(i)
        lh = k.dram(f"loc_h{i}", (512, 2080), F32)
        self.sdma(lh[:, :], lambda dyn: gh.t[bass.ds(dyn["base"] * 128, 512), :], r=[gh], w=[lh])
        self.seg_weights(lh, lh[:, 2048:2080], 32, 1.0)
        k.op("dve", lambda e: e.memset(HT[:, :], 0.0), w=[HT])
        ysb, sm = self.ysb, self.small
        v3 = lambda ap, a: ap.rearrange("p (a b) -> p a b", a=a)
        for q in range(4):
            self.sdma(ysb[:, :], lh[q * 128:(q + 1) * 128, 0:2048], r=[lh], w=[ysb])
            k.op("dve", lambda e, q=q: e.tensor_tensor(out=v3(ysb[:, :], 32), in0=v3(ysb[:, :], 32), in1=sm[:, 4 + q, 0:32].unsqueeze(2).to_broadcast([128, 32, 64]), op=ALU.mult), r=[ysb, sm], w=[ysb])
            k.op("dve", lambda e: e.tensor_tensor(out=HT[:, :], in0=HT[:, :], in1=ysb[:, :], op=ALU.add), r=[HT, ysb], w=[HT])
        k.op("dve", lambda e: e.tensor_copy(out=hist[:, :, :], in_=hsave[:, :, :]), r=[hsave], w=[hist])
        for ti in range(NT):
            self.load_xs(ti)
            self.mixer_ssd(i, TP, ("p", ti), phase="p2")
            self.finish_tile(i, ti)

    def layer_gla(self, i):
        k, cfg = self.k, self.cfg
        NT, TP = cfg.NT, cfg.TP
        S = self.S
        for ti in range(NT):
            self.load_xs(ti)
            self.mixer_gla(i, TP, ("p", ti), phase="p1")
        bs = k.dram(f"bnc_s{i}", (128, 1028), F32)
        self.sdma(bs[:, 0:1024], S[:, :], r=[S], w=[bs])
        self.sdma(bs[:, 1024:1028], self.blsum[:, :], r=[self.blsum], w=[bs])
        gs = self.allgather(f"s{i}", bs, 128, 1028, F32)
        self.sample_layer(i)
        ls = k.dram(f"loc_s{i}", (512, 1028), F32)
        self.sdma(ls[:, :], lambda dyn: gs.t[bass.ds(dyn["base"] * 128, 512), :], r=[gs], w=[ls])
        self.seg_weights(ls, ls[:, 1024:1028], 4, 1.0 / 16)
        k.op("dve", lambda e: e.memset(S[:, :], 0.0), w=[S])
        ysb, sm = self.ysb, self.small
        v3 = lambda ap, a: ap.rearrange("p (a b) -> p a b", a=a)
        for q in range(4):
            self.sdma(ysb[:, 0:1024], ls[q * 128:(q + 1) * 128, 0:1024], r=[ls], w=[ysb])
            k.op("dve", lambda e, q=q: e.tensor_tensor(out=v3(ysb[:, 0:1024], 4), in0=v3(ysb[:, 0:1024], 4), in1=sm[:, 4 + q, 0:4].unsqueeze(2).to_broadcast([128, 4, 256]), op=ALU.mult), r=[ysb, sm], w=[ysb])
            k.op("dve", lambda e: e.tensor_tensor(out=S[:, :], in0=S[:, :], in1=ysb[:, 0:1024], op=ALU.add), r=[S, ysb], w=[S])
        k.op("act", lambda e: e.copy(out=self.Sb[:, :], in_=S[:, :]), r=[S], w=[self.Sb])
        for ti in range(NT):
            self.load_xs(ti)
            self.mixer_gla(i, TP, ("p", ti), phase="p2")
            self.finish_tile(i, ti)

    def layer_att(self, i):
        k, cfg = self.k, self.cfg
        NT, TP = cfg.NT, cfg.TP
        for ti in range(NT):
            self.load_xs(ti)
            self.mixer_att(i, TP, ("p", ti), phase="p1")
        self.Gk = self.allgather("kt", self.kt_hist, 6 * 128, cfg.SEQ, BF16, src_ap=self.kt_hist.t.rearrange("c p t -> (c p) t"))
        self.Gv = self.allgather("v", self.v_hist, cfg.SEQ, 768, BF16)
        self.sample_layer(i)
        self.loc_k = k.dram("loc_k", (768, cfg.SEQ), BF16)
        self.loc_v = k.dram("loc_v", (cfg.SEQ, 768), BF16)
        Gk, Gv, SEQ_ = self.Gk, self.Gv, cfg.SEQ
        self.sdma(self.loc_k[:, :], lambda dyn: Gk.t[bass.ds(dyn["prev"] * 768, 768), :], r=[Gk], w=[self.loc_k])
        self.sdma(self.loc_v[:, :], lambda dyn: Gv.t[bass.ds(dyn["prev"] * SEQ_, SEQ_), :], r=[Gv], w=[self.loc_v])
        for ti in range(NT):
            self.load_xs(ti)
            self.mixer_att(i, TP, ("p", ti), phase="p2")
            self.finish_tile(i, ti)

    def epilogue(self):
        self.k.wait_all("pool", self.out_sems)


def make_consts(cfg=None, sgi=0):
    i = np.arange(128)
    tri = (i[:, None] <= i[None, :]).astype(np.float32)
    upper = (i[:, None] > i[None, :]).astype(np.float32)
    negmask = np.where(i[None, :] >= i[:, None], 0.0, -1e30).astype(np.float32)
    cm = np.concatenate([np.eye(128, dtype=np.float32), tri, upper, negmask, np.ones((128, 128), np.float32)], axis=1)
    out = {"cmat": np.ascontiguousarray(cm)}
    if cfg is not None and cfg.DEPTH > 2:
        ms = []
        for (W, d) in AGROUPS:
            c = np.arange(W + 128)[None, :]
            diff = W + i[:, None] - c
            ok = (diff >= 0) & (diff <= W) & (diff % d == 0)
            ms.append(np.where(ok, 0.0, -30000.0).astype(np.float32))
        out["amask"] = np.ascontiguousarray(np.concatenate(ms, axis=1))
        NU = cfg.SEQ // 128
        pos = np.concatenate([sgi * cfg.SEQ + np.arange(cfg.SEQ), cfg.PAST + np.arange(128)]).astype(np.float32)
        inv_freq = (np.float32(ROPE_THETA) ** (-np.arange(0, AROT, 2, dtype=np.float32) / np.float32(AROT))).astype(np.float32)
        ang = (pos[:, None] * inv_freq[None, :]).astype(np.float32)
        tab = np.concatenate([np.cos(ang), np.sin(ang)], axis=1).astype(np.float32)
        out["rope"] = np.ascontiguousarray(tab.reshape(NU + 1, 128, 16).transpose(1, 0, 2)).reshape(128, (NU + 1) * 16)
    return out


def kernel(**inputs):
    return run(Cfg(), inputs)


def run(cfg, inputs, trace=False):
    prog = Prog(cfg)
    nc = prog.build()
    in_maps = make_in_maps(cfg, inputs)
    return launch(cfg, nc, in_maps, np.asarray(inputs["x_prompt"]).shape[0], trace)


def make_in_maps(cfg, inputs):
    wp = weight_plan(cfg.DEPTH, inputs)
    pp = param_plan(cfg.DEPTH, inputs)
    wcat, pcat0 = wp.cat(), pp.cat()
    f32 = lambda n: np.asarray(inputs[n], np.float32)
    xp, xs = f32("x_prompt"), f32("x_sample")
    B = xp.shape[0]
    NSC, NSEG, SEQ, NC = cfg.NSC, cfg.NSEG, cfg.SEQ, cfg.NCORES
    st_ssm, st_conv, st_gla = f32("state_ssm"), f32("state_ssm_conv"), f32("state_gla")
    caches = [f32(f"cache_kv_g{g}") for g in range(3)]
    coff = pp.off[("core",)][0]
    consts = [make_consts(cfg, sgi) for sgi in range(NSEG)]
    in_maps = []
    for c in range(NC):
        b_, sgi = c // NSEG, c % NSEG
        sl = slice(c * NSC, (c + 1) * NSC)
        pc = pcat0.copy()
        for m_ in range(4):
            pc[:, coff + m_] = 1.0 if m_ < sgi else 0.0
        pc[:, coff + 4] = 1.0 if sgi > 0 else 0.0
        pc[:, coff + 5] = 0.0 if sgi > 0 else -30000.0
        m = {"xp": np.ascontiguousarray(xp[b_, sgi * SEQ:(sgi + 1) * SEQ]), "xs": np.ascontiguousarray(xs[sl].reshape(-1, D)), "wcat": wcat, "pcat": pc,
             "rankinfo": np.array([[c - 1 if sgi > 0 else c, b_ * NSEG]], np.int32),
             "ssm_in_t": lay_ssm_in(st_ssm[:, sl]), "conv_in_fm": lay_conv_in(st_conv[:, sl]),
             "gla_in": np.ascontiguousarray(st_gla[0, sl])}
        nssm = len([i_ for i_ in range(cfg.DEPTH) if i_ % 3 == 0])
        m["ssm_in_t"], m["conv_in_fm"] = m["ssm_in_t"][:nssm], m["conv_in_fm"][:nssm]
        if cfg.DEPTH < 2:
            del m["gla_in"]
        for g in range(3 if cfg.DEPTH > 2 else 0):
            m[f"kvc{g}"] = np.ascontiguousarray(caches[g][0, sl].reshape(NSC, -1, 512))
        m.update(consts[sgi])
        in_maps.append(m)
    return in_maps


def launch(cfg, nc, in_maps, B, trace=False):
    NSC, NSEG, SEQ, NC = cfg.NSC, cfg.NSEG, cfg.SEQ, cfg.NCORES
    res = run_bass_kernel_spmd(nc, in_maps, core_ids=list(range(NC)), **({"trace": True} if trace else {}))
    if trace:
        print("exec_ns", res.exec_time_ns)
    R = res.results
    last = [b_ * NSEG + NSEG - 1 for b_ in range(B)]
    y_p = np.stack([np.concatenate([R[b_ * NSEG + sg]["y_p"] for sg in range(NSEG)]) for b_ in range(B)])
    y_s = np.concatenate([R[c]["y_s"].reshape(NSC, cfg.DEC, D) for c in range(NC)])
    ssm_p = np.stack([unlay_ssm(R[c]["ssm_p_t"]) for c in last], axis=1)
    ssm_s = np.concatenate([unlay_ssm(R[c]["ssm_s_t"]) for c in range(NC)], axis=1)
    conv_p = np.concatenate([unlay_conv(R[c]["conv_p_fm"], 1) for c in last], axis=1)
    conv_s = np.concatenate([unlay_conv(R[c]["conv_s_fm"], NSC) for c in range(NC)], axis=1)
    outs = [y_p, y_s, ssm_p, ssm_s, conv_p, conv_s]
    if cfg.DEPTH > 1:
        outs += [np.stack([R[c]["gla_p"] for c in last])[None], np.concatenate([R[c]["gla_s"] for c in range(NC)])[None]]
    for g in range(3 if cfg.DEPTH > 2 else 0):
        kp = np.stack([R[c][f"kvp{g}"].reshape(-1, 2, AHPG, ADH) for c in last])[None]
        ks = np.concatenate([R[c][f"kvs{g}"].reshape(NSC, -1, 2, AHPG, ADH) for c in range(NC)])[None]
        outs += [kp, ks]
    return tuple(np.ascontiguousarray(o, dtype=np.float32) for o in outs)


def lay_ssm_in(st):
    n, S = st.shape[:2]
    return np.ascontiguousarray(st.reshape(n, S, 2048, 128).transpose(0, 1, 3, 2))


def unlay_ssm(o):
    return np.ascontiguousarray(np.swapaxes(o, -1, -2)).reshape(o.shape[:-2] + (32, 64, 128))


def lay_conv_in(cv):
    n, S = cv.shape[:2]
    return np.ascontiguousarray(cv.reshape(n, S, 3, 24, 128).transpose(0, 4, 3, 1, 2)).reshape(n, 128, 24 * S * 3)


def unlay_conv(o, S):
    n = o.shape[0]
    return np.ascontiguousarray(o.reshape(n, 128, 24, S, 3).transpose(0, 3, 4, 2, 1)).reshape(n, S, 3, 3072)
```
